# Optimizing a Trainium2 kernel written in Bass

```python
import math
import jax
import jax.numpy as jnp
from jax import lax
import numpy as np

D_MODEL = 1024
BATCH = 32
SEQ = 2048
DEPTH = 2

N_A_LAYERS = DEPTH // 2
N_B_LAYERS = DEPTH - N_A_LAYERS
ALPHA = (2.0 * DEPTH) ** 0.25
BETA = (8.0 * DEPTH) ** -0.25
LN_EPS = 1e-5
RMS_EPS = 1e-6
ROPE_THETA = 10000.0
NEG_INF = -1e30
FORCE_SCORE = 1e9
TINY = 1e-30

A_HEADS = 16
A_NOPE = 64
A_ROPE = 32
A_V = 64
A_Q_LORA = D_MODEL // 2
A_KV_LORA = D_MODEL // 4
A_Q_BLOCK = 128

B_HEADS = 16
B_GROUPS = 4
B_HPG = B_HEADS // B_GROUPS
B_DK = 64
B_DV = 64
CMP_LEN = 32
CMP_STRIDE = 16
CMP_HID = 4 * B_DK
SEL_LEN = 64
SEL_N = 16
SEL_LOCAL = 2
WINDOW = 512
B_Q_BLOCK = 32

PEER_HEADS = 8
PEER_TOPK = 16
N_KEYS = 128
N_EXPERTS = N_KEYS * N_KEYS
PEER_DK = 256
PEER_CHUNK = 128

kernel_name = "yoco_mla_nsa_peer_deepnorm"


def layer_norm(x, g, b):
    xf = x.astype(jnp.float32)
    mu = jnp.mean(xf, axis=-1, keepdims=True)
    var = jnp.mean(jnp.square(xf - mu), axis=-1, keepdims=True)
    return ((xf - mu) * lax.rsqrt(var + LN_EPS) * g + b).astype(x.dtype)


def rms_norm(x, g):
    xf = x.astype(jnp.float32)
    r = lax.rsqrt(jnp.mean(jnp.square(xf), axis=-1, keepdims=True) + RMS_EPS)
    return (xf * r * g).astype(x.dtype)


def rope(x):
    s, d = x.shape[1], x.shape[-1]
    inv = ROPE_THETA ** (-jnp.arange(0, d, 2, dtype=jnp.float32) / d)
    ang = jnp.arange(s, dtype=jnp.float32)[:, None] * inv[None, :]
    cos = jnp.cos(ang)[None, :, None, :]
    sin = jnp.sin(ang)[None, :, None, :]
    xf = x.astype(jnp.float32)
    x1, x2 = xf[..., : d // 2], xf[..., d // 2:]
    return jnp.concatenate([x1 * cos - x2 * sin, x2 * cos + x1 * sin], axis=-1).astype(x.dtype)


def masked_softmax(sc, mask):
    sc = jnp.where(mask, sc, NEG_INF)
    m = jnp.max(sc, axis=-1, keepdims=True)
    e = jnp.exp(sc - m) * mask
    return e / jnp.maximum(jnp.sum(e, axis=-1, keepdims=True), TINY)


def to_blocks(t, block):
    b, s = t.shape[:2]
    return jnp.swapaxes(t.reshape((b, s // block, block) + t.shape[2:]), 0, 1)


def from_blocks(t):
    t = jnp.swapaxes(t, 0, 1)
    return t.reshape((t.shape[0], t.shape[1] * t.shape[2]) + t.shape[3:])


def mla_mixer(x, w_in, q_norm, kv_norm, w_q_up, w_kv_up, w_o):
    b, s, _ = x.shape
    proj = x @ w_in
    c_q = rms_norm(proj[..., :A_Q_LORA], q_norm)
    c_kv = rms_norm(proj[..., A_Q_LORA:A_Q_LORA + A_KV_LORA], kv_norm)
    k_pe = rope(proj[..., A_Q_LORA + A_KV_LORA:][:, :, None, :])[:, :, 0, :]
    q = (c_q @ w_q_up).reshape(b, s, A_HEADS, A_NOPE + A_ROPE)
    q_nope, q_pe = q[..., :A_NOPE], rope(q[..., A_NOPE:])
    kv = (c_kv @ w_kv_up).reshape(b, s, A_HEADS, A_NOPE + A_V)
    k_nope, v = kv[..., :A_NOPE], kv[..., A_NOPE:]
    scale = (A_NOPE + A_ROPE) ** -0.5
    key_pos = jnp.arange(s)

    def attend_block(args):
        i, qn, qp = args
        sc = (jnp.einsum("bqhd,bkhd->bhqk", qn, k_nope, preferred_element_type=jnp.float32)
              + jnp.einsum("bqhr,bkr->bhqk", qp, k_pe, preferred_element_type=jnp.float32)) * scale
        q_pos = i * A_Q_BLOCK + jnp.arange(A_Q_BLOCK)
        sc = jnp.where(key_pos[None, :] <= q_pos[:, None], sc, NEG_INF)
        p = jax.nn.softmax(sc, axis=-1).astype(v.dtype)
        return jnp.einsum("bhqk,bkhd->bqhd", p, v)

    o = lax.map(attend_block, (jnp.arange(s // A_Q_BLOCK), to_blocks(q_nope, A_Q_BLOCK), to_blocks(q_pe, A_Q_BLOCK)))
    return from_blocks(o).reshape(b, s, A_HEADS * A_V) @ w_o


def compress_tokens(tok, pos_emb, w1, b1, w2):
    b, s, g, d = tok.shape
    r = CMP_LEN // CMP_STRIDE
    nb = s // CMP_STRIDE
    n_cmp = nb - r + 1
    tb = tok.reshape(b, nb, CMP_STRIDE, g, d)
    blocks = jnp.concatenate([tb[:, j:j + n_cmp] for j in range(r)], axis=2)
    blocks = blocks + pos_emb[None, None, :, None, :]
    flat = blocks.transpose(0, 1, 3, 2, 4).reshape(b, n_cmp, g, CMP_LEN * d)
    hid = jax.nn.gelu(flat @ w1 + b1, approximate=False)
    return hid @ w2


def nsa_shared_kv(h, w_kv, pos_k, pos_v, ck_w1, ck_b1, ck_w2, cv_w1, cv_b1, cv_w2):
    b, s, _ = h.shape
    kv = (h @ w_kv).reshape(b, s, 3, B_GROUPS, B_DK + B_DV)
    k_cmp = compress_tokens(kv[:, :, 0, :, :B_DK], pos_k, ck_w1, ck_b1, ck_w2)
    v_cmp = compress_tokens(kv[:, :, 0, :, B_DK:], pos_v, cv_w1, cv_b1, cv_w2)
    n_blk = s // SEL_LEN
    k_slc = rope(kv[:, :, 1, :, :B_DK]).reshape(b, n_blk, SEL_LEN, B_GROUPS, B_DK).transpose(0, 3, 1, 2, 4)
    v_slc = kv[:, :, 1, :, B_DK:].reshape(b, n_blk, SEL_LEN, B_GROUPS, B_DV).transpose(0, 3, 1, 2, 4)
    pad = ((0, 0), (WINDOW, 0), (0, 0), (0, 0))
    k_win = jnp.pad(rope(kv[:, :, 2, :, :B_DK]), pad)
    v_win = jnp.pad(kv[:, :, 2, :, B_DK:], pad)
    return (k_cmp, v_cmp, k_slc, v_slc, k_win, v_win)


def nsa_mixer(x, shared, w_in, w_o):
    k_cmp, v_cmp, k_slc, v_slc, k_win, v_win = shared
    b, s, _ = x.shape
    proj = x @ w_in
    q = proj[..., :B_HEADS * B_DK].reshape(b, s, B_HEADS, B_DK)
    q_rot = rope(q).reshape(b, s, B_GROUPS, B_HPG, B_DK)
    q = q.reshape(b, s, B_GROUPS, B_HPG, B_DK)
    gates = jax.nn.sigmoid(proj[..., B_HEADS * B_DK:].astype(jnp.float32)).astype(x.dtype)
    gates = gates.reshape(b, s, 3, B_GROUPS, B_HPG)
    scale = B_DK ** -0.5
    n_cmp = k_cmp.shape[1]
    n_blk = k_slc.shape[2]
    n_sel = min(SEL_N, n_blk)
    cmp_start = jnp.arange(n_cmp) * CMP_STRIDE
    cmp_end = cmp_start + CMP_LEN - 1
    blk = jnp.arange(n_blk)
    blk_start = blk * SEL_LEN
    overlap = ((cmp_start[:, None] < blk_start[None, :] + SEL_LEN)
               & (cmp_end[:, None] >= blk_start[None, :])).astype(jnp.float32)
    b_idx = jnp.arange(b)[:, None, None, None]
    g_idx = jnp.arange(B_GROUPS)[None, :, None, None]
    win_off = jnp.arange(WINDOW + B_Q_BLOCK) - WINDOW

    def attend_block(args):
        i, qb, qrb, gb = args
        t = i * B_Q_BLOCK + jnp.arange(B_Q_BLOCK)
        sc = jnp.einsum("bqghd,bcgd->bghqc", qb, k_cmp, preferred_element_type=jnp.float32) * scale
        p_c = masked_softmax(sc, cmp_end[None, :] <= t[:, None])
        o_c = jnp.einsum("bghqc,bcgd->bqghd", p_c.astype(v_cmp.dtype), v_cmp)
        imp = jnp.einsum("bghqc,cn->bgqn", p_c, overlap)
        cur = t // SEL_LEN
        forced = (blk[None, :] == 0) | ((blk[None, :] <= cur[:, None]) & (blk[None, :] > cur[:, None] - SEL_LOCAL))
        imp = jnp.where(forced, FORCE_SCORE, imp)
        imp = jnp.where(blk[None, :] <= cur[:, None], imp, NEG_INF)
        _, sel = lax.top_k(imp, n_sel)
        k_g = k_slc[b_idx, g_idx, sel].reshape(b, B_GROUPS, B_Q_BLOCK, n_sel * SEL_LEN, B_DK)
        v_g = v_slc[b_idx, g_idx, sel].reshape(b, B_GROUPS, B_Q_BLOCK, n_sel * SEL_LEN, B_DV)
        tok = (sel[..., None] * SEL_LEN + jnp.arange(SEL_LEN)).reshape(b, B_GROUPS, 1, B_Q_BLOCK, n_sel * SEL_LEN)
        sc = jnp.einsum("bqghd,bgqkd->bghqk", qrb, k_g, preferred_element_type=jnp.float32) * scale
        p_s = masked_softmax(sc, tok <= t[:, None])
        o_s = jnp.einsum("bghqk,bgqkd->bqghd", p_s.astype(v_g.dtype), v_g)
        k_w = lax.dynamic_slice_in_dim(k_win, i * B_Q_BLOCK, WINDOW + B_Q_BLOCK, axis=1)
        v_w = lax.dynamic_slice_in_dim(v_win, i * B_Q_BLOCK, WINDOW + B_Q_BLOCK, axis=1)
        key_pos = i * B_Q_BLOCK + win_off
        m_w = ((key_pos[None, :] >= 0) & (key_pos[None, :] <= t[:, None])
               & (key_pos[None, :] > t[:, None] - WINDOW))
        sc = jnp.einsum("bqghd,bkgd->bghqk", qrb, k_w, preferred_element_type=jnp.float32) * scale
        p_w = masked_softmax(sc, m_w)
        o_w = jnp.einsum("bghqk,bkgd->bqghd", p_w.astype(v_w.dtype), v_w)
        return (gb[:, :, 0, :, :, None] * o_c + gb[:, :, 1, :, :, None] * o_s
                + gb[:, :, 2, :, :, None] * o_w)

    o = lax.map(attend_block, (jnp.arange(s // B_Q_BLOCK), to_blocks(q, B_Q_BLOCK),
                               to_blocks(q_rot, B_Q_BLOCK), to_blocks(gates, B_Q_BLOCK)))
    return from_blocks(o).reshape(b, s, B_HEADS * B_DV) @ w_o


def peer_ffn(x, w_q, subkeys, u, v):
    b, s, d = x.shape
    x_chunks = x.reshape(b * s // PEER_CHUNK, PEER_CHUNK, d)

    def chunk(xc):
        q = (xc @ w_q).reshape(PEER_CHUNK, PEER_HEADS, 2, PEER_DK // 2)
        sc = jnp.einsum("thpd,hpnd->thpn", q, subkeys, preferred_element_type=jnp.float32)
        s_half, i_half = lax.top_k(sc, PEER_TOPK)
        cand = (s_half[:, :, 0, :, None] + s_half[:, :, 1, None, :]).reshape(PEER_CHUNK, PEER_HEADS, PEER_TOPK * PEER_TOPK)
        cand_idx = (i_half[:, :, 0, :, None] * N_KEYS + i_half[:, :, 1, None, :]).reshape(PEER_CHUNK, PEER_HEADS, PEER_TOPK * PEER_TOPK)
        best, pos = lax.top_k(cand, PEER_TOPK)
        experts = jnp.take_along_axis(cand_idx, pos, axis=-1)
        g = jax.nn.softmax(best, axis=-1)
        u_e = u[experts]
        v_e = v[experts]
        a = jax.nn.gelu(jnp.einsum("thkd,td->thk", u_e, xc, preferred_element_type=jnp.float32), approximate=False)
        return jnp.einsum("thk,thkd->td", (g * a).astype(v.dtype), v_e)

    return lax.map(chunk, x_chunks).reshape(b, s, d)


def setup_inputs(seed: int = 0) -> dict:
    key = jax.random.key(seed)
    ks = jax.random.split(key, 24)

    def nrm(k, shape, scale):
        return jax.random.normal(k, shape, jnp.float32) * scale

    return {
        "x": nrm(ks[0], (BATCH, SEQ, D_MODEL), 1.0),
        "a_w_in": nrm(ks[1], (N_A_LAYERS, D_MODEL, A_Q_LORA + A_KV_LORA + A_ROPE), D_MODEL ** -0.5),
        "a_q_norm": 1.0 + nrm(ks[2], (N_A_LAYERS, A_Q_LORA), 0.02),
        "a_kv_norm": 1.0 + nrm(ks[3], (N_A_LAYERS, A_KV_LORA), 0.02),
        "a_w_q_up": nrm(ks[4], (N_A_LAYERS, A_Q_LORA, A_HEADS * (A_NOPE + A_ROPE)), A_Q_LORA ** -0.5),
        "a_w_kv_up": nrm(ks[5], (N_A_LAYERS, A_KV_LORA, A_HEADS * (A_NOPE + A_V)), A_KV_LORA ** -0.5),
        "a_w_o": nrm(ks[6], (N_A_LAYERS, A_HEADS * A_V, D_MODEL), BETA * (A_HEADS * A_V) ** -0.5),
        "b_w_in": nrm(ks[7], (N_B_LAYERS, D_MODEL, B_HEADS * B_DK + 3 * B_HEADS), D_MODEL ** -0.5),
        "b_w_o": nrm(ks[8], (N_B_LAYERS, B_HEADS * B_DV, D_MODEL), BETA * (B_HEADS * B_DV) ** -0.5),
        "s_w_kv": nrm(ks[9], (D_MODEL, 3 * B_GROUPS * (B_DK + B_DV)), D_MODEL ** -0.5),
        "s_cmp_pos_k": nrm(ks[10], (CMP_LEN, B_DK), 0.02),
        "s_cmp_pos_v": nrm(ks[11], (CMP_LEN, B_DV), 0.02),
        "s_cmp_k_w1": nrm(ks[12], (CMP_LEN * B_DK, CMP_HID), (CMP_LEN * B_DK) ** -0.5),
        "s_cmp_k_b1": nrm(ks[13], (CMP_HID,), 0.02),
        "s_cmp_k_w2": nrm(ks[14], (CMP_HID, B_DK), CMP_HID ** -0.5),
        "s_cmp_v_w1": nrm(ks[15], (CMP_LEN * B_DV, CMP_HID), (CMP_LEN * B_DV) ** -0.5),
        "s_cmp_v_b1": nrm(ks[16], (CMP_HID,), 0.02),
        "s_cmp_v_w2": nrm(ks[17], (CMP_HID, B_DV), CMP_HID ** -0.5),
        "p_w_q": nrm(ks[18], (DEPTH, D_MODEL, PEER_HEADS * PEER_DK), D_MODEL ** -0.5),
        "p_subkeys": nrm(ks[19], (DEPTH, PEER_HEADS, 2, N_KEYS, PEER_DK // 2), (PEER_DK // 2) ** -0.5),
        "p_u": nrm(ks[20], (DEPTH, N_EXPERTS, D_MODEL), D_MODEL ** -0.5),
        "p_v": nrm(ks[21], (DEPTH, N_EXPERTS, D_MODEL), BETA * PEER_HEADS ** -0.5),
        "ln_g": 1.0 + nrm(ks[22], (DEPTH, 2, D_MODEL), 0.02),
        "ln_b": nrm(ks[23], (DEPTH, 2, D_MODEL), 0.02),
    }


def reference(x, a_w_in, a_q_norm, a_kv_norm, a_w_q_up, a_w_kv_up, a_w_o, b_w_in, b_w_o,
              s_w_kv, s_cmp_pos_k, s_cmp_pos_v, s_cmp_k_w1, s_cmp_k_b1, s_cmp_k_w2,
              s_cmp_v_w1, s_cmp_v_b1, s_cmp_v_w2, p_w_q, p_subkeys, p_u, p_v, ln_g, ln_b):
    h = x
    shared = None
    for layer in range(DEPTH):
        if layer < N_A_LAYERS:
            mix = mla_mixer(h, a_w_in[layer], a_q_norm[layer], a_kv_norm[layer],
                            a_w_q_up[layer], a_w_kv_up[layer], a_w_o[layer])
        else:
            if layer == N_A_LAYERS:
                shared = nsa_shared_kv(h, s_w_kv, s_cmp_pos_k, s_cmp_pos_v, s_cmp_k_w1, s_cmp_k_b1,
                                       s_cmp_k_w2, s_cmp_v_w1, s_cmp_v_b1, s_cmp_v_w2)
            j = layer - N_A_LAYERS
            mix = nsa_mixer(h, shared, b_w_in[j], b_w_o[j])
        h = layer_norm(ALPHA * h + mix, ln_g[layer, 0], ln_b[layer, 0])
        ffn = peer_ffn(h, p_w_q[layer], p_subkeys[layer], p_u[layer], p_v[layer])
        h = layer_norm(ALPHA * h + ffn, ln_g[layer, 1], ln_b[layer, 1])
    return h
```

```python
import numpy as np
from contextlib import ExitStack, contextmanager
import concourse.bass as bass
import concourse.mybir as mybir
from concourse.bass_utils import run_bass_kernel_spmd

F32 = mybir.dt.float32
I32 = mybir.dt.int32
U32 = mybir.dt.uint32
BF16 = mybir.dt.bfloat16
AF = mybir.ActivationFunctionType
ALU = mybir.AluOpType
AX = mybir.AxisListType

SEQ = 2048
D = 1024
NT = 16
NCORES = 8
ALPHA = 4.0 ** 0.25
LN_EPS = 1e-5
RMS_EPS = 1e-6
NEG = -1e30


class Buf:
    __slots__ = ("name", "w", "r")

    def __init__(self, name=""):
        self.name = name
        self.w = None
        self.r = {}


class Tile:
    def __init__(self, t, name):
        self.t = t
        self.b = Buf(name)

    def __getitem__(self, k):
        return self.t[k]


class Sched:
    EPOCH = 20000

    def __init__(self, nc, es):
        self.nc = nc
        self.es = es
        self.eng = {"pe": nc.tensor, "act": nc.scalar, "dve": nc.vector, "pool": nc.gpsimd, "sp": nc.sync}
        self.cur = {}
        self.waited = {}
        self.nsem = 0
        self.ninst = 0
        for e in ("pe", "act", "dve", "pool"):
            self._new_epoch(e)
        self.rings = {}
        for q, n in (("sp", 24), ("pool", 12), ("act", 8)):
            self.rings[q] = [[self._sem(), 0] for _ in range(n)]
        self.ring_i = {q: 0 for q in self.rings}

    def _sem(self):
        self.nsem += 1
        return self.es.enter_context(self.nc.semaphore("s%d" % self.nsem))

    def _new_epoch(self, e):
        self.cur[e] = [self._sem(), 0]

    def _wait(self, e, tok, strict=False):
        sem, val, src = tok
        if src == e and e == "pe" and not strict:
            return
        key = (e, id(sem))
        if self.waited.get(key, 0) >= val:
            return
        self.eng[e].wait_ge(sem, val)
        self.ninst += 1
        self.waited[key] = val

    @staticmethod
    def _deps(r, w):
        toks = []
        for b in r:
            if b.w is not None:
                toks.append(b.w)
        for b in w:
            if b.w is not None:
                toks.append(b.w)
            toks.extend(b.r.values())
        return toks

    @staticmethod
    def _commit(tok, r, w, key):
        for b in w:
            b.w = tok
            b.r = {}
        for b in r:
            if b not in w:
                b.r[key] = tok

    def op(self, e, fn, r=(), w=()):
        for t in self._deps(r, w):
            self._wait(e, t)
        st = self.cur[e]
        if st[1] >= self.EPOCH:
            self._new_epoch(e)
            st = self.cur[e]
        ins = fn()
        st[1] += 1
        ins.then_inc(st[0], 1)
        self.ninst += 1
        tok = (st[0], st[1], e)
        self._commit(tok, r, w, e)
        return tok

    def dma(self, q, fn, r=(), w=()):
        ring = self.rings[q]
        i = self.ring_i[q]
        self.ring_i[q] = (i + 1) % len(ring)
        slot = ring[i]
        if slot[1] > 0:
            self._wait(q, (slot[0], slot[1], None), strict=True)
        for t in self._deps(r, w):
            self._wait(q, t, strict=True)
        ins = fn()
        slot[1] += 16
        ins.then_inc(slot[0], 16)
        self.ninst += 1
        tok = (slot[0], slot[1], None)
        self._commit(tok, r, w, ("dma", q, i))
        return tok

    def barrier(self):
        toks = []
        for e in ("pe", "act", "dve", "pool"):
            st = self.cur[e]
            if st[1] > 0:
                toks.append((st[0], st[1], e))
        for q, ring in self.rings.items():
            for slot in ring:
                if slot[1] > 0:
                    toks.append((slot[0], slot[1], None))
        for e in ("pe", "act", "dve", "pool", "sp"):
            for t in toks:
                self._wait(e, t, strict=(t[2] is None))


class Phase:
    def __init__(self, K):
        self.K = K
        self.es = ExitStack()

    def tile(self, name, shape, dtype=F32):
        self.K.uid += 1
        nm = "%s_%d" % (name, self.K.uid)
        return Tile(self.es.enter_context(self.K.nc.sbuf_tensor(nm, list(shape), dtype)), nm)


class Ctx:
    pass


def host_consts():
    c = {}
    c["ident"] = np.eye(128, dtype=np.float32)
    kk = np.arange(128)[:, None]
    qq = np.arange(128)[None, :]
    c["tri_le"] = (kk <= qq).astype(np.float32)
    c["tri_gt"] = (kk > qq).astype(np.float32)
    pos = np.arange(SEQ, dtype=np.float32)[:, None]
    for d in (32, 64):
        inv = (10000.0 ** (-np.arange(0, d, 2, dtype=np.float32) / d)).astype(np.float32)
        ang = (pos * inv[None, :]).astype(np.float32)
        c["rope%d" % d] = np.concatenate([np.cos(ang), np.sin(ang)], axis=1).astype(np.float32)
    c["iota16"] = np.tile(np.arange(16, dtype=np.float32)[None, :], (128, 1))
    cc = np.arange(128)
    tt = np.arange(SEQ)
    c["cmask"] = ((16 * cc[:, None] + 31 <= tt[None, :]) & (cc[:, None] < 127)).astype(np.float32)
    nn = np.arange(32)
    ovl = ((16 * cc[:, None] < 64 * nn[None, :] + 64) & (16 * cc[:, None] + 31 >= 64 * nn[None, :]) & (cc[:, None] < 127))
    c["overlap"] = ovl.astype(np.float32)
    cur = tt // 64
    forced = (nn[None, :] == 0) | ((nn[None, :] <= cur[:, None]) & (nn[None, :] > cur[:, None] - 2))
    valid = nn[None, :] <= cur[:, None]
    c["selA"] = ((~forced) & valid).astype(np.float32)
    c["selB"] = np.where(valid, np.where(forced, 1e9, 0.0), -1e30).astype(np.float32)
    E = np.zeros((32, 16, 128), np.float32)
    for kt in range(16):
        for k in range(128):
            E[(kt * 128 + k) // 64, kt, k] = 1.0
    c["Eall"] = E
    return c


def rope_tok(K, ph, x1, x2, o1, o2, cos_b, sin_b, shape, rbufs, wbufs, tmpname):
    nc, S = K.nc, K.S
    if not hasattr(ph, "rtmp"):
        ph.rtmp = {}
    key = tuple(shape)
    if key not in ph.rtmp:
        ph.rtmp[key] = (ph.tile("ropea", shape), ph.tile("ropeb", shape))
    ta, tb = ph.rtmp[key]
    sl = tuple(slice(None) for _ in shape)
    S.op("dve", lambda: nc.vector.tensor_tensor(out=ta[sl], in0=x1, in1=cos_b, op=ALU.mult), r=rbufs, w=[ta.b])
    S.op("dve", lambda: nc.vector.tensor_tensor(out=tb[sl], in0=x2, in1=sin_b, op=ALU.mult), r=rbufs, w=[tb.b])
    S.op("dve", lambda: nc.vector.tensor_tensor(out=o1, in0=ta[sl], in1=tb[sl], op=ALU.subtract), r=[ta.b, tb.b], w=wbufs)
    S.op("dve", lambda: nc.vector.tensor_tensor(out=ta[sl], in0=x2, in1=cos_b, op=ALU.mult), r=rbufs, w=[ta.b])
    S.op("dve", lambda: nc.vector.tensor_tensor(out=tb[sl], in0=x1, in1=sin_b, op=ALU.mult), r=rbufs, w=[tb.b])
    S.op("dve", lambda: nc.vector.tensor_tensor(out=o2, in0=ta[sl], in1=tb[sl], op=ALU.add), r=[ta.b, tb.b], w=wbufs)


def evac(K, i, out, in_, r, w):
    nc, S = K.nc, K.S
    if i % 2 == 0:
        S.op("act", lambda: nc.scalar.copy(out=out, in_=in_), r=r, w=w)
    else:
        S.op("dve", lambda: nc.vector.tensor_copy(out=out, in_=in_), r=r, w=w)


def transpose_to(K, src_tile, src_aps, dst_tile, dst_ap_fn, rows, group=4):
    nc, S = K.nc, K.S
    n = len(src_aps)
    for g0 in range(0, n, group):
        cnt = min(group, n - g0)
        bank = K.bank()
        for j in range(cnt):
            ap = src_aps[g0 + j]
            S.op("pe", lambda ap=ap, j=j: nc.tensor.transpose(out=bank[0:rows, j * 128:(j + 1) * 128], in_=ap,
                                                               identity=K.ident[:, :]),
                 r=[src_tile.b, K.ident.b], w=[bank.b])
        evac(K, K.evi, dst_ap_fn(g0, cnt), bank[0:rows, 0:cnt * 128], r=[bank.b], w=[dst_tile.b])
        K.evi += 1


def layer_norm(K, ph, y, g_bc, b_bc, out, tag):
    nc, S = K.nc, K.S
    st = ph.tile("lnst" + tag, [128, 2, 6])
    mv = ph.tile("lnmv" + tag, [128, 2])
    sd = ph.tile("lnsd" + tag, [128, 1])
    rs = ph.tile("lnrs" + tag, [128, 1])
    for j in range(2):
        S.op("dve", lambda j=j: nc.vector.bn_stats(out=st[:, j, :], in_=y[:, j * 512:(j + 1) * 512]), r=[y.b], w=[st.b])
    S.op("dve", lambda: nc.vector.bn_aggr(out=mv[:, :], in_=st[:, :, :].rearrange("p a b -> p (a b)")), r=[st.b], w=[mv.b])
    S.op("act", lambda: nc.scalar.activation(out=sd[:, :], in_=mv[:, 1:2], func=AF.Sqrt, bias=K.eps_ln[:, :], scale=1.0),
         r=[mv.b, K.eps_ln.b], w=[sd.b])
    S.op("dve", lambda: nc.vector.reciprocal(out=rs[:, :], in_=sd[:, :]), r=[sd.b], w=[rs.b])
    S.op("dve", lambda: nc.vector.tensor_scalar(out=out[:, :], in0=y[:, :], scalar1=mv[:, 0:1], scalar2=rs[:, 0:1],
                                                op0=ALU.subtract, op1=ALU.mult), r=[y.b, mv.b, rs.b], w=[out.b])
    S.op("dve", lambda: nc.vector.tensor_tensor(out=out[:, :], in0=out[:, :], in1=g_bc[:, :], op=ALU.mult),
         r=[out.b, g_bc.b], w=[out.b])
    S.op("dve", lambda: nc.vector.tensor_tensor(out=out[:, :], in0=out[:, :], in1=b_bc[:, :], op=ALU.add),
         r=[out.b, b_bc.b], w=[out.b])


def load_bc(K, ph, name, dram_row_ap, n):
    nc, S = K.nc, K.S
    t = ph.tile(name, [128, n])
    S.dma("sp", lambda: nc.sync.dma_start(out=t[:, :], in_=dram_row_ap.to_broadcast([128, n])), w=[t.b])
    return t


def phase_a1(K, s):
    nc, S, d = K.nc, K.S, K.d
    ph = Phase(K)
    with ph.es:
        w_in = ph.tile("w_in", [128, 8, 800])
        S.dma("sp", lambda: nc.sync.dma_start(out=w_in[:, :, :], in_=d["a_w_in"].rearrange("(c p) n -> p c n", p=128)), w=[w_in.b])
        wq = ph.tile("wq", [128, 4, 1536])
        S.dma("sp", lambda: nc.sync.dma_start(out=wq[:, :, :], in_=d["a_w_q_up"].rearrange("(c p) n -> p c n", p=128)), w=[wq.b])
        wkv = ph.tile("wkv", [128, 2, 2048])
        S.dma("sp", lambda: nc.sync.dma_start(out=wkv[:, :, :], in_=d["a_w_kv_up"].rearrange("(c p) n -> p c n", p=128)), w=[wkv.b])
        qn_bc = load_bc(K, ph, "qn_bc", d["a_q_norm"], 512)
        kvn_bc = load_bc(K, ph, "kvn_bc", d["a_kv_norm"], 256)
        xs = [ph.tile("x%d" % i, [128, 1024]) for i in range(2)]
        css = [ph.tile("cs%d" % i, [128, 32]) for i in range(2)]
        xT = ph.tile("xT", [128, 8, 128])
        junk = ph.tile("junk", [128, 512])
        ss = ph.tile("ss", [128, 2])
        rs = ph.tile("rs", [128, 2])
        rr = ph.tile("rr", [128, 2])
        cq = ph.tile("cq", [128, 512])
        ckv = ph.tile("ckv", [128, 256])
        kraw = ph.tile("kraw", [128, 32])
        kpe = ph.tile("kpe", [128, 32])
        cqT = ph.tile("cqT", [128, 4, 128])
        ckvT = ph.tile("ckvT", [128, 2, 128])
        q_sb = ph.tile("q_sb", [128, 16, 96])
        qpe = ph.tile("qpe", [128, 16, 32])
        kv_sb = ph.tile("kv_sb", [128, 16, 128])
        vps = [ph.tile("vp%d" % i, [128, 16, 65], BF16) for i in range(2)]
        qnTs = [ph.tile("qnT%d" % i, [64, 16, 128], BF16) for i in range(2)]
        qpTs = [ph.tile("qpT%d" % i, [32, 16, 128], BF16) for i in range(2)]
        knTs = [ph.tile("knT%d" % i, [64, 16, 128], BF16) for i in range(2)]
        kpTs = [ph.tile("kpT%d" % i, [32, 128], BF16) for i in range(2)]
        for v in vps:
            S.op("dve", lambda v=v: nc.vector.memset(v[:, :, 64:65], 1.0), w=[v.b])
        for t in range(NT):
            p = t % 2
            t0 = t * 128
            x, cs, vp, qnT, qpT, knT, kpT = xs[p], css[p], vps[p], qnTs[p], qpTs[p], knTs[p], kpTs[p]
            S.dma("sp", lambda: nc.sync.dma_start(out=x[:, :], in_=d["x"][s, t0:t0 + 128, :]), w=[x.b])
            S.dma("sp", lambda: nc.sync.dma_start(out=cs[:, :], in_=d["rope32"][t0:t0 + 128, :]), w=[cs.b])
            transpose_to(K, x, [x[:, c * 128:(c + 1) * 128] for c in range(8)], xT,
                         lambda g0, n: xT[:, g0:g0 + n, :].rearrange("p a b -> p (a b)"), 128)
            bA, bB = K.bank(), K.bank()
            for c in range(8):
                S.op("pe", lambda c=c: nc.tensor.matmul(bA[:, 0:512], lhsT=xT[:, c, :], rhs=w_in[:, c, 0:512],
                                                        start=(c == 0), stop=(c == 7)), r=[xT.b, w_in.b], w=[bA.b])
            for c in range(8):
                S.op("pe", lambda c=c: nc.tensor.matmul(bB[:, 0:288], lhsT=xT[:, c, :], rhs=w_in[:, c, 512:800],
                                                        start=(c == 0), stop=(c == 7)), r=[xT.b, w_in.b], w=[bB.b])
            S.op("act", lambda: nc.scalar.activation(out=junk[:, 0:512], in_=bA[:, 0:512], func=AF.Square,
                                                     accum_out=ss[:, 0:1]), r=[bA.b], w=[junk.b, ss.b])
            S.op("act", lambda: nc.scalar.activation(out=junk[:, 0:256], in_=bB[:, 0:256], func=AF.Square,
                                                     accum_out=ss[:, 1:2]), r=[bB.b], w=[junk.b, ss.b])
            S.op("act", lambda: nc.scalar.activation(out=rs[:, 0:1], in_=ss[:, 0:1], func=AF.Sqrt, bias=K.eps_rms[:, :],
                                                     scale=1.0 / 512), r=[ss.b, K.eps_rms.b], w=[rs.b])
            S.op("act", lambda: nc.scalar.activation(out=rs[:, 1:2], in_=ss[:, 1:2], func=AF.Sqrt, bias=K.eps_rms[:, :],
                                                     scale=1.0 / 256), r=[ss.b, K.eps_rms.b], w=[rs.b])
            S.op("dve", lambda: nc.vector.reciprocal(out=rr[:, :], in_=rs[:, :]), r=[rs.b], w=[rr.b])
            S.op("dve", lambda: nc.vector.scalar_tensor_tensor(out=cq[:, :], in0=bA[:, 0:512], scalar=rr[:, 0:1],
                                                               in1=qn_bc[:, :], op0=ALU.mult, op1=ALU.mult),
                 r=[bA.b, rr.b, qn_bc.b], w=[cq.b])
            S.op("dve", lambda: nc.vector.scalar_tensor_tensor(out=ckv[:, :], in0=bB[:, 0:256], scalar=rr[:, 1:2],
                                                               in1=kvn_bc[:, :], op0=ALU.mult, op1=ALU.mult),
                 r=[bB.b, rr.b, kvn_bc.b], w=[ckv.b])
            S.op("act", lambda: nc.scalar.copy(out=kraw[:, :], in_=bB[:, 256:288]), r=[bB.b], w=[kraw.b])
            rope_tok(K, ph, kraw[:, 0:16], kraw[:, 16:32], kpe[:, 0:16], kpe[:, 16:32], cs[:, 0:16], cs[:, 16:32],
                     [128, 16], [kraw.b, cs.b], [kpe.b], "rk%d" % t)
            transpose_to(K, cq, [cq[:, c * 128:(c + 1) * 128] for c in range(4)], cqT,
                         lambda g0, n: cqT[:, g0:g0 + n, :].rearrange("p a b -> p (a b)"), 128)
            transpose_to(K, ckv, [ckv[:, c * 128:(c + 1) * 128] for c in range(2)], ckvT,
                         lambda g0, n: ckvT[:, g0:g0 + n, :].rearrange("p a b -> p (a b)"), 128)
            q_flat = q_sb[:, :, :].rearrange("p a b -> p (a b)")
            for n in range(3):
                bk = K.bank()
                for c in range(4):
                    S.op("pe", lambda c=c, n=n, bk=bk: nc.tensor.matmul(bk[:, 0:512], lhsT=cqT[:, c, :],
                                                                         rhs=wq[:, c, n * 512:(n + 1) * 512],
                                                                         start=(c == 0), stop=(c == 3)),
                         r=[cqT.b, wq.b], w=[bk.b])
                evac(K, n, q_flat[:, n * 512:(n + 1) * 512], bk[:, 0:512], r=[bk.b], w=[q_sb.b])
            kv_flat = kv_sb[:, :, :].rearrange("p a b -> p (a b)")
            for n in range(4):
                bk = K.bank()
                for c in range(2):
                    S.op("pe", lambda c=c, n=n, bk=bk: nc.tensor.matmul(bk[:, 0:512], lhsT=ckvT[:, c, :],
                                                                         rhs=wkv[:, c, n * 512:(n + 1) * 512],
                                                                         start=(c == 0), stop=(c == 1)),
                         r=[ckvT.b, wkv.b], w=[bk.b])
                evac(K, n + 1, kv_flat[:, n * 512:(n + 1) * 512], bk[:, 0:512], r=[bk.b], w=[kv_sb.b])
            cos_b = cs[:, 0:16].unsqueeze(1).to_broadcast([128, 16, 16])
            sin_b = cs[:, 16:32].unsqueeze(1).to_broadcast([128, 16, 16])
            rope_tok(K, ph, q_sb[:, :, 64:80], q_sb[:, :, 80:96], qpe[:, :, 0:16], qpe[:, :, 16:32], cos_b, sin_b,
                     [128, 16, 16], [q_sb.b, cs.b], [qpe.b], "rq%d" % t)
            S.op("act", lambda: nc.scalar.copy(out=vp[:, :, 0:64], in_=kv_sb[:, :, 64:128]), r=[kv_sb.b], w=[vp.b])
            transpose_to(K, q_sb, [q_sb[:, h, 0:64] for h in range(16)], qnT,
                         lambda g0, n: qnT[:, g0:g0 + n, :].rearrange("p a b -> p (a b)"), 64)
            transpose_to(K, qpe, [qpe[:, h, :] for h in range(16)], qpT,
                         lambda g0, n: qpT[:, g0:g0 + n, :].rearrange("p a b -> p (a b)"), 32)
            transpose_to(K, kv_sb, [kv_sb[:, h, 0:64] for h in range(16)], knT,
                         lambda g0, n: knT[:, g0:g0 + n, :].rearrange("p a b -> p (a b)"), 64)
            transpose_to(K, kpe, [kpe[:, :]], kpT, lambda g0, n: kpT[:, :], 32)
            wb = [K.db("qk", s, t)]
            S.dma("sp", lambda: nc.sync.dma_start(out=d["qnT"][:, :, t0:t0 + 128].rearrange("h e t -> e h t"),
                                                  in_=qnT[:, :, :]), r=[qnT.b], w=wb)
            S.dma("sp", lambda: nc.sync.dma_start(out=d["qpT"][:, :, t0:t0 + 128].rearrange("h e t -> e h t"),
                                                  in_=qpT[:, :, :]), r=[qpT.b], w=wb)
            S.dma("sp", lambda: nc.sync.dma_start(out=d["knT"][:, :, t0:t0 + 128].rearrange("h e t -> e h t"),
                                                  in_=knT[:, :, :]), r=[knT.b], w=wb)
            S.dma("sp", lambda: nc.sync.dma_start(out=d["kpT"][:, t0:t0 + 128], in_=kpT[:, :]), r=[kpT.b], w=wb)
            S.dma("sp", lambda: nc.sync.dma_start(out=d["vp"][:, t0:t0 + 128, :].rearrange("h t e -> t h e"),
                                                  in_=vp[:, :, :]), r=[vp.b], w=wb)
        S.barrier()


def attn_core(K, ph, pairs_q, pairs_k, vp, nkeys_tile, scale, o_sb_fn, store_fn, tag):
    pass


def phase_a2(K, s):
    nc, S, d = K.nc, K.S, K.d
    scale = 96.0 ** -0.5
    ph = Phase(K)
    with ph.es:
        kp = ph.tile("kp", [32, SEQ], BF16)
        allqk = [K.db("qk", s, t) for t in range(NT)]
        S.dma("sp", lambda: nc.sync.dma_start(out=kp[:, :], in_=d["kpT"][:, :]), r=allqk, w=[kp.b])
        qns = [ph.tile("qn%d" % i, [64, SEQ], BF16) for i in range(2)]
        qps = [ph.tile("qp%d" % i, [32, SEQ], BF16) for i in range(2)]
        kns = [ph.tile("kn%d" % i, [64, SEQ], BF16) for i in range(2)]
        vpt = [ph.tile("vpa%d" % i, [128, 16, 65], BF16) for i in range(2)]
        pts = [ph.tile("pt%d" % i, [128, 512], BF16) for i in range(3)]
        osb = [ph.tile("osb%d" % i, [128, 4, 64]) for i in range(2)]
        rden = ph.tile("rden", [128, 4])
        ctr = 0
        oc = 0
        for h in range(16):
            p = h % 2
            qn, qp, kn, vp = qns[p], qps[p], kns[p], vpt[p]
            S.dma("sp", lambda: nc.sync.dma_start(out=qn[:, :], in_=d["qnT"][h, :, :]), r=allqk, w=[qn.b])
            S.dma("sp", lambda: nc.sync.dma_start(out=qp[:, :], in_=d["qpT"][h, :, :]), r=allqk, w=[qp.b])
            S.dma("sp", lambda: nc.sync.dma_start(out=kn[:, :], in_=d["knT"][h, :, :]), r=allqk, w=[kn.b])
            S.dma("sp", lambda: nc.sync.dma_start(out=vp[:, :, :], in_=d["vp"][h, :, :].rearrange("(k p) e -> p k e", p=128)),
                  r=allqk, w=[vp.b])
            for qc in range(4):
                ob = [K.ps[4 + j] for j in range(4)]
                nk = 4 * qc + 4
                for kt in range(nk):
                    st = K.ps[ctr % 4]
                    pt = pts[ctr % 3]
                    ctr += 1
                    S.op("pe", lambda: nc.tensor.matmul(st[:, 0:512], lhsT=kn[:, kt * 128:(kt + 1) * 128],
                                                        rhs=qn[:, qc * 512:(qc + 1) * 512], start=True, stop=False),
                         r=[kn.b, qn.b], w=[st.b])
                    S.op("pe", lambda: nc.tensor.matmul(st[:, 0:512], lhsT=kp[:, kt * 128:(kt + 1) * 128],
                                                        rhs=qp[:, qc * 512:(qc + 1) * 512], start=False, stop=True),
                         r=[kp.b, qp.b], w=[st.b])
                    j0 = max(0, kt - 4 * qc)
                    S.op("act", lambda: nc.scalar.activation(out=pt[:, j0 * 128:512], in_=st[:, j0 * 128:512], func=AF.Exp,
                                                             scale=scale), r=[st.b], w=[pt.b])
                    if kt >= 4 * qc:
                        S.op("dve", lambda: nc.vector.tensor_tensor(out=pt[:, j0 * 128:(j0 + 1) * 128],
                                                                    in0=pt[:, j0 * 128:(j0 + 1) * 128],
                                                                    in1=K.tri_le[:, :], op=ALU.mult),
                             r=[pt.b, K.tri_le.b], w=[pt.b])
                    for j in range(j0, 4):
                        qt = 4 * qc + j
                        S.op("pe", lambda j=j, qt=qt: nc.tensor.matmul(ob[j][:, 0:65], lhsT=pt[:, j * 128:(j + 1) * 128],
                                                                       rhs=vp[:, kt, :], start=(kt == 0), stop=(kt == qt)),
                             r=[pt.b, vp.b], w=[ob[j].b])
                o = osb[oc % 2]
                oc += 1
                for j in range(4):
                    S.op("dve", lambda j=j: nc.vector.reciprocal(out=rden[:, j:j + 1], in_=ob[j][:, 64:65]),
                         r=[ob[j].b], w=[rden.b])
                    S.op("dve", lambda j=j: nc.vector.tensor_scalar(out=o[:, j, :], in0=ob[j][:, 0:64],
                                                                    scalar1=rden[:, j:j + 1], scalar2=None, op0=ALU.mult),
                         r=[ob[j].b, rden.b], w=[o.b])
                S.dma("sp", lambda: nc.sync.dma_start(
                    out=d["attn"][qc * 512:(qc + 1) * 512, h * 64:(h + 1) * 64].rearrange("(j p) e -> p j e", p=128),
                    in_=o[:, :, :]), r=[o.b], w=[K.db("attn", s, qc, h)])
        S.barrier()


def phase_oproj_ln(K, s, w_o_ap, res_fn, res_deps_fn, g_ap, b_ap, dst_fn, dst_buf_fn, tag):
    nc, S, d = K.nc, K.S, K.d
    ph = Phase(K)
    with ph.es:
        wo = ph.tile("wo", [128, 8, 1024])
        S.dma("sp", lambda: nc.sync.dma_start(out=wo[:, :, :], in_=w_o_ap.rearrange("(c p) n -> p c n", p=128)), w=[wo.b])
        g_bc = load_bc(K, ph, "g_bc", g_ap, 1024)
        b_bc = load_bc(K, ph, "b_bc", b_ap, 1024)
        ats = [ph.tile("at%d" % i, [128, 1024]) for i in range(2)]
        xs = [ph.tile("xr%d" % i, [128, 1024]) for i in range(2)]
        aT = ph.tile("aT", [128, 8, 128])
        y = ph.tile("y", [128, 1024])
        outs = [ph.tile("ho%d" % i, [128, 1024]) for i in range(2)]
        for t in range(NT):
            p = t % 2
            t0 = t * 128
            at, x, o = ats[p], xs[p], outs[p]
            S.dma("sp", lambda: nc.sync.dma_start(out=at[:, :], in_=d["attn"][t0:t0 + 128, :]),
                  r=[K.db("attn", s, t // 4, h) for h in range(16)], w=[at.b])
            S.dma("sp", lambda: nc.sync.dma_start(out=x[:, :], in_=res_fn(t0)), r=res_deps_fn(t), w=[x.b])
            transpose_to(K, at, [at[:, c * 128:(c + 1) * 128] for c in range(8)], aT,
                         lambda g0, n: aT[:, g0:g0 + n, :].rearrange("p a b -> p (a b)"), 128)
            for n in range(2):
                bk = K.bank()
                for c in range(8):
                    S.op("pe", lambda c=c, n=n, bk=bk: nc.tensor.matmul(bk[:, 0:512], lhsT=aT[:, c, :],
                                                                         rhs=wo[:, c, n * 512:(n + 1) * 512],
                                                                         start=(c == 0), stop=(c == 7)),
                         r=[aT.b, wo.b], w=[bk.b])
                S.op("dve", lambda n=n, bk=bk: nc.vector.scalar_tensor_tensor(
                    out=y[:, n * 512:(n + 1) * 512], in0=x[:, n * 512:(n + 1) * 512], scalar=ALPHA, in1=bk[:, 0:512],
                    op0=ALU.mult, op1=ALU.add), r=[x.b, bk.b], w=[y.b])
            layer_norm(K, ph, y, g_bc, b_bc, o, "%s%d" % (tag, t))
            S.dma("sp", lambda: nc.sync.dma_start(out=dst_fn(t0), in_=o[:, :]), r=[o.b], w=[dst_buf_fn(t)])
        S.barrier()


def phase_peer(K, s, layer, src_fn, src_buf_fn, dst_fn, dst_buf_fn, ntiles=NT):
    nc, S, d = K.nc, K.S, K.d
    ph = Phase(K)
    NG = 5
    with ph.es:
        wq = ph.tile("pwq", [128, 8, 2048])
        S.dma("sp", lambda: nc.sync.dma_start(out=wq[:, :, :], in_=d["p_w_q"][layer].rearrange("(c p) n -> p c n", p=128)), w=[wq.b])
        skT = ph.tile("skT", [128, 16, 128])
        S.dma("sp", lambda: nc.sync.dma_start(out=skT[:, :, :], in_=d["p_skT"][layer]), w=[skT.b])
        g_bc = load_bc(K, ph, "pg_bc", d["ln_g"][2 * layer + 1:2 * layer + 2, :], 1024)
        b_bc = load_bc(K, ph, "pb_bc", d["ln_b"][2 * layer + 1:2 * layer + 2, :], 1024)
        iota = ph.tile("iota", [128, 16])
        S.dma("sp", lambda: nc.sync.dma_start(out=iota[:, :], in_=d["iota16"]), w=[iota.b])
        G = [ph.tile("G%d" % i, [128, 2048]) for i in range(NG)]
        h = ph.tile("ph", [128, 1024])
        hT = ph.tile("phT", [128, 8, 128])
        q_sb = ph.tile("pq", [128, 2048])
        qT = ph.tile("pqT", [128, 16, 128])
        sc = ph.tile("psc", [128, 16, 128])
        sc2 = ph.tile("psc2", [128, 128])
        m1 = ph.tile("pm1", [128, 16, 16])
        i1 = ph.tile("pi1", [128, 16, 16], U32)
        i1f = ph.tile("pi1f", [128, 16, 16])
        cand = ph.tile("pcand", [128, 8, 256])
        cand2 = ph.tile("pcand2", [128, 256])
        best = ph.tile("pbest", [128, 8, 16])
        pos = ph.tile("ppos", [128, 8, 16], U32)
        hi = ph.tile("phi", [128, 8, 16], U32)
        lo = ph.tile("plo", [128, 8, 16], U32)
        hif = ph.tile("phif", [128, 8, 16])
        lof = ph.tile("plof", [128, 8, 16])
        eq = ph.tile("peq", [128, 8, 16, 16])
        e0 = ph.tile("pe0", [128, 8, 16])
        e1 = ph.tile("pe1", [128, 8, 16])
        ef = ph.tile("pef", [128, 128])
        eidx = ph.tile("peidx", [128, 128], I32)
        bm = ph.tile("pbm", [128, 8, 16])
        se = ph.tile("pse", [128, 8])
        gw = ph.tile("pgw", [128, 128])
        a = ph.tile("pa", [128, 128])
        ga = ph.tile("pga", [128, 128])
        w = ph.tile("pw", [128, 128])
        acc = ph.tile("pacc", [128, 1024])
        prods = [ph.tile("pprod%d" % i, [128, 1024]) for i in range(2)]
        junk = prods[0]
        o = prods[1]
        gi = 0
        m1v = m1[:, :, :].rearrange("p (h two) k -> p h two k", two=2)
        i1fv = i1f[:, :, :].rearrange("p (h two) k -> p h two k", two=2)
        for t in range(ntiles):
            t0 = t * 128
            S.dma("sp", lambda: nc.sync.dma_start(out=h[:, :], in_=src_fn(t0)), r=src_buf_fn(t), w=[h.b])
            transpose_to(K, h, [h[:, c * 128:(c + 1) * 128] for c in range(8)], hT,
                         lambda g0, n: hT[:, g0:g0 + n, :].rearrange("p a b -> p (a b)"), 128)
            for n in range(4):
                bk = K.bank()
                for c in range(8):
                    S.op("pe", lambda: nc.tensor.matmul(bk[:, 0:512], lhsT=hT[:, c, :], rhs=wq[:, c, n * 512:(n + 1) * 512],
                                                        start=(c == 0), stop=(c == 7)), r=[hT.b, wq.b], w=[bk.b])
                evac(K, n, q_sb[:, n * 512:(n + 1) * 512], bk[:, 0:512], r=[bk.b], w=[q_sb.b])
            transpose_to(K, q_sb, [q_sb[:, c * 128:(c + 1) * 128] for c in range(16)], qT,
                         lambda g0, n: qT[:, g0:g0 + n, :].rearrange("p a b -> p (a b)"), 128)
            for n in range(4):
                bk = K.bank()
                for j in range(4):
                    hp = n * 4 + j
                    S.op("pe", lambda: nc.tensor.matmul(bk[:, j * 128:(j + 1) * 128], lhsT=qT[:, hp, :], rhs=skT[:, hp, :],
                                                        start=True, stop=True), r=[qT.b, skT.b], w=[bk.b])
                evac(K, n, sc[:, n * 4:(n + 1) * 4, :].rearrange("p a b -> p (a b)"), bk[:, 0:512], r=[bk.b], w=[sc.b])
            for hp in range(16):
                S.op("dve", lambda: nc.vector.max(out=m1[:, hp, 0:8], in_=sc[:, hp, :]), r=[sc.b], w=[m1.b])
                S.op("dve", lambda: nc.vector.max_index(out=i1[:, hp, 0:8], in_max=m1[:, hp, 0:8], in_values=sc[:, hp, :]),
                     r=[sc.b, m1.b], w=[i1.b])
                S.op("dve", lambda: nc.vector.match_replace(out=sc2[:, :], in_to_replace=m1[:, hp, 0:8],
                                                            in_values=sc[:, hp, :], imm_value=NEG), r=[sc.b, m1.b], w=[sc2.b])
                S.op("dve", lambda: nc.vector.max(out=m1[:, hp, 8:16], in_=sc2[:, :]), r=[sc2.b], w=[m1.b])
                S.op("dve", lambda: nc.vector.max_index(out=i1[:, hp, 8:16], in_max=m1[:, hp, 8:16], in_values=sc2[:, :]),
                     r=[sc2.b, m1.b], w=[i1.b])
            S.op("dve", lambda: nc.vector.tensor_tensor(
                out=cand[:, :, :].rearrange("p h (i j) -> p h i j", j=16),
                in0=m1v[:, :, 0, :].unsqueeze(3).to_broadcast([128, 8, 16, 16]),
                in1=m1v[:, :, 1, :].unsqueeze(2).to_broadcast([128, 8, 16, 16]), op=ALU.add), r=[m1.b], w=[cand.b])
            for hh in range(8):
                S.op("dve", lambda: nc.vector.max(out=best[:, hh, 0:8], in_=cand[:, hh, :]), r=[cand.b], w=[best.b])
                S.op("dve", lambda: nc.vector.max_index(out=pos[:, hh, 0:8], in_max=best[:, hh, 0:8], in_values=cand[:, hh, :]),
                     r=[cand.b, best.b], w=[pos.b])
                S.op("dve", lambda: nc.vector.match_replace(out=cand2[:, :], in_to_replace=best[:, hh, 0:8],
                                                            in_values=cand[:, hh, :], imm_value=NEG), r=[cand.b, best.b], w=[cand2.b])
                S.op("dve", lambda: nc.vector.max(out=best[:, hh, 8:16], in_=cand2[:, :]), r=[cand2.b], w=[best.b])
                S.op("dve", lambda: nc.vector.max_index(out=pos[:, hh, 8:16], in_max=best[:, hh, 8:16], in_values=cand2[:, :]),
                     r=[cand2.b, best.b], w=[pos.b])
            S.op("dve", lambda: nc.vector.tensor_tensor(out=bm[:, :, :], in0=best[:, :, :],
                                                        in1=best[:, :, 0:1].to_broadcast([128, 8, 16]), op=ALU.subtract),
                 r=[best.b], w=[bm.b])
            S.op("act", lambda: nc.scalar.activation(out=bm[:, :, :], in_=bm[:, :, :], func=AF.Exp), r=[bm.b], w=[bm.b])
            S.op("dve", lambda: nc.vector.tensor_reduce(out=se[:, :], in_=bm[:, :, :], axis=AX.X, op=ALU.add), r=[bm.b], w=[se.b])
            S.op("dve", lambda: nc.vector.reciprocal(out=se[:, :], in_=se[:, :]), r=[se.b], w=[se.b])
            S.op("dve", lambda: nc.vector.tensor_tensor(out=gw[:, :].rearrange("p (h k) -> p h k", k=16), in0=bm[:, :, :],
                                                        in1=se[:, :].unsqueeze(2).to_broadcast([128, 8, 16]), op=ALU.mult),
                 r=[bm.b, se.b], w=[gw.b])
            S.op("dve", lambda: nc.vector.tensor_single_scalar(out=hi[:, :, :], in_=pos[:, :, :], scalar=4,
                                                               op=ALU.logical_shift_right), r=[pos.b], w=[hi.b])
            S.op("dve", lambda: nc.vector.tensor_single_scalar(out=lo[:, :, :], in_=pos[:, :, :], scalar=15,
                                                               op=ALU.bitwise_and), r=[pos.b], w=[lo.b])
            S.op("dve", lambda: nc.vector.tensor_copy(out=hif[:, :, :], in_=hi[:, :, :]), r=[hi.b], w=[hif.b])
            S.op("dve", lambda: nc.vector.tensor_copy(out=lof[:, :, :], in_=lo[:, :, :]), r=[lo.b], w=[lof.b])
            S.op("dve", lambda: nc.vector.tensor_copy(out=i1f[:, :, :], in_=i1[:, :, :]), r=[i1.b], w=[i1f.b])
            iota_b = iota[:, :].unsqueeze(1).unsqueeze(1).to_broadcast([128, 8, 16, 16])
            for (xf, half, eo) in ((hif, 0, e0), (lof, 1, e1)):
                S.op("dve", lambda: nc.vector.tensor_tensor(out=eq[:, :, :, :],
                                                            in0=xf[:, :, :].unsqueeze(3).to_broadcast([128, 8, 16, 16]),
                                                            in1=iota_b, op=ALU.is_equal), r=[xf.b, iota.b], w=[eq.b])
                S.op("dve", lambda: nc.vector.tensor_tensor(out=eq[:, :, :, :], in0=eq[:, :, :, :],
                                                            in1=i1fv[:, :, half, :].unsqueeze(2).to_broadcast([128, 8, 16, 16]),
                                                            op=ALU.mult), r=[eq.b, i1f.b], w=[eq.b])
                S.op("dve", lambda: nc.vector.tensor_reduce(out=eo[:, :, :], in_=eq[:, :, :, :], axis=AX.X, op=ALU.add),
                     r=[eq.b], w=[eo.b])
            S.op("dve", lambda: nc.vector.scalar_tensor_tensor(out=ef[:, :], in0=e0[:, :, :].rearrange("p h k -> p (h k)"),
                                                               scalar=128.0, in1=e1[:, :, :].rearrange("p h k -> p (h k)"),
                                                               op0=ALU.mult, op1=ALU.add), r=[e0.b, e1.b], w=[ef.b])
            S.op("dve", lambda: nc.vector.tensor_copy(out=eidx[:, :], in_=ef[:, :]), r=[ef.b], w=[eidx.b])
            for g0 in range(0, 128, 4):
                gs = []
                for sl in range(g0, g0 + 4):
                    Gt = G[gi % NG]
                    gi += 1
                    gs.append(Gt)
                    S.dma("pool", lambda: nc.gpsimd.indirect_dma_start(
                        out=Gt[:, :], out_offset=None, in_=d["p_uv%d" % layer],
                        in_offset=bass.IndirectOffsetOnAxis(ap=eidx[:, sl:sl + 1], axis=0)), r=[eidx.b], w=[Gt.b])
                    pj = prods[sl % 2]
                    S.op("dve", lambda: nc.vector.tensor_tensor(out=pj[:, :], in0=Gt[:, 0:1024], in1=h[:, :], op=ALU.mult),
                         r=[Gt.b, h.b], w=[pj.b])
                    S.op("act", lambda: nc.scalar.activation(out=pj[:, :], in_=pj[:, :], func=AF.Identity,
                                                             accum_out=a[:, sl:sl + 1]), r=[pj.b], w=[pj.b, a.b])
                S.op("act", lambda: nc.scalar.activation(out=ga[:, g0:g0 + 4], in_=a[:, g0:g0 + 4], func=AF.Gelu), r=[a.b], w=[ga.b])
                S.op("dve", lambda: nc.vector.tensor_tensor(out=w[:, g0:g0 + 4], in0=ga[:, g0:g0 + 4], in1=gw[:, g0:g0 + 4],
                                                            op=ALU.mult), r=[ga.b, gw.b], w=[w.b])
                for j, sl in enumerate(range(g0, g0 + 4)):
                    Gt = gs[j]
                    if sl == 0:
                        S.op("dve", lambda: nc.vector.tensor_scalar(out=acc[:, :], in0=Gt[:, 1024:2048], scalar1=w[:, 0:1],
                                                                    scalar2=None, op0=ALU.mult), r=[Gt.b, w.b], w=[acc.b])
                    else:
                        S.op("dve", lambda: nc.vector.scalar_tensor_tensor(out=acc[:, :], in0=Gt[:, 1024:2048],
                                                                           scalar=w[:, sl:sl + 1], in1=acc[:, :],
                                                                           op0=ALU.mult, op1=ALU.add),
                             r=[Gt.b, w.b, acc.b], w=[acc.b])
            S.op("dve", lambda: nc.vector.scalar_tensor_tensor(out=junk[:, :], in0=h[:, :], scalar=ALPHA, in1=acc[:, :],
                                                               op0=ALU.mult, op1=ALU.add), r=[h.b, acc.b], w=[junk.b])
            layer_norm(K, ph, junk, g_bc, b_bc, o, "p%d_%d" % (layer, t))
            S.dma("sp", lambda: nc.sync.dma_start(out=dst_fn(t0), in_=o[:, :]), r=[o.b], w=[dst_buf_fn(t)])
        S.barrier()


def phase_b0(K, s):
    nc, S, d = K.nc, K.S, K.d
    ph = Phase(K)
    with ph.es:
        wkv = ph.tile("swkv", [128, 8, 1536])
        S.dma("sp", lambda: nc.sync.dma_start(out=wkv[:, :, :], in_=d["s_w_kv"].rearrange("(c p) n -> p c n", p=128)), w=[wkv.b])
        wqb = ph.tile("bwin", [128, 8, 1072])
        S.dma("sp", lambda: nc.sync.dma_start(out=wqb[:, :, :], in_=d["b_w_in"].rearrange("(c p) n -> p c n", p=128)), w=[wqb.b])
        h = ph.tile("bh", [128, 1024])
        cs = ph.tile("bcs", [128, 64])
        hT = ph.tile("bhT", [128, 8, 128])
        kvs = ph.tile("kvs", [128, 3, 4, 128])
        q_sb = ph.tile("bq", [128, 16, 64])
        qr = ph.tile("bqr", [128, 16, 64])
        kr = ph.tile("bkr", [128, 2, 4, 64])
        gts = ph.tile("bgts", [128, 48])
        vps = [ph.tile("bvp%d" % i, [128, 4, 65], BF16) for i in range(2)]
        qT = ph.tile("bqT", [64, 16, 128], BF16)
        qrT = ph.tile("bqrT", [64, 16, 128], BF16)
        kT = ph.tile("bkT", [64, 8, 128])
        kTr = ph.tile("bkTr", [64, 8, 128], BF16)
        for v in vps:
            S.op("dve", lambda: nc.vector.memset(v[:, :, 64:65], 1.0), w=[v.b])
        for t in range(K.ntiles):
            t0 = t * 128
            S.dma("sp", lambda: nc.sync.dma_start(out=h[:, :], in_=d["h2"][t0:t0 + 128, :]), r=[K.db("h2", s, t)], w=[h.b])
            S.dma("sp", lambda: nc.sync.dma_start(out=cs[:, :], in_=d["rope64"][t0:t0 + 128, :]), w=[cs.b])
            transpose_to(K, h, [h[:, c * 128:(c + 1) * 128] for c in range(8)], hT,
                         lambda g0, n: hT[:, g0:g0 + n, :].rearrange("p a b -> p (a b)"), 128)
            kv_flat = kvs[:, :, :, :].rearrange("p a b c -> p (a b c)")
            for n in range(3):
                bk = K.bank()
                for c in range(8):
                    S.op("pe", lambda: nc.tensor.matmul(bk[:, 0:512], lhsT=hT[:, c, :], rhs=wkv[:, c, n * 512:(n + 1) * 512],
                                                        start=(c == 0), stop=(c == 7)), r=[hT.b, wkv.b], w=[bk.b])
                evac(K, n, kv_flat[:, n * 512:(n + 1) * 512], bk[:, 0:512], r=[bk.b], w=[kvs.b])
            q_flat = q_sb[:, :, :].rearrange("p a b -> p (a b)")
            for n in range(2):
                bk = K.bank()
                for c in range(8):
                    S.op("pe", lambda: nc.tensor.matmul(bk[:, 0:512], lhsT=hT[:, c, :], rhs=wqb[:, c, n * 512:(n + 1) * 512],
                                                        start=(c == 0), stop=(c == 7)), r=[hT.b, wqb.b], w=[bk.b])
                evac(K, n + 1, q_flat[:, n * 512:(n + 1) * 512], bk[:, 0:512], r=[bk.b], w=[q_sb.b])
            bk = K.bank()
            for c in range(8):
                S.op("pe", lambda: nc.tensor.matmul(bk[:, 0:48], lhsT=hT[:, c, :], rhs=wqb[:, c, 1024:1072],
                                                    start=(c == 0), stop=(c == 7)), r=[hT.b, wqb.b], w=[bk.b])
            S.op("act", lambda: nc.scalar.activation(out=gts[:, :], in_=bk[:, 0:48], func=AF.Sigmoid), r=[bk.b], w=[gts.b])
            S.dma("sp", lambda: nc.sync.dma_start(out=d["gates"][t0:t0 + 128, :], in_=gts[:, :]), r=[gts.b], w=[K.db("b0", s, t)])
            cos_b = cs[:, 0:32].unsqueeze(1).to_broadcast([128, 16, 32])
            sin_b = cs[:, 32:64].unsqueeze(1).to_broadcast([128, 16, 32])
            rope_tok(K, ph, q_sb[:, :, 0:32], q_sb[:, :, 32:64], qr[:, :, 0:32], qr[:, :, 32:64], cos_b, sin_b,
                     [128, 16, 32], [q_sb.b, cs.b], [qr.b], "brq%d" % t)
            cos_k = cs[:, 0:32].unsqueeze(1).to_broadcast([128, 4, 32])
            sin_k = cs[:, 32:64].unsqueeze(1).to_broadcast([128, 4, 32])
            for br in range(2):
                rope_tok(K, ph, kvs[:, 1 + br, :, 0:32], kvs[:, 1 + br, :, 32:64], kr[:, br, :, 0:32], kr[:, br, :, 32:64],
                         cos_k, sin_k, [128, 4, 32], [kvs.b, cs.b], [kr.b], "brk%d_%d" % (t, br))
            transpose_to(K, q_sb, [q_sb[:, hh, :] for hh in range(16)], qT,
                         lambda g0, n: qT[:, g0:g0 + n, :].rearrange("p a b -> p (a b)"), 64)
            transpose_to(K, qr, [qr[:, hh, :] for hh in range(16)], qrT,
                         lambda g0, n: qrT[:, g0:g0 + n, :].rearrange("p a b -> p (a b)"), 64)
            srcs = [kvs[:, 0, g, 0:64] for g in range(4)] + [kvs[:, 0, g, 64:128] for g in range(4)]
            transpose_to(K, kvs, srcs, kT, lambda g0, n: kT[:, g0:g0 + n, :].rearrange("p a b -> p (a b)"), 64)
            srcs = [kr[:, 0, g, :] for g in range(4)] + [kr[:, 1, g, :] for g in range(4)]
            transpose_to(K, kr, srcs, kTr, lambda g0, n: kTr[:, g0:g0 + n, :].rearrange("p a b -> p (a b)"), 64)
            wb = [K.db("b0", s, t)]
            S.dma("sp", lambda: nc.sync.dma_start(out=d["bqT"][:, :, t0:t0 + 128].rearrange("h e t -> e h t"), in_=qT[:, :, :]),
                  r=[qT.b], w=wb)
            S.dma("sp", lambda: nc.sync.dma_start(out=d["bqrT"][:, :, t0:t0 + 128].rearrange("h e t -> e h t"), in_=qrT[:, :, :]),
                  r=[qrT.b], w=wb)
            S.dma("sp", lambda: nc.sync.dma_start(out=d["bkcT"][:, :, t0:t0 + 128].rearrange("h e t -> e h t"), in_=kT[:, :, :]),
                  r=[kT.b], w=wb)
            S.dma("sp", lambda: nc.sync.dma_start(out=d["bkrT"][:, :, t0:t0 + 128].rearrange("h e t -> e h t"), in_=kTr[:, :, :]),
                  r=[kTr.b], w=wb)
            for br in range(2):
                vp = vps[br]
                S.op("act", lambda: nc.scalar.copy(out=vp[:, :, 0:64], in_=kvs[:, 1 + br, :, 64:128]), r=[kvs.b], w=[vp.b])
                S.dma("sp", lambda: nc.sync.dma_start(out=d["bvp"][br, :, t0:t0 + 128, :].rearrange("g t e -> t g e"),
                                                      in_=vp[:, :, :]), r=[vp.b], w=wb)
        S.barrier()


def phase_b1(K, s):
    nc, S, d = K.nc, K.S, K.d
    ph = Phase(K)
    allb0 = [K.db("b0", s, t) for t in range(NT)]
    with ph.es:
        ovl = ph.tile("ovl", [128, 32])
        S.dma("sp", lambda: nc.sync.dma_start(out=ovl[:, :], in_=d["overlap"]), w=[ovl.b])
        for kv in range(2):
            nm = "k" if kv == 0 else "v"
            w1 = ph.tile("cw1" + nm, [64, 32, 256])
            S.dma("sp", lambda: nc.sync.dma_start(out=w1[:, :, :], in_=d["s_cmp_%s_w1" % nm].rearrange("(l e) n -> e l n", e=64)), w=[w1.b])
            w2 = ph.tile("cw2" + nm, [128, 2, 64])
            S.dma("sp", lambda: nc.sync.dma_start(out=w2[:, :, :], in_=d["s_cmp_%s_w2" % nm].rearrange("(c p) n -> p c n", p=128)), w=[w2.b])
            b1 = ph.tile("cb1" + nm, [128, 2])
            S.dma("sp", lambda: nc.sync.dma_start(out=b1[:, :], in_=d["s_cmp_%s_b1" % nm]), w=[b1.b])
            posT = ph.tile("cpos" + nm, [64, 32])
            S.dma("sp", lambda: nc.sync.dma_start(out=posT[:, :], in_=d["s_cmp_pos_%sT" % nm]), w=[posT.b])
            xT = ph.tile("cxT" + nm, [64, SEQ])
            Xp = ph.tile("cXp" + nm, [64, 32, 127])
            hid = ph.tile("chid" + nm, [128, 2, 127])
            res = ph.tile("cres" + nm, [128, 128], BF16)
            for g in range(4):
                S.dma("sp", lambda: nc.sync.dma_start(out=xT[:, :], in_=d["bkcT"][kv * 4 + g, :, :]), r=allb0, w=[xT.b])
                for half in range(2):
                    S.op("dve", lambda: nc.vector.tensor_tensor(
                        out=Xp[:, half * 16:(half + 1) * 16, :],
                        in0=xT[:, half * 16:half * 16 + 2032].rearrange("p (j l) -> p l j", l=16),
                        in1=posT[:, half * 16:(half + 1) * 16].unsqueeze(2).to_broadcast([64, 16, 127]), op=ALU.add),
                        r=[xT.b, posT.b], w=[Xp.b])
                for hc in range(2):
                    bk = K.bank()
                    for l in range(32):
                        S.op("pe", lambda: nc.tensor.matmul(bk[:, 0:127], lhsT=w1[:, l, hc * 128:(hc + 1) * 128], rhs=Xp[:, l, :],
                                                            start=(l == 0), stop=(l == 31)), r=[w1.b, Xp.b], w=[bk.b])
                    S.op("act", lambda: nc.scalar.activation(out=hid[:, hc, :], in_=bk[:, 0:127], func=AF.Gelu, bias=b1[:, hc:hc + 1],
                                                             scale=1.0), r=[bk.b, b1.b], w=[hid.b])
                bk = K.bank()
                if kv == 0:
                    for hc in range(2):
                        S.op("pe", lambda: nc.tensor.matmul(bk[0:64, 0:127], lhsT=w2[:, hc, :], rhs=hid[:, hc, :],
                                                            start=(hc == 0), stop=(hc == 1)), r=[w2.b, hid.b], w=[bk.b])
                    S.op("dve", lambda: nc.vector.memset(res[0:64, :], 0.0), w=[res.b])
                    S.op("dve", lambda: nc.vector.tensor_copy(out=res[0:64, 0:127], in_=bk[0:64, 0:127]), r=[bk.b], w=[res.b])
                    S.dma("sp", lambda: nc.sync.dma_start(out=d["kcmpT"][g, :, :], in_=res[0:64, :]), r=[res.b], w=[K.db("b1", s)])
                else:
                    for hc in range(2):
                        S.op("pe", lambda: nc.tensor.matmul(bk[0:127, 0:64], lhsT=hid[:, hc, :], rhs=w2[:, hc, :],
                                                            start=(hc == 0), stop=(hc == 1)), r=[w2.b, hid.b], w=[bk.b])
                    S.op("dve", lambda: nc.vector.memset(res[:, :], 0.0), w=[res.b])
                    S.op("dve", lambda: nc.vector.tensor_copy(out=res[0:127, 0:64], in_=bk[0:127, 0:64]), r=[bk.b], w=[res.b])
                    S.op("dve", lambda: nc.vector.memset(res[0:127, 64:65], 1.0), r=[], w=[res.b])
                    S.op("dve", lambda: nc.vector.tensor_copy(out=res[0:127, 65:97], in_=ovl[0:127, :]), r=[ovl.b], w=[res.b])
                    S.dma("sp", lambda: nc.sync.dma_start(out=d["vcmp"][g, :, :], in_=res[:, 0:97]), r=[res.b], w=[K.db("b1", s)])
        S.barrier()


def phase_b2(K, s):
    nc, S, d = K.nc, K.S, K.d
    scale = 64.0 ** -0.5
    ph = Phase(K)
    allb0 = [K.db("b0", s, t) for t in range(NT)]
    b1b = [K.db("b1", s)]
    with ph.es:
        cmask = ph.tile("cmask", [128, SEQ])
        S.dma("sp", lambda: nc.sync.dma_start(out=cmask[:, :], in_=d["cmask"]), w=[cmask.b])
        Eall32 = ph.tile("Eall32", [32, 16, 128])
        S.dma("sp", lambda: nc.sync.dma_start(out=Eall32[:, :, :], in_=d["Eall"]), w=[Eall32.b])
        Eall = ph.tile("Eall", [32, 16, 128], BF16)
        S.op("dve", lambda: nc.vector.tensor_copy(out=Eall[:, :, :], in_=Eall32[:, :, :]), r=[Eall32.b], w=[Eall.b])
        At = ph.tile("At", [128, 16, 32])
        Bt = ph.tile("Bt", [128, 16, 32])
        S.dma("sp", lambda: nc.sync.dma_start(out=At[:, :, :], in_=d["selA"].rearrange("(q p) n -> p q n", p=128)), w=[At.b])
        S.dma("sp", lambda: nc.sync.dma_start(out=Bt[:, :, :], in_=d["selB"].rearrange("(q p) n -> p q n", p=128)), w=[Bt.b])
        gts = ph.tile("gts", [128, 16, 48])
        S.dma("sp", lambda: nc.sync.dma_start(out=gts[:, :, :], in_=d["gates"].rearrange("(q p) n -> p q n", p=128)), r=allb0, w=[gts.b])
        ksT = ph.tile("ksT", [64, SEQ], BF16)
        kwT = ph.tile("kwT", [64, SEQ], BF16)
        vs = ph.tile("vs", [128, 16, 65], BF16)
        vw = ph.tile("vw", [128, 16, 65], BF16)
        kcT = ph.tile("kcT", [64, 128], BF16)
        vc = ph.tile("vc", [128, 97], BF16)
        qTs = [ph.tile("nqT%d" % i, [64, SEQ], BF16) for i in range(4)]
        qrTs = [ph.tile("nqrT%d" % i, [64, SEQ], BF16) for i in range(4)]
        comb = ph.tile("comb", [128, 4, 16, 64])
        imp = ph.tile("imp", [128, 16, 32])
        sel = ph.tile("sel", [128, 16, 32])
        selT = ph.tile("selT", [32, SEQ], BF16)
        tmp32 = ph.tile("tmp32", [128, 32])
        m8 = ph.tile("m8", [128, 16])
        pts = [ph.tile("npt%d" % i, [128, 512], BF16) for i in range(3)]
        mts = [ph.tile("nmt%d" % i, [128, 512], BF16) for i in range(2)]
        rd = ph.tile("nrd", [128, 2])
        otmp = ph.tile("notmp", [128, 64])
        ctr = 0
        for g in range(4):
            S.dma("sp", lambda: nc.sync.dma_start(out=ksT[:, :], in_=d["bkrT"][g, :, :]), r=allb0, w=[ksT.b])
            S.dma("sp", lambda: nc.sync.dma_start(out=kwT[:, :], in_=d["bkrT"][4 + g, :, :]), r=allb0, w=[kwT.b])
            S.dma("sp", lambda: nc.sync.dma_start(out=vs[:, :, :], in_=d["bvp"][0, g, :, :].rearrange("(k p) e -> p k e", p=128)), r=allb0, w=[vs.b])
            S.dma("sp", lambda: nc.sync.dma_start(out=vw[:, :, :], in_=d["bvp"][1, g, :, :].rearrange("(k p) e -> p k e", p=128)), r=allb0, w=[vw.b])
            S.dma("sp", lambda: nc.sync.dma_start(out=kcT[:, :], in_=d["kcmpT"][g, :, :]), r=b1b, w=[kcT.b])
            S.dma("sp", lambda: nc.sync.dma_start(out=vc[:, :], in_=d["vcmp"][g, :, :]), r=b1b, w=[vc.b])
            for j in range(4):
                hh = g * 4 + j
                S.dma("sp", lambda: nc.sync.dma_start(out=qTs[j][:, :], in_=d["bqT"][hh, :, :]), r=allb0, w=[qTs[j].b])
                S.dma("sp", lambda: nc.sync.dma_start(out=qrTs[j][:, :], in_=d["bqrT"][hh, :, :]), r=allb0, w=[qrTs[j].b])
            for j in range(4):
                hh = g * 4 + j
                qT = qTs[j]
                for qc in range(4):
                    st = K.bank()
                    pt = pts[ctr % 3]
                    ctr += 1
                    S.op("pe", lambda: nc.tensor.matmul(st[0:127, 0:512], lhsT=kcT[:, 0:127], rhs=qT[:, qc * 512:(qc + 1) * 512],
                                                        start=True, stop=True), r=[kcT.b, qT.b], w=[st.b])
                    S.op("act", lambda: nc.scalar.activation(out=pt[0:127, :], in_=st[0:127, 0:512], func=AF.Exp, scale=scale),
                         r=[st.b], w=[pt.b])
                    S.op("dve", lambda: nc.vector.tensor_tensor(out=pt[0:127, :], in0=pt[0:127, :],
                                                                in1=cmask[0:127, qc * 512:(qc + 1) * 512], op=ALU.mult),
                         r=[pt.b, cmask.b], w=[pt.b])
                    for jj in range(4):
                        qt = qc * 4 + jj
                        ob = K.bank()
                        S.op("pe", lambda: nc.tensor.matmul(ob[:, 0:97], lhsT=pt[0:127, jj * 128:(jj + 1) * 128], rhs=vc[0:127, :],
                                                            start=True, stop=True), r=[pt.b, vc.b], w=[ob.b])
                        S.op("dve", lambda: nc.vector.tensor_scalar_max(out=rd[:, 0:1], in0=ob[:, 64:65], scalar1=1e-30),
                             r=[ob.b], w=[rd.b])
                        S.op("dve", lambda: nc.vector.reciprocal(out=rd[:, 1:2], in_=rd[:, 0:1]), r=[rd.b], w=[rd.b])
                        S.op("dve", lambda: nc.vector.tensor_scalar(out=comb[:, j, qt, :], in0=ob[:, 0:64], scalar1=rd[:, 1:2],
                                                                    scalar2=gts[:, qt, hh:hh + 1], op0=ALU.mult, op1=ALU.mult),
                             r=[ob.b, rd.b, gts.b], w=[comb.b])
                        if j == 0:
                            S.op("dve", lambda: nc.vector.tensor_scalar(out=imp[:, qt, :], in0=ob[:, 65:97], scalar1=rd[:, 1:2],
                                                                        scalar2=None, op0=ALU.mult), r=[ob.b, rd.b], w=[imp.b])
                        else:
                            S.op("dve", lambda: nc.vector.scalar_tensor_tensor(out=imp[:, qt, :], in0=ob[:, 65:97], scalar=rd[:, 1:2],
                                                                               in1=imp[:, qt, :], op0=ALU.mult, op1=ALU.add),
                                 r=[ob.b, rd.b, imp.b], w=[imp.b])
            S.op("dve", lambda: nc.vector.tensor_tensor(out=imp[:, :, :], in0=imp[:, :, :], in1=At[:, :, :], op=ALU.mult),
                 r=[imp.b, At.b], w=[imp.b])
            S.op("dve", lambda: nc.vector.tensor_tensor(out=imp[:, :, :], in0=imp[:, :, :], in1=Bt[:, :, :], op=ALU.add),
                 r=[imp.b, Bt.b], w=[imp.b])
            for qt in range(16):
                S.op("dve", lambda: nc.vector.max(out=m8[:, 0:8], in_=imp[:, qt, :]), r=[imp.b], w=[m8.b])
                S.op("dve", lambda: nc.vector.match_replace(out=tmp32[:, :], in_to_replace=m8[:, 0:8], in_values=imp[:, qt, :],
                                                            imm_value=-3e38), r=[imp.b, m8.b], w=[tmp32.b])
                S.op("dve", lambda: nc.vector.max(out=m8[:, 8:16], in_=tmp32[:, :]), r=[tmp32.b], w=[m8.b])
                S.op("dve", lambda: nc.vector.tensor_scalar(out=sel[:, qt, :], in0=imp[:, qt, :], scalar1=m8[:, 15:16], scalar2=None,
                                                            op0=ALU.is_ge), r=[imp.b, m8.b], w=[sel.b])
            transpose_to(K, sel, [sel[:, qt, :] for qt in range(16)], selT,
                         lambda g0, n: selT[:, g0 * 128:(g0 + n) * 128], 32)
            for j in range(4):
                hh = g * 4 + j
                qrT = qrTs[j]
                for qc in range(4):
                    ob = [K.ps[4 + jj] for jj in range(4)]
                    nk = 4 * qc + 4
                    for kt in range(nk):
                        st = K.ps[ctr % 2]
                        mb = K.ps[2 + ctr % 2]
                        pt = pts[ctr % 3]
                        ctr += 1
                        S.op("pe", lambda: nc.tensor.matmul(st[:, 0:512], lhsT=ksT[:, kt * 128:(kt + 1) * 128],
                                                            rhs=qrT[:, qc * 512:(qc + 1) * 512], start=True, stop=True),
                             r=[ksT.b, qrT.b], w=[st.b])
                        S.op("pe", lambda: nc.tensor.matmul(mb[:, 0:512], lhsT=Eall[:, kt, :], rhs=selT[:, qc * 512:(qc + 1) * 512],
                                                            start=True, stop=True), r=[Eall.b, selT.b], w=[mb.b])
                        j0 = max(0, kt - 4 * qc)
                        S.op("act", lambda: nc.scalar.activation(out=pt[:, j0 * 128:512], in_=st[:, j0 * 128:512], func=AF.Exp,
                                                                 scale=scale), r=[st.b], w=[pt.b])
                        S.op("dve", lambda: nc.vector.tensor_tensor(out=pt[:, j0 * 128:512], in0=pt[:, j0 * 128:512],
                                                                    in1=mb[:, j0 * 128:512], op=ALU.mult), r=[pt.b, mb.b], w=[pt.b])
                        if kt >= 4 * qc:
                            S.op("dve", lambda: nc.vector.tensor_tensor(out=pt[:, j0 * 128:(j0 + 1) * 128],
                                                                        in0=pt[:, j0 * 128:(j0 + 1) * 128],
                                                                        in1=K.tri_le[:, :], op=ALU.mult),
                                 r=[pt.b, K.tri_le.b], w=[pt.b])
                        for jj in range(j0, 4):
                            qt = 4 * qc + jj
                            S.op("pe", lambda: nc.tensor.matmul(ob[jj][:, 0:65], lhsT=pt[:, jj * 128:(jj + 1) * 128],
                                                                rhs=vs[:, kt, :], start=(kt == 0), stop=(kt == qt)),
                                 r=[pt.b, vs.b], w=[ob[jj].b])
                    for jj in range(4):
                        qt = 4 * qc + jj
                        nsa_combine(K, ob[jj], rd, otmp, comb, j, qt, gts, 16 + hh)
                for qt in range(16):
                    kts = list(range(max(0, qt - 4), qt + 1))
                    ob = K.ps[4 + qt % 4]
                    stA = K.ps[ctr % 2]
                    stB = K.ps[2 + ctr % 2]
                    ptA = pts[ctr % 3]
                    ptB = mts[ctr % 2]
                    ctr += 1
                    for i, kt in enumerate(kts):
                        st = stA if i < 4 else stB
                        S.op("pe", lambda: nc.tensor.matmul(st[:, (i % 4) * 128:(i % 4 + 1) * 128], lhsT=kwT[:, kt * 128:(kt + 1) * 128],
                                                            rhs=qrT[:, qt * 128:(qt + 1) * 128], start=True, stop=True),
                             r=[kwT.b, qrT.b], w=[st.b])
                    na = min(4, len(kts))
                    S.op("act", lambda: nc.scalar.activation(out=ptA[:, 0:na * 128], in_=stA[:, 0:na * 128], func=AF.Exp, scale=scale),
                         r=[stA.b], w=[ptA.b])
                    if len(kts) == 5:
                        S.op("act", lambda: nc.scalar.activation(out=ptB[:, 0:128], in_=stB[:, 0:128], func=AF.Exp, scale=scale),
                             r=[stB.b], w=[ptB.b])
                    for i, kt in enumerate(kts):
                        pt = ptA if i < 4 else ptB
                        sl = slice((i % 4) * 128, (i % 4 + 1) * 128)
                        if kt == qt:
                            S.op("dve", lambda: nc.vector.tensor_tensor(out=pt[:, sl], in0=pt[:, sl], in1=K.tri_le[:, :], op=ALU.mult),
                                 r=[pt.b, K.tri_le.b], w=[pt.b])
                        elif kt == qt - 4:
                            S.op("dve", lambda: nc.vector.tensor_tensor(out=pt[:, sl], in0=pt[:, sl], in1=K.tri_gt[:, :], op=ALU.mult),
                                 r=[pt.b, K.tri_gt.b], w=[pt.b])
                        S.op("pe", lambda: nc.tensor.matmul(ob[:, 0:65], lhsT=pt[:, sl], rhs=vw[:, kt, :], start=(i == 0),
                                                            stop=(i == len(kts) - 1)), r=[pt.b, vw.b], w=[ob.b])
                    nsa_combine(K, ob, rd, otmp, comb, j, qt, gts, 32 + hh)
                S.dma("sp", lambda: nc.sync.dma_start(
                    out=d["attn"][:, hh * 64:(hh + 1) * 64].rearrange("(q p) e -> p q e", p=128), in_=comb[:, j, :, :]),
                    r=[comb.b], w=[K.db("attn", s, qq, hh) for qq in range(4)])
        S.barrier()


def nsa_combine(K, ob, rd, otmp, comb, j, qt, gts, gcol):
    nc, S = K.nc, K.S
    S.op("dve", lambda: nc.vector.reciprocal(out=rd[:, 1:2], in_=ob[:, 64:65]), r=[ob.b], w=[rd.b])
    S.op("dve", lambda: nc.vector.tensor_scalar(out=otmp[:, :], in0=ob[:, 0:64], scalar1=rd[:, 1:2], scalar2=gts[:, qt, gcol:gcol + 1],
                                                op0=ALU.mult, op1=ALU.mult), r=[ob.b, rd.b, gts.b], w=[otmp.b])
    S.op("dve", lambda: nc.vector.tensor_tensor(out=comb[:, j, qt, :], in0=comb[:, j, qt, :], in1=otmp[:, :], op=ALU.add),
         r=[comb.b, otmp.b], w=[comb.b])

def build(nseq=4, stages=("a1", "a2", "a3"), dbg=(), ntiles=NT):
    nc = bass.Bass("TRN2", target_bir_lowering=False)
    K = Ctx()
    K.nc = nc
    K.uid = 0
    K.evi = 0
    K.ntiles = ntiles
    es = ExitStack()
    K.es = es
    d = {}
    K.d = d

    def din(name, shape, dtype=F32):
        d[name] = nc.dram_tensor(name, list(shape), dtype, kind="ExternalInput").ap()

    def dscr(name, shape, dtype=F32):
        kind = "ExternalOutput" if name in dbg else "Internal"
        if name + "_in" in dbg:
            kind = "ExternalInput"
        d[name] = nc.dram_tensor(name, list(shape), dtype, kind=kind).ap()

    din("x", [nseq, SEQ, D])
    din("a_w_in", [1024, 800])
    din("a_q_norm", [1, 512])
    din("a_kv_norm", [1, 256])
    din("a_w_q_up", [512, 1536])
    din("a_w_kv_up", [256, 2048])
    din("a_w_o", [1024, 1024])
    din("s_w_kv", [1024, 1536])
    din("b_w_in", [1024, 1072])
    din("b_w_o", [1024, 1024])
    for nm in ("k", "v"):
        din("s_cmp_%s_w1" % nm, [2048, 256])
        din("s_cmp_%s_b1" % nm, [128, 2])
        din("s_cmp_%s_w2" % nm, [256, 64])
        din("s_cmp_pos_%sT" % nm, [64, 32])
    din("p_w_q", [2, 1024, 2048])
    din("p_skT", [2, 128, 16, 128])
    din("p_uv0", [16384, 2048])
    din("p_uv1", [16384, 2048])
    din("ln_g", [4, 1024])
    din("ln_b", [4, 1024])
    hc = host_consts()
    for k, v in hc.items():
        din(k, v.shape)
    d["out"] = nc.dram_tensor("out", [nseq, SEQ, D], F32, kind="ExternalOutput").ap()
    dscr("qnT", [16, 64, SEQ], BF16)
    dscr("qpT", [16, 32, SEQ], BF16)
    dscr("knT", [16, 64, SEQ], BF16)
    dscr("kpT", [32, SEQ], BF16)
    dscr("vp", [16, SEQ, 65], BF16)
    dscr("attn", [SEQ, 1024])
    dscr("h1", [SEQ, 1024])
    dscr("h2", [SEQ, 1024])
    dscr("h3", [SEQ, 1024])
    dscr("gates", [SEQ, 48])
    dscr("bqT", [16, 64, SEQ], BF16)
    dscr("bqrT", [16, 64, SEQ], BF16)
    dscr("bkcT", [8, 64, SEQ])
    dscr("bkrT", [8, 64, SEQ], BF16)
    dscr("bvp", [2, 4, SEQ, 65], BF16)
    dscr("kcmpT", [4, 64, 128], BF16)
    dscr("vcmp", [4, 128, 97], BF16)

    with es:
        S = Sched(nc, es)
        K.S = S
        K.ps = [Tile(es.enter_context(nc.psum_tensor("ps%d" % i, [128, 512], F32)), "ps%d" % i) for i in range(8)]
        K.bank_i = 0

        def bank():
            b = K.ps[K.bank_i % 8]
            K.bank_i += 1
            return b
        K.bank = bank
        K.dbufs = {}

        def db(*key):
            if key not in K.dbufs:
                K.dbufs[key] = Buf(str(key))
            return K.dbufs[key]
        K.db = db
        gl = Phase(K)
        gl.es = es
        K.ident = gl.tile("ident", [128, 128])
        K.tri_le = gl.tile("tri_le", [128, 128])
        K.tri_gt = gl.tile("tri_gt", [128, 128])
        K.eps_ln = gl.tile("eps_ln", [128, 1])
        K.eps_rms = gl.tile("eps_rms", [128, 1])
        for nm in ("ident", "tri_le", "tri_gt"):
            tl = getattr(K, nm)
            S.dma("sp", lambda tl=tl, nm=nm: nc.sync.dma_start(out=tl[:, :], in_=d[nm]), w=[tl.b])
        S.op("dve", lambda: nc.vector.memset(K.eps_ln[:, :], LN_EPS), w=[K.eps_ln.b])
        S.op("dve", lambda: nc.vector.memset(K.eps_rms[:, :], RMS_EPS), w=[K.eps_rms.b])

        for s in range(nseq):
            if "a1" in stages:
                phase_a1(K, s)
            if "a2" in stages:
                phase_a2(K, s)
            if "a3" in stages:
                dstname = "h1" if "h1" in dbg or len(stages) > 3 else "h1"
                phase_oproj_ln(K, s, d["a_w_o"], lambda t0: d["x"][s, t0:t0 + 128, :], lambda t: [],
                               d["ln_g"][0:1, :], d["ln_b"][0:1, :],
                               lambda t0: d["h1"][t0:t0 + 128, :], lambda t: K.db("h1", s, t), "a")
            if "p0" in stages:
                phase_peer(K, s, 0, lambda t0: d["h1"][t0:t0 + 128, :], lambda t: [K.db("h1", s, t)],
                           lambda t0: d["h2"][t0:t0 + 128, :], lambda t: K.db("h2", s, t), ntiles=K.ntiles)
            if "b0" in stages:
                phase_b0(K, s)
            if "b1" in stages:
                phase_b1(K, s)
            if "b2" in stages:
                phase_b2(K, s)
            if "b3" in stages:
                phase_oproj_ln(K, s, d["b_w_o"], lambda t0: d["h2"][t0:t0 + 128, :], lambda t: [K.db("h2", s, t)],
                               d["ln_g"][2:3, :], d["ln_b"][2:3, :],
                               lambda t0: d["h3"][t0:t0 + 128, :], lambda t: K.db("h3", s, t), "b")
            if "p1" in stages:
                phase_peer(K, s, 1, lambda t0: d["h3"][t0:t0 + 128, :], lambda t: [K.db("h3", s, t)],
                           lambda t0: d["out"][s, t0:t0 + 128, :], lambda t: K.db("out", s, t), ntiles=K.ntiles)
        S.barrier()
    K.hc = hc
    return nc, K


ALL_STAGES = ("a1", "a2", "a3", "p0", "b0", "b1", "b2", "b3", "p1")
_CACHE = {}


def _f32(a):
    return np.ascontiguousarray(np.asarray(a), dtype=np.float32)


def kernel(x, a_w_in, a_q_norm, a_kv_norm, a_w_q_up, a_w_kv_up, a_w_o, b_w_in, b_w_o,
           s_w_kv, s_cmp_pos_k, s_cmp_pos_v, s_cmp_k_w1, s_cmp_k_b1, s_cmp_k_w2,
           s_cmp_v_w1, s_cmp_v_b1, s_cmp_v_w2, p_w_q, p_subkeys, p_u, p_v, ln_g, ln_b):
    x = np.asarray(x)
    B = x.shape[0]
    if "nc" not in _CACHE:
        _CACHE["nc"] = build(nseq=B // NCORES, stages=ALL_STAGES)
    nc, K = _CACHE["nc"]
    p_subkeys = np.asarray(p_subkeys)
    p_u = np.asarray(p_u)
    p_v = np.asarray(p_v)
    w = {
        "a_w_in": _f32(np.asarray(a_w_in)[0]), "a_q_norm": _f32(np.asarray(a_q_norm).reshape(1, 512)),
        "a_kv_norm": _f32(np.asarray(a_kv_norm).reshape(1, 256)), "a_w_q_up": _f32(np.asarray(a_w_q_up)[0]),
        "a_w_kv_up": _f32(np.asarray(a_w_kv_up)[0]), "a_w_o": _f32(np.asarray(a_w_o)[0]),
        "b_w_in": _f32(np.asarray(b_w_in)[0]), "b_w_o": _f32(np.asarray(b_w_o)[0]), "s_w_kv": _f32(s_w_kv),
        "s_cmp_k_w1": _f32(s_cmp_k_w1), "s_cmp_v_w1": _f32(s_cmp_v_w1),
        "s_cmp_k_w2": _f32(s_cmp_k_w2), "s_cmp_v_w2": _f32(s_cmp_v_w2),
        "s_cmp_k_b1": _f32(np.asarray(s_cmp_k_b1).reshape(2, 128).T), "s_cmp_v_b1": _f32(np.asarray(s_cmp_v_b1).reshape(2, 128).T),
        "s_cmp_pos_kT": _f32(np.asarray(s_cmp_pos_k).T), "s_cmp_pos_vT": _f32(np.asarray(s_cmp_pos_v).T),
        "p_w_q": _f32(p_w_q),
        "p_skT": _f32(np.stack([p_subkeys[l].reshape(16, 128, 128).transpose(2, 0, 1) for l in range(2)])),
        "p_uv0": _f32(np.concatenate([p_u[0], p_v[0]], axis=1)), "p_uv1": _f32(np.concatenate([p_u[1], p_v[1]], axis=1)),
        "ln_g": _f32(np.asarray(ln_g).reshape(4, 1024)), "ln_b": _f32(np.asarray(ln_b).reshape(4, 1024)),
    }
    w.update({k: _f32(v) for k, v in K.hc.items()})
    nper = B // NCORES
    in_maps = []
    for c in range(NCORES):
        m = dict(w)
        m["x"] = _f32(x[c * nper:(c + 1) * nper])
        in_maps.append(m)
    res = run_bass_kernel_spmd(nc, in_maps, core_ids=list(range(NCORES)))
    out = np.empty((B, SEQ, D), dtype=np.float32)
    for c in range(NCORES):
        out[c * nper:(c + 1) * nper] = np.asarray(res.results[c]["out"]).reshape(nper, SEQ, D)
    return out
```

```python
import numpy as np
from contextlib import ExitStack, contextmanager
import concourse.bass as bass
import concourse.mybir as mybir
from concourse.bass_utils import run_bass_kernel_spmd

F32 = mybir.dt.float32
I32 = mybir.dt.int32
U32 = mybir.dt.uint32
BF16 = mybir.dt.bfloat16
AF = mybir.ActivationFunctionType
ALU = mybir.AluOpType
AX = mybir.AxisListType

SEQ = 2048
D = 1024
NT = 16
NCORES = 8
ALPHA = 4.0 ** 0.25
LN_EPS = 1e-5
RMS_EPS = 1e-6
NEG = -1e30


class Buf:
    __slots__ = ("name", "w", "r")

    def __init__(self, name=""):
        self.name = name
        self.w = None
        self.r = {}


class Tile:
    def __init__(self, t, name):
        self.t = t
        self.b = Buf(name)

    def __getitem__(self, k):
        return self.t[k]


class Sched:
    EPOCH = 20000

    def __init__(self, nc, es):
        self.nc = nc
        self.es = es
        self.eng = {"pe": nc.tensor, "act": nc.scalar, "dve": nc.vector, "pool": nc.gpsimd, "sp": nc.sync}
        self.cur = {}
        self.waited = {}
        self.nsem = 0
        self.ninst = 0
        for e in ("pe", "act", "dve", "pool"):
            self._new_epoch(e)
        self.rings = {}
        for q, n in (("sp", 24), ("pool", 12), ("act", 8)):
            self.rings[q] = [[self._sem(), 0] for _ in range(n)]
        self.ring_i = {q: 0 for q in self.rings}

    def _sem(self):
        self.nsem += 1
        return self.es.enter_context(self.nc.semaphore("s%d" % self.nsem))

    def _new_epoch(self, e):
        self.cur[e] = [self._sem(), 0]

    def _wait(self, e, tok, strict=False):
        sem, val, src = tok
        if src == e and e == "pe" and not strict:
            return
        key = (e, id(sem))
        if self.waited.get(key, 0) >= val:
            return
        self.eng[e].wait_ge(sem, val)
        self.ninst += 1
        self.waited[key] = val

    @staticmethod
    def _deps(r, w):
        toks = []
        for b in r:
            if b.w is not None:
                toks.append(b.w)
        for b in w:
            if b.w is not None:
                toks.append(b.w)
            toks.extend(b.r.values())
        return toks

    @staticmethod
    def _commit(tok, r, w, key):
        for b in w:
            b.w = tok
            b.r = {}
        for b in r:
            if b not in w:
                b.r[key] = tok

    def op(self, e, fn, r=(), w=()):
        for t in self._deps(r, w):
            self._wait(e, t)
        st = self.cur[e]
        if st[1] >= self.EPOCH:
            self._new_epoch(e)
            st = self.cur[e]
        ins = fn()
        st[1] += 1
        ins.then_inc(st[0], 1)
        self.ninst += 1
        tok = (st[0], st[1], e)
        self._commit(tok, r, w, e)
        return tok

    def dma(self, q, fn, r=(), w=()):
        ring = self.rings[q]
        i = self.ring_i[q]
        self.ring_i[q] = (i + 1) % len(ring)
        slot = ring[i]
        if slot[1] > 0:
            self._wait(q, (slot[0], slot[1], None), strict=True)
        for t in self._deps(r, w):
            self._wait(q, t, strict=True)
        ins = fn()
        slot[1] += 16
        ins.then_inc(slot[0], 16)
        self.ninst += 1
        tok = (slot[0], slot[1], None)
        self._commit(tok, r, w, ("dma", q, i))
        return tok

    def barrier(self):
        toks = []
        for e in ("pe", "act", "dve", "pool"):
            st = self.cur[e]
            if st[1] > 0:
                toks.append((st[0], st[1], e))
        for q, ring in self.rings.items():
            for slot in ring:
                if slot[1] > 0:
                    toks.append((slot[0], slot[1], None))
        for e in ("pe", "act", "dve", "pool", "sp"):
            for t in toks:
                self._wait(e, t, strict=(t[2] is None))


class Phase:
    def __init__(self, K):
        self.K = K
        self.es = ExitStack()

    def tile(self, name, shape, dtype=F32):
        self.K.uid += 1
        nm = "%s_%d" % (name, self.K.uid)
        return Tile(self.es.enter_context(self.K.nc.sbuf_tensor(nm, list(shape), dtype)), nm)


class Ctx:
    pass


def host_consts():
    c = {}
    c["ident"] = np.eye(128, dtype=np.float32)
    kk = np.arange(128)[:, None]
    qq = np.arange(128)[None, :]
    c["tri_le"] = (kk <= qq).astype(np.float32)
    c["tri_gt"] = (kk > qq).astype(np.float32)
    pos = np.arange(SEQ, dtype=np.float32)[:, None]
    for d in (32, 64):
        inv = (10000.0 ** (-np.arange(0, d, 2, dtype=np.float32) / d)).astype(np.float32)
        ang = (pos * inv[None, :]).astype(np.float32)
        c["rope%d" % d] = np.concatenate([np.cos(ang), np.sin(ang)], axis=1).astype(np.float32)
    c["iota16"] = np.tile(np.arange(16, dtype=np.float32)[None, :], (128, 1))
    cc = np.arange(128)
    tt = np.arange(SEQ)
    c["cmask"] = ((16 * cc[:, None] + 31 <= tt[None, :]) & (cc[:, None] < 127)).astype(np.float32)
    nn = np.arange(32)
    ovl = ((16 * cc[:, None] < 64 * nn[None, :] + 64) & (16 * cc[:, None] + 31 >= 64 * nn[None, :]) & (cc[:, None] < 127))
    c["overlap"] = ovl.astype(np.float32)
    cur = tt // 64
    forced = (nn[None, :] == 0) | ((nn[None, :] <= cur[:, None]) & (nn[None, :] > cur[:, None] - 2))
    valid = nn[None, :] <= cur[:, None]
    c["selA"] = ((~forced) & valid).astype(np.float32)
    c["selB"] = np.where(valid, np.where(forced, 1e9, 0.0), -1e30).astype(np.float32)
    E = np.zeros((32, 16, 128), np.float32)
    for kt in range(16):
        for k in range(128):
            E[(kt * 128 + k) // 64, kt, k] = 1.0
    c["Eall"] = E
    return c


def rope_tok(K, ph, x1, x2, o1, o2, cos_b, sin_b, shape, rbufs, wbufs, tmpname):
    nc, S = K.nc, K.S
    if not hasattr(ph, "rtmp"):
        ph.rtmp = {}
    key = tuple(shape)
    if key not in ph.rtmp:
        ph.rtmp[key] = (ph.tile("ropea", shape), ph.tile("ropeb", shape))
    ta, tb = ph.rtmp[key]
    sl = tuple(slice(None) for _ in shape)
    S.op("dve", lambda: nc.vector.tensor_tensor(out=ta[sl], in0=x1, in1=cos_b, op=ALU.mult), r=rbufs, w=[ta.b])
    S.op("dve", lambda: nc.vector.tensor_tensor(out=tb[sl], in0=x2, in1=sin_b, op=ALU.mult), r=rbufs, w=[tb.b])
    S.op("dve", lambda: nc.vector.tensor_tensor(out=o1, in0=ta[sl], in1=tb[sl], op=ALU.subtract), r=[ta.b, tb.b], w=wbufs)
    S.op("dve", lambda: nc.vector.tensor_tensor(out=ta[sl], in0=x2, in1=cos_b, op=ALU.mult), r=rbufs, w=[ta.b])
    S.op("dve", lambda: nc.vector.tensor_tensor(out=tb[sl], in0=x1, in1=sin_b, op=ALU.mult), r=rbufs, w=[tb.b])
    S.op("dve", lambda: nc.vector.tensor_tensor(out=o2, in0=ta[sl], in1=tb[sl], op=ALU.add), r=[ta.b, tb.b], w=wbufs)


def evac(K, i, out, in_, r, w):
    nc, S = K.nc, K.S
    if i % 2 == 0:
        S.op("act", lambda: nc.scalar.copy(out=out, in_=in_), r=r, w=w)
    else:
        S.op("dve", lambda: nc.vector.tensor_copy(out=out, in_=in_), r=r, w=w)


def transpose_to(K, src_tile, src_aps, dst_tile, dst_ap_fn, rows, group=4):
    nc, S = K.nc, K.S
    n = len(src_aps)
    for g0 in range(0, n, group):
        cnt = min(group, n - g0)
        bank = K.bank()
        for j in range(cnt):
            ap = src_aps[g0 + j]
            S.op("pe", lambda ap=ap, j=j: nc.tensor.transpose(out=bank[0:rows, j * 128:(j + 1) * 128], in_=ap,
                                                               identity=K.ident[:, :]),
                 r=[src_tile.b, K.ident.b], w=[bank.b])
        evac(K, K.evi, dst_ap_fn(g0, cnt), bank[0:rows, 0:cnt * 128], r=[bank.b], w=[dst_tile.b])
        K.evi += 1


def layer_norm(K, ph, y, g_bc, b_bc, out, tag):
    nc, S = K.nc, K.S
    if not hasattr(ph, "lntmp"):
        ph.lntmp = (ph.tile("lnst", [128, 2, 6]), ph.tile("lnmv", [128, 2]), ph.tile("lnsd", [128, 1]), ph.tile("lnrs", [128, 1]))
    st, mv, sd, rs = ph.lntmp
    for j in range(2):
        S.op("dve", lambda j=j: nc.vector.bn_stats(out=st[:, j, :], in_=y[:, j * 512:(j + 1) * 512]), r=[y.b], w=[st.b])
    S.op("dve", lambda: nc.vector.bn_aggr(out=mv[:, :], in_=st[:, :, :].rearrange("p a b -> p (a b)")), r=[st.b], w=[mv.b])
    S.op("act", lambda: nc.scalar.activation(out=sd[:, :], in_=mv[:, 1:2], func=AF.Sqrt, bias=K.eps_ln[:, :], scale=1.0),
         r=[mv.b, K.eps_ln.b], w=[sd.b])
    S.op("dve", lambda: nc.vector.reciprocal(out=rs[:, :], in_=sd[:, :]), r=[sd.b], w=[rs.b])
    S.op("dve", lambda: nc.vector.tensor_scalar(out=out[:, :], in0=y[:, :], scalar1=mv[:, 0:1], scalar2=rs[:, 0:1],
                                                op0=ALU.subtract, op1=ALU.mult), r=[y.b, mv.b, rs.b], w=[out.b])
    S.op("dve", lambda: nc.vector.tensor_tensor(out=out[:, :], in0=out[:, :], in1=g_bc[:, :], op=ALU.mult),
         r=[out.b, g_bc.b], w=[out.b])
    S.op("dve", lambda: nc.vector.tensor_tensor(out=out[:, :], in0=out[:, :], in1=b_bc[:, :], op=ALU.add),
         r=[out.b, b_bc.b], w=[out.b])


def load_bc(K, ph, name, dram_row_ap, n):
    nc, S = K.nc, K.S
    t = ph.tile(name, [128, n])
    S.dma("sp", lambda: nc.sync.dma_start(out=t[:, :], in_=dram_row_ap.to_broadcast([128, n])), w=[t.b])
    return t


def phase_a1(K, s):
    nc, S, d = K.nc, K.S, K.d
    ph = Phase(K)
    with ph.es:
        w_in = ph.tile("w_in", [128, 8, 800])
        S.dma("sp", lambda: nc.sync.dma_start(out=w_in[:, :, :], in_=d["a_w_in"].rearrange("(c p) n -> p c n", p=128)), w=[w_in.b])
        wq = ph.tile("wq", [128, 4, 1536])
        S.dma("sp", lambda: nc.sync.dma_start(out=wq[:, :, :], in_=d["a_w_q_up"].rearrange("(c p) n -> p c n", p=128)), w=[wq.b])
        wkv = ph.tile("wkv", [128, 2, 2048])
        S.dma("sp", lambda: nc.sync.dma_start(out=wkv[:, :, :], in_=d["a_w_kv_up"].rearrange("(c p) n -> p c n", p=128)), w=[wkv.b])
        qn_bc = load_bc(K, ph, "qn_bc", d["a_q_norm"], 512)
        kvn_bc = load_bc(K, ph, "kvn_bc", d["a_kv_norm"], 256)
        xs = [ph.tile("x%d" % i, [128, 1024]) for i in range(2)]
        css = [ph.tile("cs%d" % i, [128, 32]) for i in range(2)]
        xT = ph.tile("xT", [128, 8, 128])
        junk = ph.tile("junk", [128, 512])
        ss = ph.tile("ss", [128, 2])
        rs = ph.tile("rs", [128, 2])
        rr = ph.tile("rr", [128, 2])
        cq = ph.tile("cq", [128, 512])
        ckv = ph.tile("ckv", [128, 256])
        kraw = ph.tile("kraw", [128, 32])
        kpe = ph.tile("kpe", [128, 32])
        cqT = ph.tile("cqT", [128, 4, 128])
        ckvT = ph.tile("ckvT", [128, 2, 128])
        q_sb = ph.tile("q_sb", [128, 16, 96])
        qpe = ph.tile("qpe", [128, 16, 32])
        kv_sb = ph.tile("kv_sb", [128, 16, 128])
        vps = [ph.tile("vp%d" % i, [128, 16, 65], BF16) for i in range(2)]
        qnTs = [ph.tile("qnT%d" % i, [64, 16, 128], BF16) for i in range(2)]
        qpTs = [ph.tile("qpT%d" % i, [32, 16, 128], BF16) for i in range(2)]
        knTs = [ph.tile("knT%d" % i, [64, 16, 128], BF16) for i in range(2)]
        kpTs = [ph.tile("kpT%d" % i, [32, 128], BF16) for i in range(2)]
        for v in vps:
            S.op("dve", lambda v=v: nc.vector.memset(v[:, :, 64:65], 1.0), w=[v.b])
        for t in range(NT):
            p = t % 2
            t0 = t * 128
            x, cs, vp, qnT, qpT, knT, kpT = xs[p], css[p], vps[p], qnTs[p], qpTs[p], knTs[p], kpTs[p]
            S.dma("sp", lambda: nc.sync.dma_start(out=x[:, :], in_=d["x"][s, t0:t0 + 128, :]), w=[x.b])
            S.dma("sp", lambda: nc.sync.dma_start(out=cs[:, :], in_=d["rope32"][t0:t0 + 128, :]), w=[cs.b])
            transpose_to(K, x, [x[:, c * 128:(c + 1) * 128] for c in range(8)], xT,
                         lambda g0, n: xT[:, g0:g0 + n, :].rearrange("p a b -> p (a b)"), 128)
            bA, bB = K.bank(), K.bank()
            for c in range(8):
                S.op("pe", lambda c=c: nc.tensor.matmul(bA[:, 0:512], lhsT=xT[:, c, :], rhs=w_in[:, c, 0:512],
                                                        start=(c == 0), stop=(c == 7)), r=[xT.b, w_in.b], w=[bA.b])
            for c in range(8):
                S.op("pe", lambda c=c: nc.tensor.matmul(bB[:, 0:288], lhsT=xT[:, c, :], rhs=w_in[:, c, 512:800],
                                                        start=(c == 0), stop=(c == 7)), r=[xT.b, w_in.b], w=[bB.b])
            S.op("act", lambda: nc.scalar.activation(out=junk[:, 0:512], in_=bA[:, 0:512], func=AF.Square,
                                                     accum_out=ss[:, 0:1]), r=[bA.b], w=[junk.b, ss.b])
            S.op("act", lambda: nc.scalar.activation(out=junk[:, 0:256], in_=bB[:, 0:256], func=AF.Square,
                                                     accum_out=ss[:, 1:2]), r=[bB.b], w=[junk.b, ss.b])
            S.op("act", lambda: nc.scalar.activation(out=rs[:, 0:1], in_=ss[:, 0:1], func=AF.Sqrt, bias=K.eps_rms[:, :],
                                                     scale=1.0 / 512), r=[ss.b, K.eps_rms.b], w=[rs.b])
            S.op("act", lambda: nc.scalar.activation(out=rs[:, 1:2], in_=ss[:, 1:2], func=AF.Sqrt, bias=K.eps_rms[:, :],
                                                     scale=1.0 / 256), r=[ss.b, K.eps_rms.b], w=[rs.b])
            S.op("dve", lambda: nc.vector.reciprocal(out=rr[:, :], in_=rs[:, :]), r=[rs.b], w=[rr.b])
            S.op("dve", lambda: nc.vector.scalar_tensor_tensor(out=cq[:, :], in0=bA[:, 0:512], scalar=rr[:, 0:1],
                                                               in1=qn_bc[:, :], op0=ALU.mult, op1=ALU.mult),
                 r=[bA.b, rr.b, qn_bc.b], w=[cq.b])
            S.op("dve", lambda: nc.vector.scalar_tensor_tensor(out=ckv[:, :], in0=bB[:, 0:256], scalar=rr[:, 1:2],
                                                               in1=kvn_bc[:, :], op0=ALU.mult, op1=ALU.mult),
                 r=[bB.b, rr.b, kvn_bc.b], w=[ckv.b])
            S.op("act", lambda: nc.scalar.copy(out=kraw[:, :], in_=bB[:, 256:288]), r=[bB.b], w=[kraw.b])
            rope_tok(K, ph, kraw[:, 0:16], kraw[:, 16:32], kpe[:, 0:16], kpe[:, 16:32], cs[:, 0:16], cs[:, 16:32],
                     [128, 16], [kraw.b, cs.b], [kpe.b], "rk%d" % t)
            transpose_to(K, cq, [cq[:, c * 128:(c + 1) * 128] for c in range(4)], cqT,
                         lambda g0, n: cqT[:, g0:g0 + n, :].rearrange("p a b -> p (a b)"), 128)
            transpose_to(K, ckv, [ckv[:, c * 128:(c + 1) * 128] for c in range(2)], ckvT,
                         lambda g0, n: ckvT[:, g0:g0 + n, :].rearrange("p a b -> p (a b)"), 128)
            q_flat = q_sb[:, :, :].rearrange("p a b -> p (a b)")
            for n in range(3):
                bk = K.bank()
                for c in range(4):
                    S.op("pe", lambda c=c, n=n, bk=bk: nc.tensor.matmul(bk[:, 0:512], lhsT=cqT[:, c, :],
                                                                         rhs=wq[:, c, n * 512:(n + 1) * 512],
                                                                         start=(c == 0), stop=(c == 3)),
                         r=[cqT.b, wq.b], w=[bk.b])
                evac(K, n, q_flat[:, n * 512:(n + 1) * 512], bk[:, 0:512], r=[bk.b], w=[q_sb.b])
            kv_flat = kv_sb[:, :, :].rearrange("p a b -> p (a b)")
            for n in range(4):
                bk = K.bank()
                for c in range(2):
                    S.op("pe", lambda c=c, n=n, bk=bk: nc.tensor.matmul(bk[:, 0:512], lhsT=ckvT[:, c, :],
                                                                         rhs=wkv[:, c, n * 512:(n + 1) * 512],
                                                                         start=(c == 0), stop=(c == 1)),
                         r=[ckvT.b, wkv.b], w=[bk.b])
                evac(K, n + 1, kv_flat[:, n * 512:(n + 1) * 512], bk[:, 0:512], r=[bk.b], w=[kv_sb.b])
            cos_b = cs[:, 0:16].unsqueeze(1).to_broadcast([128, 16, 16])
            sin_b = cs[:, 16:32].unsqueeze(1).to_broadcast([128, 16, 16])
            rope_tok(K, ph, q_sb[:, :, 64:80], q_sb[:, :, 80:96], qpe[:, :, 0:16], qpe[:, :, 16:32], cos_b, sin_b,
                     [128, 16, 16], [q_sb.b, cs.b], [qpe.b], "rq%d" % t)
            S.op("act", lambda: nc.scalar.copy(out=vp[:, :, 0:64], in_=kv_sb[:, :, 64:128]), r=[kv_sb.b], w=[vp.b])
            transpose_to(K, q_sb, [q_sb[:, h, 0:64] for h in range(16)], qnT,
                         lambda g0, n: qnT[:, g0:g0 + n, :].rearrange("p a b -> p (a b)"), 64)
            transpose_to(K, qpe, [qpe[:, h, :] for h in range(16)], qpT,
                         lambda g0, n: qpT[:, g0:g0 + n, :].rearrange("p a b -> p (a b)"), 32)
            transpose_to(K, kv_sb, [kv_sb[:, h, 0:64] for h in range(16)], knT,
                         lambda g0, n: knT[:, g0:g0 + n, :].rearrange("p a b -> p (a b)"), 64)
            transpose_to(K, kpe, [kpe[:, :]], kpT, lambda g0, n: kpT[:, :], 32)
            wb = [K.db("qk", s, t)]
            S.dma("sp", lambda: nc.sync.dma_start(out=d["qnT"][:, :, t0:t0 + 128].rearrange("h e t -> e h t"),
                                                  in_=qnT[:, :, :]), r=[qnT.b], w=wb)
            S.dma("sp", lambda: nc.sync.dma_start(out=d["qpT"][:, :, t0:t0 + 128].rearrange("h e t -> e h t"),
                                                  in_=qpT[:, :, :]), r=[qpT.b], w=wb)
            S.dma("sp", lambda: nc.sync.dma_start(out=d["knT"][:, :, t0:t0 + 128].rearrange("h e t -> e h t"),
                                                  in_=knT[:, :, :]), r=[knT.b], w=wb)
            S.dma("sp", lambda: nc.sync.dma_start(out=d["kpT"][:, t0:t0 + 128], in_=kpT[:, :]), r=[kpT.b], w=wb)
            S.dma("sp", lambda: nc.sync.dma_start(out=d["vp"][:, t0:t0 + 128, :].rearrange("h t e -> t h e"),
                                                  in_=vp[:, :, :]), r=[vp.b], w=wb)
        S.barrier()


def attn_core(K, ph, pairs_q, pairs_k, vp, nkeys_tile, scale, o_sb_fn, store_fn, tag):
    pass


def phase_a2(K, s):
    nc, S, d = K.nc, K.S, K.d
    scale = 96.0 ** -0.5
    ph = Phase(K)
    with ph.es:
        kp = ph.tile("kp", [32, SEQ], BF16)
        allqk = [K.db("qk", s, t) for t in range(NT)]
        S.dma("sp", lambda: nc.sync.dma_start(out=kp[:, :], in_=d["kpT"][:, :]), r=allqk, w=[kp.b])
        qns = [ph.tile("qn%d" % i, [64, SEQ], BF16) for i in range(2)]
        qps = [ph.tile("qp%d" % i, [32, SEQ], BF16) for i in range(2)]
        kns = [ph.tile("kn%d" % i, [64, SEQ], BF16) for i in range(2)]
        vpt = [ph.tile("vpa%d" % i, [128, 16, 65], BF16) for i in range(2)]
        pts = [ph.tile("pt%d" % i, [128, 512], BF16) for i in range(3)]
        osb = [ph.tile("osb%d" % i, [128, 4, 64]) for i in range(2)]
        rden = ph.tile("rden", [128, 4])
        ctr = 0
        oc = 0
        for h in range(16):
            p = h % 2
            qn, qp, kn, vp = qns[p], qps[p], kns[p], vpt[p]
            S.dma("sp", lambda: nc.sync.dma_start(out=qn[:, :], in_=d["qnT"][h, :, :]), r=allqk, w=[qn.b])
            S.dma("sp", lambda: nc.sync.dma_start(out=qp[:, :], in_=d["qpT"][h, :, :]), r=allqk, w=[qp.b])
            S.dma("sp", lambda: nc.sync.dma_start(out=kn[:, :], in_=d["knT"][h, :, :]), r=allqk, w=[kn.b])
            S.dma("sp", lambda: nc.sync.dma_start(out=vp[:, :, :], in_=d["vp"][h, :, :].rearrange("(k p) e -> p k e", p=128)),
                  r=allqk, w=[vp.b])
            for qc in range(4):
                ob = [K.ps[4 + j] for j in range(4)]
                nk = 4 * qc + 4
                for kt in range(nk):
                    st = K.ps[ctr % 4]
                    pt = pts[ctr % 3]
                    ctr += 1
                    S.op("pe", lambda: nc.tensor.matmul(st[:, 0:512], lhsT=kn[:, kt * 128:(kt + 1) * 128],
                                                        rhs=qn[:, qc * 512:(qc + 1) * 512], start=True, stop=False),
                         r=[kn.b, qn.b], w=[st.b])
                    S.op("pe", lambda: nc.tensor.matmul(st[:, 0:512], lhsT=kp[:, kt * 128:(kt + 1) * 128],
                                                        rhs=qp[:, qc * 512:(qc + 1) * 512], start=False, stop=True),
                         r=[kp.b, qp.b], w=[st.b])
                    j0 = max(0, kt - 4 * qc)
                    S.op("act", lambda: nc.scalar.activation(out=pt[:, j0 * 128:512], in_=st[:, j0 * 128:512], func=AF.Exp,
                                                             scale=scale), r=[st.b], w=[pt.b])
                    if kt >= 4 * qc:
                        S.op("dve", lambda: nc.vector.tensor_tensor(out=pt[:, j0 * 128:(j0 + 1) * 128],
                                                                    in0=pt[:, j0 * 128:(j0 + 1) * 128],
                                                                    in1=K.tri_le[:, :], op=ALU.mult),
                             r=[pt.b, K.tri_le.b], w=[pt.b])
                    for j in range(j0, 4):
                        qt = 4 * qc + j
                        S.op("pe", lambda j=j, qt=qt: nc.tensor.matmul(ob[j][:, 0:65], lhsT=pt[:, j * 128:(j + 1) * 128],
                                                                       rhs=vp[:, kt, :], start=(kt == 0), stop=(kt == qt)),
                             r=[pt.b, vp.b], w=[ob[j].b])
                o = osb[oc % 2]
                oc += 1
                for j in range(4):
                    S.op("dve", lambda j=j: nc.vector.reciprocal(out=rden[:, j:j + 1], in_=ob[j][:, 64:65]),
                         r=[ob[j].b], w=[rden.b])
                    S.op("dve", lambda j=j: nc.vector.tensor_scalar(out=o[:, j, :], in0=ob[j][:, 0:64],
                                                                    scalar1=rden[:, j:j + 1], scalar2=None, op0=ALU.mult),
                         r=[ob[j].b, rden.b], w=[o.b])
                S.dma("sp", lambda: nc.sync.dma_start(
                    out=d["attn"][qc * 512:(qc + 1) * 512, h * 64:(h + 1) * 64].rearrange("(j p) e -> p j e", p=128),
                    in_=o[:, :, :]), r=[o.b], w=[K.db("attn", s, qc, h)])
        S.barrier()


def phase_oproj_ln(K, s, w_o_ap, res_fn, res_deps_fn, g_ap, b_ap, dst_fn, dst_buf_fn, tag):
    nc, S, d = K.nc, K.S, K.d
    ph = Phase(K)
    with ph.es:
        wo = ph.tile("wo", [128, 8, 1024])
        S.dma("sp", lambda: nc.sync.dma_start(out=wo[:, :, :], in_=w_o_ap.rearrange("(c p) n -> p c n", p=128)), w=[wo.b])
        g_bc = load_bc(K, ph, "g_bc", g_ap, 1024)
        b_bc = load_bc(K, ph, "b_bc", b_ap, 1024)
        ats = [ph.tile("at%d" % i, [128, 1024]) for i in range(2)]
        xs = [ph.tile("xr%d" % i, [128, 1024]) for i in range(2)]
        aT = ph.tile("aT", [128, 8, 128])
        y = ph.tile("y", [128, 1024])
        outs = [ph.tile("ho%d" % i, [128, 1024]) for i in range(2)]
        for t in range(NT):
            p = t % 2
            t0 = t * 128
            at, x, o = ats[p], xs[p], outs[p]
            S.dma("sp", lambda: nc.sync.dma_start(out=at[:, :], in_=d["attn"][t0:t0 + 128, :]),
                  r=[K.db("attn", s, t // 4, h) for h in range(16)], w=[at.b])
            S.dma("sp", lambda: nc.sync.dma_start(out=x[:, :], in_=res_fn(t0)), r=res_deps_fn(t), w=[x.b])
            transpose_to(K, at, [at[:, c * 128:(c + 1) * 128] for c in range(8)], aT,
                         lambda g0, n: aT[:, g0:g0 + n, :].rearrange("p a b -> p (a b)"), 128)
            for n in range(2):
                bk = K.bank()
                for c in range(8):
                    S.op("pe", lambda c=c, n=n, bk=bk: nc.tensor.matmul(bk[:, 0:512], lhsT=aT[:, c, :],
                                                                         rhs=wo[:, c, n * 512:(n + 1) * 512],
                                                                         start=(c == 0), stop=(c == 7)),
                         r=[aT.b, wo.b], w=[bk.b])
                S.op("dve", lambda n=n, bk=bk: nc.vector.scalar_tensor_tensor(
                    out=y[:, n * 512:(n + 1) * 512], in0=x[:, n * 512:(n + 1) * 512], scalar=ALPHA, in1=bk[:, 0:512],
                    op0=ALU.mult, op1=ALU.add), r=[x.b, bk.b], w=[y.b])
            layer_norm(K, ph, y, g_bc, b_bc, o, "%s%d" % (tag, t))
            S.dma("sp", lambda: nc.sync.dma_start(out=dst_fn(t0), in_=o[:, :]), r=[o.b], w=[dst_buf_fn(t)])
        S.barrier()


def prologue_tables(K):
    nc, S, d = K.nc, K.S, K.d
    ph = Phase(K)
    with ph.es:
        inb = [ph.tile("tin%d" % i, [128, 4, 2048]) for i in range(3)]
        outb = [ph.tile("tout%d" % i, [128, 4, 2048], BF16) for i in range(3)]
        k = 0
        for layer in range(2):
            src = d["p_uv%d" % layer].rearrange("(p j) n -> p j n", p=128)
            dst = d["p_uvb%d" % layer].rearrange("(p j) n -> p j n", p=128)
            for c in range(32):
                a, b = inb[k % 3], outb[k % 3]
                S.dma("sp", lambda: nc.sync.dma_start(out=a[:, :, :], in_=src[:, 4 * c:4 * c + 4, :]), w=[a.b])
                if k % 2 == 0:
                    S.op("act", lambda: nc.scalar.copy(out=b[:, :, :], in_=a[:, :, :]), r=[a.b], w=[b.b])
                else:
                    S.op("dve", lambda: nc.vector.tensor_copy(out=b[:, :, :], in_=a[:, :, :]), r=[a.b], w=[b.b])
                S.dma("sp", lambda: nc.sync.dma_start(out=dst[:, 4 * c:4 * c + 4, :], in_=b[:, :, :]), r=[b.b], w=[K.db("uvb", layer, c)])
                k += 1
        S.barrier()


def phase_peer(K, s, layer, src_fn, src_buf_fn, dst_fn, dst_buf_fn, ntiles=NT):
    nc, S, d = K.nc, K.S, K.d
    ph = Phase(K)
    NG = 9
    K.bank_list = [0, 1, 2, 3, 4, 5]
    accb = [K.ps[6], K.ps[7]]
    with ph.es:
        wq = ph.tile("pwq", [128, 8, 2048])
        S.dma("sp", lambda: nc.sync.dma_start(out=wq[:, :, :], in_=d["p_w_q"][layer].rearrange("(c p) n -> p c n", p=128)), w=[wq.b])
        skT = ph.tile("skT", [128, 16, 128])
        S.dma("sp", lambda: nc.sync.dma_start(out=skT[:, :, :], in_=d["p_skT"][layer]), w=[skT.b])
        g_bc = load_bc(K, ph, "pg_bc", d["ln_g"][2 * layer + 1:2 * layer + 2, :], 1024)
        b_bc = load_bc(K, ph, "pb_bc", d["ln_b"][2 * layer + 1:2 * layer + 2, :], 1024)
        iota = ph.tile("iota", [128, 16])
        S.dma("sp", lambda: nc.sync.dma_start(out=iota[:, :], in_=d["iota16"]), w=[iota.b])
        identb = ph.tile("identb", [128, 128], BF16)
        S.op("dve", lambda: nc.vector.tensor_copy(out=identb[:, :], in_=K.ident[:, :]), r=[K.ident.b], w=[identb.b])
        G = [ph.tile("G%d" % i, [128, 2048], BF16) for i in range(NG)]
        hs = [ph.tile("ph%d" % i, [128, 1024]) for i in range(2)]
        eidxs = [ph.tile("peidx%d" % i, [128, 128], I32) for i in range(2)]
        gws = [ph.tile("pgw%d" % i, [128, 128]) for i in range(2)]
        hT = ph.tile("phT", [128, 8, 128])
        q_sb = ph.tile("pq", [128, 2048])
        qT = ph.tile("pqT", [128, 16, 128])
        sc = ph.tile("psc", [128, 16, 128])
        sc2 = ph.tile("psc2", [128, 128])
        m1 = ph.tile("pm1", [128, 16, 16])
        i1 = ph.tile("pi1", [128, 16, 16], U32)
        i1f = ph.tile("pi1f", [128, 16, 16])
        cand = ph.tile("pcand", [128, 8, 256])
        cand2 = ph.tile("pcand2", [128, 256])
        best = ph.tile("pbest", [128, 8, 16])
        pos = ph.tile("ppos", [128, 8, 16], U32)
        hi = ph.tile("phi", [128, 8, 16], U32)
        lo = ph.tile("plo", [128, 8, 16], U32)
        hif = ph.tile("phif", [128, 8, 16])
        lof = ph.tile("plof", [128, 8, 16])
        eq = ph.tile("peq", [128, 8, 16, 16])
        e0 = ph.tile("pe0", [128, 8, 16])
        e1 = ph.tile("pe1", [128, 8, 16])
        ef = ph.tile("pef", [128, 128])
        bm = ph.tile("pbm", [128, 8, 16])
        se = ph.tile("pse", [128, 8])
        a = ph.tile("pa", [128, 128])
        ga = ph.tile("pga", [128, 128])
        w = ph.tile("pw", [128, 128])
        diags = [ph.tile("pdiag%d" % i, [128, 4, 128], BF16) for i in range(3)]
        prods = [ph.tile("pprod%d" % i, [128, 1024]) for i in range(3)]
        y = ph.tile("py", [128, 1024])
        o = ph.tile("po", [128, 1024])
        m1v = m1[:, :, :].rearrange("p (h two) k -> p h two k", two=2)
        i1fv = i1f[:, :, :].rearrange("p (h two) k -> p h two k", two=2)
        iota_b = iota[:, :].unsqueeze(1).unsqueeze(1).to_broadcast([128, 8, 16, 16])
        ident_b = identb[:, :].unsqueeze(1).to_broadcast([128, 4, 128])

        def front(t):
            t0 = t * 128
            h, eidx, gw = hs[t % 2], eidxs[t % 2], gws[t % 2]
            S.dma("sp", lambda: nc.sync.dma_start(out=h[:, :], in_=src_fn(t0)), r=src_buf_fn(t), w=[h.b])
            transpose_to(K, h, [h[:, c * 128:(c + 1) * 128] for c in range(8)], hT,
                         lambda g0, n: hT[:, g0:g0 + n, :].rearrange("p a b -> p (a b)"), 128)
            yield
            for n in range(4):
                bk = K.bank()
                for c in range(8):
                    S.op("pe", lambda: nc.tensor.matmul(bk[:, 0:512], lhsT=hT[:, c, :], rhs=wq[:, c, n * 512:(n + 1) * 512],
                                                        start=(c == 0), stop=(c == 7)), r=[hT.b, wq.b], w=[bk.b])
                evac(K, n, q_sb[:, n * 512:(n + 1) * 512], bk[:, 0:512], r=[bk.b], w=[q_sb.b])
                yield
            transpose_to(K, q_sb, [q_sb[:, c * 128:(c + 1) * 128] for c in range(16)], qT,
                         lambda g0, n: qT[:, g0:g0 + n, :].rearrange("p a b -> p (a b)"), 128)
            yield
            for n in range(4):
                bk = K.bank()
                for j in range(4):
                    hp = n * 4 + j
                    S.op("pe", lambda: nc.tensor.matmul(bk[:, j * 128:(j + 1) * 128], lhsT=qT[:, hp, :], rhs=skT[:, hp, :],
                                                        start=True, stop=True), r=[qT.b, skT.b], w=[bk.b])
                evac(K, n, sc[:, n * 4:(n + 1) * 4, :].rearrange("p a b -> p (a b)"), bk[:, 0:512], r=[bk.b], w=[sc.b])
            yield
            for hp in range(16):
                S.op("dve", lambda: nc.vector.max(out=m1[:, hp, 0:8], in_=sc[:, hp, :]), r=[sc.b], w=[m1.b])
                S.op("dve", lambda: nc.vector.max_index(out=i1[:, hp, 0:8], in_max=m1[:, hp, 0:8], in_values=sc[:, hp, :]),
                     r=[sc.b, m1.b], w=[i1.b])
                S.op("dve", lambda: nc.vector.match_replace(out=sc2[:, :], in_to_replace=m1[:, hp, 0:8],
                                                            in_values=sc[:, hp, :], imm_value=NEG), r=[sc.b, m1.b], w=[sc2.b])
                S.op("dve", lambda: nc.vector.max(out=m1[:, hp, 8:16], in_=sc2[:, :]), r=[sc2.b], w=[m1.b])
                S.op("dve", lambda: nc.vector.max_index(out=i1[:, hp, 8:16], in_max=m1[:, hp, 8:16], in_values=sc2[:, :]),
                     r=[sc2.b, m1.b], w=[i1.b])
                yield
            S.op("dve", lambda: nc.vector.tensor_tensor(
                out=cand[:, :, :].rearrange("p h (i j) -> p h i j", j=16),
                in0=m1v[:, :, 0, :].unsqueeze(3).to_broadcast([128, 8, 16, 16]),
                in1=m1v[:, :, 1, :].unsqueeze(2).to_broadcast([128, 8, 16, 16]), op=ALU.add), r=[m1.b], w=[cand.b])
            for hh in range(8):
                S.op("dve", lambda: nc.vector.max(out=best[:, hh, 0:8], in_=cand[:, hh, :]), r=[cand.b], w=[best.b])
                S.op("dve", lambda: nc.vector.max_index(out=pos[:, hh, 0:8], in_max=best[:, hh, 0:8], in_values=cand[:, hh, :]),
                     r=[cand.b, best.b], w=[pos.b])
                S.op("dve", lambda: nc.vector.match_replace(out=cand2[:, :], in_to_replace=best[:, hh, 0:8],
                                                            in_values=cand[:, hh, :], imm_value=NEG), r=[cand.b, best.b], w=[cand2.b])
                S.op("dve", lambda: nc.vector.max(out=best[:, hh, 8:16], in_=cand2[:, :]), r=[cand2.b], w=[best.b])
                S.op("dve", lambda: nc.vector.max_index(out=pos[:, hh, 8:16], in_max=best[:, hh, 8:16], in_values=cand2[:, :]),
                     r=[cand2.b, best.b], w=[pos.b])
                yield
            S.op("dve", lambda: nc.vector.tensor_tensor(out=bm[:, :, :], in0=best[:, :, :],
                                                        in1=best[:, :, 0:1].to_broadcast([128, 8, 16]), op=ALU.subtract),
                 r=[best.b], w=[bm.b])
            S.op("act", lambda: nc.scalar.activation(out=bm[:, :, :], in_=bm[:, :, :], func=AF.Exp), r=[bm.b], w=[bm.b])
            S.op("dve", lambda: nc.vector.tensor_reduce(out=se[:, :], in_=bm[:, :, :], axis=AX.X, op=ALU.add), r=[bm.b], w=[se.b])
            S.op("dve", lambda: nc.vector.reciprocal(out=se[:, :], in_=se[:, :]), r=[se.b], w=[se.b])
            S.op("dve", lambda: nc.vector.tensor_tensor(out=gw[:, :].rearrange("p (h k) -> p h k", k=16), in0=bm[:, :, :],
                                                        in1=se[:, :].unsqueeze(2).to_broadcast([128, 8, 16]), op=ALU.mult),
                 r=[bm.b, se.b], w=[gw.b])
            yield
            S.op("dve", lambda: nc.vector.tensor_single_scalar(out=hi[:, :, :], in_=pos[:, :, :], scalar=4,
                                                               op=ALU.logical_shift_right), r=[pos.b], w=[hi.b])
            S.op("dve", lambda: nc.vector.tensor_single_scalar(out=lo[:, :, :], in_=pos[:, :, :], scalar=15,
                                                               op=ALU.bitwise_and), r=[pos.b], w=[lo.b])
            S.op("dve", lambda: nc.vector.tensor_copy(out=hif[:, :, :], in_=hi[:, :, :]), r=[hi.b], w=[hif.b])
            S.op("dve", lambda: nc.vector.tensor_copy(out=lof[:, :, :], in_=lo[:, :, :]), r=[lo.b], w=[lof.b])
            S.op("dve", lambda: nc.vector.tensor_copy(out=i1f[:, :, :], in_=i1[:, :, :]), r=[i1.b], w=[i1f.b])
            yield
            for (xf, half, eo) in ((hif, 0, e0), (lof, 1, e1)):
                S.op("dve", lambda: nc.vector.tensor_tensor(out=eq[:, :, :, :],
                                                            in0=xf[:, :, :].unsqueeze(3).to_broadcast([128, 8, 16, 16]),
                                                            in1=iota_b, op=ALU.is_equal), r=[xf.b, iota.b], w=[eq.b])
                S.op("dve", lambda: nc.vector.tensor_tensor(out=eq[:, :, :, :], in0=eq[:, :, :, :],
                                                            in1=i1fv[:, :, half, :].unsqueeze(2).to_broadcast([128, 8, 16, 16]),
                                                            op=ALU.mult), r=[eq.b, i1f.b], w=[eq.b])
                S.op("dve", lambda: nc.vector.tensor_reduce(out=eo[:, :, :], in_=eq[:, :, :, :], axis=AX.X, op=ALU.add),
                     r=[eq.b], w=[eo.b])
                yield
            S.op("dve", lambda: nc.vector.scalar_tensor_tensor(out=ef[:, :], in0=e0[:, :, :].rearrange("p h k -> p (h k)"),
                                                               scalar=128.0, in1=e1[:, :, :].rearrange("p h k -> p (h k)"),
                                                               op0=ALU.mult, op1=ALU.add), r=[e0.b, e1.b], w=[ef.b])
            S.op("dve", lambda: nc.vector.tensor_copy(out=eidx[:, :], in_=ef[:, :]), r=[ef.b], w=[eidx.b])
            yield

        gi = 0
        pi = 0
        di = 0

        def main(t, gen):
            nonlocal gi, pi, di
            t0 = t * 128
            h, eidx, gw = hs[t % 2], eidxs[t % 2], gws[t % 2]
            for g0 in range(0, 128, 4):
                gs = []
                for sl in range(g0, g0 + 4):
                    Gt = G[gi % NG]
                    gi += 1
                    gs.append(Gt)
                    S.dma("pool", lambda: nc.gpsimd.indirect_dma_start(
                        out=Gt[:, :], out_offset=None, in_=d["p_uvb%d" % layer],
                        in_offset=bass.IndirectOffsetOnAxis(ap=eidx[:, sl:sl + 1], axis=0)), r=[eidx.b], w=[Gt.b])
                    pj = prods[pi % 3]
                    pi += 1
                    S.op("dve", lambda: nc.vector.tensor_tensor(out=pj[:, :], in0=Gt[:, 0:1024], in1=h[:, :], op=ALU.mult),
                         r=[Gt.b, h.b], w=[pj.b])
                    S.op("act", lambda: nc.scalar.activation(out=pj[:, :], in_=pj[:, :], func=AF.Identity,
                                                             accum_out=a[:, sl:sl + 1]), r=[pj.b], w=[pj.b, a.b])
                S.op("act", lambda: nc.scalar.activation(out=ga[:, g0:g0 + 4], in_=a[:, g0:g0 + 4], func=AF.Gelu), r=[a.b], w=[ga.b])
                S.op("dve", lambda: nc.vector.tensor_tensor(out=w[:, g0:g0 + 4], in0=ga[:, g0:g0 + 4], in1=gw[:, g0:g0 + 4],
                                                            op=ALU.mult), r=[ga.b, gw.b], w=[w.b])
                dg = diags[di % 3]
                di += 1
                S.op("dve", lambda: nc.vector.tensor_tensor(out=dg[:, :, :], in0=ident_b,
                                                            in1=w[:, g0:g0 + 4].unsqueeze(2).to_broadcast([128, 4, 128]),
                                                            op=ALU.mult), r=[identb.b, w.b], w=[dg.b])
                for j, sl in enumerate(range(g0, g0 + 4)):
                    Gt = gs[j]
                    for n in range(2):
                        S.op("pe", lambda: nc.tensor.matmul(accb[n][:, 0:512], lhsT=dg[:, j, :],
                                                            rhs=Gt[:, 1024 + n * 512:1024 + (n + 1) * 512],
                                                            start=(sl == 0), stop=(sl == 127)), r=[dg.b, Gt.b], w=[accb[n].b])
                if gen is not None:
                    next(gen, None)
                    next(gen, None)
            for n in range(2):
                S.op("dve", lambda: nc.vector.scalar_tensor_tensor(out=y[:, n * 512:(n + 1) * 512], in0=h[:, n * 512:(n + 1) * 512],
                                                                   scalar=ALPHA, in1=accb[n][:, 0:512], op0=ALU.mult, op1=ALU.add),
                     r=[h.b, accb[n].b], w=[y.b])
            layer_norm(K, ph, y, g_bc, b_bc, o, "p%d_%d" % (layer, t))
            S.dma("sp", lambda: nc.sync.dma_start(out=dst_fn(t0), in_=o[:, :]), r=[o.b], w=[dst_buf_fn(t)])

        for _ in front(0):
            pass
        for t in range(ntiles):
            gen = front(t + 1) if t + 1 < ntiles else None
            main(t, gen)
            if gen is not None:
                for _ in gen:
                    pass
        S.barrier()
    K.bank_list = list(range(8))


def phase_b0(K, s):
    nc, S, d = K.nc, K.S, K.d
    ph = Phase(K)
    with ph.es:
        wkv = ph.tile("swkv", [128, 8, 1536])
        S.dma("sp", lambda: nc.sync.dma_start(out=wkv[:, :, :], in_=d["s_w_kv"].rearrange("(c p) n -> p c n", p=128)), w=[wkv.b])
        wqb = ph.tile("bwin", [128, 8, 1072])
        S.dma("sp", lambda: nc.sync.dma_start(out=wqb[:, :, :], in_=d["b_w_in"].rearrange("(c p) n -> p c n", p=128)), w=[wqb.b])
        h = ph.tile("bh", [128, 1024])
        cs = ph.tile("bcs", [128, 64])
        hT = ph.tile("bhT", [128, 8, 128])
        kvs = ph.tile("kvs", [128, 3, 4, 128])
        q_sb = ph.tile("bq", [128, 16, 64])
        qr = ph.tile("bqr", [128, 16, 64])
        kr = ph.tile("bkr", [128, 2, 4, 64])
        gts = ph.tile("bgts", [128, 48])
        vps = [ph.tile("bvp%d" % i, [128, 4, 65], BF16) for i in range(2)]
        qT = ph.tile("bqT", [64, 16, 128], BF16)
        qrT = ph.tile("bqrT", [64, 16, 128], BF16)
        kT = ph.tile("bkT", [64, 8, 128])
        kTr = ph.tile("bkTr", [64, 8, 128], BF16)
        for v in vps:
            S.op("dve", lambda: nc.vector.memset(v[:, :, 64:65], 1.0), w=[v.b])
        for t in range(K.ntiles):
            t0 = t * 128
            S.dma("sp", lambda: nc.sync.dma_start(out=h[:, :], in_=d["h2"][t0:t0 + 128, :]), r=[K.db("h2", s, t)], w=[h.b])
            S.dma("sp", lambda: nc.sync.dma_start(out=cs[:, :], in_=d["rope64"][t0:t0 + 128, :]), w=[cs.b])
            transpose_to(K, h, [h[:, c * 128:(c + 1) * 128] for c in range(8)], hT,
                         lambda g0, n: hT[:, g0:g0 + n, :].rearrange("p a b -> p (a b)"), 128)
            kv_flat = kvs[:, :, :, :].rearrange("p a b c -> p (a b c)")
            for n in range(3):
                bk = K.bank()
                for c in range(8):
                    S.op("pe", lambda: nc.tensor.matmul(bk[:, 0:512], lhsT=hT[:, c, :], rhs=wkv[:, c, n * 512:(n + 1) * 512],
                                                        start=(c == 0), stop=(c == 7)), r=[hT.b, wkv.b], w=[bk.b])
                evac(K, n, kv_flat[:, n * 512:(n + 1) * 512], bk[:, 0:512], r=[bk.b], w=[kvs.b])
            q_flat = q_sb[:, :, :].rearrange("p a b -> p (a b)")
            for n in range(2):
                bk = K.bank()
                for c in range(8):
                    S.op("pe", lambda: nc.tensor.matmul(bk[:, 0:512], lhsT=hT[:, c, :], rhs=wqb[:, c, n * 512:(n + 1) * 512],
                                                        start=(c == 0), stop=(c == 7)), r=[hT.b, wqb.b], w=[bk.b])
                evac(K, n + 1, q_flat[:, n * 512:(n + 1) * 512], bk[:, 0:512], r=[bk.b], w=[q_sb.b])
            bk = K.bank()
            for c in range(8):
                S.op("pe", lambda: nc.tensor.matmul(bk[:, 0:48], lhsT=hT[:, c, :], rhs=wqb[:, c, 1024:1072],
                                                    start=(c == 0), stop=(c == 7)), r=[hT.b, wqb.b], w=[bk.b])
            S.op("act", lambda: nc.scalar.activation(out=gts[:, :], in_=bk[:, 0:48], func=AF.Sigmoid), r=[bk.b], w=[gts.b])
            S.dma("sp", lambda: nc.sync.dma_start(out=d["gates"][t0:t0 + 128, :], in_=gts[:, :]), r=[gts.b], w=[K.db("b0", s, t)])
            cos_b = cs[:, 0:32].unsqueeze(1).to_broadcast([128, 16, 32])
            sin_b = cs[:, 32:64].unsqueeze(1).to_broadcast([128, 16, 32])
            rope_tok(K, ph, q_sb[:, :, 0:32], q_sb[:, :, 32:64], qr[:, :, 0:32], qr[:, :, 32:64], cos_b, sin_b,
                     [128, 16, 32], [q_sb.b, cs.b], [qr.b], "brq%d" % t)
            cos_k = cs[:, 0:32].unsqueeze(1).to_broadcast([128, 4, 32])
            sin_k = cs[:, 32:64].unsqueeze(1).to_broadcast([128, 4, 32])
            for br in range(2):
                rope_tok(K, ph, kvs[:, 1 + br, :, 0:32], kvs[:, 1 + br, :, 32:64], kr[:, br, :, 0:32], kr[:, br, :, 32:64],
                         cos_k, sin_k, [128, 4, 32], [kvs.b, cs.b], [kr.b], "brk%d_%d" % (t, br))
            transpose_to(K, q_sb, [q_sb[:, hh, :] for hh in range(16)], qT,
                         lambda g0, n: qT[:, g0:g0 + n, :].rearrange("p a b -> p (a b)"), 64)
            transpose_to(K, qr, [qr[:, hh, :] for hh in range(16)], qrT,
                         lambda g0, n: qrT[:, g0:g0 + n, :].rearrange("p a b -> p (a b)"), 64)
            srcs = [kvs[:, 0, g, 0:64] for g in range(4)] + [kvs[:, 0, g, 64:128] for g in range(4)]
            transpose_to(K, kvs, srcs, kT, lambda g0, n: kT[:, g0:g0 + n, :].rearrange("p a b -> p (a b)"), 64)
            srcs = [kr[:, 0, g, :] for g in range(4)] + [kr[:, 1, g, :] for g in range(4)]
            transpose_to(K, kr, srcs, kTr, lambda g0, n: kTr[:, g0:g0 + n, :].rearrange("p a b -> p (a b)"), 64)
            wb = [K.db("b0", s, t)]
            S.dma("sp", lambda: nc.sync.dma_start(out=d["bqT"][:, :, t0:t0 + 128].rearrange("h e t -> e h t"), in_=qT[:, :, :]),
                  r=[qT.b], w=wb)
            S.dma("sp", lambda: nc.sync.dma_start(out=d["bqrT"][:, :, t0:t0 + 128].rearrange("h e t -> e h t"), in_=qrT[:, :, :]),
                  r=[qrT.b], w=wb)
            S.dma("sp", lambda: nc.sync.dma_start(out=d["bkcT"][:, :, t0:t0 + 128].rearrange("h e t -> e h t"), in_=kT[:, :, :]),
                  r=[kT.b], w=wb)
            S.dma("sp", lambda: nc.sync.dma_start(out=d["bkrT"][:, :, t0:t0 + 128].rearrange("h e t -> e h t"), in_=kTr[:, :, :]),
                  r=[kTr.b], w=wb)
            for br in range(2):
                vp = vps[br]
                S.op("act", lambda: nc.scalar.copy(out=vp[:, :, 0:64], in_=kvs[:, 1 + br, :, 64:128]), r=[kvs.b], w=[vp.b])
                S.dma("sp", lambda: nc.sync.dma_start(out=d["bvp"][br, :, t0:t0 + 128, :].rearrange("g t e -> t g e"),
                                                      in_=vp[:, :, :]), r=[vp.b], w=wb)
        S.barrier()


def phase_b1(K, s):
    nc, S, d = K.nc, K.S, K.d
    ph = Phase(K)
    allb0 = [K.db("b0", s, t) for t in range(NT)]
    with ph.es:
        ovl = ph.tile("ovl", [128, 32])
        S.dma("sp", lambda: nc.sync.dma_start(out=ovl[:, :], in_=d["overlap"]), w=[ovl.b])
        for kv in range(2):
            nm = "k" if kv == 0 else "v"
            w1 = ph.tile("cw1" + nm, [64, 32, 256])
            S.dma("sp", lambda: nc.sync.dma_start(out=w1[:, :, :], in_=d["s_cmp_%s_w1" % nm].rearrange("(l e) n -> e l n", e=64)), w=[w1.b])
            w2 = ph.tile("cw2" + nm, [128, 2, 64])
            S.dma("sp", lambda: nc.sync.dma_start(out=w2[:, :, :], in_=d["s_cmp_%s_w2" % nm].rearrange("(c p) n -> p c n", p=128)), w=[w2.b])
            b1 = ph.tile("cb1" + nm, [128, 2])
            S.dma("sp", lambda: nc.sync.dma_start(out=b1[:, :], in_=d["s_cmp_%s_b1" % nm]), w=[b1.b])
            posT = ph.tile("cpos" + nm, [64, 32])
            S.dma("sp", lambda: nc.sync.dma_start(out=posT[:, :], in_=d["s_cmp_pos_%sT" % nm]), w=[posT.b])
            xT = ph.tile("cxT" + nm, [64, SEQ])
            Xp = ph.tile("cXp" + nm, [64, 32, 127])
            hid = ph.tile("chid" + nm, [128, 2, 127])
            res = ph.tile("cres" + nm, [128, 128], BF16)
            for g in range(4):
                S.dma("sp", lambda: nc.sync.dma_start(out=xT[:, :], in_=d["bkcT"][kv * 4 + g, :, :]), r=allb0, w=[xT.b])
                for half in range(2):
                    S.op("dve", lambda: nc.vector.tensor_tensor(
                        out=Xp[:, half * 16:(half + 1) * 16, :],
                        in0=xT[:, half * 16:half * 16 + 2032].rearrange("p (j l) -> p l j", l=16),
                        in1=posT[:, half * 16:(half + 1) * 16].unsqueeze(2).to_broadcast([64, 16, 127]), op=ALU.add),
                        r=[xT.b, posT.b], w=[Xp.b])
                for hc in range(2):
                    bk = K.bank()
                    for l in range(32):
                        S.op("pe", lambda: nc.tensor.matmul(bk[:, 0:127], lhsT=w1[:, l, hc * 128:(hc + 1) * 128], rhs=Xp[:, l, :],
                                                            start=(l == 0), stop=(l == 31)), r=[w1.b, Xp.b], w=[bk.b])
                    S.op("act", lambda: nc.scalar.activation(out=hid[:, hc, :], in_=bk[:, 0:127], func=AF.Gelu, bias=b1[:, hc:hc + 1],
                                                             scale=1.0), r=[bk.b, b1.b], w=[hid.b])
                bk = K.bank()
                if kv == 0:
                    for hc in range(2):
                        S.op("pe", lambda: nc.tensor.matmul(bk[0:64, 0:127], lhsT=w2[:, hc, :], rhs=hid[:, hc, :],
                                                            start=(hc == 0), stop=(hc == 1)), r=[w2.b, hid.b], w=[bk.b])
                    S.op("dve", lambda: nc.vector.memset(res[0:64, :], 0.0), w=[res.b])
                    S.op("dve", lambda: nc.vector.tensor_copy(out=res[0:64, 0:127], in_=bk[0:64, 0:127]), r=[bk.b], w=[res.b])
                    S.dma("sp", lambda: nc.sync.dma_start(out=d["kcmpT"][g, :, :], in_=res[0:64, :]), r=[res.b], w=[K.db("b1", s)])
                else:
                    for hc in range(2):
                        S.op("pe", lambda: nc.tensor.matmul(bk[0:127, 0:64], lhsT=hid[:, hc, :], rhs=w2[:, hc, :],
                                                            start=(hc == 0), stop=(hc == 1)), r=[w2.b, hid.b], w=[bk.b])
                    S.op("dve", lambda: nc.vector.memset(res[:, :], 0.0), w=[res.b])
                    S.op("dve", lambda: nc.vector.tensor_copy(out=res[0:127, 0:64], in_=bk[0:127, 0:64]), r=[bk.b], w=[res.b])
                    S.op("dve", lambda: nc.vector.memset(res[0:127, 64:65], 1.0), r=[], w=[res.b])
                    S.op("dve", lambda: nc.vector.tensor_copy(out=res[0:127, 65:97], in_=ovl[0:127, :]), r=[ovl.b], w=[res.b])
                    S.dma("sp", lambda: nc.sync.dma_start(out=d["vcmp"][g, :, :], in_=res[:, 0:97]), r=[res.b], w=[K.db("b1", s)])
        S.barrier()


def phase_b2(K, s):
    nc, S, d = K.nc, K.S, K.d
    scale = 64.0 ** -0.5
    ph = Phase(K)
    allb0 = [K.db("b0", s, t) for t in range(NT)]
    b1b = [K.db("b1", s)]
    with ph.es:
        cmask = ph.tile("cmask", [128, SEQ])
        S.dma("sp", lambda: nc.sync.dma_start(out=cmask[:, :], in_=d["cmask"]), w=[cmask.b])
        Eall32 = ph.tile("Eall32", [32, 16, 128])
        S.dma("sp", lambda: nc.sync.dma_start(out=Eall32[:, :, :], in_=d["Eall"]), w=[Eall32.b])
        Eall = ph.tile("Eall", [32, 16, 128], BF16)
        S.op("dve", lambda: nc.vector.tensor_copy(out=Eall[:, :, :], in_=Eall32[:, :, :]), r=[Eall32.b], w=[Eall.b])
        At = ph.tile("At", [128, 16, 32])
        Bt = ph.tile("Bt", [128, 16, 32])
        S.dma("sp", lambda: nc.sync.dma_start(out=At[:, :, :], in_=d["selA"].rearrange("(q p) n -> p q n", p=128)), w=[At.b])
        S.dma("sp", lambda: nc.sync.dma_start(out=Bt[:, :, :], in_=d["selB"].rearrange("(q p) n -> p q n", p=128)), w=[Bt.b])
        gts = ph.tile("gts", [128, 16, 48])
        S.dma("sp", lambda: nc.sync.dma_start(out=gts[:, :, :], in_=d["gates"].rearrange("(q p) n -> p q n", p=128)), r=allb0, w=[gts.b])
        ksT = ph.tile("ksT", [64, SEQ], BF16)
        kwT = ph.tile("kwT", [64, SEQ], BF16)
        vs = ph.tile("vs", [128, 16, 65], BF16)
        vw = ph.tile("vw", [128, 16, 65], BF16)
        kcT = ph.tile("kcT", [64, 128], BF16)
        vc = ph.tile("vc", [128, 97], BF16)
        qTs = [ph.tile("nqT%d" % i, [64, SEQ], BF16) for i in range(4)]
        qrTs = [ph.tile("nqrT%d" % i, [64, SEQ], BF16) for i in range(4)]
        comb = ph.tile("comb", [128, 4, 16, 64])
        imp = ph.tile("imp", [128, 16, 32])
        sel = ph.tile("sel", [128, 16, 32])
        selT = ph.tile("selT", [32, SEQ], BF16)
        tmp32 = ph.tile("tmp32", [128, 32])
        m8 = ph.tile("m8", [128, 16])
        pts = [ph.tile("npt%d" % i, [128, 512], BF16) for i in range(3)]
        mts = [ph.tile("nmt%d" % i, [128, 512], BF16) for i in range(2)]
        rd = ph.tile("nrd", [128, 2])
        otmp = ph.tile("notmp", [128, 64])
        ctr = 0
        for g in range(4):
            S.dma("sp", lambda: nc.sync.dma_start(out=ksT[:, :], in_=d["bkrT"][g, :, :]), r=allb0, w=[ksT.b])
            S.dma("sp", lambda: nc.sync.dma_start(out=kwT[:, :], in_=d["bkrT"][4 + g, :, :]), r=allb0, w=[kwT.b])
            S.dma("sp", lambda: nc.sync.dma_start(out=vs[:, :, :], in_=d["bvp"][0, g, :, :].rearrange("(k p) e -> p k e", p=128)), r=allb0, w=[vs.b])
            S.dma("sp", lambda: nc.sync.dma_start(out=vw[:, :, :], in_=d["bvp"][1, g, :, :].rearrange("(k p) e -> p k e", p=128)), r=allb0, w=[vw.b])
            S.dma("sp", lambda: nc.sync.dma_start(out=kcT[:, :], in_=d["kcmpT"][g, :, :]), r=b1b, w=[kcT.b])
            S.dma("sp", lambda: nc.sync.dma_start(out=vc[:, :], in_=d["vcmp"][g, :, :]), r=b1b, w=[vc.b])
            for j in range(4):
                hh = g * 4 + j
                S.dma("sp", lambda: nc.sync.dma_start(out=qTs[j][:, :], in_=d["bqT"][hh, :, :]), r=allb0, w=[qTs[j].b])
                S.dma("sp", lambda: nc.sync.dma_start(out=qrTs[j][:, :], in_=d["bqrT"][hh, :, :]), r=allb0, w=[qrTs[j].b])
            for j in range(4):
                hh = g * 4 + j
                qT = qTs[j]
                for qc in range(4):
                    st = K.bank()
                    pt = pts[ctr % 3]
                    ctr += 1
                    S.op("pe", lambda: nc.tensor.matmul(st[0:127, 0:512], lhsT=kcT[:, 0:127], rhs=qT[:, qc * 512:(qc + 1) * 512],
                                                        start=True, stop=True), r=[kcT.b, qT.b], w=[st.b])
                    S.op("act", lambda: nc.scalar.activation(out=pt[0:127, :], in_=st[0:127, 0:512], func=AF.Exp, scale=scale),
                         r=[st.b], w=[pt.b])
                    S.op("dve", lambda: nc.vector.tensor_tensor(out=pt[0:127, :], in0=pt[0:127, :],
                                                                in1=cmask[0:127, qc * 512:(qc + 1) * 512], op=ALU.mult),
                         r=[pt.b, cmask.b], w=[pt.b])
                    for jj in range(4):
                        qt = qc * 4 + jj
                        ob = K.bank()
                        S.op("pe", lambda: nc.tensor.matmul(ob[:, 0:97], lhsT=pt[0:127, jj * 128:(jj + 1) * 128], rhs=vc[0:127, :],
                                                            start=True, stop=True), r=[pt.b, vc.b], w=[ob.b])
                        S.op("dve", lambda: nc.vector.tensor_scalar_max(out=rd[:, 0:1], in0=ob[:, 64:65], scalar1=1e-30),
                             r=[ob.b], w=[rd.b])
                        S.op("dve", lambda: nc.vector.reciprocal(out=rd[:, 1:2], in_=rd[:, 0:1]), r=[rd.b], w=[rd.b])
                        S.op("dve", lambda: nc.vector.tensor_scalar(out=comb[:, j, qt, :], in0=ob[:, 0:64], scalar1=rd[:, 1:2],
                                                                    scalar2=gts[:, qt, hh:hh + 1], op0=ALU.mult, op1=ALU.mult),
                             r=[ob.b, rd.b, gts.b], w=[comb.b])
                        if j == 0:
                            S.op("dve", lambda: nc.vector.tensor_scalar(out=imp[:, qt, :], in0=ob[:, 65:97], scalar1=rd[:, 1:2],
                                                                        scalar2=None, op0=ALU.mult), r=[ob.b, rd.b], w=[imp.b])
                        else:
                            S.op("dve", lambda: nc.vector.scalar_tensor_tensor(out=imp[:, qt, :], in0=ob[:, 65:97], scalar=rd[:, 1:2],
                                                                               in1=imp[:, qt, :], op0=ALU.mult, op1=ALU.add),
                                 r=[ob.b, rd.b, imp.b], w=[imp.b])
            S.op("dve", lambda: nc.vector.tensor_tensor(out=imp[:, :, :], in0=imp[:, :, :], in1=At[:, :, :], op=ALU.mult),
                 r=[imp.b, At.b], w=[imp.b])
            S.op("dve", lambda: nc.vector.tensor_tensor(out=imp[:, :, :], in0=imp[:, :, :], in1=Bt[:, :, :], op=ALU.add),
                 r=[imp.b, Bt.b], w=[imp.b])
            for qt in range(16):
                S.op("dve", lambda: nc.vector.max(out=m8[:, 0:8], in_=imp[:, qt, :]), r=[imp.b], w=[m8.b])
                S.op("dve", lambda: nc.vector.match_replace(out=tmp32[:, :], in_to_replace=m8[:, 0:8], in_values=imp[:, qt, :],
                                                            imm_value=-3e38), r=[imp.b, m8.b], w=[tmp32.b])
                S.op("dve", lambda: nc.vector.max(out=m8[:, 8:16], in_=tmp32[:, :]), r=[tmp32.b], w=[m8.b])
                S.op("dve", lambda: nc.vector.tensor_scalar(out=sel[:, qt, :], in0=imp[:, qt, :], scalar1=m8[:, 15:16], scalar2=None,
                                                            op0=ALU.is_ge), r=[imp.b, m8.b], w=[sel.b])
            transpose_to(K, sel, [sel[:, qt, :] for qt in range(16)], selT,
                         lambda g0, n: selT[:, g0 * 128:(g0 + n) * 128], 32)
            for j in range(4):
                hh = g * 4 + j
                qrT = qrTs[j]
                for qc in range(4):
                    ob = [K.ps[4 + jj] for jj in range(4)]
                    nk = 4 * qc + 4
                    for kt in range(nk):
                        st = K.ps[ctr % 2]
                        mb = K.ps[2 + ctr % 2]
                        pt = pts[ctr % 3]
                        ctr += 1
                        S.op("pe", lambda: nc.tensor.matmul(st[:, 0:512], lhsT=ksT[:, kt * 128:(kt + 1) * 128],
                                                            rhs=qrT[:, qc * 512:(qc + 1) * 512], start=True, stop=True),
                             r=[ksT.b, qrT.b], w=[st.b])
                        S.op("pe", lambda: nc.tensor.matmul(mb[:, 0:512], lhsT=Eall[:, kt, :], rhs=selT[:, qc * 512:(qc + 1) * 512],
                                                            start=True, stop=True), r=[Eall.b, selT.b], w=[mb.b])
                        j0 = max(0, kt - 4 * qc)
                        S.op("act", lambda: nc.scalar.activation(out=pt[:, j0 * 128:512], in_=st[:, j0 * 128:512], func=AF.Exp,
                                                                 scale=scale), r=[st.b], w=[pt.b])
                        S.op("dve", lambda: nc.vector.tensor_tensor(out=pt[:, j0 * 128:512], in0=pt[:, j0 * 128:512],
                                                                    in1=mb[:, j0 * 128:512], op=ALU.mult), r=[pt.b, mb.b], w=[pt.b])
                        if kt >= 4 * qc:
                            S.op("dve", lambda: nc.vector.tensor_tensor(out=pt[:, j0 * 128:(j0 + 1) * 128],
                                                                        in0=pt[:, j0 * 128:(j0 + 1) * 128],
                                                                        in1=K.tri_le[:, :], op=ALU.mult),
                                 r=[pt.b, K.tri_le.b], w=[pt.b])
                        for jj in range(j0, 4):
                            qt = 4 * qc + jj
                            S.op("pe", lambda: nc.tensor.matmul(ob[jj][:, 0:65], lhsT=pt[:, jj * 128:(jj + 1) * 128],
                                                                rhs=vs[:, kt, :], start=(kt == 0), stop=(kt == qt)),
                                 r=[pt.b, vs.b], w=[ob[jj].b])
                    for jj in range(4):
                        qt = 4 * qc + jj
                        nsa_combine(K, ob[jj], rd, otmp, comb, j, qt, gts, 16 + hh)
                for qt in range(16):
                    kts = list(range(max(0, qt - 4), qt + 1))
                    ob = K.ps[4 + qt % 4]
                    stA = K.ps[ctr % 2]
                    stB = K.ps[2 + ctr % 2]
                    ptA = pts[ctr % 3]
                    ptB = mts[ctr % 2]
                    ctr += 1
                    for i, kt in enumerate(kts):
                        st = stA if i < 4 else stB
                        S.op("pe", lambda: nc.tensor.matmul(st[:, (i % 4) * 128:(i % 4 + 1) * 128], lhsT=kwT[:, kt * 128:(kt + 1) * 128],
                                                            rhs=qrT[:, qt * 128:(qt + 1) * 128], start=True, stop=True),
                             r=[kwT.b, qrT.b], w=[st.b])
                    na = min(4, len(kts))
                    S.op("act", lambda: nc.scalar.activation(out=ptA[:, 0:na * 128], in_=stA[:, 0:na * 128], func=AF.Exp, scale=scale),
                         r=[stA.b], w=[ptA.b])
                    if len(kts) == 5:
                        S.op("act", lambda: nc.scalar.activation(out=ptB[:, 0:128], in_=stB[:, 0:128], func=AF.Exp, scale=scale),
                             r=[stB.b], w=[ptB.b])
                    for i, kt in enumerate(kts):
                        pt = ptA if i < 4 else ptB
                        sl = slice((i % 4) * 128, (i % 4 + 1) * 128)
                        if kt == qt:
                            S.op("dve", lambda: nc.vector.tensor_tensor(out=pt[:, sl], in0=pt[:, sl], in1=K.tri_le[:, :], op=ALU.mult),
                                 r=[pt.b, K.tri_le.b], w=[pt.b])
                        elif kt == qt - 4:
                            S.op("dve", lambda: nc.vector.tensor_tensor(out=pt[:, sl], in0=pt[:, sl], in1=K.tri_gt[:, :], op=ALU.mult),
                                 r=[pt.b, K.tri_gt.b], w=[pt.b])
                        S.op("pe", lambda: nc.tensor.matmul(ob[:, 0:65], lhsT=pt[:, sl], rhs=vw[:, kt, :], start=(i == 0),
                                                            stop=(i == len(kts) - 1)), r=[pt.b, vw.b], w=[ob.b])
                    nsa_combine(K, ob, rd, otmp, comb, j, qt, gts, 32 + hh)
                S.dma("sp", lambda: nc.sync.dma_start(
                    out=d["attn"][:, hh * 64:(hh + 1) * 64].rearrange("(q p) e -> p q e", p=128), in_=comb[:, j, :, :]),
                    r=[comb.b], w=[K.db("attn", s, qq, hh) for qq in range(4)])
        S.barrier()


def nsa_combine(K, ob, rd, otmp, comb, j, qt, gts, gcol):
    nc, S = K.nc, K.S
    S.op("dve", lambda: nc.vector.reciprocal(out=rd[:, 1:2], in_=ob[:, 64:65]), r=[ob.b], w=[rd.b])
    S.op("dve", lambda: nc.vector.tensor_scalar(out=otmp[:, :], in0=ob[:, 0:64], scalar1=rd[:, 1:2], scalar2=gts[:, qt, gcol:gcol + 1],
                                                op0=ALU.mult, op1=ALU.mult), r=[ob.b, rd.b, gts.b], w=[otmp.b])
    S.op("dve", lambda: nc.vector.tensor_tensor(out=comb[:, j, qt, :], in0=comb[:, j, qt, :], in1=otmp[:, :], op=ALU.add),
         r=[comb.b, otmp.b], w=[comb.b])

def build(nseq=4, stages=("a1", "a2", "a3"), dbg=(), ntiles=NT):
    nc = bass.Bass("TRN2", target_bir_lowering=False)
    K = Ctx()
    K.nc = nc
    K.uid = 0
    K.evi = 0
    K.ntiles = ntiles
    es = ExitStack()
    K.es = es
    d = {}
    K.d = d

    def din(name, shape, dtype=F32):
        d[name] = nc.dram_tensor(name, list(shape), dtype, kind="ExternalInput").ap()

    def dscr(name, shape, dtype=F32):
        kind = "ExternalOutput" if name in dbg else "Internal"
        if name + "_in" in dbg:
            kind = "ExternalInput"
        d[name] = nc.dram_tensor(name, list(shape), dtype, kind=kind).ap()

    din("x", [nseq, SEQ, D])
    din("a_w_in", [1024, 800])
    din("a_q_norm", [1, 512])
    din("a_kv_norm", [1, 256])
    din("a_w_q_up", [512, 1536])
    din("a_w_kv_up", [256, 2048])
    din("a_w_o", [1024, 1024])
    din("s_w_kv", [1024, 1536])
    din("b_w_in", [1024, 1072])
    din("b_w_o", [1024, 1024])
    for nm in ("k", "v"):
        din("s_cmp_%s_w1" % nm, [2048, 256])
        din("s_cmp_%s_b1" % nm, [128, 2])
        din("s_cmp_%s_w2" % nm, [256, 64])
        din("s_cmp_pos_%sT" % nm, [64, 32])
    din("p_w_q", [2, 1024, 2048])
    din("p_skT", [2, 128, 16, 128])
    din("p_uv0", [16384, 2048])
    din("p_uv1", [16384, 2048])
    din("ln_g", [4, 1024])
    din("ln_b", [4, 1024])
    hc = host_consts()
    for k, v in hc.items():
        din(k, v.shape)
    d["out"] = nc.dram_tensor("out", [nseq, SEQ, D], F32, kind="ExternalOutput").ap()
    dscr("qnT", [16, 64, SEQ], BF16)
    dscr("qpT", [16, 32, SEQ], BF16)
    dscr("knT", [16, 64, SEQ], BF16)
    dscr("kpT", [32, SEQ], BF16)
    dscr("vp", [16, SEQ, 65], BF16)
    dscr("attn", [SEQ, 1024])
    dscr("h1", [SEQ, 1024])
    dscr("h2", [SEQ, 1024])
    dscr("p_uvb0", [16384, 2048], BF16)
    dscr("p_uvb1", [16384, 2048], BF16)
    dscr("h3", [SEQ, 1024])
    dscr("gates", [SEQ, 48])
    dscr("bqT", [16, 64, SEQ], BF16)
    dscr("bqrT", [16, 64, SEQ], BF16)
    dscr("bkcT", [8, 64, SEQ])
    dscr("bkrT", [8, 64, SEQ], BF16)
    dscr("bvp", [2, 4, SEQ, 65], BF16)
    dscr("kcmpT", [4, 64, 128], BF16)
    dscr("vcmp", [4, 128, 97], BF16)

    with es:
        S = Sched(nc, es)
        K.S = S
        K.ps = [Tile(es.enter_context(nc.psum_tensor("ps%d" % i, [128, 512], F32)), "ps%d" % i) for i in range(8)]
        K.bank_i = 0

        K.bank_list = list(range(8))

        def bank():
            b = K.ps[K.bank_list[K.bank_i % len(K.bank_list)]]
            K.bank_i += 1
            return b
        K.bank = bank
        K.dbufs = {}

        def db(*key):
            if key not in K.dbufs:
                K.dbufs[key] = Buf(str(key))
            return K.dbufs[key]
        K.db = db
        gl = Phase(K)
        gl.es = es
        K.ident = gl.tile("ident", [128, 128])
        K.tri_le = gl.tile("tri_le", [128, 128])
        K.tri_gt = gl.tile("tri_gt", [128, 128])
        K.eps_ln = gl.tile("eps_ln", [128, 1])
        K.eps_rms = gl.tile("eps_rms", [128, 1])
        for nm in ("ident", "tri_le", "tri_gt"):
            tl = getattr(K, nm)
            S.dma("sp", lambda tl=tl, nm=nm: nc.sync.dma_start(out=tl[:, :], in_=d[nm]), w=[tl.b])
        S.op("dve", lambda: nc.vector.memset(K.eps_ln[:, :], LN_EPS), w=[K.eps_ln.b])
        S.op("dve", lambda: nc.vector.memset(K.eps_rms[:, :], RMS_EPS), w=[K.eps_rms.b])

        if "p0" in stages or "p1" in stages:
            prologue_tables(K)
        for s in range(nseq):
            if "a1" in stages:
                phase_a1(K, s)
            if "a2" in stages:
                phase_a2(K, s)
            if "a3" in stages:
                dstname = "h1" if "h1" in dbg or len(stages) > 3 else "h1"
                phase_oproj_ln(K, s, d["a_w_o"], lambda t0: d["x"][s, t0:t0 + 128, :], lambda t: [],
                               d["ln_g"][0:1, :], d["ln_b"][0:1, :],
                               lambda t0: d["h1"][t0:t0 + 128, :], lambda t: K.db("h1", s, t), "a")
            if "p0" in stages:
                phase_peer(K, s, 0, lambda t0: d["h1"][t0:t0 + 128, :], lambda t: [K.db("h1", s, t)],
                           lambda t0: d["h2"][t0:t0 + 128, :], lambda t: K.db("h2", s, t), ntiles=K.ntiles)
            if "b0" in stages:
                phase_b0(K, s)
            if "b1" in stages:
                phase_b1(K, s)
            if "b2" in stages:
                phase_b2(K, s)
            if "b3" in stages:
                phase_oproj_ln(K, s, d["b_w_o"], lambda t0: d["h2"][t0:t0 + 128, :], lambda t: [K.db("h2", s, t)],
                               d["ln_g"][2:3, :], d["ln_b"][2:3, :],
                               lambda t0: d["h3"][t0:t0 + 128, :], lambda t: K.db("h3", s, t), "b")
            if "p1" in stages:
                phase_peer(K, s, 1, lambda t0: d["h3"][t0:t0 + 128, :], lambda t: [K.db("h3", s, t)],
                           lambda t0: d["out"][s, t0:t0 + 128, :], lambda t: K.db("out", s, t), ntiles=K.ntiles)
        S.barrier()
    K.hc = hc
    return nc, K


ALL_STAGES = ("a1", "a2", "a3", "p0", "b0", "b1", "b2", "b3", "p1")
_CACHE = {}


def _f32(a):
    return np.ascontiguousarray(np.asarray(a), dtype=np.float32)


def kernel(x, a_w_in, a_q_norm, a_kv_norm, a_w_q_up, a_w_kv_up, a_w_o, b_w_in, b_w_o,
           s_w_kv, s_cmp_pos_k, s_cmp_pos_v, s_cmp_k_w1, s_cmp_k_b1, s_cmp_k_w2,
           s_cmp_v_w1, s_cmp_v_b1, s_cmp_v_w2, p_w_q, p_subkeys, p_u, p_v, ln_g, ln_b):
    x = np.asarray(x)
    B = x.shape[0]
    if "nc" not in _CACHE:
        _CACHE["nc"] = build(nseq=B // NCORES, stages=ALL_STAGES)
    nc, K = _CACHE["nc"]
    p_subkeys = np.asarray(p_subkeys)
    p_u = np.asarray(p_u)
    p_v = np.asarray(p_v)
    w = {
        "a_w_in": _f32(np.asarray(a_w_in)[0]), "a_q_norm": _f32(np.asarray(a_q_norm).reshape(1, 512)),
        "a_kv_norm": _f32(np.asarray(a_kv_norm).reshape(1, 256)), "a_w_q_up": _f32(np.asarray(a_w_q_up)[0]),
        "a_w_kv_up": _f32(np.asarray(a_w_kv_up)[0]), "a_w_o": _f32(np.asarray(a_w_o)[0]),
        "b_w_in": _f32(np.asarray(b_w_in)[0]), "b_w_o": _f32(np.asarray(b_w_o)[0]), "s_w_kv": _f32(s_w_kv),
        "s_cmp_k_w1": _f32(s_cmp_k_w1), "s_cmp_v_w1": _f32(s_cmp_v_w1),
        "s_cmp_k_w2": _f32(s_cmp_k_w2), "s_cmp_v_w2": _f32(s_cmp_v_w2),
        "s_cmp_k_b1": _f32(np.asarray(s_cmp_k_b1).reshape(2, 128).T), "s_cmp_v_b1": _f32(np.asarray(s_cmp_v_b1).reshape(2, 128).T),
        "s_cmp_pos_kT": _f32(np.asarray(s_cmp_pos_k).T), "s_cmp_pos_vT": _f32(np.asarray(s_cmp_pos_v).T),
        "p_w_q": _f32(p_w_q),
        "p_skT": _f32(np.stack([p_subkeys[l].reshape(16, 128, 128).transpose(2, 0, 1) for l in range(2)])),
        "p_uv0": _f32(np.concatenate([p_u[0], p_v[0]], axis=1)), "p_uv1": _f32(np.concatenate([p_u[1], p_v[1]], axis=1)),
        "ln_g": _f32(np.asarray(ln_g).reshape(4, 1024)), "ln_b": _f32(np.asarray(ln_b).reshape(4, 1024)),
    }
    w.update({k: _f32(v) for k, v in K.hc.items()})
    nper = B // NCORES
    in_maps = []
    for c in range(NCORES):
        m = dict(w)
        m["x"] = _f32(x[c * nper:(c + 1) * nper])
        in_maps.append(m)
    res = run_bass_kernel_spmd(nc, in_maps, core_ids=list(range(NCORES)))
    out = np.empty((B, SEQ, D), dtype=np.float32)
    for c in range(NCORES):
        out[c * nper:(c + 1) * nper] = np.asarray(res.results[c]["out"]).reshape(nper, SEQ, D)
    return out
```

```python
import numpy as np
import os
XP = os.environ.get('XP', '')
from contextlib import ExitStack, contextmanager
import concourse.bass as bass
import concourse.mybir as mybir
from concourse.bass_utils import run_bass_kernel_spmd

F32 = mybir.dt.float32
I32 = mybir.dt.int32
U32 = mybir.dt.uint32
BF16 = mybir.dt.bfloat16
AF = mybir.ActivationFunctionType
ALU = mybir.AluOpType
AX = mybir.AxisListType

SEQ = 2048
D = 1024
NT = 16
NCORES = 8
ALPHA = 4.0 ** 0.25
LN_EPS = 1e-5
RMS_EPS = 1e-6
NEG = -1e30


class Buf:
    __slots__ = ("name", "w", "r")

    def __init__(self, name=""):
        self.name = name
        self.w = None
        self.r = {}


class Tile:
    def __init__(self, t, name):
        self.t = t
        self.b = Buf(name)

    def __getitem__(self, k):
        return self.t[k]


class Sched:
    EPOCH = 20000

    def __init__(self, nc, es):
        self.nc = nc
        self.es = es
        self.eng = {"pe": nc.tensor, "act": nc.scalar, "dve": nc.vector, "pool": nc.gpsimd, "sp": nc.sync}
        self.cur = {}
        self.waited = {}
        self.nsem = 0
        self.ninst = 0
        for e in ("pe", "act", "dve", "pool"):
            self._new_epoch(e)
        self.rings = {}
        for q, n in (("sp", 24), ("pool", 12), ("act", 8)):
            self.rings[q] = [[self._sem(), 0] for _ in range(n)]
        self.ring_i = {q: 0 for q in self.rings}

    def _sem(self):
        self.nsem += 1
        return self.es.enter_context(self.nc.semaphore("s%d" % self.nsem))

    def _new_epoch(self, e):
        self.cur[e] = [self._sem(), 0]

    def _wait(self, e, tok, strict=False):
        sem, val, src = tok
        if src == e and e == "pe" and not strict:
            return
        key = (e, id(sem))
        if self.waited.get(key, 0) >= val:
            return
        self.eng[e].wait_ge(sem, val)
        self.ninst += 1
        self.waited[key] = val

    @staticmethod
    def _deps(r, w):
        toks = []
        for b in r:
            if b.w is not None:
                toks.append(b.w)
        for b in w:
            if b.w is not None:
                toks.append(b.w)
            toks.extend(b.r.values())
        return toks

    @staticmethod
    def _commit(tok, r, w, key):
        for b in w:
            b.w = tok
            b.r = {}
        for b in r:
            if b not in w:
                b.r[key] = tok

    def op(self, e, fn, r=(), w=()):
        for t in self._deps(r, w):
            self._wait(e, t)
        st = self.cur[e]
        if st[1] >= self.EPOCH:
            self._new_epoch(e)
            st = self.cur[e]
        ins = fn()
        st[1] += 1
        ins.then_inc(st[0], 1)
        self.ninst += 1
        tok = (st[0], st[1], e)
        self._commit(tok, r, w, e)
        return tok

    def dma(self, q, fn, r=(), w=()):
        ring = self.rings[q]
        i = self.ring_i[q]
        self.ring_i[q] = (i + 1) % len(ring)
        slot = ring[i]
        if slot[1] > 0:
            self._wait(q, (slot[0], slot[1], None), strict=True)
        for t in self._deps(r, w):
            self._wait(q, t, strict=True)
        ins = fn()
        slot[1] += 16
        ins.then_inc(slot[0], 16)
        self.ninst += 1
        tok = (slot[0], slot[1], None)
        self._commit(tok, r, w, ("dma", q, i))
        return tok

    def barrier(self):
        toks = []
        for e in ("pe", "act", "dve", "pool"):
            st = self.cur[e]
            if st[1] > 0:
                toks.append((st[0], st[1], e))
        for q, ring in self.rings.items():
            for slot in ring:
                if slot[1] > 0:
                    toks.append((slot[0], slot[1], None))
        for e in ("pe", "act", "dve", "pool", "sp"):
            for t in toks:
                self._wait(e, t, strict=(t[2] is None))


class Phase:
    def __init__(self, K):
        self.K = K
        self.es = ExitStack()

    def tile(self, name, shape, dtype=F32):
        self.K.uid += 1
        nm = "%s_%d" % (name, self.K.uid)
        return Tile(self.es.enter_context(self.K.nc.sbuf_tensor(nm, list(shape), dtype)), nm)


class Ctx:
    pass


def host_consts():
    c = {}
    c["ident"] = np.eye(128, dtype=np.float32)
    kk = np.arange(128)[:, None]
    qq = np.arange(128)[None, :]
    c["tri_le"] = (kk <= qq).astype(np.float32)
    c["tri_gt"] = (kk > qq).astype(np.float32)
    pos = np.arange(SEQ, dtype=np.float32)[:, None]
    for d in (32, 64):
        inv = (10000.0 ** (-np.arange(0, d, 2, dtype=np.float32) / d)).astype(np.float32)
        ang = (pos * inv[None, :]).astype(np.float32)
        c["rope%d" % d] = np.concatenate([np.cos(ang), np.sin(ang)], axis=1).astype(np.float32)
    c["iota16"] = np.tile(np.arange(16, dtype=np.float32)[None, :], (128, 1))
    cc = np.arange(128)
    tt = np.arange(SEQ)
    c["cmask"] = ((16 * cc[:, None] + 31 <= tt[None, :]) & (cc[:, None] < 127)).astype(np.float32)
    nn = np.arange(32)
    ovl = ((16 * cc[:, None] < 64 * nn[None, :] + 64) & (16 * cc[:, None] + 31 >= 64 * nn[None, :]) & (cc[:, None] < 127))
    c["overlap"] = ovl.astype(np.float32)
    cur = tt // 64
    forced = (nn[None, :] == 0) | ((nn[None, :] <= cur[:, None]) & (nn[None, :] > cur[:, None] - 2))
    valid = nn[None, :] <= cur[:, None]
    c["selA"] = ((~forced) & valid).astype(np.float32)
    c["selB"] = np.where(valid, np.where(forced, 1e9, 0.0), -1e30).astype(np.float32)
    E = np.zeros((32, 16, 128), np.float32)
    for kt in range(16):
        for k in range(128):
            E[(kt * 128 + k) // 64, kt, k] = 1.0
    c["Eall"] = E
    return c


def rope_tok(K, ph, x1, x2, o1, o2, cos_b, sin_b, shape, rbufs, wbufs, tmpname):
    nc, S = K.nc, K.S
    if not hasattr(ph, "rtmp"):
        ph.rtmp = {}
    key = tuple(shape)
    if key not in ph.rtmp:
        ph.rtmp[key] = (ph.tile("ropea", shape), ph.tile("ropeb", shape))
    ta, tb = ph.rtmp[key]
    sl = tuple(slice(None) for _ in shape)
    S.op("dve", lambda: nc.vector.tensor_tensor(out=ta[sl], in0=x1, in1=cos_b, op=ALU.mult), r=rbufs, w=[ta.b])
    S.op("dve", lambda: nc.vector.tensor_tensor(out=tb[sl], in0=x2, in1=sin_b, op=ALU.mult), r=rbufs, w=[tb.b])
    S.op("dve", lambda: nc.vector.tensor_tensor(out=o1, in0=ta[sl], in1=tb[sl], op=ALU.subtract), r=[ta.b, tb.b], w=wbufs)
    S.op("dve", lambda: nc.vector.tensor_tensor(out=ta[sl], in0=x2, in1=cos_b, op=ALU.mult), r=rbufs, w=[ta.b])
    S.op("dve", lambda: nc.vector.tensor_tensor(out=tb[sl], in0=x1, in1=sin_b, op=ALU.mult), r=rbufs, w=[tb.b])
    S.op("dve", lambda: nc.vector.tensor_tensor(out=o2, in0=ta[sl], in1=tb[sl], op=ALU.add), r=[ta.b, tb.b], w=wbufs)


def evac(K, i, out, in_, r, w):
    nc, S = K.nc, K.S
    if i % 2 == 0:
        S.op("act", lambda: nc.scalar.copy(out=out, in_=in_), r=r, w=w)
    else:
        S.op("dve", lambda: nc.vector.tensor_copy(out=out, in_=in_), r=r, w=w)


def transpose_to(K, src_tile, src_aps, dst_tile, dst_ap_fn, rows, group=4):
    nc, S = K.nc, K.S
    n = len(src_aps)
    for g0 in range(0, n, group):
        cnt = min(group, n - g0)
        bank = K.bank()
        for j in range(cnt):
            ap = src_aps[g0 + j]
            S.op("pe", lambda ap=ap, j=j: nc.tensor.transpose(out=bank[0:rows, j * 128:(j + 1) * 128], in_=ap,
                                                               identity=K.ident[:, :]),
                 r=[src_tile.b, K.ident.b], w=[bank.b])
        evac(K, K.evi, dst_ap_fn(g0, cnt), bank[0:rows, 0:cnt * 128], r=[bank.b], w=[dst_tile.b])
        K.evi += 1


def layer_norm(K, ph, y, g_bc, b_bc, out, tag):
    nc, S = K.nc, K.S
    if not hasattr(ph, "lntmp"):
        ph.lntmp = (ph.tile("lnst", [128, 2, 6]), ph.tile("lnmv", [128, 2]), ph.tile("lnsd", [128, 1]), ph.tile("lnrs", [128, 1]))
    st, mv, sd, rs = ph.lntmp
    for j in range(2):
        S.op("dve", lambda j=j: nc.vector.bn_stats(out=st[:, j, :], in_=y[:, j * 512:(j + 1) * 512]), r=[y.b], w=[st.b])
    S.op("dve", lambda: nc.vector.bn_aggr(out=mv[:, :], in_=st[:, :, :].rearrange("p a b -> p (a b)")), r=[st.b], w=[mv.b])
    S.op("act", lambda: nc.scalar.activation(out=sd[:, :], in_=mv[:, 1:2], func=AF.Sqrt, bias=K.eps_ln[:, :], scale=1.0),
         r=[mv.b, K.eps_ln.b], w=[sd.b])
    S.op("dve", lambda: nc.vector.reciprocal(out=rs[:, :], in_=sd[:, :]), r=[sd.b], w=[rs.b])
    S.op("dve", lambda: nc.vector.tensor_scalar(out=out[:, :], in0=y[:, :], scalar1=mv[:, 0:1], scalar2=rs[:, 0:1],
                                                op0=ALU.subtract, op1=ALU.mult), r=[y.b, mv.b, rs.b], w=[out.b])
    S.op("dve", lambda: nc.vector.tensor_tensor(out=out[:, :], in0=out[:, :], in1=g_bc[:, :], op=ALU.mult),
         r=[out.b, g_bc.b], w=[out.b])
    S.op("dve", lambda: nc.vector.tensor_tensor(out=out[:, :], in0=out[:, :], in1=b_bc[:, :], op=ALU.add),
         r=[out.b, b_bc.b], w=[out.b])


def load_bc(K, ph, name, dram_row_ap, n):
    nc, S = K.nc, K.S
    t = ph.tile(name, [128, n])
    S.dma("sp", lambda: nc.sync.dma_start(out=t[:, :], in_=dram_row_ap.to_broadcast([128, n])), w=[t.b])
    return t


def phase_a1(K, s):
    nc, S, d = K.nc, K.S, K.d
    ph = Phase(K)
    with ph.es:
        w_in = ph.tile("w_in", [128, 8, 800])
        S.dma("sp", lambda: nc.sync.dma_start(out=w_in[:, :, :], in_=d["a_w_in"].rearrange("(c p) n -> p c n", p=128)), w=[w_in.b])
        wq = ph.tile("wq", [128, 4, 1536])
        S.dma("sp", lambda: nc.sync.dma_start(out=wq[:, :, :], in_=d["a_w_q_up"].rearrange("(c p) n -> p c n", p=128)), w=[wq.b])
        wkv = ph.tile("wkv", [128, 2, 2048])
        S.dma("sp", lambda: nc.sync.dma_start(out=wkv[:, :, :], in_=d["a_w_kv_up"].rearrange("(c p) n -> p c n", p=128)), w=[wkv.b])
        qn_bc = load_bc(K, ph, "qn_bc", d["a_q_norm"], 512)
        kvn_bc = load_bc(K, ph, "kvn_bc", d["a_kv_norm"], 256)
        xs = [ph.tile("x%d" % i, [128, 1024]) for i in range(2)]
        css = [ph.tile("cs%d" % i, [128, 32]) for i in range(2)]
        xT = ph.tile("xT", [128, 8, 128])
        junk = ph.tile("junk", [128, 512])
        ss = ph.tile("ss", [128, 2])
        rs = ph.tile("rs", [128, 2])
        rr = ph.tile("rr", [128, 2])
        cq = ph.tile("cq", [128, 512])
        ckv = ph.tile("ckv", [128, 256])
        kraw = ph.tile("kraw", [128, 32])
        kpe = ph.tile("kpe", [128, 32])
        cqT = ph.tile("cqT", [128, 4, 128])
        ckvT = ph.tile("ckvT", [128, 2, 128])
        q_sb = ph.tile("q_sb", [128, 16, 96])
        qpe = ph.tile("qpe", [128, 16, 32])
        kv_sb = ph.tile("kv_sb", [128, 16, 128])
        vps = [ph.tile("vp%d" % i, [128, 16, 65], BF16) for i in range(2)]
        qnTs = [ph.tile("qnT%d" % i, [64, 16, 128], BF16) for i in range(2)]
        qpTs = [ph.tile("qpT%d" % i, [32, 16, 128], BF16) for i in range(2)]
        knTs = [ph.tile("knT%d" % i, [64, 16, 128], BF16) for i in range(2)]
        kpTs = [ph.tile("kpT%d" % i, [32, 128], BF16) for i in range(2)]
        for v in vps:
            S.op("dve", lambda v=v: nc.vector.memset(v[:, :, 64:65], 1.0), w=[v.b])
        for t in range(NT):
            p = t % 2
            t0 = t * 128
            x, cs, vp, qnT, qpT, knT, kpT = xs[p], css[p], vps[p], qnTs[p], qpTs[p], knTs[p], kpTs[p]
            S.dma("sp", lambda: nc.sync.dma_start(out=x[:, :], in_=d["x"][s, t0:t0 + 128, :]), w=[x.b])
            S.dma("sp", lambda: nc.sync.dma_start(out=cs[:, :], in_=d["rope32"][t0:t0 + 128, :]), w=[cs.b])
            transpose_to(K, x, [x[:, c * 128:(c + 1) * 128] for c in range(8)], xT,
                         lambda g0, n: xT[:, g0:g0 + n, :].rearrange("p a b -> p (a b)"), 128)
            bA, bB = K.bank(), K.bank()
            for c in range(8):
                S.op("pe", lambda c=c: nc.tensor.matmul(bA[:, 0:512], lhsT=xT[:, c, :], rhs=w_in[:, c, 0:512],
                                                        start=(c == 0), stop=(c == 7)), r=[xT.b, w_in.b], w=[bA.b])
            for c in range(8):
                S.op("pe", lambda c=c: nc.tensor.matmul(bB[:, 0:288], lhsT=xT[:, c, :], rhs=w_in[:, c, 512:800],
                                                        start=(c == 0), stop=(c == 7)), r=[xT.b, w_in.b], w=[bB.b])
            S.op("act", lambda: nc.scalar.activation(out=junk[:, 0:512], in_=bA[:, 0:512], func=AF.Square,
                                                     accum_out=ss[:, 0:1]), r=[bA.b], w=[junk.b, ss.b])
            S.op("act", lambda: nc.scalar.activation(out=junk[:, 0:256], in_=bB[:, 0:256], func=AF.Square,
                                                     accum_out=ss[:, 1:2]), r=[bB.b], w=[junk.b, ss.b])
            S.op("act", lambda: nc.scalar.activation(out=rs[:, 0:1], in_=ss[:, 0:1], func=AF.Sqrt, bias=K.eps_rms[:, :],
                                                     scale=1.0 / 512), r=[ss.b, K.eps_rms.b], w=[rs.b])
            S.op("act", lambda: nc.scalar.activation(out=rs[:, 1:2], in_=ss[:, 1:2], func=AF.Sqrt, bias=K.eps_rms[:, :],
                                                     scale=1.0 / 256), r=[ss.b, K.eps_rms.b], w=[rs.b])
            S.op("dve", lambda: nc.vector.reciprocal(out=rr[:, :], in_=rs[:, :]), r=[rs.b], w=[rr.b])
            S.op("dve", lambda: nc.vector.scalar_tensor_tensor(out=cq[:, :], in0=bA[:, 0:512], scalar=rr[:, 0:1],
                                                               in1=qn_bc[:, :], op0=ALU.mult, op1=ALU.mult),
                 r=[bA.b, rr.b, qn_bc.b], w=[cq.b])
            S.op("dve", lambda: nc.vector.scalar_tensor_tensor(out=ckv[:, :], in0=bB[:, 0:256], scalar=rr[:, 1:2],
                                                               in1=kvn_bc[:, :], op0=ALU.mult, op1=ALU.mult),
                 r=[bB.b, rr.b, kvn_bc.b], w=[ckv.b])
            S.op("act", lambda: nc.scalar.copy(out=kraw[:, :], in_=bB[:, 256:288]), r=[bB.b], w=[kraw.b])
            rope_tok(K, ph, kraw[:, 0:16], kraw[:, 16:32], kpe[:, 0:16], kpe[:, 16:32], cs[:, 0:16], cs[:, 16:32],
                     [128, 16], [kraw.b, cs.b], [kpe.b], "rk%d" % t)
            transpose_to(K, cq, [cq[:, c * 128:(c + 1) * 128] for c in range(4)], cqT,
                         lambda g0, n: cqT[:, g0:g0 + n, :].rearrange("p a b -> p (a b)"), 128)
            transpose_to(K, ckv, [ckv[:, c * 128:(c + 1) * 128] for c in range(2)], ckvT,
                         lambda g0, n: ckvT[:, g0:g0 + n, :].rearrange("p a b -> p (a b)"), 128)
            q_flat = q_sb[:, :, :].rearrange("p a b -> p (a b)")
            for n in range(3):
                bk = K.bank()
                for c in range(4):
                    S.op("pe", lambda c=c, n=n, bk=bk: nc.tensor.matmul(bk[:, 0:512], lhsT=cqT[:, c, :],
                                                                         rhs=wq[:, c, n * 512:(n + 1) * 512],
                                                                         start=(c == 0), stop=(c == 3)),
                         r=[cqT.b, wq.b], w=[bk.b])
                evac(K, n, q_flat[:, n * 512:(n + 1) * 512], bk[:, 0:512], r=[bk.b], w=[q_sb.b])
            kv_flat = kv_sb[:, :, :].rearrange("p a b -> p (a b)")
            for n in range(4):
                bk = K.bank()
                for c in range(2):
                    S.op("pe", lambda c=c, n=n, bk=bk: nc.tensor.matmul(bk[:, 0:512], lhsT=ckvT[:, c, :],
                                                                         rhs=wkv[:, c, n * 512:(n + 1) * 512],
                                                                         start=(c == 0), stop=(c == 1)),
                         r=[ckvT.b, wkv.b], w=[bk.b])
                evac(K, n + 1, kv_flat[:, n * 512:(n + 1) * 512], bk[:, 0:512], r=[bk.b], w=[kv_sb.b])
            cos_b = cs[:, 0:16].unsqueeze(1).to_broadcast([128, 16, 16])
            sin_b = cs[:, 16:32].unsqueeze(1).to_broadcast([128, 16, 16])
            rope_tok(K, ph, q_sb[:, :, 64:80], q_sb[:, :, 80:96], qpe[:, :, 0:16], qpe[:, :, 16:32], cos_b, sin_b,
                     [128, 16, 16], [q_sb.b, cs.b], [qpe.b], "rq%d" % t)
            S.op("act", lambda: nc.scalar.copy(out=vp[:, :, 0:64], in_=kv_sb[:, :, 64:128]), r=[kv_sb.b], w=[vp.b])
            transpose_to(K, q_sb, [q_sb[:, h, 0:64] for h in range(16)], qnT,
                         lambda g0, n: qnT[:, g0:g0 + n, :].rearrange("p a b -> p (a b)"), 64)
            transpose_to(K, qpe, [qpe[:, h, :] for h in range(16)], qpT,
                         lambda g0, n: qpT[:, g0:g0 + n, :].rearrange("p a b -> p (a b)"), 32)
            transpose_to(K, kv_sb, [kv_sb[:, h, 0:64] for h in range(16)], knT,
                         lambda g0, n: knT[:, g0:g0 + n, :].rearrange("p a b -> p (a b)"), 64)
            transpose_to(K, kpe, [kpe[:, :]], kpT, lambda g0, n: kpT[:, :], 32)
            wb = [K.db("qk", s, t)]
            S.dma("sp", lambda: nc.sync.dma_start(out=d["qnT"][:, :, t0:t0 + 128].rearrange("h e t -> e h t"),
                                                  in_=qnT[:, :, :]), r=[qnT.b], w=wb)
            S.dma("sp", lambda: nc.sync.dma_start(out=d["qpT"][:, :, t0:t0 + 128].rearrange("h e t -> e h t"),
                                                  in_=qpT[:, :, :]), r=[qpT.b], w=wb)
            S.dma("sp", lambda: nc.sync.dma_start(out=d["knT"][:, :, t0:t0 + 128].rearrange("h e t -> e h t"),
                                                  in_=knT[:, :, :]), r=[knT.b], w=wb)
            S.dma("sp", lambda: nc.sync.dma_start(out=d["kpT"][:, t0:t0 + 128], in_=kpT[:, :]), r=[kpT.b], w=wb)
            S.dma("sp", lambda: nc.sync.dma_start(out=d["vp"][:, t0:t0 + 128, :].rearrange("h t e -> t h e"),
                                                  in_=vp[:, :, :]), r=[vp.b], w=wb)
        S.barrier()


def attn_core(K, ph, pairs_q, pairs_k, vp, nkeys_tile, scale, o_sb_fn, store_fn, tag):
    pass


def phase_a2(K, s):
    nc, S, d = K.nc, K.S, K.d
    scale = 96.0 ** -0.5
    ph = Phase(K)
    with ph.es:
        kp = ph.tile("kp", [32, SEQ], BF16)
        allqk = [K.db("qk", s, t) for t in range(NT)]
        S.dma("sp", lambda: nc.sync.dma_start(out=kp[:, :], in_=d["kpT"][:, :]), r=allqk, w=[kp.b])
        qns = [ph.tile("qn%d" % i, [64, SEQ], BF16) for i in range(2)]
        qps = [ph.tile("qp%d" % i, [32, SEQ], BF16) for i in range(2)]
        kns = [ph.tile("kn%d" % i, [64, SEQ], BF16) for i in range(2)]
        vpt = [ph.tile("vpa%d" % i, [128, 16, 65], BF16) for i in range(2)]
        pts = [ph.tile("pt%d" % i, [128, 512], BF16) for i in range(3)]
        osb = [ph.tile("osb%d" % i, [128, 4, 64]) for i in range(2)]
        rden = ph.tile("rden", [128, 4])
        ctr = 0
        oc = 0
        for h in range(16):
            p = h % 2
            qn, qp, kn, vp = qns[p], qps[p], kns[p], vpt[p]
            S.dma("sp", lambda: nc.sync.dma_start(out=qn[:, :], in_=d["qnT"][h, :, :]), r=allqk, w=[qn.b])
            S.dma("sp", lambda: nc.sync.dma_start(out=qp[:, :], in_=d["qpT"][h, :, :]), r=allqk, w=[qp.b])
            S.dma("sp", lambda: nc.sync.dma_start(out=kn[:, :], in_=d["knT"][h, :, :]), r=allqk, w=[kn.b])
            S.dma("sp", lambda: nc.sync.dma_start(out=vp[:, :, :], in_=d["vp"][h, :, :].rearrange("(k p) e -> p k e", p=128)),
                  r=allqk, w=[vp.b])
            for qc in range(4):
                ob = [K.ps[4 + j] for j in range(4)]
                nk = 4 * qc + 4
                for kt in range(nk):
                    st = K.ps[ctr % 4]
                    pt = pts[ctr % 3]
                    ctr += 1
                    S.op("pe", lambda: nc.tensor.matmul(st[:, 0:512], lhsT=kn[:, kt * 128:(kt + 1) * 128],
                                                        rhs=qn[:, qc * 512:(qc + 1) * 512], start=True, stop=False),
                         r=[kn.b, qn.b], w=[st.b])
                    S.op("pe", lambda: nc.tensor.matmul(st[:, 0:512], lhsT=kp[:, kt * 128:(kt + 1) * 128],
                                                        rhs=qp[:, qc * 512:(qc + 1) * 512], start=False, stop=True),
                         r=[kp.b, qp.b], w=[st.b])
                    j0 = max(0, kt - 4 * qc)
                    S.op("act", lambda: nc.scalar.activation(out=pt[:, j0 * 128:512], in_=st[:, j0 * 128:512], func=AF.Exp,
                                                             scale=scale), r=[st.b], w=[pt.b])
                    if kt >= 4 * qc:
                        S.op("dve", lambda: nc.vector.tensor_tensor(out=pt[:, j0 * 128:(j0 + 1) * 128],
                                                                    in0=pt[:, j0 * 128:(j0 + 1) * 128],
                                                                    in1=K.tri_le[:, :], op=ALU.mult),
                             r=[pt.b, K.tri_le.b], w=[pt.b])
                    for j in range(j0, 4):
                        qt = 4 * qc + j
                        S.op("pe", lambda j=j, qt=qt: nc.tensor.matmul(ob[j][:, 0:65], lhsT=pt[:, j * 128:(j + 1) * 128],
                                                                       rhs=vp[:, kt, :], start=(kt == 0), stop=(kt == qt)),
                             r=[pt.b, vp.b], w=[ob[j].b])
                o = osb[oc % 2]
                oc += 1
                for j in range(4):
                    S.op("dve", lambda j=j: nc.vector.reciprocal(out=rden[:, j:j + 1], in_=ob[j][:, 64:65]),
                         r=[ob[j].b], w=[rden.b])
                    S.op("dve", lambda j=j: nc.vector.tensor_scalar(out=o[:, j, :], in0=ob[j][:, 0:64],
                                                                    scalar1=rden[:, j:j + 1], scalar2=None, op0=ALU.mult),
                         r=[ob[j].b, rden.b], w=[o.b])
                S.dma("sp", lambda: nc.sync.dma_start(
                    out=d["attn"][qc * 512:(qc + 1) * 512, h * 64:(h + 1) * 64].rearrange("(j p) e -> p j e", p=128),
                    in_=o[:, :, :]), r=[o.b], w=[K.db("attn", s, qc, h)])
        S.barrier()


def phase_oproj_ln(K, s, w_o_ap, res_fn, res_deps_fn, g_ap, b_ap, dst_fn, dst_buf_fn, tag):
    nc, S, d = K.nc, K.S, K.d
    ph = Phase(K)
    with ph.es:
        wo = ph.tile("wo", [128, 8, 1024])
        S.dma("sp", lambda: nc.sync.dma_start(out=wo[:, :, :], in_=w_o_ap.rearrange("(c p) n -> p c n", p=128)), w=[wo.b])
        g_bc = load_bc(K, ph, "g_bc", g_ap, 1024)
        b_bc = load_bc(K, ph, "b_bc", b_ap, 1024)
        ats = [ph.tile("at%d" % i, [128, 1024]) for i in range(2)]
        xs = [ph.tile("xr%d" % i, [128, 1024]) for i in range(2)]
        aT = ph.tile("aT", [128, 8, 128])
        y = ph.tile("y", [128, 1024])
        outs = [ph.tile("ho%d" % i, [128, 1024]) for i in range(2)]
        for t in range(NT):
            p = t % 2
            t0 = t * 128
            at, x, o = ats[p], xs[p], outs[p]
            S.dma("sp", lambda: nc.sync.dma_start(out=at[:, :], in_=d["attn"][t0:t0 + 128, :]),
                  r=[K.db("attn", s, t // 4, h) for h in range(16)], w=[at.b])
            S.dma("sp", lambda: nc.sync.dma_start(out=x[:, :], in_=res_fn(t0)), r=res_deps_fn(t), w=[x.b])
            transpose_to(K, at, [at[:, c * 128:(c + 1) * 128] for c in range(8)], aT,
                         lambda g0, n: aT[:, g0:g0 + n, :].rearrange("p a b -> p (a b)"), 128)
            for n in range(2):
                bk = K.bank()
                for c in range(8):
                    S.op("pe", lambda c=c, n=n, bk=bk: nc.tensor.matmul(bk[:, 0:512], lhsT=aT[:, c, :],
                                                                         rhs=wo[:, c, n * 512:(n + 1) * 512],
                                                                         start=(c == 0), stop=(c == 7)),
                         r=[aT.b, wo.b], w=[bk.b])
                S.op("dve", lambda n=n, bk=bk: nc.vector.scalar_tensor_tensor(
                    out=y[:, n * 512:(n + 1) * 512], in0=x[:, n * 512:(n + 1) * 512], scalar=ALPHA, in1=bk[:, 0:512],
                    op0=ALU.mult, op1=ALU.add), r=[x.b, bk.b], w=[y.b])
            layer_norm(K, ph, y, g_bc, b_bc, o, "%s%d" % (tag, t))
            S.dma("sp", lambda: nc.sync.dma_start(out=dst_fn(t0), in_=o[:, :]), r=[o.b], w=[dst_buf_fn(t)])
        S.barrier()


def prologue_tables(K):
    nc, S, d = K.nc, K.S, K.d
    ph = Phase(K)
    with ph.es:
        inb = [ph.tile("tin%d" % i, [128, 4, 2048]) for i in range(3)]
        outb = [ph.tile("tout%d" % i, [128, 4, 2048], BF16) for i in range(3)]
        k = 0
        for layer in range(2):
            src = d["p_uv%d" % layer].rearrange("(p j) n -> p j n", p=128)
            dst = d["p_uvb%d" % layer].rearrange("(p j) n -> p j n", p=128)
            for c in range(32):
                a, b = inb[k % 3], outb[k % 3]
                S.dma("sp", lambda: nc.sync.dma_start(out=a[:, :, :], in_=src[:, 4 * c:4 * c + 4, :]), w=[a.b])
                if k % 2 == 0:
                    S.op("act", lambda: nc.scalar.copy(out=b[:, :, :], in_=a[:, :, :]), r=[a.b], w=[b.b])
                else:
                    S.op("dve", lambda: nc.vector.tensor_copy(out=b[:, :, :], in_=a[:, :, :]), r=[a.b], w=[b.b])
                S.dma("sp", lambda: nc.sync.dma_start(out=dst[:, 4 * c:4 * c + 4, :], in_=b[:, :, :]), r=[b.b], w=[K.db("uvb", layer, c)])
                k += 1
        S.barrier()


def phase_peer(K, s, layer, src_fn, src_buf_fn, dst_fn, dst_buf_fn, ntiles=NT):
    nc, S, d = K.nc, K.S, K.d
    ph = Phase(K)
    NG = 11
    K.bank_list = [0, 1, 2, 3, 4, 5]
    accb = [K.ps[6], K.ps[7]]
    with ph.es:
        wq = ph.tile("pwq", [128, 8, 2048])
        S.dma("sp", lambda: nc.sync.dma_start(out=wq[:, :, :], in_=d["p_w_q"][layer].rearrange("(c p) n -> p c n", p=128)), w=[wq.b])
        skT = ph.tile("skT", [128, 16, 128])
        S.dma("sp", lambda: nc.sync.dma_start(out=skT[:, :, :], in_=d["p_skT"][layer]), w=[skT.b])
        g_bc = load_bc(K, ph, "pg_bc", d["ln_g"][2 * layer + 1:2 * layer + 2, :], 1024)
        b_bc = load_bc(K, ph, "pb_bc", d["ln_b"][2 * layer + 1:2 * layer + 2, :], 1024)
        iota = ph.tile("iota", [128, 16])
        S.dma("sp", lambda: nc.sync.dma_start(out=iota[:, :], in_=d["iota16"]), w=[iota.b])
        identb = ph.tile("identb", [128, 128], BF16)
        S.op("dve", lambda: nc.vector.tensor_copy(out=identb[:, :], in_=K.ident[:, :]), r=[K.ident.b], w=[identb.b])
        G = [ph.tile("G%d" % i, [128, 2048], BF16) for i in range(NG)]
        hs = [ph.tile("ph%d" % i, [128, 1024]) for i in range(2)]
        eidxs = [ph.tile("peidx%d" % i, [128, 128], I32) for i in range(2)]
        gws = [ph.tile("pgw%d" % i, [128, 128]) for i in range(2)]
        hT = ph.tile("phT", [128, 8, 128])
        q_sb = ph.tile("pq", [128, 2048])
        qT = ph.tile("pqT", [128, 16, 128])
        sc = Tile(q_sb.t[:, :].rearrange("p (a b) -> p a b", b=128), "sc_alias")
        sc.b = q_sb.b
        sc2 = ph.tile("psc2", [128, 128])
        m1 = ph.tile("pm1", [128, 16, 16])
        i1 = ph.tile("pi1", [128, 16, 16], U32)
        i1f = ph.tile("pi1f", [128, 16, 16])
        cand2 = ph.tile("pcand2", [128, 256])
        best = ph.tile("pbest", [128, 8, 16])
        pos = ph.tile("ppos", [128, 8, 16], U32)
        hi = ph.tile("phi", [128, 8, 16], U32)
        lo = ph.tile("plo", [128, 8, 16], U32)
        hif = ph.tile("phif", [128, 8, 16])
        lof = ph.tile("plof", [128, 8, 16])
        eq = ph.tile("peq", [128, 8, 16, 16])
        cand = Tile(eq.t[:, :, :, :].rearrange("p h i j -> p h (i j)"), "cand_alias")
        cand.b = eq.b
        e0 = ph.tile("pe0", [128, 8, 16])
        e1 = ph.tile("pe1", [128, 8, 16])
        ef = ph.tile("pef", [128, 128])
        bm = ph.tile("pbm", [128, 8, 16])
        se = ph.tile("pse", [128, 8])
        a = ph.tile("pa", [128, 128])
        ga = ph.tile("pga", [128, 128])
        w = ph.tile("pw", [128, 128])
        diags = [ph.tile("pdiag%d" % i, [128, 4, 128], BF16) for i in range(3)]
        prods = [ph.tile("pprod%d" % i, [128, 1024], BF16) for i in range(6)]
        hbs = [ph.tile("phb%d" % i, [128, 1024], BF16) for i in range(2)]
        y = ph.tile("py", [128, 1024])
        o = ph.tile("po", [128, 1024])
        m1v = m1[:, :, :].rearrange("p (h two) k -> p h two k", two=2)
        i1fv = i1f[:, :, :].rearrange("p (h two) k -> p h two k", two=2)
        iota_b = iota[:, :].unsqueeze(1).unsqueeze(1).to_broadcast([128, 8, 16, 16])
        ident_b = identb[:, :].unsqueeze(1).to_broadcast([128, 4, 128])

        def front(t):
            t0 = t * 128
            h, eidx, gw = hs[t % 2], eidxs[t % 2], gws[t % 2]
            S.dma("sp", lambda: nc.sync.dma_start(out=h[:, :], in_=src_fn(t0)), r=src_buf_fn(t), w=[h.b])
            transpose_to(K, h, [h[:, c * 128:(c + 1) * 128] for c in range(8)], hT,
                         lambda g0, n: hT[:, g0:g0 + n, :].rearrange("p a b -> p (a b)"), 128)
            yield
            for n in range(4):
                bk = K.bank()
                for c in range(8):
                    S.op("pe", lambda: nc.tensor.matmul(bk[:, 0:512], lhsT=hT[:, c, :], rhs=wq[:, c, n * 512:(n + 1) * 512],
                                                        start=(c == 0), stop=(c == 7)), r=[hT.b, wq.b], w=[bk.b])
                evac(K, n, q_sb[:, n * 512:(n + 1) * 512], bk[:, 0:512], r=[bk.b], w=[q_sb.b])
                yield
            transpose_to(K, q_sb, [q_sb[:, c * 128:(c + 1) * 128] for c in range(16)], qT,
                         lambda g0, n: qT[:, g0:g0 + n, :].rearrange("p a b -> p (a b)"), 128)
            yield
            for n in range(4):
                bk = K.bank()
                for j in range(4):
                    hp = n * 4 + j
                    S.op("pe", lambda: nc.tensor.matmul(bk[:, j * 128:(j + 1) * 128], lhsT=qT[:, hp, :], rhs=skT[:, hp, :],
                                                        start=True, stop=True), r=[qT.b, skT.b], w=[bk.b])
                evac(K, n, sc[:, n * 4:(n + 1) * 4, :].rearrange("p a b -> p (a b)"), bk[:, 0:512], r=[bk.b], w=[sc.b])
            yield
            for hp in range(16):
                S.op("dve", lambda: nc.vector.max(out=m1[:, hp, 0:8], in_=sc[:, hp, :]), r=[sc.b], w=[m1.b])
                S.op("dve", lambda: nc.vector.max_index(out=i1[:, hp, 0:8], in_max=m1[:, hp, 0:8], in_values=sc[:, hp, :]),
                     r=[sc.b, m1.b], w=[i1.b])
                S.op("dve", lambda: nc.vector.match_replace(out=sc2[:, :], in_to_replace=m1[:, hp, 0:8],
                                                            in_values=sc[:, hp, :], imm_value=NEG), r=[sc.b, m1.b], w=[sc2.b])
                S.op("dve", lambda: nc.vector.max(out=m1[:, hp, 8:16], in_=sc2[:, :]), r=[sc2.b], w=[m1.b])
                S.op("dve", lambda: nc.vector.max_index(out=i1[:, hp, 8:16], in_max=m1[:, hp, 8:16], in_values=sc2[:, :]),
                     r=[sc2.b, m1.b], w=[i1.b])
                yield
            S.op("dve", lambda: nc.vector.tensor_tensor(
                out=cand[:, :, :].rearrange("p h (i j) -> p h i j", j=16),
                in0=m1v[:, :, 0, :].unsqueeze(3).to_broadcast([128, 8, 16, 16]),
                in1=m1v[:, :, 1, :].unsqueeze(2).to_broadcast([128, 8, 16, 16]), op=ALU.add), r=[m1.b], w=[cand.b])
            for hh in range(8):
                S.op("dve", lambda: nc.vector.max(out=best[:, hh, 0:8], in_=cand[:, hh, :]), r=[cand.b], w=[best.b])
                S.op("dve", lambda: nc.vector.max_index(out=pos[:, hh, 0:8], in_max=best[:, hh, 0:8], in_values=cand[:, hh, :]),
                     r=[cand.b, best.b], w=[pos.b])
                S.op("dve", lambda: nc.vector.match_replace(out=cand2[:, :], in_to_replace=best[:, hh, 0:8],
                                                            in_values=cand[:, hh, :], imm_value=NEG), r=[cand.b, best.b], w=[cand2.b])
                S.op("dve", lambda: nc.vector.max(out=best[:, hh, 8:16], in_=cand2[:, :]), r=[cand2.b], w=[best.b])
                S.op("dve", lambda: nc.vector.max_index(out=pos[:, hh, 8:16], in_max=best[:, hh, 8:16], in_values=cand2[:, :]),
                     r=[cand2.b, best.b], w=[pos.b])
                yield
            S.op("dve", lambda: nc.vector.tensor_tensor(out=bm[:, :, :], in0=best[:, :, :],
                                                        in1=best[:, :, 0:1].to_broadcast([128, 8, 16]), op=ALU.subtract),
                 r=[best.b], w=[bm.b])
            S.op("act", lambda: nc.scalar.activation(out=bm[:, :, :], in_=bm[:, :, :], func=AF.Exp), r=[bm.b], w=[bm.b])
            S.op("dve", lambda: nc.vector.tensor_reduce(out=se[:, :], in_=bm[:, :, :], axis=AX.X, op=ALU.add), r=[bm.b], w=[se.b])
            S.op("dve", lambda: nc.vector.reciprocal(out=se[:, :], in_=se[:, :]), r=[se.b], w=[se.b])
            S.op("dve", lambda: nc.vector.tensor_tensor(out=gw[:, :].rearrange("p (h k) -> p h k", k=16), in0=bm[:, :, :],
                                                        in1=se[:, :].unsqueeze(2).to_broadcast([128, 8, 16]), op=ALU.mult),
                 r=[bm.b, se.b], w=[gw.b])
            yield
            S.op("dve", lambda: nc.vector.tensor_single_scalar(out=hi[:, :, :], in_=pos[:, :, :], scalar=4,
                                                               op=ALU.logical_shift_right), r=[pos.b], w=[hi.b])
            S.op("dve", lambda: nc.vector.tensor_single_scalar(out=lo[:, :, :], in_=pos[:, :, :], scalar=15,
                                                               op=ALU.bitwise_and), r=[pos.b], w=[lo.b])
            S.op("dve", lambda: nc.vector.tensor_copy(out=hif[:, :, :], in_=hi[:, :, :]), r=[hi.b], w=[hif.b])
            S.op("dve", lambda: nc.vector.tensor_copy(out=lof[:, :, :], in_=lo[:, :, :]), r=[lo.b], w=[lof.b])
            S.op("dve", lambda: nc.vector.tensor_copy(out=i1f[:, :, :], in_=i1[:, :, :]), r=[i1.b], w=[i1f.b])
            yield
            for (xf, half, eo) in ((hif, 0, e0), (lof, 1, e1)):
                S.op("dve", lambda: nc.vector.tensor_tensor(out=eq[:, :, :, :],
                                                            in0=xf[:, :, :].unsqueeze(3).to_broadcast([128, 8, 16, 16]),
                                                            in1=iota_b, op=ALU.is_equal), r=[xf.b, iota.b], w=[eq.b])
                S.op("dve", lambda: nc.vector.tensor_tensor(out=eq[:, :, :, :], in0=eq[:, :, :, :],
                                                            in1=i1fv[:, :, half, :].unsqueeze(2).to_broadcast([128, 8, 16, 16]),
                                                            op=ALU.mult), r=[eq.b, i1f.b], w=[eq.b])
                S.op("dve", lambda: nc.vector.tensor_reduce(out=eo[:, :, :], in_=eq[:, :, :, :], axis=AX.X, op=ALU.add),
                     r=[eq.b], w=[eo.b])
                yield
            S.op("dve", lambda: nc.vector.scalar_tensor_tensor(out=ef[:, :], in0=e0[:, :, :].rearrange("p h k -> p (h k)"),
                                                               scalar=128.0, in1=e1[:, :, :].rearrange("p h k -> p (h k)"),
                                                               op0=ALU.mult, op1=ALU.add), r=[e0.b, e1.b], w=[ef.b])
            S.op("dve", lambda: nc.vector.tensor_copy(out=eidx[:, :], in_=ef[:, :]), r=[ef.b], w=[eidx.b])
            yield

        gi = 0
        pi = 0
        di = 0
        a_b = [Buf("a%d" % i) for i in range(128)]
        ga_b = [Buf("ga%d" % i) for i in range(32)]
        w_b = [Buf("w%d" % i) for i in range(32)]

        def main(t, gen):
            nonlocal gi, pi, di
            t0 = t * 128
            h, eidx, gw = hs[t % 2], eidxs[t % 2], gws[t % 2]
            hb = hbs[t % 2]
            S.op("act", lambda: nc.scalar.copy(out=hb[:, :], in_=h[:, :]), r=[h.b], w=[hb.b])
            for g0 in range(0, 128, 4):
                gs = []
                for sl in range(g0, g0 + 4):
                    Gt = G[gi % NG]
                    gi += 1
                    gs.append(Gt)
                    S.dma("pool", lambda: nc.gpsimd.indirect_dma_start(
                        out=Gt[:, :], out_offset=None, in_=d["p_uvb%d" % layer],
                        in_offset=bass.IndirectOffsetOnAxis(ap=eidx[:, sl:sl + 1], axis=0)), r=[eidx.b], w=[Gt.b])
                    pj = prods[pi % 6]
                    pi += 1
                    S.op("dve", lambda: nc.vector.tensor_tensor(out=pj[:, :], in0=Gt[:, 0:1024], in1=hb[:, :], op=ALU.mult),
                         r=[Gt.b, hb.b], w=[pj.b])
                    if 'noacc' not in XP:
                        S.op("act", lambda: nc.scalar.activation(out=pj[:, :], in_=pj[:, :], func=AF.Identity,
                                                                 accum_out=a[:, sl:sl + 1]), r=[pj.b], w=[pj.b, a_b[sl]])
                gq = g0 // 4
                S.op("act", lambda: nc.scalar.activation(out=ga[:, g0:g0 + 4], in_=a[:, g0:g0 + 4], func=AF.Gelu),
                     r=a_b[g0:g0 + 4], w=[ga_b[gq]])
                S.op("dve", lambda: nc.vector.tensor_tensor(out=w[:, g0:g0 + 4], in0=ga[:, g0:g0 + 4], in1=gw[:, g0:g0 + 4],
                                                            op=ALU.mult), r=[ga_b[gq], gw.b], w=[w_b[gq]])
                dg = diags[di % 3]
                di += 1
                S.op("dve", lambda: nc.vector.tensor_tensor(out=dg[:, :, :], in0=ident_b,
                                                            in1=w[:, g0:g0 + 4].unsqueeze(2).to_broadcast([128, 4, 128]),
                                                            op=ALU.mult), r=[identb.b, w_b[gq]], w=[dg.b])
                for j, sl in enumerate(range(g0, g0 + 4)):
                    Gt = gs[j]
                    for n in range(0 if ('nope' in XP and sl not in (0, 127)) else 2):
                        S.op("pe", lambda: nc.tensor.matmul(accb[n][:, 0:512], lhsT=dg[:, j, :],
                                                            rhs=Gt[:, 1024 + n * 512:1024 + (n + 1) * 512],
                                                            start=(sl == 0), stop=(sl == 127)), r=[dg.b, Gt.b], w=[accb[n].b])
                if gen is not None and 'nofront' not in XP:
                    next(gen, None)
                    next(gen, None)
            for n in range(2):
                S.op("dve", lambda: nc.vector.scalar_tensor_tensor(out=y[:, n * 512:(n + 1) * 512], in0=h[:, n * 512:(n + 1) * 512],
                                                                   scalar=ALPHA, in1=accb[n][:, 0:512], op0=ALU.mult, op1=ALU.add),
                     r=[h.b, accb[n].b], w=[y.b])
            layer_norm(K, ph, y, g_bc, b_bc, o, "p%d_%d" % (layer, t))
            S.dma("sp", lambda: nc.sync.dma_start(out=dst_fn(t0), in_=o[:, :]), r=[o.b], w=[dst_buf_fn(t)])

        for _ in front(0):
            pass
        for t in range(ntiles):
            gen = front(t + 1) if t + 1 < ntiles else None
            main(t, gen)
            if gen is not None:
                for _ in gen:
                    pass
        S.barrier()
    K.bank_list = list(range(8))


def phase_b0(K, s):
    nc, S, d = K.nc, K.S, K.d
    ph = Phase(K)
    with ph.es:
        wkv = ph.tile("swkv", [128, 8, 1536])
        S.dma("sp", lambda: nc.sync.dma_start(out=wkv[:, :, :], in_=d["s_w_kv"].rearrange("(c p) n -> p c n", p=128)), w=[wkv.b])
        wqb = ph.tile("bwin", [128, 8, 1072])
        S.dma("sp", lambda: nc.sync.dma_start(out=wqb[:, :, :], in_=d["b_w_in"].rearrange("(c p) n -> p c n", p=128)), w=[wqb.b])
        h = ph.tile("bh", [128, 1024])
        cs = ph.tile("bcs", [128, 64])
        hT = ph.tile("bhT", [128, 8, 128])
        kvs = ph.tile("kvs", [128, 3, 4, 128])
        q_sb = ph.tile("bq", [128, 16, 64])
        qr = ph.tile("bqr", [128, 16, 64])
        kr = ph.tile("bkr", [128, 2, 4, 64])
        gts = ph.tile("bgts", [128, 48])
        vps = [ph.tile("bvp%d" % i, [128, 4, 65], BF16) for i in range(2)]
        qT = ph.tile("bqT", [64, 16, 128], BF16)
        qrT = ph.tile("bqrT", [64, 16, 128], BF16)
        kT = ph.tile("bkT", [64, 8, 128])
        kTr = ph.tile("bkTr", [64, 8, 128], BF16)
        for v in vps:
            S.op("dve", lambda: nc.vector.memset(v[:, :, 64:65], 1.0), w=[v.b])
        for t in range(K.ntiles):
            t0 = t * 128
            S.dma("sp", lambda: nc.sync.dma_start(out=h[:, :], in_=d["h2"][t0:t0 + 128, :]), r=[K.db("h2", s, t)], w=[h.b])
            S.dma("sp", lambda: nc.sync.dma_start(out=cs[:, :], in_=d["rope64"][t0:t0 + 128, :]), w=[cs.b])
            transpose_to(K, h, [h[:, c * 128:(c + 1) * 128] for c in range(8)], hT,
                         lambda g0, n: hT[:, g0:g0 + n, :].rearrange("p a b -> p (a b)"), 128)
            kv_flat = kvs[:, :, :, :].rearrange("p a b c -> p (a b c)")
            for n in range(3):
                bk = K.bank()
                for c in range(8):
                    S.op("pe", lambda: nc.tensor.matmul(bk[:, 0:512], lhsT=hT[:, c, :], rhs=wkv[:, c, n * 512:(n + 1) * 512],
                                                        start=(c == 0), stop=(c == 7)), r=[hT.b, wkv.b], w=[bk.b])
                evac(K, n, kv_flat[:, n * 512:(n + 1) * 512], bk[:, 0:512], r=[bk.b], w=[kvs.b])
            q_flat = q_sb[:, :, :].rearrange("p a b -> p (a b)")
            for n in range(2):
                bk = K.bank()
                for c in range(8):
                    S.op("pe", lambda: nc.tensor.matmul(bk[:, 0:512], lhsT=hT[:, c, :], rhs=wqb[:, c, n * 512:(n + 1) * 512],
                                                        start=(c == 0), stop=(c == 7)), r=[hT.b, wqb.b], w=[bk.b])
                evac(K, n + 1, q_flat[:, n * 512:(n + 1) * 512], bk[:, 0:512], r=[bk.b], w=[q_sb.b])
            bk = K.bank()
            for c in range(8):
                S.op("pe", lambda: nc.tensor.matmul(bk[:, 0:48], lhsT=hT[:, c, :], rhs=wqb[:, c, 1024:1072],
                                                    start=(c == 0), stop=(c == 7)), r=[hT.b, wqb.b], w=[bk.b])
            S.op("act", lambda: nc.scalar.activation(out=gts[:, :], in_=bk[:, 0:48], func=AF.Sigmoid), r=[bk.b], w=[gts.b])
            S.dma("sp", lambda: nc.sync.dma_start(out=d["gates"][t0:t0 + 128, :], in_=gts[:, :]), r=[gts.b], w=[K.db("b0", s, t)])
            cos_b = cs[:, 0:32].unsqueeze(1).to_broadcast([128, 16, 32])
            sin_b = cs[:, 32:64].unsqueeze(1).to_broadcast([128, 16, 32])
            rope_tok(K, ph, q_sb[:, :, 0:32], q_sb[:, :, 32:64], qr[:, :, 0:32], qr[:, :, 32:64], cos_b, sin_b,
                     [128, 16, 32], [q_sb.b, cs.b], [qr.b], "brq%d" % t)
            cos_k = cs[:, 0:32].unsqueeze(1).to_broadcast([128, 4, 32])
            sin_k = cs[:, 32:64].unsqueeze(1).to_broadcast([128, 4, 32])
            for br in range(2):
                rope_tok(K, ph, kvs[:, 1 + br, :, 0:32], kvs[:, 1 + br, :, 32:64], kr[:, br, :, 0:32], kr[:, br, :, 32:64],
                         cos_k, sin_k, [128, 4, 32], [kvs.b, cs.b], [kr.b], "brk%d_%d" % (t, br))
            transpose_to(K, q_sb, [q_sb[:, hh, :] for hh in range(16)], qT,
                         lambda g0, n: qT[:, g0:g0 + n, :].rearrange("p a b -> p (a b)"), 64)
            transpose_to(K, qr, [qr[:, hh, :] for hh in range(16)], qrT,
                         lambda g0, n: qrT[:, g0:g0 + n, :].rearrange("p a b -> p (a b)"), 64)
            srcs = [kvs[:, 0, g, 0:64] for g in range(4)] + [kvs[:, 0, g, 64:128] for g in range(4)]
            transpose_to(K, kvs, srcs, kT, lambda g0, n: kT[:, g0:g0 + n, :].rearrange("p a b -> p (a b)"), 64)
            srcs = [kr[:, 0, g, :] for g in range(4)] + [kr[:, 1, g, :] for g in range(4)]
            transpose_to(K, kr, srcs, kTr, lambda g0, n: kTr[:, g0:g0 + n, :].rearrange("p a b -> p (a b)"), 64)
            wb = [K.db("b0", s, t)]
            S.dma("sp", lambda: nc.sync.dma_start(out=d["bqT"][:, :, t0:t0 + 128].rearrange("h e t -> e h t"), in_=qT[:, :, :]),
                  r=[qT.b], w=wb)
            S.dma("sp", lambda: nc.sync.dma_start(out=d["bqrT"][:, :, t0:t0 + 128].rearrange("h e t -> e h t"), in_=qrT[:, :, :]),
                  r=[qrT.b], w=wb)
            S.dma("sp", lambda: nc.sync.dma_start(out=d["bkcT"][:, :, t0:t0 + 128].rearrange("h e t -> e h t"), in_=kT[:, :, :]),
                  r=[kT.b], w=wb)
            S.dma("sp", lambda: nc.sync.dma_start(out=d["bkrT"][:, :, t0:t0 + 128].rearrange("h e t -> e h t"), in_=kTr[:, :, :]),
                  r=[kTr.b], w=wb)
            for br in range(2):
                vp = vps[br]
                S.op("act", lambda: nc.scalar.copy(out=vp[:, :, 0:64], in_=kvs[:, 1 + br, :, 64:128]), r=[kvs.b], w=[vp.b])
                S.dma("sp", lambda: nc.sync.dma_start(out=d["bvp"][br, :, t0:t0 + 128, :].rearrange("g t e -> t g e"),
                                                      in_=vp[:, :, :]), r=[vp.b], w=wb)
        S.barrier()


def phase_b1(K, s):
    nc, S, d = K.nc, K.S, K.d
    ph = Phase(K)
    allb0 = [K.db("b0", s, t) for t in range(NT)]
    with ph.es:
        ovl = ph.tile("ovl", [128, 32])
        S.dma("sp", lambda: nc.sync.dma_start(out=ovl[:, :], in_=d["overlap"]), w=[ovl.b])
        for kv in range(2):
            nm = "k" if kv == 0 else "v"
            w1 = ph.tile("cw1" + nm, [64, 32, 256])
            S.dma("sp", lambda: nc.sync.dma_start(out=w1[:, :, :], in_=d["s_cmp_%s_w1" % nm].rearrange("(l e) n -> e l n", e=64)), w=[w1.b])
            w2 = ph.tile("cw2" + nm, [128, 2, 64])
            S.dma("sp", lambda: nc.sync.dma_start(out=w2[:, :, :], in_=d["s_cmp_%s_w2" % nm].rearrange("(c p) n -> p c n", p=128)), w=[w2.b])
            b1 = ph.tile("cb1" + nm, [128, 2])
            S.dma("sp", lambda: nc.sync.dma_start(out=b1[:, :], in_=d["s_cmp_%s_b1" % nm]), w=[b1.b])
            posT = ph.tile("cpos" + nm, [64, 32])
            S.dma("sp", lambda: nc.sync.dma_start(out=posT[:, :], in_=d["s_cmp_pos_%sT" % nm]), w=[posT.b])
            xT = ph.tile("cxT" + nm, [64, SEQ])
            Xp = ph.tile("cXp" + nm, [64, 32, 127])
            hid = ph.tile("chid" + nm, [128, 2, 127])
            res = ph.tile("cres" + nm, [128, 128], BF16)
            for g in range(4):
                S.dma("sp", lambda: nc.sync.dma_start(out=xT[:, :], in_=d["bkcT"][kv * 4 + g, :, :]), r=allb0, w=[xT.b])
                for half in range(2):
                    S.op("dve", lambda: nc.vector.tensor_tensor(
                        out=Xp[:, half * 16:(half + 1) * 16, :],
                        in0=xT[:, half * 16:half * 16 + 2032].rearrange("p (j l) -> p l j", l=16),
                        in1=posT[:, half * 16:(half + 1) * 16].unsqueeze(2).to_broadcast([64, 16, 127]), op=ALU.add),
                        r=[xT.b, posT.b], w=[Xp.b])
                for hc in range(2):
                    bk = K.bank()
                    for l in range(32):
                        S.op("pe", lambda: nc.tensor.matmul(bk[:, 0:127], lhsT=w1[:, l, hc * 128:(hc + 1) * 128], rhs=Xp[:, l, :],
                                                            start=(l == 0), stop=(l == 31)), r=[w1.b, Xp.b], w=[bk.b])
                    S.op("act", lambda: nc.scalar.activation(out=hid[:, hc, :], in_=bk[:, 0:127], func=AF.Gelu, bias=b1[:, hc:hc + 1],
                                                             scale=1.0), r=[bk.b, b1.b], w=[hid.b])
                bk = K.bank()
                if kv == 0:
                    for hc in range(2):
                        S.op("pe", lambda: nc.tensor.matmul(bk[0:64, 0:127], lhsT=w2[:, hc, :], rhs=hid[:, hc, :],
                                                            start=(hc == 0), stop=(hc == 1)), r=[w2.b, hid.b], w=[bk.b])
                    S.op("dve", lambda: nc.vector.memset(res[0:64, :], 0.0), w=[res.b])
                    S.op("dve", lambda: nc.vector.tensor_copy(out=res[0:64, 0:127], in_=bk[0:64, 0:127]), r=[bk.b], w=[res.b])
                    S.dma("sp", lambda: nc.sync.dma_start(out=d["kcmpT"][g, :, :], in_=res[0:64, :]), r=[res.b], w=[K.db("b1", s)])
                else:
                    for hc in range(2):
                        S.op("pe", lambda: nc.tensor.matmul(bk[0:127, 0:64], lhsT=hid[:, hc, :], rhs=w2[:, hc, :],
                                                            start=(hc == 0), stop=(hc == 1)), r=[w2.b, hid.b], w=[bk.b])
                    S.op("dve", lambda: nc.vector.memset(res[:, :], 0.0), w=[res.b])
                    S.op("dve", lambda: nc.vector.tensor_copy(out=res[0:127, 0:64], in_=bk[0:127, 0:64]), r=[bk.b], w=[res.b])
                    S.op("dve", lambda: nc.vector.memset(res[0:127, 64:65], 1.0), r=[], w=[res.b])
                    S.op("dve", lambda: nc.vector.tensor_copy(out=res[0:127, 65:97], in_=ovl[0:127, :]), r=[ovl.b], w=[res.b])
                    S.dma("sp", lambda: nc.sync.dma_start(out=d["vcmp"][g, :, :], in_=res[:, 0:97]), r=[res.b], w=[K.db("b1", s)])
        S.barrier()


def phase_b2(K, s):
    nc, S, d = K.nc, K.S, K.d
    scale = 64.0 ** -0.5
    ph = Phase(K)
    allb0 = [K.db("b0", s, t) for t in range(NT)]
    b1b = [K.db("b1", s)]
    with ph.es:
        cmask = ph.tile("cmask", [128, SEQ])
        S.dma("sp", lambda: nc.sync.dma_start(out=cmask[:, :], in_=d["cmask"]), w=[cmask.b])
        Eall32 = ph.tile("Eall32", [32, 16, 128])
        S.dma("sp", lambda: nc.sync.dma_start(out=Eall32[:, :, :], in_=d["Eall"]), w=[Eall32.b])
        Eall = ph.tile("Eall", [32, 16, 128], BF16)
        S.op("dve", lambda: nc.vector.tensor_copy(out=Eall[:, :, :], in_=Eall32[:, :, :]), r=[Eall32.b], w=[Eall.b])
        At = ph.tile("At", [128, 16, 32])
        Bt = ph.tile("Bt", [128, 16, 32])
        S.dma("sp", lambda: nc.sync.dma_start(out=At[:, :, :], in_=d["selA"].rearrange("(q p) n -> p q n", p=128)), w=[At.b])
        S.dma("sp", lambda: nc.sync.dma_start(out=Bt[:, :, :], in_=d["selB"].rearrange("(q p) n -> p q n", p=128)), w=[Bt.b])
        gts = ph.tile("gts", [128, 16, 48])
        S.dma("sp", lambda: nc.sync.dma_start(out=gts[:, :, :], in_=d["gates"].rearrange("(q p) n -> p q n", p=128)), r=allb0, w=[gts.b])
        ksT = ph.tile("ksT", [64, SEQ], BF16)
        kwT = ph.tile("kwT", [64, SEQ], BF16)
        vs = ph.tile("vs", [128, 16, 65], BF16)
        vw = ph.tile("vw", [128, 16, 65], BF16)
        kcT = ph.tile("kcT", [64, 128], BF16)
        vc = ph.tile("vc", [128, 97], BF16)
        qTs = [ph.tile("nqT%d" % i, [64, SEQ], BF16) for i in range(4)]
        qrTs = [ph.tile("nqrT%d" % i, [64, SEQ], BF16) for i in range(4)]
        comb = ph.tile("comb", [128, 4, 16, 64])
        imp = ph.tile("imp", [128, 16, 32])
        sel = ph.tile("sel", [128, 16, 32])
        selT = ph.tile("selT", [32, SEQ], BF16)
        tmp32 = ph.tile("tmp32", [128, 32])
        m8 = ph.tile("m8", [128, 16])
        pts = [ph.tile("npt%d" % i, [128, 512], BF16) for i in range(3)]
        mts = [ph.tile("nmt%d" % i, [128, 512], BF16) for i in range(2)]
        rd = ph.tile("nrd", [128, 2])
        otmp = ph.tile("notmp", [128, 64])
        ctr = 0
        for g in range(4):
            S.dma("sp", lambda: nc.sync.dma_start(out=ksT[:, :], in_=d["bkrT"][g, :, :]), r=allb0, w=[ksT.b])
            S.dma("sp", lambda: nc.sync.dma_start(out=kwT[:, :], in_=d["bkrT"][4 + g, :, :]), r=allb0, w=[kwT.b])
            S.dma("sp", lambda: nc.sync.dma_start(out=vs[:, :, :], in_=d["bvp"][0, g, :, :].rearrange("(k p) e -> p k e", p=128)), r=allb0, w=[vs.b])
            S.dma("sp", lambda: nc.sync.dma_start(out=vw[:, :, :], in_=d["bvp"][1, g, :, :].rearrange("(k p) e -> p k e", p=128)), r=allb0, w=[vw.b])
            S.dma("sp", lambda: nc.sync.dma_start(out=kcT[:, :], in_=d["kcmpT"][g, :, :]), r=b1b, w=[kcT.b])
            S.dma("sp", lambda: nc.sync.dma_start(out=vc[:, :], in_=d["vcmp"][g, :, :]), r=b1b, w=[vc.b])
            for j in range(4):
                hh = g * 4 + j
                S.dma("sp", lambda: nc.sync.dma_start(out=qTs[j][:, :], in_=d["bqT"][hh, :, :]), r=allb0, w=[qTs[j].b])
                S.dma("sp", lambda: nc.sync.dma_start(out=qrTs[j][:, :], in_=d["bqrT"][hh, :, :]), r=allb0, w=[qrTs[j].b])
            for j in range(4):
                hh = g * 4 + j
                qT = qTs[j]
                for qc in range(4):
                    st = K.bank()
                    pt = pts[ctr % 3]
                    ctr += 1
                    S.op("pe", lambda: nc.tensor.matmul(st[0:127, 0:512], lhsT=kcT[:, 0:127], rhs=qT[:, qc * 512:(qc + 1) * 512],
                                                        start=True, stop=True), r=[kcT.b, qT.b], w=[st.b])
                    S.op("act", lambda: nc.scalar.activation(out=pt[0:127, :], in_=st[0:127, 0:512], func=AF.Exp, scale=scale),
                         r=[st.b], w=[pt.b])
                    S.op("dve", lambda: nc.vector.tensor_tensor(out=pt[0:127, :], in0=pt[0:127, :],
                                                                in1=cmask[0:127, qc * 512:(qc + 1) * 512], op=ALU.mult),
                         r=[pt.b, cmask.b], w=[pt.b])
                    for jj in range(4):
                        qt = qc * 4 + jj
                        ob = K.bank()
                        S.op("pe", lambda: nc.tensor.matmul(ob[:, 0:97], lhsT=pt[0:127, jj * 128:(jj + 1) * 128], rhs=vc[0:127, :],
                                                            start=True, stop=True), r=[pt.b, vc.b], w=[ob.b])
                        S.op("dve", lambda: nc.vector.tensor_scalar_max(out=rd[:, 0:1], in0=ob[:, 64:65], scalar1=1e-30),
                             r=[ob.b], w=[rd.b])
                        S.op("dve", lambda: nc.vector.reciprocal(out=rd[:, 1:2], in_=rd[:, 0:1]), r=[rd.b], w=[rd.b])
                        S.op("dve", lambda: nc.vector.tensor_scalar(out=comb[:, j, qt, :], in0=ob[:, 0:64], scalar1=rd[:, 1:2],
                                                                    scalar2=gts[:, qt, hh:hh + 1], op0=ALU.mult, op1=ALU.mult),
                             r=[ob.b, rd.b, gts.b], w=[comb.b])
                        if j == 0:
                            S.op("dve", lambda: nc.vector.tensor_scalar(out=imp[:, qt, :], in0=ob[:, 65:97], scalar1=rd[:, 1:2],
                                                                        scalar2=None, op0=ALU.mult), r=[ob.b, rd.b], w=[imp.b])
                        else:
                            S.op("dve", lambda: nc.vector.scalar_tensor_tensor(out=imp[:, qt, :], in0=ob[:, 65:97], scalar=rd[:, 1:2],
                                                                               in1=imp[:, qt, :], op0=ALU.mult, op1=ALU.add),
                                 r=[ob.b, rd.b, imp.b], w=[imp.b])
            S.op("dve", lambda: nc.vector.tensor_tensor(out=imp[:, :, :], in0=imp[:, :, :], in1=At[:, :, :], op=ALU.mult),
                 r=[imp.b, At.b], w=[imp.b])
            S.op("dve", lambda: nc.vector.tensor_tensor(out=imp[:, :, :], in0=imp[:, :, :], in1=Bt[:, :, :], op=ALU.add),
                 r=[imp.b, Bt.b], w=[imp.b])
            for qt in range(16):
                S.op("dve", lambda: nc.vector.max(out=m8[:, 0:8], in_=imp[:, qt, :]), r=[imp.b], w=[m8.b])
                S.op("dve", lambda: nc.vector.match_replace(out=tmp32[:, :], in_to_replace=m8[:, 0:8], in_values=imp[:, qt, :],
                                                            imm_value=-3e38), r=[imp.b, m8.b], w=[tmp32.b])
                S.op("dve", lambda: nc.vector.max(out=m8[:, 8:16], in_=tmp32[:, :]), r=[tmp32.b], w=[m8.b])
                S.op("dve", lambda: nc.vector.tensor_scalar(out=sel[:, qt, :], in0=imp[:, qt, :], scalar1=m8[:, 15:16], scalar2=None,
                                                            op0=ALU.is_ge), r=[imp.b, m8.b], w=[sel.b])
            transpose_to(K, sel, [sel[:, qt, :] for qt in range(16)], selT,
                         lambda g0, n: selT[:, g0 * 128:(g0 + n) * 128], 32)
            for j in range(4):
                hh = g * 4 + j
                qrT = qrTs[j]
                for qc in range(4):
                    ob = [K.ps[4 + jj] for jj in range(4)]
                    nk = 4 * qc + 4
                    for kt in range(nk):
                        st = K.ps[ctr % 2]
                        mb = K.ps[2 + ctr % 2]
                        pt = pts[ctr % 3]
                        ctr += 1
                        S.op("pe", lambda: nc.tensor.matmul(st[:, 0:512], lhsT=ksT[:, kt * 128:(kt + 1) * 128],
                                                            rhs=qrT[:, qc * 512:(qc + 1) * 512], start=True, stop=True),
                             r=[ksT.b, qrT.b], w=[st.b])
                        S.op("pe", lambda: nc.tensor.matmul(mb[:, 0:512], lhsT=Eall[:, kt, :], rhs=selT[:, qc * 512:(qc + 1) * 512],
                                                            start=True, stop=True), r=[Eall.b, selT.b], w=[mb.b])
                        j0 = max(0, kt - 4 * qc)
                        S.op("act", lambda: nc.scalar.activation(out=pt[:, j0 * 128:512], in_=st[:, j0 * 128:512], func=AF.Exp,
                                                                 scale=scale), r=[st.b], w=[pt.b])
                        S.op("dve", lambda: nc.vector.tensor_tensor(out=pt[:, j0 * 128:512], in0=pt[:, j0 * 128:512],
                                                                    in1=mb[:, j0 * 128:512], op=ALU.mult), r=[pt.b, mb.b], w=[pt.b])
                        if kt >= 4 * qc:
                            S.op("dve", lambda: nc.vector.tensor_tensor(out=pt[:, j0 * 128:(j0 + 1) * 128],
                                                                        in0=pt[:, j0 * 128:(j0 + 1) * 128],
                                                                        in1=K.tri_le[:, :], op=ALU.mult),
                                 r=[pt.b, K.tri_le.b], w=[pt.b])
                        for jj in range(j0, 4):
                            qt = 4 * qc + jj
                            S.op("pe", lambda: nc.tensor.matmul(ob[jj][:, 0:65], lhsT=pt[:, jj * 128:(jj + 1) * 128],
                                                                rhs=vs[:, kt, :], start=(kt == 0), stop=(kt == qt)),
                                 r=[pt.b, vs.b], w=[ob[jj].b])
                    for jj in range(4):
                        qt = 4 * qc + jj
                        nsa_combine(K, ob[jj], rd, otmp, comb, j, qt, gts, 16 + hh)
                for qt in range(16):
                    kts = list(range(max(0, qt - 4), qt + 1))
                    ob = K.ps[4 + qt % 4]
                    stA = K.ps[ctr % 2]
                    stB = K.ps[2 + ctr % 2]
                    ptA = pts[ctr % 3]
                    ptB = mts[ctr % 2]
                    ctr += 1
                    for i, kt in enumerate(kts):
                        st = stA if i < 4 else stB
                        S.op("pe", lambda: nc.tensor.matmul(st[:, (i % 4) * 128:(i % 4 + 1) * 128], lhsT=kwT[:, kt * 128:(kt + 1) * 128],
                                                            rhs=qrT[:, qt * 128:(qt + 1) * 128], start=True, stop=True),
                             r=[kwT.b, qrT.b], w=[st.b])
                    na = min(4, len(kts))
                    S.op("act", lambda: nc.scalar.activation(out=ptA[:, 0:na * 128], in_=stA[:, 0:na * 128], func=AF.Exp, scale=scale),
                         r=[stA.b], w=[ptA.b])
                    if len(kts) == 5:
                        S.op("act", lambda: nc.scalar.activation(out=ptB[:, 0:128], in_=stB[:, 0:128], func=AF.Exp, scale=scale),
                             r=[stB.b], w=[ptB.b])
                    for i, kt in enumerate(kts):
                        pt = ptA if i < 4 else ptB
                        sl = slice((i % 4) * 128, (i % 4 + 1) * 128)
                        if kt == qt:
                            S.op("dve", lambda: nc.vector.tensor_tensor(out=pt[:, sl], in0=pt[:, sl], in1=K.tri_le[:, :], op=ALU.mult),
                                 r=[pt.b, K.tri_le.b], w=[pt.b])
                        elif kt == qt - 4:
                            S.op("dve", lambda: nc.vector.tensor_tensor(out=pt[:, sl], in0=pt[:, sl], in1=K.tri_gt[:, :], op=ALU.mult),
                                 r=[pt.b, K.tri_gt.b], w=[pt.b])
                        S.op("pe", lambda: nc.tensor.matmul(ob[:, 0:65], lhsT=pt[:, sl], rhs=vw[:, kt, :], start=(i == 0),
                                                            stop=(i == len(kts) - 1)), r=[pt.b, vw.b], w=[ob.b])
                    nsa_combine(K, ob, rd, otmp, comb, j, qt, gts, 32 + hh)
                S.dma("sp", lambda: nc.sync.dma_start(
                    out=d["attn"][:, hh * 64:(hh + 1) * 64].rearrange("(q p) e -> p q e", p=128), in_=comb[:, j, :, :]),
                    r=[comb.b], w=[K.db("attn", s, qq, hh) for qq in range(4)])
        S.barrier()


def nsa_combine(K, ob, rd, otmp, comb, j, qt, gts, gcol):
    nc, S = K.nc, K.S
    S.op("dve", lambda: nc.vector.reciprocal(out=rd[:, 1:2], in_=ob[:, 64:65]), r=[ob.b], w=[rd.b])
    S.op("dve", lambda: nc.vector.tensor_scalar(out=otmp[:, :], in0=ob[:, 0:64], scalar1=rd[:, 1:2], scalar2=gts[:, qt, gcol:gcol + 1],
                                                op0=ALU.mult, op1=ALU.mult), r=[ob.b, rd.b, gts.b], w=[otmp.b])
    S.op("dve", lambda: nc.vector.tensor_tensor(out=comb[:, j, qt, :], in0=comb[:, j, qt, :], in1=otmp[:, :], op=ALU.add),
         r=[comb.b, otmp.b], w=[comb.b])

def build(nseq=4, stages=("a1", "a2", "a3"), dbg=(), ntiles=NT):
    nc = bass.Bass("TRN2", target_bir_lowering=False)
    K = Ctx()
    K.nc = nc
    K.uid = 0
    K.evi = 0
    K.ntiles = ntiles
    es = ExitStack()
    K.es = es
    d = {}
    K.d = d

    def din(name, shape, dtype=F32):
        d[name] = nc.dram_tensor(name, list(shape), dtype, kind="ExternalInput").ap()

    def dscr(name, shape, dtype=F32):
        kind = "ExternalOutput" if name in dbg else "Internal"
        if name + "_in" in dbg:
            kind = "ExternalInput"
        d[name] = nc.dram_tensor(name, list(shape), dtype, kind=kind).ap()

    din("x", [nseq, SEQ, D])
    din("a_w_in", [1024, 800])
    din("a_q_norm", [1, 512])
    din("a_kv_norm", [1, 256])
    din("a_w_q_up", [512, 1536])
    din("a_w_kv_up", [256, 2048])
    din("a_w_o", [1024, 1024])
    din("s_w_kv", [1024, 1536])
    din("b_w_in", [1024, 1072])
    din("b_w_o", [1024, 1024])
    for nm in ("k", "v"):
        din("s_cmp_%s_w1" % nm, [2048, 256])
        din("s_cmp_%s_b1" % nm, [128, 2])
        din("s_cmp_%s_w2" % nm, [256, 64])
        din("s_cmp_pos_%sT" % nm, [64, 32])
    din("p_w_q", [2, 1024, 2048])
    din("p_skT", [2, 128, 16, 128])
    din("p_uv0", [16384, 2048])
    din("p_uv1", [16384, 2048])
    din("ln_g", [4, 1024])
    din("ln_b", [4, 1024])
    hc = host_consts()
    for k, v in hc.items():
        din(k, v.shape)
    d["out"] = nc.dram_tensor("out", [nseq, SEQ, D], F32, kind="ExternalOutput").ap()
    dscr("qnT", [16, 64, SEQ], BF16)
    dscr("qpT", [16, 32, SEQ], BF16)
    dscr("knT", [16, 64, SEQ], BF16)
    dscr("kpT", [32, SEQ], BF16)
    dscr("vp", [16, SEQ, 65], BF16)
    dscr("attn", [SEQ, 1024])
    dscr("h1", [SEQ, 1024])
    dscr("h2", [SEQ, 1024])
    dscr("p_uvb0", [16384, 2048], BF16)
    dscr("p_uvb1", [16384, 2048], BF16)
    dscr("h3", [SEQ, 1024])
    dscr("gates", [SEQ, 48])
    dscr("bqT", [16, 64, SEQ], BF16)
    dscr("bqrT", [16, 64, SEQ], BF16)
    dscr("bkcT", [8, 64, SEQ])
    dscr("bkrT", [8, 64, SEQ], BF16)
    dscr("bvp", [2, 4, SEQ, 65], BF16)
    dscr("kcmpT", [4, 64, 128], BF16)
    dscr("vcmp", [4, 128, 97], BF16)

    with es:
        S = Sched(nc, es)
        K.S = S
        K.ps = [Tile(es.enter_context(nc.psum_tensor("ps%d" % i, [128, 512], F32)), "ps%d" % i) for i in range(8)]
        K.bank_i = 0

        K.bank_list = list(range(8))

        def bank():
            b = K.ps[K.bank_list[K.bank_i % len(K.bank_list)]]
            K.bank_i += 1
            return b
        K.bank = bank
        K.dbufs = {}

        def db(*key):
            if key not in K.dbufs:
                K.dbufs[key] = Buf(str(key))
            return K.dbufs[key]
        K.db = db
        gl = Phase(K)
        gl.es = es
        K.ident = gl.tile("ident", [128, 128])
        K.tri_le = gl.tile("tri_le", [128, 128])
        K.tri_gt = gl.tile("tri_gt", [128, 128])
        K.eps_ln = gl.tile("eps_ln", [128, 1])
        K.eps_rms = gl.tile("eps_rms", [128, 1])
        for nm in ("ident", "tri_le", "tri_gt"):
            tl = getattr(K, nm)
            S.dma("sp", lambda tl=tl, nm=nm: nc.sync.dma_start(out=tl[:, :], in_=d[nm]), w=[tl.b])
        S.op("dve", lambda: nc.vector.memset(K.eps_ln[:, :], LN_EPS), w=[K.eps_ln.b])
        S.op("dve", lambda: nc.vector.memset(K.eps_rms[:, :], RMS_EPS), w=[K.eps_rms.b])

        if "p0" in stages or "p1" in stages:
            prologue_tables(K)
        for s in range(nseq):
            if "a1" in stages:
                phase_a1(K, s)
            if "a2" in stages:
                phase_a2(K, s)
            if "a3" in stages:
                dstname = "h1" if "h1" in dbg or len(stages) > 3 else "h1"
                phase_oproj_ln(K, s, d["a_w_o"], lambda t0: d["x"][s, t0:t0 + 128, :], lambda t: [],
                               d["ln_g"][0:1, :], d["ln_b"][0:1, :],
                               lambda t0: d["h1"][t0:t0 + 128, :], lambda t: K.db("h1", s, t), "a")
            if "p0" in stages:
                phase_peer(K, s, 0, lambda t0: d["h1"][t0:t0 + 128, :], lambda t: [K.db("h1", s, t)],
                           lambda t0: d["h2"][t0:t0 + 128, :], lambda t: K.db("h2", s, t), ntiles=K.ntiles)
            if "b0" in stages:
                phase_b0(K, s)
            if "b1" in stages:
                phase_b1(K, s)
            if "b2" in stages:
                phase_b2(K, s)
            if "b3" in stages:
                phase_oproj_ln(K, s, d["b_w_o"], lambda t0: d["h2"][t0:t0 + 128, :], lambda t: [K.db("h2", s, t)],
                               d["ln_g"][2:3, :], d["ln_b"][2:3, :],
                               lambda t0: d["h3"][t0:t0 + 128, :], lambda t: K.db("h3", s, t), "b")
            if "p1" in stages:
                phase_peer(K, s, 1, lambda t0: d["h3"][t0:t0 + 128, :], lambda t: [K.db("h3", s, t)],
                           lambda t0: d["out"][s, t0:t0 + 128, :], lambda t: K.db("out", s, t), ntiles=K.ntiles)
        S.barrier()
    K.hc = hc
    return nc, K


ALL_STAGES = ("a1", "a2", "a3", "p0", "b0", "b1", "b2", "b3", "p1")
_CACHE = {}


def _f32(a):
    return np.ascontiguousarray(np.asarray(a), dtype=np.float32)


def kernel(x, a_w_in, a_q_norm, a_kv_norm, a_w_q_up, a_w_kv_up, a_w_o, b_w_in, b_w_o,
           s_w_kv, s_cmp_pos_k, s_cmp_pos_v, s_cmp_k_w1, s_cmp_k_b1, s_cmp_k_w2,
           s_cmp_v_w1, s_cmp_v_b1, s_cmp_v_w2, p_w_q, p_subkeys, p_u, p_v, ln_g, ln_b):
    x = np.asarray(x)
    B = x.shape[0]
    if "nc" not in _CACHE:
        _CACHE["nc"] = build(nseq=B // NCORES, stages=ALL_STAGES)
    nc, K = _CACHE["nc"]
    p_subkeys = np.asarray(p_subkeys)
    p_u = np.asarray(p_u)
    p_v = np.asarray(p_v)
    w = {
        "a_w_in": _f32(np.asarray(a_w_in)[0]), "a_q_norm": _f32(np.asarray(a_q_norm).reshape(1, 512)),
        "a_kv_norm": _f32(np.asarray(a_kv_norm).reshape(1, 256)), "a_w_q_up": _f32(np.asarray(a_w_q_up)[0]),
        "a_w_kv_up": _f32(np.asarray(a_w_kv_up)[0]), "a_w_o": _f32(np.asarray(a_w_o)[0]),
        "b_w_in": _f32(np.asarray(b_w_in)[0]), "b_w_o": _f32(np.asarray(b_w_o)[0]), "s_w_kv": _f32(s_w_kv),
        "s_cmp_k_w1": _f32(s_cmp_k_w1), "s_cmp_v_w1": _f32(s_cmp_v_w1),
        "s_cmp_k_w2": _f32(s_cmp_k_w2), "s_cmp_v_w2": _f32(s_cmp_v_w2),
        "s_cmp_k_b1": _f32(np.asarray(s_cmp_k_b1).reshape(2, 128).T), "s_cmp_v_b1": _f32(np.asarray(s_cmp_v_b1).reshape(2, 128).T),
        "s_cmp_pos_kT": _f32(np.asarray(s_cmp_pos_k).T), "s_cmp_pos_vT": _f32(np.asarray(s_cmp_pos_v).T),
        "p_w_q": _f32(p_w_q),
        "p_skT": _f32(np.stack([p_subkeys[l].reshape(16, 128, 128).transpose(2, 0, 1) for l in range(2)])),
        "p_uv0": _f32(np.concatenate([p_u[0], p_v[0]], axis=1)), "p_uv1": _f32(np.concatenate([p_u[1], p_v[1]], axis=1)),
        "ln_g": _f32(np.asarray(ln_g).reshape(4, 1024)), "ln_b": _f32(np.asarray(ln_b).reshape(4, 1024)),
    }
    w.update({k: _f32(v) for k, v in K.hc.items()})
    nper = B // NCORES
    in_maps = []
    for c in range(NCORES):
        m = dict(w)
        m["x"] = _f32(x[c * nper:(c + 1) * nper])
        in_maps.append(m)
    res = run_bass_kernel_spmd(nc, in_maps, core_ids=list(range(NCORES)))
    out = np.empty((B, SEQ, D), dtype=np.float32)
    for c in range(NCORES):
        out[c * nper:(c + 1) * nper] = np.asarray(res.results[c]["out"]).reshape(nper, SEQ, D)
    return out
```

```python
import numpy as np
import os
XP = os.environ.get('XP', '')
from contextlib import ExitStack, contextmanager
import concourse.bass as bass
import concourse.mybir as mybir
from concourse.bass_utils import run_bass_kernel_spmd

F32 = mybir.dt.float32
I32 = mybir.dt.int32
U32 = mybir.dt.uint32
BF16 = mybir.dt.bfloat16
AF = mybir.ActivationFunctionType
ALU = mybir.AluOpType
AX = mybir.AxisListType

SEQ = 2048
D = 1024
NT = 16
NCORES = 8
ALPHA = 4.0 ** 0.25
LN_EPS = 1e-5
RMS_EPS = 1e-6
NEG = -1e30


class Buf:
    __slots__ = ("name", "w", "r")

    def __init__(self, name=""):
        self.name = name
        self.w = None
        self.r = {}


class Tile:
    def __init__(self, t, name):
        self.t = t
        self.b = Buf(name)

    def __getitem__(self, k):
        return self.t[k]


class Sched:
    EPOCH = 20000

    def __init__(self, nc, es):
        self.nc = nc
        self.es = es
        self.eng = {"pe": nc.tensor, "act": nc.scalar, "dve": nc.vector, "pool": nc.gpsimd, "sp": nc.sync}
        self.cur = {}
        self.waited = {}
        self.nsem = 0
        self.ninst = 0
        for e in ("pe", "act", "dve", "pool"):
            self._new_epoch(e)
        self.rings = {}
        for q, n in (("sp", 24), ("pool", 12), ("act", 8)):
            self.rings[q] = [[self._sem(), 0] for _ in range(n)]
        self.ring_i = {q: 0 for q in self.rings}

    def _sem(self):
        self.nsem += 1
        return self.es.enter_context(self.nc.semaphore("s%d" % self.nsem))

    def _new_epoch(self, e):
        self.cur[e] = [self._sem(), 0]

    def _wait(self, e, tok, strict=False):
        sem, val, src = tok
        if src == e and e == "pe" and not strict:
            return
        key = (e, id(sem))
        if self.waited.get(key, 0) >= val:
            return
        self.eng[e].wait_ge(sem, val)
        self.ninst += 1
        self.waited[key] = val

    @staticmethod
    def _deps(r, w):
        toks = []
        for b in r:
            if b.w is not None:
                toks.append(b.w)
        for b in w:
            if b.w is not None:
                toks.append(b.w)
            toks.extend(b.r.values())
        return toks

    @staticmethod
    def _commit(tok, r, w, key):
        for b in w:
            b.w = tok
            b.r = {}
        for b in r:
            if b not in w:
                b.r[key] = tok

    def op(self, e, fn, r=(), w=()):
        for t in self._deps(r, w):
            self._wait(e, t)
        st = self.cur[e]
        if st[1] >= self.EPOCH:
            self._new_epoch(e)
            st = self.cur[e]
        ins = fn()
        st[1] += 1
        ins.then_inc(st[0], 1)
        self.ninst += 1
        tok = (st[0], st[1], e)
        self._commit(tok, r, w, e)
        return tok

    def dma(self, q, fn, r=(), w=()):
        ring = self.rings[q]
        i = self.ring_i[q]
        self.ring_i[q] = (i + 1) % len(ring)
        slot = ring[i]
        if slot[1] > 0:
            self._wait(q, (slot[0], slot[1], None), strict=True)
        for t in self._deps(r, w):
            self._wait(q, t, strict=True)
        ins = fn()
        slot[1] += 16
        ins.then_inc(slot[0], 16)
        self.ninst += 1
        tok = (slot[0], slot[1], None)
        self._commit(tok, r, w, ("dma", q, i))
        return tok

    def barrier(self):
        toks = []
        for e in ("pe", "act", "dve", "pool"):
            st = self.cur[e]
            if st[1] > 0:
                toks.append((st[0], st[1], e))
        for q, ring in self.rings.items():
            for slot in ring:
                if slot[1] > 0:
                    toks.append((slot[0], slot[1], None))
        for e in ("pe", "act", "dve", "pool", "sp"):
            for t in toks:
                self._wait(e, t, strict=(t[2] is None))


class Phase:
    def __init__(self, K):
        self.K = K
        self.es = ExitStack()

    def tile(self, name, shape, dtype=F32):
        self.K.uid += 1
        nm = "%s_%d" % (name, self.K.uid)
        return Tile(self.es.enter_context(self.K.nc.sbuf_tensor(nm, list(shape), dtype)), nm)


class Ctx:
    pass


def host_consts():
    c = {}
    c["ident"] = np.eye(128, dtype=np.float32)
    kk = np.arange(128)[:, None]
    qq = np.arange(128)[None, :]
    c["tri_le"] = (kk <= qq).astype(np.float32)
    c["tri_gt"] = (kk > qq).astype(np.float32)
    pos = np.arange(SEQ, dtype=np.float32)[:, None]
    for d in (32, 64):
        inv = (10000.0 ** (-np.arange(0, d, 2, dtype=np.float32) / d)).astype(np.float32)
        ang = (pos * inv[None, :]).astype(np.float32)
        c["rope%d" % d] = np.concatenate([np.cos(ang), np.sin(ang)], axis=1).astype(np.float32)
    c["iota16"] = np.tile(np.arange(16, dtype=np.float32)[None, :], (128, 1))
    cc = np.arange(128)
    tt = np.arange(SEQ)
    c["cmask"] = ((16 * cc[:, None] + 31 <= tt[None, :]) & (cc[:, None] < 127)).astype(np.float32)
    nn = np.arange(32)
    ovl = ((16 * cc[:, None] < 64 * nn[None, :] + 64) & (16 * cc[:, None] + 31 >= 64 * nn[None, :]) & (cc[:, None] < 127))
    c["overlap"] = ovl.astype(np.float32)
    cur = tt // 64
    forced = (nn[None, :] == 0) | ((nn[None, :] <= cur[:, None]) & (nn[None, :] > cur[:, None] - 2))
    valid = nn[None, :] <= cur[:, None]
    c["selA"] = ((~forced) & valid).astype(np.float32)
    c["selB"] = np.where(valid, np.where(forced, 1e9, 0.0), -1e30).astype(np.float32)
    E = np.zeros((32, 16, 128), np.float32)
    for kt in range(16):
        for k in range(128):
            E[(kt * 128 + k) // 64, kt, k] = 1.0
    c["Eall"] = E
    return c


def rope_tok(K, ph, x1, x2, o1, o2, cos_b, sin_b, shape, rbufs, wbufs, tmpname):
    nc, S = K.nc, K.S
    if not hasattr(ph, "rtmp"):
        ph.rtmp = {}
    key = tuple(shape)
    if key not in ph.rtmp:
        ph.rtmp[key] = (ph.tile("ropea", shape), ph.tile("ropeb", shape))
    ta, tb = ph.rtmp[key]
    sl = tuple(slice(None) for _ in shape)
    S.op("dve", lambda: nc.vector.tensor_tensor(out=ta[sl], in0=x1, in1=cos_b, op=ALU.mult), r=rbufs, w=[ta.b])
    S.op("dve", lambda: nc.vector.tensor_tensor(out=tb[sl], in0=x2, in1=sin_b, op=ALU.mult), r=rbufs, w=[tb.b])
    S.op("dve", lambda: nc.vector.tensor_tensor(out=o1, in0=ta[sl], in1=tb[sl], op=ALU.subtract), r=[ta.b, tb.b], w=wbufs)
    S.op("dve", lambda: nc.vector.tensor_tensor(out=ta[sl], in0=x2, in1=cos_b, op=ALU.mult), r=rbufs, w=[ta.b])
    S.op("dve", lambda: nc.vector.tensor_tensor(out=tb[sl], in0=x1, in1=sin_b, op=ALU.mult), r=rbufs, w=[tb.b])
    S.op("dve", lambda: nc.vector.tensor_tensor(out=o2, in0=ta[sl], in1=tb[sl], op=ALU.add), r=[ta.b, tb.b], w=wbufs)


def evac(K, i, out, in_, r, w):
    nc, S = K.nc, K.S
    if i % 2 == 0:
        S.op("act", lambda: nc.scalar.copy(out=out, in_=in_), r=r, w=w)
    else:
        S.op("dve", lambda: nc.vector.tensor_copy(out=out, in_=in_), r=r, w=w)


def transpose_to(K, src_tile, src_aps, dst_tile, dst_ap_fn, rows, group=4):
    nc, S = K.nc, K.S
    n = len(src_aps)
    for g0 in range(0, n, group):
        cnt = min(group, n - g0)
        bank = K.bank()
        for j in range(cnt):
            ap = src_aps[g0 + j]
            S.op("pe", lambda ap=ap, j=j: nc.tensor.transpose(out=bank[0:rows, j * 128:(j + 1) * 128], in_=ap,
                                                               identity=K.ident[:, :]),
                 r=[src_tile.b, K.ident.b], w=[bank.b])
        evac(K, K.evi, dst_ap_fn(g0, cnt), bank[0:rows, 0:cnt * 128], r=[bank.b], w=[dst_tile.b])
        K.evi += 1


def layer_norm(K, ph, y, g_bc, b_bc, out, tag):
    nc, S = K.nc, K.S
    if not hasattr(ph, "lntmp"):
        ph.lntmp = (ph.tile("lnst", [128, 2, 6]), ph.tile("lnmv", [128, 2]), ph.tile("lnsd", [128, 1]), ph.tile("lnrs", [128, 1]))
    st, mv, sd, rs = ph.lntmp
    for j in range(2):
        S.op("dve", lambda j=j: nc.vector.bn_stats(out=st[:, j, :], in_=y[:, j * 512:(j + 1) * 512]), r=[y.b], w=[st.b])
    S.op("dve", lambda: nc.vector.bn_aggr(out=mv[:, :], in_=st[:, :, :].rearrange("p a b -> p (a b)")), r=[st.b], w=[mv.b])
    S.op("act", lambda: nc.scalar.activation(out=sd[:, :], in_=mv[:, 1:2], func=AF.Sqrt, bias=K.eps_ln[:, :], scale=1.0),
         r=[mv.b, K.eps_ln.b], w=[sd.b])
    S.op("dve", lambda: nc.vector.reciprocal(out=rs[:, :], in_=sd[:, :]), r=[sd.b], w=[rs.b])
    S.op("dve", lambda: nc.vector.tensor_scalar(out=out[:, :], in0=y[:, :], scalar1=mv[:, 0:1], scalar2=rs[:, 0:1],
                                                op0=ALU.subtract, op1=ALU.mult), r=[y.b, mv.b, rs.b], w=[out.b])
    S.op("dve", lambda: nc.vector.tensor_tensor(out=out[:, :], in0=out[:, :], in1=g_bc[:, :], op=ALU.mult),
         r=[out.b, g_bc.b], w=[out.b])
    S.op("dve", lambda: nc.vector.tensor_tensor(out=out[:, :], in0=out[:, :], in1=b_bc[:, :], op=ALU.add),
         r=[out.b, b_bc.b], w=[out.b])


def load_bc(K, ph, name, dram_row_ap, n):
    nc, S = K.nc, K.S
    t = ph.tile(name, [128, n])
    S.dma("sp", lambda: nc.sync.dma_start(out=t[:, :], in_=dram_row_ap.to_broadcast([128, n])), w=[t.b])
    return t


def load_w_bf16(K, ph, name, dram_ap, shape, stage):
    nc, S = K.nc, K.S
    n = 1
    for v in shape[1:]:
        n *= v
    wt = ph.tile(name, shape, BF16)
    letters = "abcd"[:len(shape) - 1]
    pat = "p (%s) -> p %s" % (" ".join(letters), " ".join(letters))
    kw = {letters[i]: shape[1 + i] for i in range(1, len(letters))}
    sv = stage[:, 0:n].rearrange(pat, **kw) if len(shape) > 2 else stage[:, 0:n]
    S.dma("sp", lambda: nc.sync.dma_start(out=sv, in_=dram_ap), w=[stage.b])
    flat_w = wt[tuple(slice(None) for _ in shape)]
    S.op("dve", lambda: nc.vector.tensor_copy(out=flat_w, in_=sv), r=[stage.b], w=[wt.b])
    return wt

def phase_a1(K, s):
    nc, S, d = K.nc, K.S, K.d
    ph = Phase(K)
    with ph.es:
        stage = ph.tile("wstage", [128, 6400])
        w_in = load_w_bf16(K, ph, "w_in", d["a_w_in"].rearrange("(c p) n -> p c n", p=128), [128, 8, 800], stage)
        wq = load_w_bf16(K, ph, "wq", d["a_w_q_up"].rearrange("(c p) n -> p c n", p=128), [128, 4, 1536], stage)
        wkv = load_w_bf16(K, ph, "wkv", d["a_w_kv_up"].rearrange("(c p) n -> p c n", p=128), [128, 2, 2048], stage)
        qn_bc = load_bc(K, ph, "qn_bc", d["a_q_norm"], 512)
        kvn_bc = load_bc(K, ph, "kvn_bc", d["a_kv_norm"], 256)
        xs = [ph.tile("x%d" % i, [128, 1024]) for i in range(2)]
        css = [ph.tile("cs%d" % i, [128, 32]) for i in range(2)]
        xT = ph.tile("xT", [128, 8, 128], BF16)
        junk = ph.tile("junk", [128, 512])
        ss = ph.tile("ss", [128, 2])
        rs = ph.tile("rs", [128, 2])
        rr = ph.tile("rr", [128, 2])
        cq = ph.tile("cq", [128, 512])
        ckv = ph.tile("ckv", [128, 256])
        kraw = ph.tile("kraw", [128, 32])
        kpe = ph.tile("kpe", [128, 32])
        cqT = ph.tile("cqT", [128, 4, 128], BF16)
        ckvT = ph.tile("ckvT", [128, 2, 128], BF16)
        q_sb = ph.tile("q_sb", [128, 16, 96])
        qpe = ph.tile("qpe", [128, 16, 32])
        kv_sb = ph.tile("kv_sb", [128, 16, 128])
        vps = [ph.tile("vp%d" % i, [128, 16, 65], BF16) for i in range(2)]
        qnTs = [ph.tile("qnT%d" % i, [64, 16, 128], BF16) for i in range(2)]
        qpTs = [ph.tile("qpT%d" % i, [32, 16, 128], BF16) for i in range(2)]
        knTs = [ph.tile("knT%d" % i, [64, 16, 128], BF16) for i in range(2)]
        kpTs = [ph.tile("kpT%d" % i, [32, 128], BF16) for i in range(2)]
        for v in vps:
            S.op("dve", lambda v=v: nc.vector.memset(v[:, :, 64:65], 1.0), w=[v.b])
        for t in range(NT):
            p = t % 2
            t0 = t * 128
            x, cs, vp, qnT, qpT, knT, kpT = xs[p], css[p], vps[p], qnTs[p], qpTs[p], knTs[p], kpTs[p]
            S.dma("sp", lambda: nc.sync.dma_start(out=x[:, :], in_=d["x"][s, t0:t0 + 128, :]), w=[x.b])
            S.dma("sp", lambda: nc.sync.dma_start(out=cs[:, :], in_=d["rope32"][t0:t0 + 128, :]), w=[cs.b])
            transpose_to(K, x, [x[:, c * 128:(c + 1) * 128] for c in range(8)], xT,
                         lambda g0, n: xT[:, g0:g0 + n, :].rearrange("p a b -> p (a b)"), 128)
            bA, bB = K.bank(), K.bank()
            for c in range(8):
                S.op("pe", lambda c=c: nc.tensor.matmul(bA[:, 0:512], lhsT=xT[:, c, :], rhs=w_in[:, c, 0:512],
                                                        start=(c == 0), stop=(c == 7)), r=[xT.b, w_in.b], w=[bA.b])
            for c in range(8):
                S.op("pe", lambda c=c: nc.tensor.matmul(bB[:, 0:288], lhsT=xT[:, c, :], rhs=w_in[:, c, 512:800],
                                                        start=(c == 0), stop=(c == 7)), r=[xT.b, w_in.b], w=[bB.b])
            S.op("act", lambda: nc.scalar.activation(out=junk[:, 0:512], in_=bA[:, 0:512], func=AF.Square,
                                                     accum_out=ss[:, 0:1]), r=[bA.b], w=[junk.b, ss.b])
            S.op("act", lambda: nc.scalar.activation(out=junk[:, 0:256], in_=bB[:, 0:256], func=AF.Square,
                                                     accum_out=ss[:, 1:2]), r=[bB.b], w=[junk.b, ss.b])
            S.op("act", lambda: nc.scalar.activation(out=rs[:, 0:1], in_=ss[:, 0:1], func=AF.Sqrt, bias=K.eps_rms[:, :],
                                                     scale=1.0 / 512), r=[ss.b, K.eps_rms.b], w=[rs.b])
            S.op("act", lambda: nc.scalar.activation(out=rs[:, 1:2], in_=ss[:, 1:2], func=AF.Sqrt, bias=K.eps_rms[:, :],
                                                     scale=1.0 / 256), r=[ss.b, K.eps_rms.b], w=[rs.b])
            S.op("dve", lambda: nc.vector.reciprocal(out=rr[:, :], in_=rs[:, :]), r=[rs.b], w=[rr.b])
            S.op("dve", lambda: nc.vector.scalar_tensor_tensor(out=cq[:, :], in0=bA[:, 0:512], scalar=rr[:, 0:1],
                                                               in1=qn_bc[:, :], op0=ALU.mult, op1=ALU.mult),
                 r=[bA.b, rr.b, qn_bc.b], w=[cq.b])
            S.op("dve", lambda: nc.vector.scalar_tensor_tensor(out=ckv[:, :], in0=bB[:, 0:256], scalar=rr[:, 1:2],
                                                               in1=kvn_bc[:, :], op0=ALU.mult, op1=ALU.mult),
                 r=[bB.b, rr.b, kvn_bc.b], w=[ckv.b])
            S.op("act", lambda: nc.scalar.copy(out=kraw[:, :], in_=bB[:, 256:288]), r=[bB.b], w=[kraw.b])
            rope_tok(K, ph, kraw[:, 0:16], kraw[:, 16:32], kpe[:, 0:16], kpe[:, 16:32], cs[:, 0:16], cs[:, 16:32],
                     [128, 16], [kraw.b, cs.b], [kpe.b], "rk%d" % t)
            transpose_to(K, cq, [cq[:, c * 128:(c + 1) * 128] for c in range(4)], cqT,
                         lambda g0, n: cqT[:, g0:g0 + n, :].rearrange("p a b -> p (a b)"), 128)
            transpose_to(K, ckv, [ckv[:, c * 128:(c + 1) * 128] for c in range(2)], ckvT,
                         lambda g0, n: ckvT[:, g0:g0 + n, :].rearrange("p a b -> p (a b)"), 128)
            q_flat = q_sb[:, :, :].rearrange("p a b -> p (a b)")
            for n in range(3):
                bk = K.bank()
                for c in range(4):
                    S.op("pe", lambda c=c, n=n, bk=bk: nc.tensor.matmul(bk[:, 0:512], lhsT=cqT[:, c, :],
                                                                         rhs=wq[:, c, n * 512:(n + 1) * 512],
                                                                         start=(c == 0), stop=(c == 3)),
                         r=[cqT.b, wq.b], w=[bk.b])
                evac(K, n, q_flat[:, n * 512:(n + 1) * 512], bk[:, 0:512], r=[bk.b], w=[q_sb.b])
            kv_flat = kv_sb[:, :, :].rearrange("p a b -> p (a b)")
            for n in range(4):
                bk = K.bank()
                for c in range(2):
                    S.op("pe", lambda c=c, n=n, bk=bk: nc.tensor.matmul(bk[:, 0:512], lhsT=ckvT[:, c, :],
                                                                         rhs=wkv[:, c, n * 512:(n + 1) * 512],
                                                                         start=(c == 0), stop=(c == 1)),
                         r=[ckvT.b, wkv.b], w=[bk.b])
                evac(K, n + 1, kv_flat[:, n * 512:(n + 1) * 512], bk[:, 0:512], r=[bk.b], w=[kv_sb.b])
            cos_b = cs[:, 0:16].unsqueeze(1).to_broadcast([128, 16, 16])
            sin_b = cs[:, 16:32].unsqueeze(1).to_broadcast([128, 16, 16])
            rope_tok(K, ph, q_sb[:, :, 64:80], q_sb[:, :, 80:96], qpe[:, :, 0:16], qpe[:, :, 16:32], cos_b, sin_b,
                     [128, 16, 16], [q_sb.b, cs.b], [qpe.b], "rq%d" % t)
            S.op("act", lambda: nc.scalar.copy(out=vp[:, :, 0:64], in_=kv_sb[:, :, 64:128]), r=[kv_sb.b], w=[vp.b])
            transpose_to(K, q_sb, [q_sb[:, h, 0:64] for h in range(16)], qnT,
                         lambda g0, n: qnT[:, g0:g0 + n, :].rearrange("p a b -> p (a b)"), 64)
            transpose_to(K, qpe, [qpe[:, h, :] for h in range(16)], qpT,
                         lambda g0, n: qpT[:, g0:g0 + n, :].rearrange("p a b -> p (a b)"), 32)
            transpose_to(K, kv_sb, [kv_sb[:, h, 0:64] for h in range(16)], knT,
                         lambda g0, n: knT[:, g0:g0 + n, :].rearrange("p a b -> p (a b)"), 64)
            transpose_to(K, kpe, [kpe[:, :]], kpT, lambda g0, n: kpT[:, :], 32)
            wb = [K.db("qk", s, t)]
            S.dma("sp", lambda: nc.sync.dma_start(out=d["qnT"][:, :, t0:t0 + 128].rearrange("h e t -> e h t"),
                                                  in_=qnT[:, :, :]), r=[qnT.b], w=wb)
            S.dma("sp", lambda: nc.sync.dma_start(out=d["qpT"][:, :, t0:t0 + 128].rearrange("h e t -> e h t"),
                                                  in_=qpT[:, :, :]), r=[qpT.b], w=wb)
            S.dma("sp", lambda: nc.sync.dma_start(out=d["knT"][:, :, t0:t0 + 128].rearrange("h e t -> e h t"),
                                                  in_=knT[:, :, :]), r=[knT.b], w=wb)
            S.dma("sp", lambda: nc.sync.dma_start(out=d["kpT"][:, t0:t0 + 128], in_=kpT[:, :]), r=[kpT.b], w=wb)
            S.dma("sp", lambda: nc.sync.dma_start(out=d["vp"][:, t0:t0 + 128, :].rearrange("h t e -> t h e"),
                                                  in_=vp[:, :, :]), r=[vp.b], w=wb)
        S.barrier()


def attn_core(K, ph, pairs_q, pairs_k, vp, nkeys_tile, scale, o_sb_fn, store_fn, tag):
    pass


def phase_a2(K, s):
    nc, S, d = K.nc, K.S, K.d
    scale = 96.0 ** -0.5
    ph = Phase(K)
    with ph.es:
        kp = ph.tile("kp", [32, SEQ], BF16)
        allqk = [K.db("qk", s, t) for t in range(NT)]
        S.dma("sp", lambda: nc.sync.dma_start(out=kp[:, :], in_=d["kpT"][:, :]), r=allqk, w=[kp.b])
        qns = [ph.tile("qn%d" % i, [64, SEQ], BF16) for i in range(2)]
        qps = [ph.tile("qp%d" % i, [32, SEQ], BF16) for i in range(2)]
        kns = [ph.tile("kn%d" % i, [64, SEQ], BF16) for i in range(2)]
        vpt = [ph.tile("vpa%d" % i, [128, 16, 65], BF16) for i in range(2)]
        pts = [ph.tile("pt%d" % i, [128, 512], BF16) for i in range(3)]
        osb = [ph.tile("osb%d" % i, [128, 4, 64]) for i in range(2)]
        rden = ph.tile("rden", [128, 4])
        ctr = 0
        oc = 0
        for h in range(16):
            p = h % 2
            qn, qp, kn, vp = qns[p], qps[p], kns[p], vpt[p]
            S.dma("sp", lambda: nc.sync.dma_start(out=qn[:, :], in_=d["qnT"][h, :, :]), r=allqk, w=[qn.b])
            S.dma("sp", lambda: nc.sync.dma_start(out=qp[:, :], in_=d["qpT"][h, :, :]), r=allqk, w=[qp.b])
            S.dma("sp", lambda: nc.sync.dma_start(out=kn[:, :], in_=d["knT"][h, :, :]), r=allqk, w=[kn.b])
            S.dma("sp", lambda: nc.sync.dma_start(out=vp[:, :, :], in_=d["vp"][h, :, :].rearrange("(k p) e -> p k e", p=128)),
                  r=allqk, w=[vp.b])
            for qc in range(4):
                ob = [K.ps[4 + j] for j in range(4)]
                nk = 4 * qc + 4
                for kt in range(nk):
                    st = K.ps[ctr % 4]
                    pt = pts[ctr % 3]
                    ctr += 1
                    S.op("pe", lambda: nc.tensor.matmul(st[:, 0:512], lhsT=kn[:, kt * 128:(kt + 1) * 128],
                                                        rhs=qn[:, qc * 512:(qc + 1) * 512], start=True, stop=False),
                         r=[kn.b, qn.b], w=[st.b])
                    S.op("pe", lambda: nc.tensor.matmul(st[:, 0:512], lhsT=kp[:, kt * 128:(kt + 1) * 128],
                                                        rhs=qp[:, qc * 512:(qc + 1) * 512], start=False, stop=True),
                         r=[kp.b, qp.b], w=[st.b])
                    j0 = max(0, kt - 4 * qc)
                    S.op("act", lambda: nc.scalar.activation(out=pt[:, j0 * 128:512], in_=st[:, j0 * 128:512], func=AF.Exp,
                                                             scale=scale), r=[st.b], w=[pt.b])
                    if kt >= 4 * qc:
                        S.op("dve", lambda: nc.vector.tensor_tensor(out=pt[:, j0 * 128:(j0 + 1) * 128],
                                                                    in0=pt[:, j0 * 128:(j0 + 1) * 128],
                                                                    in1=K.tri_le[:, :], op=ALU.mult),
                             r=[pt.b, K.tri_le.b], w=[pt.b])
                    for j in range(j0, 4):
                        qt = 4 * qc + j
                        S.op("pe", lambda j=j, qt=qt: nc.tensor.matmul(ob[j][:, 0:65], lhsT=pt[:, j * 128:(j + 1) * 128],
                                                                       rhs=vp[:, kt, :], start=(kt == 0), stop=(kt == qt)),
                             r=[pt.b, vp.b], w=[ob[j].b])
                o = osb[oc % 2]
                oc += 1
                for j in range(4):
                    S.op("dve", lambda j=j: nc.vector.reciprocal(out=rden[:, j:j + 1], in_=ob[j][:, 64:65]),
                         r=[ob[j].b], w=[rden.b])
                    S.op("dve", lambda j=j: nc.vector.tensor_scalar(out=o[:, j, :], in0=ob[j][:, 0:64],
                                                                    scalar1=rden[:, j:j + 1], scalar2=None, op0=ALU.mult),
                         r=[ob[j].b, rden.b], w=[o.b])
                S.dma("sp", lambda: nc.sync.dma_start(
                    out=d["attn"][qc * 512:(qc + 1) * 512, h * 64:(h + 1) * 64].rearrange("(j p) e -> p j e", p=128),
                    in_=o[:, :, :]), r=[o.b], w=[K.db("attn", s, qc, h)])
        S.barrier()


def phase_oproj_ln(K, s, w_o_ap, res_fn, res_deps_fn, g_ap, b_ap, dst_fn, dst_buf_fn, tag):
    nc, S, d = K.nc, K.S, K.d
    ph = Phase(K)
    with ph.es:
        stage = ph.tile("wstage", [128, 8192])
        wo = load_w_bf16(K, ph, "wo", w_o_ap.rearrange("(c p) n -> p c n", p=128), [128, 8, 1024], stage)
        g_bc = load_bc(K, ph, "g_bc", g_ap, 1024)
        b_bc = load_bc(K, ph, "b_bc", b_ap, 1024)
        ats = [ph.tile("at%d" % i, [128, 1024]) for i in range(2)]
        xs = [ph.tile("xr%d" % i, [128, 1024]) for i in range(2)]
        aT = ph.tile("aT", [128, 8, 128], BF16)
        y = ph.tile("y", [128, 1024])
        outs = [ph.tile("ho%d" % i, [128, 1024]) for i in range(2)]
        for t in range(NT):
            p = t % 2
            t0 = t * 128
            at, x, o = ats[p], xs[p], outs[p]
            S.dma("sp", lambda: nc.sync.dma_start(out=at[:, :], in_=d["attn"][t0:t0 + 128, :]),
                  r=[K.db("attn", s, t // 4, h) for h in range(16)], w=[at.b])
            S.dma("sp", lambda: nc.sync.dma_start(out=x[:, :], in_=res_fn(t0)), r=res_deps_fn(t), w=[x.b])
            transpose_to(K, at, [at[:, c * 128:(c + 1) * 128] for c in range(8)], aT,
                         lambda g0, n: aT[:, g0:g0 + n, :].rearrange("p a b -> p (a b)"), 128)
            for n in range(2):
                bk = K.bank()
                for c in range(8):
                    S.op("pe", lambda c=c, n=n, bk=bk: nc.tensor.matmul(bk[:, 0:512], lhsT=aT[:, c, :],
                                                                         rhs=wo[:, c, n * 512:(n + 1) * 512],
                                                                         start=(c == 0), stop=(c == 7)),
                         r=[aT.b, wo.b], w=[bk.b])
                S.op("dve", lambda n=n, bk=bk: nc.vector.scalar_tensor_tensor(
                    out=y[:, n * 512:(n + 1) * 512], in0=x[:, n * 512:(n + 1) * 512], scalar=ALPHA, in1=bk[:, 0:512],
                    op0=ALU.mult, op1=ALU.add), r=[x.b, bk.b], w=[y.b])
            layer_norm(K, ph, y, g_bc, b_bc, o, "%s%d" % (tag, t))
            S.dma("sp", lambda: nc.sync.dma_start(out=dst_fn(t0), in_=o[:, :]), r=[o.b], w=[dst_buf_fn(t)])
        S.barrier()


def prologue_tables(K):
    nc, S, d = K.nc, K.S, K.d
    ph = Phase(K)
    with ph.es:
        inb = [ph.tile("tin%d" % i, [128, 4, 2048]) for i in range(3)]
        outb = [ph.tile("tout%d" % i, [128, 4, 2048], BF16) for i in range(3)]
        k = 0
        for layer in range(2):
            src = d["p_uv%d" % layer].rearrange("(p j) n -> p j n", p=128)
            dst = d["p_uvb%d" % layer].rearrange("(p j) n -> p j n", p=128)
            for c in range(32):
                a, b = inb[k % 3], outb[k % 3]
                S.dma("sp", lambda: nc.sync.dma_start(out=a[:, :, :], in_=src[:, 4 * c:4 * c + 4, :]), w=[a.b])
                if k % 2 == 0:
                    S.op("act", lambda: nc.scalar.copy(out=b[:, :, :], in_=a[:, :, :]), r=[a.b], w=[b.b])
                else:
                    S.op("dve", lambda: nc.vector.tensor_copy(out=b[:, :, :], in_=a[:, :, :]), r=[a.b], w=[b.b])
                S.dma("sp", lambda: nc.sync.dma_start(out=dst[:, 4 * c:4 * c + 4, :], in_=b[:, :, :]), r=[b.b], w=[K.db("uvb", layer, c)])
                k += 1
        S.barrier()


def phase_peer(K, s, layer, src_fn, src_buf_fn, dst_fn, dst_buf_fn, ntiles=NT):
    nc, S, d = K.nc, K.S, K.d
    ph = Phase(K)
    NG = 11
    K.bank_list = [0, 1, 2, 3, 4, 5]
    accb = [K.ps[6], K.ps[7]]
    with ph.es:
        wq = ph.tile("pwq", [128, 8, 2048])
        S.dma("sp", lambda: nc.sync.dma_start(out=wq[:, :, :], in_=d["p_w_q"][layer].rearrange("(c p) n -> p c n", p=128)), w=[wq.b])
        skT = ph.tile("skT", [128, 16, 128])
        S.dma("sp", lambda: nc.sync.dma_start(out=skT[:, :, :], in_=d["p_skT"][layer]), w=[skT.b])
        g_bc = load_bc(K, ph, "pg_bc", d["ln_g"][2 * layer + 1:2 * layer + 2, :], 1024)
        b_bc = load_bc(K, ph, "pb_bc", d["ln_b"][2 * layer + 1:2 * layer + 2, :], 1024)
        iota = ph.tile("iota", [128, 16])
        S.dma("sp", lambda: nc.sync.dma_start(out=iota[:, :], in_=d["iota16"]), w=[iota.b])
        identb = ph.tile("identb", [128, 128], BF16)
        S.op("dve", lambda: nc.vector.tensor_copy(out=identb[:, :], in_=K.ident[:, :]), r=[K.ident.b], w=[identb.b])
        G = [ph.tile("G%d" % i, [128, 2048], BF16) for i in range(NG)]
        hs = [ph.tile("ph%d" % i, [128, 1024]) for i in range(2)]
        eidxs = [ph.tile("peidx%d" % i, [128, 128], I32) for i in range(2)]
        gws = [ph.tile("pgw%d" % i, [128, 128]) for i in range(2)]
        hT = ph.tile("phT", [128, 8, 128])
        q_sb = ph.tile("pq", [128, 2048])
        qT = ph.tile("pqT", [128, 16, 128])
        sc = Tile(q_sb.t[:, :].rearrange("p (a b) -> p a b", b=128), "sc_alias")
        sc.b = q_sb.b
        sc2 = ph.tile("psc2", [128, 128])
        m1 = ph.tile("pm1", [128, 16, 16])
        i1 = ph.tile("pi1", [128, 16, 16], U32)
        i1f = ph.tile("pi1f", [128, 16, 16])
        cand2 = ph.tile("pcand2", [128, 256])
        best = ph.tile("pbest", [128, 8, 16])
        pos = ph.tile("ppos", [128, 8, 16], U32)
        hi = ph.tile("phi", [128, 8, 16], U32)
        lo = ph.tile("plo", [128, 8, 16], U32)
        hif = ph.tile("phif", [128, 8, 16])
        lof = ph.tile("plof", [128, 8, 16])
        eq = ph.tile("peq", [128, 8, 16, 16])
        cand = Tile(eq.t[:, :, :, :].rearrange("p h i j -> p h (i j)"), "cand_alias")
        cand.b = eq.b
        e0 = ph.tile("pe0", [128, 8, 16])
        e1 = ph.tile("pe1", [128, 8, 16])
        ef = ph.tile("pef", [128, 128])
        bm = ph.tile("pbm", [128, 8, 16])
        se = ph.tile("pse", [128, 8])
        a = ph.tile("pa", [128, 128])
        ga = ph.tile("pga", [128, 128])
        w = ph.tile("pw", [128, 128])
        diags = [ph.tile("pdiag%d" % i, [128, 4, 128], BF16) for i in range(3)]
        prods = [ph.tile("pprod%d" % i, [128, 1024], BF16) for i in range(6)]
        hbs = [ph.tile("phb%d" % i, [128, 1024], BF16) for i in range(2)]
        y = ph.tile("py", [128, 1024])
        o = ph.tile("po", [128, 1024])
        m1v = m1[:, :, :].rearrange("p (h two) k -> p h two k", two=2)
        i1fv = i1f[:, :, :].rearrange("p (h two) k -> p h two k", two=2)
        iota_b = iota[:, :].unsqueeze(1).unsqueeze(1).to_broadcast([128, 8, 16, 16])
        ident_b = identb[:, :].unsqueeze(1).to_broadcast([128, 4, 128])

        def front(t):
            t0 = t * 128
            h, eidx, gw = hs[t % 2], eidxs[t % 2], gws[t % 2]
            S.dma("sp", lambda: nc.sync.dma_start(out=h[:, :], in_=src_fn(t0)), r=src_buf_fn(t), w=[h.b])
            transpose_to(K, h, [h[:, c * 128:(c + 1) * 128] for c in range(8)], hT,
                         lambda g0, n: hT[:, g0:g0 + n, :].rearrange("p a b -> p (a b)"), 128)
            yield
            for n in range(4):
                bk = K.bank()
                for c in range(8):
                    S.op("pe", lambda: nc.tensor.matmul(bk[:, 0:512], lhsT=hT[:, c, :], rhs=wq[:, c, n * 512:(n + 1) * 512],
                                                        start=(c == 0), stop=(c == 7)), r=[hT.b, wq.b], w=[bk.b])
                evac(K, n, q_sb[:, n * 512:(n + 1) * 512], bk[:, 0:512], r=[bk.b], w=[q_sb.b])
                yield
            transpose_to(K, q_sb, [q_sb[:, c * 128:(c + 1) * 128] for c in range(16)], qT,
                         lambda g0, n: qT[:, g0:g0 + n, :].rearrange("p a b -> p (a b)"), 128)
            yield
            for n in range(4):
                bk = K.bank()
                for j in range(4):
                    hp = n * 4 + j
                    S.op("pe", lambda: nc.tensor.matmul(bk[:, j * 128:(j + 1) * 128], lhsT=qT[:, hp, :], rhs=skT[:, hp, :],
                                                        start=True, stop=True), r=[qT.b, skT.b], w=[bk.b])
                evac(K, n, sc[:, n * 4:(n + 1) * 4, :].rearrange("p a b -> p (a b)"), bk[:, 0:512], r=[bk.b], w=[sc.b])
            yield
            for hp in range(16):
                S.op("dve", lambda: nc.vector.max(out=m1[:, hp, 0:8], in_=sc[:, hp, :]), r=[sc.b], w=[m1.b])
                S.op("dve", lambda: nc.vector.max_index(out=i1[:, hp, 0:8], in_max=m1[:, hp, 0:8], in_values=sc[:, hp, :]),
                     r=[sc.b, m1.b], w=[i1.b])
                S.op("dve", lambda: nc.vector.match_replace(out=sc2[:, :], in_to_replace=m1[:, hp, 0:8],
                                                            in_values=sc[:, hp, :], imm_value=NEG), r=[sc.b, m1.b], w=[sc2.b])
                S.op("dve", lambda: nc.vector.max(out=m1[:, hp, 8:16], in_=sc2[:, :]), r=[sc2.b], w=[m1.b])
                S.op("dve", lambda: nc.vector.max_index(out=i1[:, hp, 8:16], in_max=m1[:, hp, 8:16], in_values=sc2[:, :]),
                     r=[sc2.b, m1.b], w=[i1.b])
                yield
            S.op("dve", lambda: nc.vector.tensor_tensor(
                out=cand[:, :, :].rearrange("p h (i j) -> p h i j", j=16),
                in0=m1v[:, :, 0, :].unsqueeze(3).to_broadcast([128, 8, 16, 16]),
                in1=m1v[:, :, 1, :].unsqueeze(2).to_broadcast([128, 8, 16, 16]), op=ALU.add), r=[m1.b], w=[cand.b])
            for hh in range(8):
                S.op("dve", lambda: nc.vector.max(out=best[:, hh, 0:8], in_=cand[:, hh, :]), r=[cand.b], w=[best.b])
                S.op("dve", lambda: nc.vector.max_index(out=pos[:, hh, 0:8], in_max=best[:, hh, 0:8], in_values=cand[:, hh, :]),
                     r=[cand.b, best.b], w=[pos.b])
                S.op("dve", lambda: nc.vector.match_replace(out=cand2[:, :], in_to_replace=best[:, hh, 0:8],
                                                            in_values=cand[:, hh, :], imm_value=NEG), r=[cand.b, best.b], w=[cand2.b])
                S.op("dve", lambda: nc.vector.max(out=best[:, hh, 8:16], in_=cand2[:, :]), r=[cand2.b], w=[best.b])
                S.op("dve", lambda: nc.vector.max_index(out=pos[:, hh, 8:16], in_max=best[:, hh, 8:16], in_values=cand2[:, :]),
                     r=[cand2.b, best.b], w=[pos.b])
                yield
            S.op("dve", lambda: nc.vector.tensor_tensor(out=bm[:, :, :], in0=best[:, :, :],
                                                        in1=best[:, :, 0:1].to_broadcast([128, 8, 16]), op=ALU.subtract),
                 r=[best.b], w=[bm.b])
            S.op("act", lambda: nc.scalar.activation(out=bm[:, :, :], in_=bm[:, :, :], func=AF.Exp), r=[bm.b], w=[bm.b])
            S.op("dve", lambda: nc.vector.tensor_reduce(out=se[:, :], in_=bm[:, :, :], axis=AX.X, op=ALU.add), r=[bm.b], w=[se.b])
            S.op("dve", lambda: nc.vector.reciprocal(out=se[:, :], in_=se[:, :]), r=[se.b], w=[se.b])
            S.op("dve", lambda: nc.vector.tensor_tensor(out=gw[:, :].rearrange("p (h k) -> p h k", k=16), in0=bm[:, :, :],
                                                        in1=se[:, :].unsqueeze(2).to_broadcast([128, 8, 16]), op=ALU.mult),
                 r=[bm.b, se.b], w=[gw.b])
            yield
            S.op("dve", lambda: nc.vector.tensor_single_scalar(out=hi[:, :, :], in_=pos[:, :, :], scalar=4,
                                                               op=ALU.logical_shift_right), r=[pos.b], w=[hi.b])
            S.op("dve", lambda: nc.vector.tensor_single_scalar(out=lo[:, :, :], in_=pos[:, :, :], scalar=15,
                                                               op=ALU.bitwise_and), r=[pos.b], w=[lo.b])
            S.op("dve", lambda: nc.vector.tensor_copy(out=hif[:, :, :], in_=hi[:, :, :]), r=[hi.b], w=[hif.b])
            S.op("dve", lambda: nc.vector.tensor_copy(out=lof[:, :, :], in_=lo[:, :, :]), r=[lo.b], w=[lof.b])
            S.op("dve", lambda: nc.vector.tensor_copy(out=i1f[:, :, :], in_=i1[:, :, :]), r=[i1.b], w=[i1f.b])
            yield
            for (xf, half, eo) in ((hif, 0, e0), (lof, 1, e1)):
                S.op("dve", lambda: nc.vector.tensor_tensor(out=eq[:, :, :, :],
                                                            in0=xf[:, :, :].unsqueeze(3).to_broadcast([128, 8, 16, 16]),
                                                            in1=iota_b, op=ALU.is_equal), r=[xf.b, iota.b], w=[eq.b])
                S.op("dve", lambda: nc.vector.tensor_tensor(out=eq[:, :, :, :], in0=eq[:, :, :, :],
                                                            in1=i1fv[:, :, half, :].unsqueeze(2).to_broadcast([128, 8, 16, 16]),
                                                            op=ALU.mult), r=[eq.b, i1f.b], w=[eq.b])
                S.op("dve", lambda: nc.vector.tensor_reduce(out=eo[:, :, :], in_=eq[:, :, :, :], axis=AX.X, op=ALU.add),
                     r=[eq.b], w=[eo.b])
                yield
            S.op("dve", lambda: nc.vector.scalar_tensor_tensor(out=ef[:, :], in0=e0[:, :, :].rearrange("p h k -> p (h k)"),
                                                               scalar=128.0, in1=e1[:, :, :].rearrange("p h k -> p (h k)"),
                                                               op0=ALU.mult, op1=ALU.add), r=[e0.b, e1.b], w=[ef.b])
            S.op("dve", lambda: nc.vector.tensor_copy(out=eidx[:, :], in_=ef[:, :]), r=[ef.b], w=[eidx.b])
            yield

        gi = 0
        pi = 0
        di = 0
        a_b = [Buf("a%d" % i) for i in range(128)]
        ga_b = [Buf("ga%d" % i) for i in range(32)]
        w_b = [Buf("w%d" % i) for i in range(32)]

        def main(t, gen):
            nonlocal gi, pi, di
            t0 = t * 128
            h, eidx, gw = hs[t % 2], eidxs[t % 2], gws[t % 2]
            hb = hbs[t % 2]
            S.op("act", lambda: nc.scalar.copy(out=hb[:, :], in_=h[:, :]), r=[h.b], w=[hb.b])
            for g0 in range(0, 128, 4):
                gs = []
                for sl in range(g0, g0 + 4):
                    Gt = G[gi % NG]
                    gi += 1
                    gs.append(Gt)
                    S.dma("pool", lambda: nc.gpsimd.indirect_dma_start(
                        out=Gt[:, :], out_offset=None, in_=d["p_uvb%d" % layer],
                        in_offset=bass.IndirectOffsetOnAxis(ap=eidx[:, sl:sl + 1], axis=0)), r=[eidx.b], w=[Gt.b])
                    pj = prods[pi % 6]
                    pi += 1
                    S.op("dve", lambda: nc.vector.tensor_tensor(out=pj[:, :], in0=Gt[:, 0:1024], in1=hb[:, :], op=ALU.mult),
                         r=[Gt.b, hb.b], w=[pj.b])
                    if 'noacc' not in XP:
                        S.op("act", lambda: nc.scalar.activation(out=pj[:, :], in_=pj[:, :], func=AF.Identity,
                                                                 accum_out=a[:, sl:sl + 1]), r=[pj.b], w=[pj.b, a_b[sl]])
                gq = g0 // 4
                S.op("act", lambda: nc.scalar.activation(out=ga[:, g0:g0 + 4], in_=a[:, g0:g0 + 4], func=AF.Gelu),
                     r=a_b[g0:g0 + 4], w=[ga_b[gq]])
                S.op("dve", lambda: nc.vector.tensor_tensor(out=w[:, g0:g0 + 4], in0=ga[:, g0:g0 + 4], in1=gw[:, g0:g0 + 4],
                                                            op=ALU.mult), r=[ga_b[gq], gw.b], w=[w_b[gq]])
                dg = diags[di % 3]
                di += 1
                S.op("dve", lambda: nc.vector.tensor_tensor(out=dg[:, :, :], in0=ident_b,
                                                            in1=w[:, g0:g0 + 4].unsqueeze(2).to_broadcast([128, 4, 128]),
                                                            op=ALU.mult), r=[identb.b, w_b[gq]], w=[dg.b])
                for j, sl in enumerate(range(g0, g0 + 4)):
                    Gt = gs[j]
                    for n in range(0 if ('nope' in XP and sl not in (0, 127)) else 2):
                        S.op("pe", lambda: nc.tensor.matmul(accb[n][:, 0:512], lhsT=dg[:, j, :],
                                                            rhs=Gt[:, 1024 + n * 512:1024 + (n + 1) * 512],
                                                            start=(sl == 0), stop=(sl == 127)), r=[dg.b, Gt.b], w=[accb[n].b])
                if gen is not None and 'nofront' not in XP:
                    next(gen, None)
                    next(gen, None)
            for n in range(2):
                S.op("dve", lambda: nc.vector.scalar_tensor_tensor(out=y[:, n * 512:(n + 1) * 512], in0=h[:, n * 512:(n + 1) * 512],
                                                                   scalar=ALPHA, in1=accb[n][:, 0:512], op0=ALU.mult, op1=ALU.add),
                     r=[h.b, accb[n].b], w=[y.b])
            layer_norm(K, ph, y, g_bc, b_bc, o, "p%d_%d" % (layer, t))
            S.dma("sp", lambda: nc.sync.dma_start(out=dst_fn(t0), in_=o[:, :]), r=[o.b], w=[dst_buf_fn(t)])

        for _ in front(0):
            pass
        for t in range(ntiles):
            gen = front(t + 1) if t + 1 < ntiles else None
            main(t, gen)
            if gen is not None:
                for _ in gen:
                    pass
        S.barrier()
    K.bank_list = list(range(8))


def phase_b0(K, s):
    nc, S, d = K.nc, K.S, K.d
    ph = Phase(K)
    with ph.es:
        stage = ph.tile("wstage", [128, 12288])
        wkv = load_w_bf16(K, ph, "swkv", d["s_w_kv"].rearrange("(c p) n -> p c n", p=128), [128, 8, 1536], stage)
        wqb = load_w_bf16(K, ph, "bwin", d["b_w_in"].rearrange("(c p) n -> p c n", p=128), [128, 8, 1072], stage)
        h = ph.tile("bh", [128, 1024])
        cs = ph.tile("bcs", [128, 64])
        hT = ph.tile("bhT", [128, 8, 128], BF16)
        kvs = ph.tile("kvs", [128, 3, 4, 128])
        q_sb = ph.tile("bq", [128, 16, 64])
        qr = ph.tile("bqr", [128, 16, 64])
        kr = ph.tile("bkr", [128, 2, 4, 64])
        gts = ph.tile("bgts", [128, 48])
        vps = [ph.tile("bvp%d" % i, [128, 4, 65], BF16) for i in range(2)]
        qT = ph.tile("bqT", [64, 16, 128], BF16)
        qrT = ph.tile("bqrT", [64, 16, 128], BF16)
        kT = ph.tile("bkT", [64, 8, 128])
        kTr = ph.tile("bkTr", [64, 8, 128], BF16)
        for v in vps:
            S.op("dve", lambda: nc.vector.memset(v[:, :, 64:65], 1.0), w=[v.b])
        for t in range(K.ntiles):
            t0 = t * 128
            S.dma("sp", lambda: nc.sync.dma_start(out=h[:, :], in_=d["h2"][t0:t0 + 128, :]), r=[K.db("h2", s, t)], w=[h.b])
            S.dma("sp", lambda: nc.sync.dma_start(out=cs[:, :], in_=d["rope64"][t0:t0 + 128, :]), w=[cs.b])
            transpose_to(K, h, [h[:, c * 128:(c + 1) * 128] for c in range(8)], hT,
                         lambda g0, n: hT[:, g0:g0 + n, :].rearrange("p a b -> p (a b)"), 128)
            kv_flat = kvs[:, :, :, :].rearrange("p a b c -> p (a b c)")
            for n in range(3):
                bk = K.bank()
                for c in range(8):
                    S.op("pe", lambda: nc.tensor.matmul(bk[:, 0:512], lhsT=hT[:, c, :], rhs=wkv[:, c, n * 512:(n + 1) * 512],
                                                        start=(c == 0), stop=(c == 7)), r=[hT.b, wkv.b], w=[bk.b])
                evac(K, n, kv_flat[:, n * 512:(n + 1) * 512], bk[:, 0:512], r=[bk.b], w=[kvs.b])
            q_flat = q_sb[:, :, :].rearrange("p a b -> p (a b)")
            for n in range(2):
                bk = K.bank()
                for c in range(8):
                    S.op("pe", lambda: nc.tensor.matmul(bk[:, 0:512], lhsT=hT[:, c, :], rhs=wqb[:, c, n * 512:(n + 1) * 512],
                                                        start=(c == 0), stop=(c == 7)), r=[hT.b, wqb.b], w=[bk.b])
                evac(K, n + 1, q_flat[:, n * 512:(n + 1) * 512], bk[:, 0:512], r=[bk.b], w=[q_sb.b])
            bk = K.bank()
            for c in range(8):
                S.op("pe", lambda: nc.tensor.matmul(bk[:, 0:48], lhsT=hT[:, c, :], rhs=wqb[:, c, 1024:1072],
                                                    start=(c == 0), stop=(c == 7)), r=[hT.b, wqb.b], w=[bk.b])
            S.op("act", lambda: nc.scalar.activation(out=gts[:, :], in_=bk[:, 0:48], func=AF.Sigmoid), r=[bk.b], w=[gts.b])
            S.dma("sp", lambda: nc.sync.dma_start(out=d["gates"][t0:t0 + 128, :], in_=gts[:, :]), r=[gts.b], w=[K.db("b0", s, t)])
            cos_b = cs[:, 0:32].unsqueeze(1).to_broadcast([128, 16, 32])
            sin_b = cs[:, 32:64].unsqueeze(1).to_broadcast([128, 16, 32])
            rope_tok(K, ph, q_sb[:, :, 0:32], q_sb[:, :, 32:64], qr[:, :, 0:32], qr[:, :, 32:64], cos_b, sin_b,
                     [128, 16, 32], [q_sb.b, cs.b], [qr.b], "brq%d" % t)
            cos_k = cs[:, 0:32].unsqueeze(1).to_broadcast([128, 4, 32])
            sin_k = cs[:, 32:64].unsqueeze(1).to_broadcast([128, 4, 32])
            for br in range(2):
                rope_tok(K, ph, kvs[:, 1 + br, :, 0:32], kvs[:, 1 + br, :, 32:64], kr[:, br, :, 0:32], kr[:, br, :, 32:64],
                         cos_k, sin_k, [128, 4, 32], [kvs.b, cs.b], [kr.b], "brk%d_%d" % (t, br))
            transpose_to(K, q_sb, [q_sb[:, hh, :] for hh in range(16)], qT,
                         lambda g0, n: qT[:, g0:g0 + n, :].rearrange("p a b -> p (a b)"), 64)
            transpose_to(K, qr, [qr[:, hh, :] for hh in range(16)], qrT,
                         lambda g0, n: qrT[:, g0:g0 + n, :].rearrange("p a b -> p (a b)"), 64)
            srcs = [kvs[:, 0, g, 0:64] for g in range(4)] + [kvs[:, 0, g, 64:128] for g in range(4)]
            transpose_to(K, kvs, srcs, kT, lambda g0, n: kT[:, g0:g0 + n, :].rearrange("p a b -> p (a b)"), 64)
            srcs = [kr[:, 0, g, :] for g in range(4)] + [kr[:, 1, g, :] for g in range(4)]
            transpose_to(K, kr, srcs, kTr, lambda g0, n: kTr[:, g0:g0 + n, :].rearrange("p a b -> p (a b)"), 64)
            wb = [K.db("b0", s, t)]
            S.dma("sp", lambda: nc.sync.dma_start(out=d["bqT"][:, :, t0:t0 + 128].rearrange("h e t -> e h t"), in_=qT[:, :, :]),
                  r=[qT.b], w=wb)
            S.dma("sp", lambda: nc.sync.dma_start(out=d["bqrT"][:, :, t0:t0 + 128].rearrange("h e t -> e h t"), in_=qrT[:, :, :]),
                  r=[qrT.b], w=wb)
            S.dma("sp", lambda: nc.sync.dma_start(out=d["bkcT"][:, :, t0:t0 + 128].rearrange("h e t -> e h t"), in_=kT[:, :, :]),
                  r=[kT.b], w=wb)
            S.dma("sp", lambda: nc.sync.dma_start(out=d["bkrT"][:, :, t0:t0 + 128].rearrange("h e t -> e h t"), in_=kTr[:, :, :]),
                  r=[kTr.b], w=wb)
            for br in range(2):
                vp = vps[br]
                S.op("act", lambda: nc.scalar.copy(out=vp[:, :, 0:64], in_=kvs[:, 1 + br, :, 64:128]), r=[kvs.b], w=[vp.b])
                S.dma("sp", lambda: nc.sync.dma_start(out=d["bvp"][br, :, t0:t0 + 128, :].rearrange("g t e -> t g e"),
                                                      in_=vp[:, :, :]), r=[vp.b], w=wb)
        S.barrier()


def phase_b1(K, s):
    nc, S, d = K.nc, K.S, K.d
    ph = Phase(K)
    allb0 = [K.db("b0", s, t) for t in range(NT)]
    with ph.es:
        ovl = ph.tile("ovl", [128, 32])
        S.dma("sp", lambda: nc.sync.dma_start(out=ovl[:, :], in_=d["overlap"]), w=[ovl.b])
        for kv in range(2):
            nm = "k" if kv == 0 else "v"
            w1 = ph.tile("cw1" + nm, [64, 32, 256])
            S.dma("sp", lambda: nc.sync.dma_start(out=w1[:, :, :], in_=d["s_cmp_%s_w1" % nm].rearrange("(l e) n -> e l n", e=64)), w=[w1.b])
            w2 = ph.tile("cw2" + nm, [128, 2, 64])
            S.dma("sp", lambda: nc.sync.dma_start(out=w2[:, :, :], in_=d["s_cmp_%s_w2" % nm].rearrange("(c p) n -> p c n", p=128)), w=[w2.b])
            b1 = ph.tile("cb1" + nm, [128, 2])
            S.dma("sp", lambda: nc.sync.dma_start(out=b1[:, :], in_=d["s_cmp_%s_b1" % nm]), w=[b1.b])
            posT = ph.tile("cpos" + nm, [64, 32])
            S.dma("sp", lambda: nc.sync.dma_start(out=posT[:, :], in_=d["s_cmp_pos_%sT" % nm]), w=[posT.b])
            xT = ph.tile("cxT" + nm, [64, SEQ])
            Xp = ph.tile("cXp" + nm, [64, 32, 127])
            hid = ph.tile("chid" + nm, [128, 2, 127])
            res = ph.tile("cres" + nm, [128, 128], BF16)
            for g in range(4):
                S.dma("sp", lambda: nc.sync.dma_start(out=xT[:, :], in_=d["bkcT"][kv * 4 + g, :, :]), r=allb0, w=[xT.b])
                for half in range(2):
                    S.op("dve", lambda: nc.vector.tensor_tensor(
                        out=Xp[:, half * 16:(half + 1) * 16, :],
                        in0=xT[:, half * 16:half * 16 + 2032].rearrange("p (j l) -> p l j", l=16),
                        in1=posT[:, half * 16:(half + 1) * 16].unsqueeze(2).to_broadcast([64, 16, 127]), op=ALU.add),
                        r=[xT.b, posT.b], w=[Xp.b])
                for hc in range(2):
                    bk = K.bank()
                    for l in range(32):
                        S.op("pe", lambda: nc.tensor.matmul(bk[:, 0:127], lhsT=w1[:, l, hc * 128:(hc + 1) * 128], rhs=Xp[:, l, :],
                                                            start=(l == 0), stop=(l == 31)), r=[w1.b, Xp.b], w=[bk.b])
                    S.op("act", lambda: nc.scalar.activation(out=hid[:, hc, :], in_=bk[:, 0:127], func=AF.Gelu, bias=b1[:, hc:hc + 1],
                                                             scale=1.0), r=[bk.b, b1.b], w=[hid.b])
                bk = K.bank()
                if kv == 0:
                    for hc in range(2):
                        S.op("pe", lambda: nc.tensor.matmul(bk[0:64, 0:127], lhsT=w2[:, hc, :], rhs=hid[:, hc, :],
                                                            start=(hc == 0), stop=(hc == 1)), r=[w2.b, hid.b], w=[bk.b])
                    S.op("dve", lambda: nc.vector.memset(res[0:64, :], 0.0), w=[res.b])
                    S.op("dve", lambda: nc.vector.tensor_copy(out=res[0:64, 0:127], in_=bk[0:64, 0:127]), r=[bk.b], w=[res.b])
                    S.dma("sp", lambda: nc.sync.dma_start(out=d["kcmpT"][g, :, :], in_=res[0:64, :]), r=[res.b], w=[K.db("b1", s)])
                else:
                    for hc in range(2):
                        S.op("pe", lambda: nc.tensor.matmul(bk[0:127, 0:64], lhsT=hid[:, hc, :], rhs=w2[:, hc, :],
                                                            start=(hc == 0), stop=(hc == 1)), r=[w2.b, hid.b], w=[bk.b])
                    S.op("dve", lambda: nc.vector.memset(res[:, :], 0.0), w=[res.b])
                    S.op("dve", lambda: nc.vector.tensor_copy(out=res[0:127, 0:64], in_=bk[0:127, 0:64]), r=[bk.b], w=[res.b])
                    S.op("dve", lambda: nc.vector.memset(res[0:127, 64:65], 1.0), r=[], w=[res.b])
                    S.op("dve", lambda: nc.vector.tensor_copy(out=res[0:127, 65:97], in_=ovl[0:127, :]), r=[ovl.b], w=[res.b])
                    S.dma("sp", lambda: nc.sync.dma_start(out=d["vcmp"][g, :, :], in_=res[:, 0:97]), r=[res.b], w=[K.db("b1", s)])
        S.barrier()


def phase_b2(K, s):
    nc, S, d = K.nc, K.S, K.d
    scale = 64.0 ** -0.5
    ph = Phase(K)
    allb0 = [K.db("b0", s, t) for t in range(NT)]
    b1b = [K.db("b1", s)]
    with ph.es:
        cmask = ph.tile("cmask", [128, SEQ])
        S.dma("sp", lambda: nc.sync.dma_start(out=cmask[:, :], in_=d["cmask"]), w=[cmask.b])
        Eall32 = ph.tile("Eall32", [32, 16, 128])
        S.dma("sp", lambda: nc.sync.dma_start(out=Eall32[:, :, :], in_=d["Eall"]), w=[Eall32.b])
        Eall = ph.tile("Eall", [32, 16, 128], BF16)
        S.op("dve", lambda: nc.vector.tensor_copy(out=Eall[:, :, :], in_=Eall32[:, :, :]), r=[Eall32.b], w=[Eall.b])
        At = ph.tile("At", [128, 16, 32])
        Bt = ph.tile("Bt", [128, 16, 32])
        S.dma("sp", lambda: nc.sync.dma_start(out=At[:, :, :], in_=d["selA"].rearrange("(q p) n -> p q n", p=128)), w=[At.b])
        S.dma("sp", lambda: nc.sync.dma_start(out=Bt[:, :, :], in_=d["selB"].rearrange("(q p) n -> p q n", p=128)), w=[Bt.b])
        gts = ph.tile("gts", [128, 16, 48])
        S.dma("sp", lambda: nc.sync.dma_start(out=gts[:, :, :], in_=d["gates"].rearrange("(q p) n -> p q n", p=128)), r=allb0, w=[gts.b])
        ksT = ph.tile("ksT", [64, SEQ], BF16)
        kwT = ph.tile("kwT", [64, SEQ], BF16)
        vs = ph.tile("vs", [128, 16, 65], BF16)
        vw = ph.tile("vw", [128, 16, 65], BF16)
        kcT = ph.tile("kcT", [64, 128], BF16)
        vc = ph.tile("vc", [128, 97], BF16)
        qTs = [ph.tile("nqT%d" % i, [64, SEQ], BF16) for i in range(4)]
        qrTs = [ph.tile("nqrT%d" % i, [64, SEQ], BF16) for i in range(4)]
        comb = ph.tile("comb", [128, 4, 16, 64])
        imp = ph.tile("imp", [128, 16, 32])
        sel = ph.tile("sel", [128, 16, 32])
        selT = ph.tile("selT", [32, SEQ], BF16)
        tmp32 = ph.tile("tmp32", [128, 32])
        m8 = ph.tile("m8", [128, 16])
        pts = [ph.tile("npt%d" % i, [128, 512], BF16) for i in range(3)]
        mts = [ph.tile("nmt%d" % i, [128, 512], BF16) for i in range(2)]
        rd = ph.tile("nrd", [128, 2])
        otmp = ph.tile("notmp", [128, 64])
        ctr = 0
        for g in range(4):
            S.dma("sp", lambda: nc.sync.dma_start(out=ksT[:, :], in_=d["bkrT"][g, :, :]), r=allb0, w=[ksT.b])
            S.dma("sp", lambda: nc.sync.dma_start(out=kwT[:, :], in_=d["bkrT"][4 + g, :, :]), r=allb0, w=[kwT.b])
            S.dma("sp", lambda: nc.sync.dma_start(out=vs[:, :, :], in_=d["bvp"][0, g, :, :].rearrange("(k p) e -> p k e", p=128)), r=allb0, w=[vs.b])
            S.dma("sp", lambda: nc.sync.dma_start(out=vw[:, :, :], in_=d["bvp"][1, g, :, :].rearrange("(k p) e -> p k e", p=128)), r=allb0, w=[vw.b])
            S.dma("sp", lambda: nc.sync.dma_start(out=kcT[:, :], in_=d["kcmpT"][g, :, :]), r=b1b, w=[kcT.b])
            S.dma("sp", lambda: nc.sync.dma_start(out=vc[:, :], in_=d["vcmp"][g, :, :]), r=b1b, w=[vc.b])
            for j in range(4):
                hh = g * 4 + j
                S.dma("sp", lambda: nc.sync.dma_start(out=qTs[j][:, :], in_=d["bqT"][hh, :, :]), r=allb0, w=[qTs[j].b])
                S.dma("sp", lambda: nc.sync.dma_start(out=qrTs[j][:, :], in_=d["bqrT"][hh, :, :]), r=allb0, w=[qrTs[j].b])
            for j in range(4):
                hh = g * 4 + j
                qT = qTs[j]
                for qc in range(4):
                    st = K.bank()
                    pt = pts[ctr % 3]
                    ctr += 1
                    S.op("pe", lambda: nc.tensor.matmul(st[0:127, 0:512], lhsT=kcT[:, 0:127], rhs=qT[:, qc * 512:(qc + 1) * 512],
                                                        start=True, stop=True), r=[kcT.b, qT.b], w=[st.b])
                    S.op("act", lambda: nc.scalar.activation(out=pt[0:127, :], in_=st[0:127, 0:512], func=AF.Exp, scale=scale),
                         r=[st.b], w=[pt.b])
                    S.op("dve", lambda: nc.vector.tensor_tensor(out=pt[0:127, :], in0=pt[0:127, :],
                                                                in1=cmask[0:127, qc * 512:(qc + 1) * 512], op=ALU.mult),
                         r=[pt.b, cmask.b], w=[pt.b])
                    for jj in range(4):
                        qt = qc * 4 + jj
                        ob = K.bank()
                        S.op("pe", lambda: nc.tensor.matmul(ob[:, 0:97], lhsT=pt[0:127, jj * 128:(jj + 1) * 128], rhs=vc[0:127, :],
                                                            start=True, stop=True), r=[pt.b, vc.b], w=[ob.b])
                        S.op("dve", lambda: nc.vector.tensor_scalar_max(out=rd[:, 0:1], in0=ob[:, 64:65], scalar1=1e-30),
                             r=[ob.b], w=[rd.b])
                        S.op("dve", lambda: nc.vector.reciprocal(out=rd[:, 1:2], in_=rd[:, 0:1]), r=[rd.b], w=[rd.b])
                        S.op("dve", lambda: nc.vector.tensor_scalar(out=comb[:, j, qt, :], in0=ob[:, 0:64], scalar1=rd[:, 1:2],
                                                                    scalar2=gts[:, qt, hh:hh + 1], op0=ALU.mult, op1=ALU.mult),
                             r=[ob.b, rd.b, gts.b], w=[comb.b])
                        if j == 0:
                            S.op("dve", lambda: nc.vector.tensor_scalar(out=imp[:, qt, :], in0=ob[:, 65:97], scalar1=rd[:, 1:2],
                                                                        scalar2=None, op0=ALU.mult), r=[ob.b, rd.b], w=[imp.b])
                        else:
                            S.op("dve", lambda: nc.vector.scalar_tensor_tensor(out=imp[:, qt, :], in0=ob[:, 65:97], scalar=rd[:, 1:2],
                                                                               in1=imp[:, qt, :], op0=ALU.mult, op1=ALU.add),
                                 r=[ob.b, rd.b, imp.b], w=[imp.b])
            S.op("dve", lambda: nc.vector.tensor_tensor(out=imp[:, :, :], in0=imp[:, :, :], in1=At[:, :, :], op=ALU.mult),
                 r=[imp.b, At.b], w=[imp.b])
            S.op("dve", lambda: nc.vector.tensor_tensor(out=imp[:, :, :], in0=imp[:, :, :], in1=Bt[:, :, :], op=ALU.add),
                 r=[imp.b, Bt.b], w=[imp.b])
            for qt in range(16):
                S.op("dve", lambda: nc.vector.max(out=m8[:, 0:8], in_=imp[:, qt, :]), r=[imp.b], w=[m8.b])
                S.op("dve", lambda: nc.vector.match_replace(out=tmp32[:, :], in_to_replace=m8[:, 0:8], in_values=imp[:, qt, :],
                                                            imm_value=-3e38), r=[imp.b, m8.b], w=[tmp32.b])
                S.op("dve", lambda: nc.vector.max(out=m8[:, 8:16], in_=tmp32[:, :]), r=[tmp32.b], w=[m8.b])
                S.op("dve", lambda: nc.vector.tensor_scalar(out=sel[:, qt, :], in0=imp[:, qt, :], scalar1=m8[:, 15:16], scalar2=None,
                                                            op0=ALU.is_ge), r=[imp.b, m8.b], w=[sel.b])
            transpose_to(K, sel, [sel[:, qt, :] for qt in range(16)], selT,
                         lambda g0, n: selT[:, g0 * 128:(g0 + n) * 128], 32)
            for j in range(4):
                hh = g * 4 + j
                qrT = qrTs[j]
                for qc in range(4):
                    ob = [K.ps[4 + jj] for jj in range(4)]
                    nk = 4 * qc + 4
                    for kt in range(nk):
                        st = K.ps[ctr % 2]
                        mb = K.ps[2 + ctr % 2]
                        pt = pts[ctr % 3]
                        ctr += 1
                        S.op("pe", lambda: nc.tensor.matmul(st[:, 0:512], lhsT=ksT[:, kt * 128:(kt + 1) * 128],
                                                            rhs=qrT[:, qc * 512:(qc + 1) * 512], start=True, stop=True),
                             r=[ksT.b, qrT.b], w=[st.b])
                        S.op("pe", lambda: nc.tensor.matmul(mb[:, 0:512], lhsT=Eall[:, kt, :], rhs=selT[:, qc * 512:(qc + 1) * 512],
                                                            start=True, stop=True), r=[Eall.b, selT.b], w=[mb.b])
                        j0 = max(0, kt - 4 * qc)
                        S.op("act", lambda: nc.scalar.activation(out=pt[:, j0 * 128:512], in_=st[:, j0 * 128:512], func=AF.Exp,
                                                                 scale=scale), r=[st.b], w=[pt.b])
                        S.op("dve", lambda: nc.vector.tensor_tensor(out=pt[:, j0 * 128:512], in0=pt[:, j0 * 128:512],
                                                                    in1=mb[:, j0 * 128:512], op=ALU.mult), r=[pt.b, mb.b], w=[pt.b])
                        if kt >= 4 * qc:
                            S.op("dve", lambda: nc.vector.tensor_tensor(out=pt[:, j0 * 128:(j0 + 1) * 128],
                                                                        in0=pt[:, j0 * 128:(j0 + 1) * 128],
                                                                        in1=K.tri_le[:, :], op=ALU.mult),
                                 r=[pt.b, K.tri_le.b], w=[pt.b])
                        for jj in range(j0, 4):
                            qt = 4 * qc + jj
                            S.op("pe", lambda: nc.tensor.matmul(ob[jj][:, 0:65], lhsT=pt[:, jj * 128:(jj + 1) * 128],
                                                                rhs=vs[:, kt, :], start=(kt == 0), stop=(kt == qt)),
                                 r=[pt.b, vs.b], w=[ob[jj].b])
                    for jj in range(4):
                        qt = 4 * qc + jj
                        nsa_combine(K, ob[jj], rd, otmp, comb, j, qt, gts, 16 + hh)
                for qt in range(16):
                    kts = list(range(max(0, qt - 4), qt + 1))
                    ob = K.ps[4 + qt % 4]
                    stA = K.ps[ctr % 2]
                    stB = K.ps[2 + ctr % 2]
                    ptA = pts[ctr % 3]
                    ptB = mts[ctr % 2]
                    ctr += 1
                    for i, kt in enumerate(kts):
                        st = stA if i < 4 else stB
                        S.op("pe", lambda: nc.tensor.matmul(st[:, (i % 4) * 128:(i % 4 + 1) * 128], lhsT=kwT[:, kt * 128:(kt + 1) * 128],
                                                            rhs=qrT[:, qt * 128:(qt + 1) * 128], start=True, stop=True),
                             r=[kwT.b, qrT.b], w=[st.b])
                    na = min(4, len(kts))
                    S.op("act", lambda: nc.scalar.activation(out=ptA[:, 0:na * 128], in_=stA[:, 0:na * 128], func=AF.Exp, scale=scale),
                         r=[stA.b], w=[ptA.b])
                    if len(kts) == 5:
                        S.op("act", lambda: nc.scalar.activation(out=ptB[:, 0:128], in_=stB[:, 0:128], func=AF.Exp, scale=scale),
                             r=[stB.b], w=[ptB.b])
                    for i, kt in enumerate(kts):
                        pt = ptA if i < 4 else ptB
                        sl = slice((i % 4) * 128, (i % 4 + 1) * 128)
                        if kt == qt:
                            S.op("dve", lambda: nc.vector.tensor_tensor(out=pt[:, sl], in0=pt[:, sl], in1=K.tri_le[:, :], op=ALU.mult),
                                 r=[pt.b, K.tri_le.b], w=[pt.b])
                        elif kt == qt - 4:
                            S.op("dve", lambda: nc.vector.tensor_tensor(out=pt[:, sl], in0=pt[:, sl], in1=K.tri_gt[:, :], op=ALU.mult),
                                 r=[pt.b, K.tri_gt.b], w=[pt.b])
                        S.op("pe", lambda: nc.tensor.matmul(ob[:, 0:65], lhsT=pt[:, sl], rhs=vw[:, kt, :], start=(i == 0),
                                                            stop=(i == len(kts) - 1)), r=[pt.b, vw.b], w=[ob.b])
                    nsa_combine(K, ob, rd, otmp, comb, j, qt, gts, 32 + hh)
                S.dma("sp", lambda: nc.sync.dma_start(
                    out=d["attn"][:, hh * 64:(hh + 1) * 64].rearrange("(q p) e -> p q e", p=128), in_=comb[:, j, :, :]),
                    r=[comb.b], w=[K.db("attn", s, qq, hh) for qq in range(4)])
        S.barrier()


def nsa_combine(K, ob, rd, otmp, comb, j, qt, gts, gcol):
    nc, S = K.nc, K.S
    S.op("dve", lambda: nc.vector.reciprocal(out=rd[:, 1:2], in_=ob[:, 64:65]), r=[ob.b], w=[rd.b])
    S.op("dve", lambda: nc.vector.tensor_scalar(out=otmp[:, :], in0=ob[:, 0:64], scalar1=rd[:, 1:2], scalar2=gts[:, qt, gcol:gcol + 1],
                                                op0=ALU.mult, op1=ALU.mult), r=[ob.b, rd.b, gts.b], w=[otmp.b])
    S.op("dve", lambda: nc.vector.tensor_tensor(out=comb[:, j, qt, :], in0=comb[:, j, qt, :], in1=otmp[:, :], op=ALU.add),
         r=[comb.b, otmp.b], w=[comb.b])

def build(nseq=4, stages=("a1", "a2", "a3"), dbg=(), ntiles=NT):
    nc = bass.Bass("TRN2", target_bir_lowering=False)
    K = Ctx()
    K.nc = nc
    K.uid = 0
    K.evi = 0
    K.ntiles = ntiles
    es = ExitStack()
    K.es = es
    d = {}
    K.d = d

    def din(name, shape, dtype=F32):
        d[name] = nc.dram_tensor(name, list(shape), dtype, kind="ExternalInput").ap()

    def dscr(name, shape, dtype=F32):
        kind = "ExternalOutput" if name in dbg else "Internal"
        if name + "_in" in dbg:
            kind = "ExternalInput"
        d[name] = nc.dram_tensor(name, list(shape), dtype, kind=kind).ap()

    din("x", [nseq, SEQ, D])
    din("a_w_in", [1024, 800])
    din("a_q_norm", [1, 512])
    din("a_kv_norm", [1, 256])
    din("a_w_q_up", [512, 1536])
    din("a_w_kv_up", [256, 2048])
    din("a_w_o", [1024, 1024])
    din("s_w_kv", [1024, 1536])
    din("b_w_in", [1024, 1072])
    din("b_w_o", [1024, 1024])
    for nm in ("k", "v"):
        din("s_cmp_%s_w1" % nm, [2048, 256])
        din("s_cmp_%s_b1" % nm, [128, 2])
        din("s_cmp_%s_w2" % nm, [256, 64])
        din("s_cmp_pos_%sT" % nm, [64, 32])
    din("p_w_q", [2, 1024, 2048])
    din("p_skT", [2, 128, 16, 128])
    din("p_uv0", [16384, 2048])
    din("p_uv1", [16384, 2048])
    din("ln_g", [4, 1024])
    din("ln_b", [4, 1024])
    hc = host_consts()
    for k, v in hc.items():
        din(k, v.shape)
    d["out"] = nc.dram_tensor("out", [nseq, SEQ, D], F32, kind="ExternalOutput").ap()
    dscr("qnT", [16, 64, SEQ], BF16)
    dscr("qpT", [16, 32, SEQ], BF16)
    dscr("knT", [16, 64, SEQ], BF16)
    dscr("kpT", [32, SEQ], BF16)
    dscr("vp", [16, SEQ, 65], BF16)
    dscr("attn", [SEQ, 1024])
    dscr("h1", [SEQ, 1024])
    dscr("h2", [SEQ, 1024])
    dscr("p_uvb0", [16384, 2048], BF16)
    dscr("p_uvb1", [16384, 2048], BF16)
    dscr("h3", [SEQ, 1024])
    dscr("gates", [SEQ, 48])
    dscr("bqT", [16, 64, SEQ], BF16)
    dscr("bqrT", [16, 64, SEQ], BF16)
    dscr("bkcT", [8, 64, SEQ])
    dscr("bkrT", [8, 64, SEQ], BF16)
    dscr("bvp", [2, 4, SEQ, 65], BF16)
    dscr("kcmpT", [4, 64, 128], BF16)
    dscr("vcmp", [4, 128, 97], BF16)

    with es:
        S = Sched(nc, es)
        K.S = S
        K.ps = [Tile(es.enter_context(nc.psum_tensor("ps%d" % i, [128, 512], F32)), "ps%d" % i) for i in range(8)]
        K.bank_i = 0

        K.bank_list = list(range(8))

        def bank():
            b = K.ps[K.bank_list[K.bank_i % len(K.bank_list)]]
            K.bank_i += 1
            return b
        K.bank = bank
        K.dbufs = {}

        def db(*key):
            if key not in K.dbufs:
                K.dbufs[key] = Buf(str(key))
            return K.dbufs[key]
        K.db = db
        gl = Phase(K)
        gl.es = es
        K.ident = gl.tile("ident", [128, 128])
        K.tri_le = gl.tile("tri_le", [128, 128])
        K.tri_gt = gl.tile("tri_gt", [128, 128])
        K.eps_ln = gl.tile("eps_ln", [128, 1])
        K.eps_rms = gl.tile("eps_rms", [128, 1])
        for nm in ("ident", "tri_le", "tri_gt"):
            tl = getattr(K, nm)
            S.dma("sp", lambda tl=tl, nm=nm: nc.sync.dma_start(out=tl[:, :], in_=d[nm]), w=[tl.b])
        S.op("dve", lambda: nc.vector.memset(K.eps_ln[:, :], LN_EPS), w=[K.eps_ln.b])
        S.op("dve", lambda: nc.vector.memset(K.eps_rms[:, :], RMS_EPS), w=[K.eps_rms.b])

        if "p0" in stages or "p1" in stages:
            prologue_tables(K)
        for s in range(nseq):
            if "a1" in stages:
                phase_a1(K, s)
            if "a2" in stages:
                phase_a2(K, s)
            if "a3" in stages:
                dstname = "h1" if "h1" in dbg or len(stages) > 3 else "h1"
                phase_oproj_ln(K, s, d["a_w_o"], lambda t0: d["x"][s, t0:t0 + 128, :], lambda t: [],
                               d["ln_g"][0:1, :], d["ln_b"][0:1, :],
                               lambda t0: d["h1"][t0:t0 + 128, :], lambda t: K.db("h1", s, t), "a")
            if "p0" in stages:
                phase_peer(K, s, 0, lambda t0: d["h1"][t0:t0 + 128, :], lambda t: [K.db("h1", s, t)],
                           lambda t0: d["h2"][t0:t0 + 128, :], lambda t: K.db("h2", s, t), ntiles=K.ntiles)
            if "b0" in stages:
                phase_b0(K, s)
            if "b1" in stages:
                phase_b1(K, s)
            if "b2" in stages:
                phase_b2(K, s)
            if "b3" in stages:
                phase_oproj_ln(K, s, d["b_w_o"], lambda t0: d["h2"][t0:t0 + 128, :], lambda t: [K.db("h2", s, t)],
                               d["ln_g"][2:3, :], d["ln_b"][2:3, :],
                               lambda t0: d["h3"][t0:t0 + 128, :], lambda t: K.db("h3", s, t), "b")
            if "p1" in stages:
                phase_peer(K, s, 1, lambda t0: d["h3"][t0:t0 + 128, :], lambda t: [K.db("h3", s, t)],
                           lambda t0: d["out"][s, t0:t0 + 128, :], lambda t: K.db("out", s, t), ntiles=K.ntiles)
        S.barrier()
    K.hc = hc
    return nc, K


ALL_STAGES = ("a1", "a2", "a3", "p0", "b0", "b1", "b2", "b3", "p1")
_CACHE = {}


def _f32(a):
    return np.ascontiguousarray(np.asarray(a), dtype=np.float32)


def kernel(x, a_w_in, a_q_norm, a_kv_norm, a_w_q_up, a_w_kv_up, a_w_o, b_w_in, b_w_o,
           s_w_kv, s_cmp_pos_k, s_cmp_pos_v, s_cmp_k_w1, s_cmp_k_b1, s_cmp_k_w2,
           s_cmp_v_w1, s_cmp_v_b1, s_cmp_v_w2, p_w_q, p_subkeys, p_u, p_v, ln_g, ln_b):
    x = np.asarray(x)
    B = x.shape[0]
    if "nc" not in _CACHE:
        _CACHE["nc"] = build(nseq=B // NCORES, stages=ALL_STAGES)
    nc, K = _CACHE["nc"]
    p_subkeys = np.asarray(p_subkeys)
    p_u = np.asarray(p_u)
    p_v = np.asarray(p_v)
    w = {
        "a_w_in": _f32(np.asarray(a_w_in)[0]), "a_q_norm": _f32(np.asarray(a_q_norm).reshape(1, 512)),
        "a_kv_norm": _f32(np.asarray(a_kv_norm).reshape(1, 256)), "a_w_q_up": _f32(np.asarray(a_w_q_up)[0]),
        "a_w_kv_up": _f32(np.asarray(a_w_kv_up)[0]), "a_w_o": _f32(np.asarray(a_w_o)[0]),
        "b_w_in": _f32(np.asarray(b_w_in)[0]), "b_w_o": _f32(np.asarray(b_w_o)[0]), "s_w_kv": _f32(s_w_kv),
        "s_cmp_k_w1": _f32(s_cmp_k_w1), "s_cmp_v_w1": _f32(s_cmp_v_w1),
        "s_cmp_k_w2": _f32(s_cmp_k_w2), "s_cmp_v_w2": _f32(s_cmp_v_w2),
        "s_cmp_k_b1": _f32(np.asarray(s_cmp_k_b1).reshape(2, 128).T), "s_cmp_v_b1": _f32(np.asarray(s_cmp_v_b1).reshape(2, 128).T),
        "s_cmp_pos_kT": _f32(np.asarray(s_cmp_pos_k).T), "s_cmp_pos_vT": _f32(np.asarray(s_cmp_pos_v).T),
        "p_w_q": _f32(p_w_q),
        "p_skT": _f32(np.stack([p_subkeys[l].reshape(16, 128, 128).transpose(2, 0, 1) for l in range(2)])),
        "p_uv0": _f32(np.concatenate([p_u[0], p_v[0]], axis=1)), "p_uv1": _f32(np.concatenate([p_u[1], p_v[1]], axis=1)),
        "ln_g": _f32(np.asarray(ln_g).reshape(4, 1024)), "ln_b": _f32(np.asarray(ln_b).reshape(4, 1024)),
    }
    w.update({k: _f32(v) for k, v in K.hc.items()})
    nper = B // NCORES
    in_maps = []
    for c in range(NCORES):
        m = dict(w)
        m["x"] = _f32(x[c * nper:(c + 1) * nper])
        in_maps.append(m)
    res = run_bass_kernel_spmd(nc, in_maps, core_ids=list(range(NCORES)))
    out = np.empty((B, SEQ, D), dtype=np.float32)
    for c in range(NCORES):
        out[c * nper:(c + 1) * nper] = np.asarray(res.results[c]["out"]).reshape(nper, SEQ, D)
    return out
```

```python
import numpy as np
import os
XP = os.environ.get('XP', '')
from contextlib import ExitStack, contextmanager
import concourse.bass as bass
import concourse.mybir as mybir
from concourse.bass_utils import run_bass_kernel_spmd

F32 = mybir.dt.float32
I32 = mybir.dt.int32
U32 = mybir.dt.uint32
BF16 = mybir.dt.bfloat16
AF = mybir.ActivationFunctionType
ALU = mybir.AluOpType
AX = mybir.AxisListType

SEQ = 2048
D = 1024
NT = 16
NCORES = 8
ALPHA = 4.0 ** 0.25
LN_EPS = 1e-5
RMS_EPS = 1e-6
NEG = -1e30


class Buf:
    __slots__ = ("name", "w", "r")

    def __init__(self, name=""):
        self.name = name
        self.w = None
        self.r = {}


class Tile:
    def __init__(self, t, name):
        self.t = t
        self.b = Buf(name)

    def __getitem__(self, k):
        return self.t[k]


class Sched:
    EPOCH = 20000

    def __init__(self, nc, es):
        self.nc = nc
        self.es = es
        self.eng = {"pe": nc.tensor, "act": nc.scalar, "dve": nc.vector, "pool": nc.gpsimd, "sp": nc.sync}
        self.cur = {}
        self.waited = {}
        self.nsem = 0
        self.ninst = 0
        for e in ("pe", "act", "dve", "pool"):
            self._new_epoch(e)
        self.rings = {}
        for q, n in (("sp", 24), ("pool", 12), ("act", 8)):
            self.rings[q] = [[self._sem(), 0] for _ in range(n)]
        self.ring_i = {q: 0 for q in self.rings}

    def _sem(self):
        self.nsem += 1
        return self.es.enter_context(self.nc.semaphore("s%d" % self.nsem))

    def _new_epoch(self, e):
        self.cur[e] = [self._sem(), 0]

    def _wait(self, e, tok, strict=False):
        sem, val, src = tok
        if src == e and e == "pe" and not strict:
            return
        key = (e, id(sem))
        if self.waited.get(key, 0) >= val:
            return
        self.eng[e].wait_ge(sem, val)
        self.ninst += 1
        self.waited[key] = val

    @staticmethod
    def _deps(r, w):
        toks = []
        for b in r:
            if b.w is not None:
                toks.append(b.w)
        for b in w:
            if b.w is not None:
                toks.append(b.w)
            toks.extend(b.r.values())
        return toks

    @staticmethod
    def _commit(tok, r, w, key):
        for b in w:
            b.w = tok
            b.r = {}
        for b in r:
            if b not in w:
                b.r[key] = tok

    def op(self, e, fn, r=(), w=()):
        for t in self._deps(r, w):
            self._wait(e, t)
        st = self.cur[e]
        if st[1] >= self.EPOCH:
            self._new_epoch(e)
            st = self.cur[e]
        ins = fn()
        st[1] += 1
        ins.then_inc(st[0], 1)
        self.ninst += 1
        tok = (st[0], st[1], e)
        self._commit(tok, r, w, e)
        return tok

    def dma(self, q, fn, r=(), w=()):
        ring = self.rings[q]
        i = self.ring_i[q]
        self.ring_i[q] = (i + 1) % len(ring)
        slot = ring[i]
        if slot[1] > 0:
            self._wait(q, (slot[0], slot[1], None), strict=True)
        for t in self._deps(r, w):
            self._wait(q, t, strict=True)
        ins = fn()
        slot[1] += 16
        ins.then_inc(slot[0], 16)
        self.ninst += 1
        tok = (slot[0], slot[1], None)
        self._commit(tok, r, w, ("dma", q, i))
        return tok

    def barrier(self):
        toks = []
        for e in ("pe", "act", "dve", "pool"):
            st = self.cur[e]
            if st[1] > 0:
                toks.append((st[0], st[1], e))
        for q, ring in self.rings.items():
            for slot in ring:
                if slot[1] > 0:
                    toks.append((slot[0], slot[1], None))
        for e in ("pe", "act", "dve", "pool", "sp"):
            for t in toks:
                self._wait(e, t, strict=(t[2] is None))


class Phase:
    def __init__(self, K):
        self.K = K
        self.es = ExitStack()

    def tile(self, name, shape, dtype=F32):
        self.K.uid += 1
        nm = "%s_%d" % (name, self.K.uid)
        return Tile(self.es.enter_context(self.K.nc.sbuf_tensor(nm, list(shape), dtype)), nm)


class Ctx:
    pass


def host_consts():
    c = {}
    c["ident"] = np.eye(128, dtype=np.float32)
    kk = np.arange(128)[:, None]
    qq = np.arange(128)[None, :]
    c["tri_le"] = (kk <= qq).astype(np.float32)
    c["tri_gt"] = (kk > qq).astype(np.float32)
    pos = np.arange(SEQ, dtype=np.float32)[:, None]
    for d in (32, 64):
        inv = (10000.0 ** (-np.arange(0, d, 2, dtype=np.float32) / d)).astype(np.float32)
        ang = (pos * inv[None, :]).astype(np.float32)
        c["rope%d" % d] = np.concatenate([np.cos(ang), np.sin(ang)], axis=1).astype(np.float32)
    c["iota16"] = np.tile(np.arange(16, dtype=np.float32)[None, :], (128, 1))
    cc = np.arange(128)
    tt = np.arange(SEQ)
    c["cmask"] = ((16 * cc[:, None] + 31 <= tt[None, :]) & (cc[:, None] < 127)).astype(np.float32)
    nn = np.arange(32)
    ovl = ((16 * cc[:, None] < 64 * nn[None, :] + 64) & (16 * cc[:, None] + 31 >= 64 * nn[None, :]) & (cc[:, None] < 127))
    c["overlap"] = ovl.astype(np.float32)
    cur = tt // 64
    forced = (nn[None, :] == 0) | ((nn[None, :] <= cur[:, None]) & (nn[None, :] > cur[:, None] - 2))
    valid = nn[None, :] <= cur[:, None]
    c["selA"] = ((~forced) & valid).astype(np.float32)
    c["selB"] = np.where(valid, np.where(forced, 1e9, 0.0), -1e30).astype(np.float32)
    E = np.zeros((32, 16, 128), np.float32)
    for kt in range(16):
        for k in range(128):
            E[(kt * 128 + k) // 64, kt, k] = 1.0
    c["Eall"] = E
    return c


def rope_tok(K, ph, x1, x2, o1, o2, cos_b, sin_b, shape, rbufs, wbufs, tmpname):
    nc, S = K.nc, K.S
    if not hasattr(ph, "rtmp"):
        ph.rtmp = {}
    key = tuple(shape)
    if key not in ph.rtmp:
        ph.rtmp[key] = (ph.tile("ropea", shape), ph.tile("ropeb", shape))
    ta, tb = ph.rtmp[key]
    sl = tuple(slice(None) for _ in shape)
    S.op("dve", lambda: nc.vector.tensor_tensor(out=ta[sl], in0=x1, in1=cos_b, op=ALU.mult), r=rbufs, w=[ta.b])
    S.op("dve", lambda: nc.vector.tensor_tensor(out=tb[sl], in0=x2, in1=sin_b, op=ALU.mult), r=rbufs, w=[tb.b])
    S.op("dve", lambda: nc.vector.tensor_tensor(out=o1, in0=ta[sl], in1=tb[sl], op=ALU.subtract), r=[ta.b, tb.b], w=wbufs)
    S.op("dve", lambda: nc.vector.tensor_tensor(out=ta[sl], in0=x2, in1=cos_b, op=ALU.mult), r=rbufs, w=[ta.b])
    S.op("dve", lambda: nc.vector.tensor_tensor(out=tb[sl], in0=x1, in1=sin_b, op=ALU.mult), r=rbufs, w=[tb.b])
    S.op("dve", lambda: nc.vector.tensor_tensor(out=o2, in0=ta[sl], in1=tb[sl], op=ALU.add), r=[ta.b, tb.b], w=wbufs)


def evac(K, i, out, in_, r, w):
    nc, S = K.nc, K.S
    if i % 2 == 0:
        S.op("act", lambda: nc.scalar.copy(out=out, in_=in_), r=r, w=w)
    else:
        S.op("dve", lambda: nc.vector.tensor_copy(out=out, in_=in_), r=r, w=w)


def transpose_to(K, src_tile, src_aps, dst_tile, dst_ap_fn, rows, group=4):
    nc, S = K.nc, K.S
    n = len(src_aps)
    for g0 in range(0, n, group):
        cnt = min(group, n - g0)
        bank = K.bank()
        for j in range(cnt):
            ap = src_aps[g0 + j]
            S.op("pe", lambda ap=ap, j=j: nc.tensor.transpose(out=bank[0:rows, j * 128:(j + 1) * 128], in_=ap,
                                                               identity=K.ident[:, :]),
                 r=[src_tile.b, K.ident.b], w=[bank.b])
        evac(K, K.evi, dst_ap_fn(g0, cnt), bank[0:rows, 0:cnt * 128], r=[bank.b], w=[dst_tile.b])
        K.evi += 1


def layer_norm(K, ph, y, g_bc, b_bc, out, tag):
    nc, S = K.nc, K.S
    if not hasattr(ph, "lntmp"):
        ph.lntmp = (ph.tile("lnst", [128, 2, 6]), ph.tile("lnmv", [128, 2]), ph.tile("lnsd", [128, 1]), ph.tile("lnrs", [128, 1]))
    st, mv, sd, rs = ph.lntmp
    for j in range(2):
        S.op("dve", lambda j=j: nc.vector.bn_stats(out=st[:, j, :], in_=y[:, j * 512:(j + 1) * 512]), r=[y.b], w=[st.b])
    S.op("dve", lambda: nc.vector.bn_aggr(out=mv[:, :], in_=st[:, :, :].rearrange("p a b -> p (a b)")), r=[st.b], w=[mv.b])
    S.op("act", lambda: nc.scalar.activation(out=sd[:, :], in_=mv[:, 1:2], func=AF.Sqrt, bias=K.eps_ln[:, :], scale=1.0),
         r=[mv.b, K.eps_ln.b], w=[sd.b])
    S.op("dve", lambda: nc.vector.reciprocal(out=rs[:, :], in_=sd[:, :]), r=[sd.b], w=[rs.b])
    S.op("dve", lambda: nc.vector.tensor_scalar(out=out[:, :], in0=y[:, :], scalar1=mv[:, 0:1], scalar2=rs[:, 0:1],
                                                op0=ALU.subtract, op1=ALU.mult), r=[y.b, mv.b, rs.b], w=[out.b])
    S.op("dve", lambda: nc.vector.tensor_tensor(out=out[:, :], in0=out[:, :], in1=g_bc[:, :], op=ALU.mult),
         r=[out.b, g_bc.b], w=[out.b])
    S.op("dve", lambda: nc.vector.tensor_tensor(out=out[:, :], in0=out[:, :], in1=b_bc[:, :], op=ALU.add),
         r=[out.b, b_bc.b], w=[out.b])


def load_bc(K, ph, name, dram_row_ap, n):
    nc, S = K.nc, K.S
    t = ph.tile(name, [128, n])
    S.dma("sp", lambda: nc.sync.dma_start(out=t[:, :], in_=dram_row_ap.to_broadcast([128, n])), w=[t.b])
    return t


def load_w_bf16(K, ph, name, dram_ap, shape, stage):
    nc, S = K.nc, K.S
    n = 1
    for v in shape[1:]:
        n *= v
    wt = ph.tile(name, shape, BF16)
    letters = "abcd"[:len(shape) - 1]
    pat = "p (%s) -> p %s" % (" ".join(letters), " ".join(letters))
    kw = {letters[i]: shape[1 + i] for i in range(1, len(letters))}
    sv = stage[:, 0:n].rearrange(pat, **kw) if len(shape) > 2 else stage[:, 0:n]
    S.dma("sp", lambda: nc.sync.dma_start(out=sv, in_=dram_ap), w=[stage.b])
    flat_w = wt[tuple(slice(None) for _ in shape)]
    S.op("dve", lambda: nc.vector.tensor_copy(out=flat_w, in_=sv), r=[stage.b], w=[wt.b])
    return wt

def phase_a1(K, s):
    nc, S, d = K.nc, K.S, K.d
    ph = Phase(K)
    with ph.es:
        stage = ph.tile("wstage", [128, 6400])
        w_in = load_w_bf16(K, ph, "w_in", d["a_w_in"].rearrange("(c p) n -> p c n", p=128), [128, 8, 800], stage)
        wq = load_w_bf16(K, ph, "wq", d["a_w_q_up"].rearrange("(c p) n -> p c n", p=128), [128, 4, 1536], stage)
        wkv = load_w_bf16(K, ph, "wkv", d["a_w_kv_up"].rearrange("(c p) n -> p c n", p=128), [128, 2, 2048], stage)
        qn_bc = load_bc(K, ph, "qn_bc", d["a_q_norm"], 512)
        kvn_bc = load_bc(K, ph, "kvn_bc", d["a_kv_norm"], 256)
        xs = [ph.tile("x%d" % i, [128, 1024]) for i in range(2)]
        css = [ph.tile("cs%d" % i, [128, 32]) for i in range(2)]
        xT = ph.tile("xT", [128, 8, 128], BF16)
        junk = ph.tile("junk", [128, 512])
        ss = ph.tile("ss", [128, 2])
        rs = ph.tile("rs", [128, 2])
        rr = ph.tile("rr", [128, 2])
        cq = ph.tile("cq", [128, 512])
        ckv = ph.tile("ckv", [128, 256])
        kraw = ph.tile("kraw", [128, 32])
        kpe = ph.tile("kpe", [128, 32])
        cqT = ph.tile("cqT", [128, 4, 128], BF16)
        ckvT = ph.tile("ckvT", [128, 2, 128], BF16)
        q_sb = ph.tile("q_sb", [128, 16, 96])
        qpe = ph.tile("qpe", [128, 16, 32])
        kv_sb = ph.tile("kv_sb", [128, 16, 128])
        vps = [ph.tile("vp%d" % i, [128, 16, 65], BF16) for i in range(2)]
        qnTs = [ph.tile("qnT%d" % i, [64, 16, 128], BF16) for i in range(2)]
        qpTs = [ph.tile("qpT%d" % i, [32, 16, 128], BF16) for i in range(2)]
        knTs = [ph.tile("knT%d" % i, [64, 16, 128], BF16) for i in range(2)]
        kpTs = [ph.tile("kpT%d" % i, [32, 128], BF16) for i in range(2)]
        for v in vps:
            S.op("dve", lambda v=v: nc.vector.memset(v[:, :, 64:65], 1.0), w=[v.b])
        for t in range(NT):
            p = t % 2
            t0 = t * 128
            x, cs, vp, qnT, qpT, knT, kpT = xs[p], css[p], vps[p], qnTs[p], qpTs[p], knTs[p], kpTs[p]
            S.dma("sp", lambda: nc.sync.dma_start(out=x[:, :], in_=d["x"][s, t0:t0 + 128, :]), w=[x.b])
            S.dma("sp", lambda: nc.sync.dma_start(out=cs[:, :], in_=d["rope32"][t0:t0 + 128, :]), w=[cs.b])
            transpose_to(K, x, [x[:, c * 128:(c + 1) * 128] for c in range(8)], xT,
                         lambda g0, n: xT[:, g0:g0 + n, :].rearrange("p a b -> p (a b)"), 128)
            bA, bB = K.bank(), K.bank()
            for c in range(8):
                S.op("pe", lambda c=c: nc.tensor.matmul(bA[:, 0:512], lhsT=xT[:, c, :], rhs=w_in[:, c, 0:512],
                                                        start=(c == 0), stop=(c == 7)), r=[xT.b, w_in.b], w=[bA.b])
            for c in range(8):
                S.op("pe", lambda c=c: nc.tensor.matmul(bB[:, 0:288], lhsT=xT[:, c, :], rhs=w_in[:, c, 512:800],
                                                        start=(c == 0), stop=(c == 7)), r=[xT.b, w_in.b], w=[bB.b])
            S.op("act", lambda: nc.scalar.activation(out=junk[:, 0:512], in_=bA[:, 0:512], func=AF.Square,
                                                     accum_out=ss[:, 0:1]), r=[bA.b], w=[junk.b, ss.b])
            S.op("act", lambda: nc.scalar.activation(out=junk[:, 0:256], in_=bB[:, 0:256], func=AF.Square,
                                                     accum_out=ss[:, 1:2]), r=[bB.b], w=[junk.b, ss.b])
            S.op("act", lambda: nc.scalar.activation(out=rs[:, 0:1], in_=ss[:, 0:1], func=AF.Sqrt, bias=K.eps_rms[:, :],
                                                     scale=1.0 / 512), r=[ss.b, K.eps_rms.b], w=[rs.b])
            S.op("act", lambda: nc.scalar.activation(out=rs[:, 1:2], in_=ss[:, 1:2], func=AF.Sqrt, bias=K.eps_rms[:, :],
                                                     scale=1.0 / 256), r=[ss.b, K.eps_rms.b], w=[rs.b])
            S.op("dve", lambda: nc.vector.reciprocal(out=rr[:, :], in_=rs[:, :]), r=[rs.b], w=[rr.b])
            S.op("dve", lambda: nc.vector.scalar_tensor_tensor(out=cq[:, :], in0=bA[:, 0:512], scalar=rr[:, 0:1],
                                                               in1=qn_bc[:, :], op0=ALU.mult, op1=ALU.mult),
                 r=[bA.b, rr.b, qn_bc.b], w=[cq.b])
            S.op("dve", lambda: nc.vector.scalar_tensor_tensor(out=ckv[:, :], in0=bB[:, 0:256], scalar=rr[:, 1:2],
                                                               in1=kvn_bc[:, :], op0=ALU.mult, op1=ALU.mult),
                 r=[bB.b, rr.b, kvn_bc.b], w=[ckv.b])
            S.op("act", lambda: nc.scalar.copy(out=kraw[:, :], in_=bB[:, 256:288]), r=[bB.b], w=[kraw.b])
            rope_tok(K, ph, kraw[:, 0:16], kraw[:, 16:32], kpe[:, 0:16], kpe[:, 16:32], cs[:, 0:16], cs[:, 16:32],
                     [128, 16], [kraw.b, cs.b], [kpe.b], "rk%d" % t)
            transpose_to(K, cq, [cq[:, c * 128:(c + 1) * 128] for c in range(4)], cqT,
                         lambda g0, n: cqT[:, g0:g0 + n, :].rearrange("p a b -> p (a b)"), 128)
            transpose_to(K, ckv, [ckv[:, c * 128:(c + 1) * 128] for c in range(2)], ckvT,
                         lambda g0, n: ckvT[:, g0:g0 + n, :].rearrange("p a b -> p (a b)"), 128)
            q_flat = q_sb[:, :, :].rearrange("p a b -> p (a b)")
            for n in range(3):
                bk = K.bank()
                for c in range(4):
                    S.op("pe", lambda c=c, n=n, bk=bk: nc.tensor.matmul(bk[:, 0:512], lhsT=cqT[:, c, :],
                                                                         rhs=wq[:, c, n * 512:(n + 1) * 512],
                                                                         start=(c == 0), stop=(c == 3)),
                         r=[cqT.b, wq.b], w=[bk.b])
                evac(K, n, q_flat[:, n * 512:(n + 1) * 512], bk[:, 0:512], r=[bk.b], w=[q_sb.b])
            kv_flat = kv_sb[:, :, :].rearrange("p a b -> p (a b)")
            for n in range(4):
                bk = K.bank()
                for c in range(2):
                    S.op("pe", lambda c=c, n=n, bk=bk: nc.tensor.matmul(bk[:, 0:512], lhsT=ckvT[:, c, :],
                                                                         rhs=wkv[:, c, n * 512:(n + 1) * 512],
                                                                         start=(c == 0), stop=(c == 1)),
                         r=[ckvT.b, wkv.b], w=[bk.b])
                evac(K, n + 1, kv_flat[:, n * 512:(n + 1) * 512], bk[:, 0:512], r=[bk.b], w=[kv_sb.b])
            cos_b = cs[:, 0:16].unsqueeze(1).to_broadcast([128, 16, 16])
            sin_b = cs[:, 16:32].unsqueeze(1).to_broadcast([128, 16, 16])
            rope_tok(K, ph, q_sb[:, :, 64:80], q_sb[:, :, 80:96], qpe[:, :, 0:16], qpe[:, :, 16:32], cos_b, sin_b,
                     [128, 16, 16], [q_sb.b, cs.b], [qpe.b], "rq%d" % t)
            S.op("act", lambda: nc.scalar.copy(out=vp[:, :, 0:64], in_=kv_sb[:, :, 64:128]), r=[kv_sb.b], w=[vp.b])
            transpose_to(K, q_sb, [q_sb[:, h, 0:64] for h in range(16)], qnT,
                         lambda g0, n: qnT[:, g0:g0 + n, :].rearrange("p a b -> p (a b)"), 64)
            transpose_to(K, qpe, [qpe[:, h, :] for h in range(16)], qpT,
                         lambda g0, n: qpT[:, g0:g0 + n, :].rearrange("p a b -> p (a b)"), 32)
            transpose_to(K, kv_sb, [kv_sb[:, h, 0:64] for h in range(16)], knT,
                         lambda g0, n: knT[:, g0:g0 + n, :].rearrange("p a b -> p (a b)"), 64)
            transpose_to(K, kpe, [kpe[:, :]], kpT, lambda g0, n: kpT[:, :], 32)
            wb = [K.db("qk", s, t)]
            S.dma("sp", lambda: nc.sync.dma_start(out=d["qnT"][:, :, t0:t0 + 128].rearrange("h e t -> e h t"),
                                                  in_=qnT[:, :, :]), r=[qnT.b], w=wb)
            S.dma("sp", lambda: nc.sync.dma_start(out=d["qpT"][:, :, t0:t0 + 128].rearrange("h e t -> e h t"),
                                                  in_=qpT[:, :, :]), r=[qpT.b], w=wb)
            S.dma("sp", lambda: nc.sync.dma_start(out=d["knT"][:, :, t0:t0 + 128].rearrange("h e t -> e h t"),
                                                  in_=knT[:, :, :]), r=[knT.b], w=wb)
            S.dma("sp", lambda: nc.sync.dma_start(out=d["kpT"][:, t0:t0 + 128], in_=kpT[:, :]), r=[kpT.b], w=wb)
            S.dma("sp", lambda: nc.sync.dma_start(out=d["vp"][:, t0:t0 + 128, :].rearrange("h t e -> t h e"),
                                                  in_=vp[:, :, :]), r=[vp.b], w=wb)
        S.barrier()


def attn_core(K, ph, pairs_q, pairs_k, vp, nkeys_tile, scale, o_sb_fn, store_fn, tag):
    pass


def phase_a2(K, s):
    nc, S, d = K.nc, K.S, K.d
    scale = 96.0 ** -0.5
    ph = Phase(K)
    with ph.es:
        kp = ph.tile("kp", [32, SEQ], BF16)
        allqk = [K.db("qk", s, t) for t in range(NT)]
        S.dma("sp", lambda: nc.sync.dma_start(out=kp[:, :], in_=d["kpT"][:, :]), r=allqk, w=[kp.b])
        qns = [ph.tile("qn%d" % i, [64, SEQ], BF16) for i in range(2)]
        qps = [ph.tile("qp%d" % i, [32, SEQ], BF16) for i in range(2)]
        kns = [ph.tile("kn%d" % i, [64, SEQ], BF16) for i in range(2)]
        vpt = [ph.tile("vpa%d" % i, [128, 16, 65], BF16) for i in range(2)]
        pts = [ph.tile("pt%d" % i, [128, 512], BF16) for i in range(4)]
        osb = [ph.tile("osb%d" % i, [128, 4, 64]) for i in range(2)]
        rden = ph.tile("rden", [128, 4])
        ctr = 0
        oc = 0
        for h in range(16):
            p = h % 2
            qn, qp, kn, vp = qns[p], qps[p], kns[p], vpt[p]
            S.dma("sp", lambda: nc.sync.dma_start(out=qn[:, :], in_=d["qnT"][h, :, :]), r=allqk, w=[qn.b])
            S.dma("sp", lambda: nc.sync.dma_start(out=qp[:, :], in_=d["qpT"][h, :, :]), r=allqk, w=[qp.b])
            S.dma("sp", lambda: nc.sync.dma_start(out=kn[:, :], in_=d["knT"][h, :, :]), r=allqk, w=[kn.b])
            S.dma("sp", lambda: nc.sync.dma_start(out=vp[:, :, :], in_=d["vp"][h, :, :].rearrange("(k p) e -> p k e", p=128)),
                  r=allqk, w=[vp.b])
            for qc in range(4):
                ob = [K.ps[4 + j] for j in range(4)]
                nk = 4 * qc + 4
                def s1(kt):
                    nonlocal ctr
                    st = K.ps[ctr % 4]
                    pt = pts[ctr % 4]
                    ctr += 1
                    S.op("pe", lambda: nc.tensor.matmul(st[:, 0:512], lhsT=kn[:, kt * 128:(kt + 1) * 128],
                                                        rhs=qn[:, qc * 512:(qc + 1) * 512], start=True, stop=False),
                         r=[kn.b, qn.b], w=[st.b])
                    S.op("pe", lambda: nc.tensor.matmul(st[:, 0:512], lhsT=kp[:, kt * 128:(kt + 1) * 128],
                                                        rhs=qp[:, qc * 512:(qc + 1) * 512], start=False, stop=True),
                         r=[kp.b, qp.b], w=[st.b])
                    j0 = max(0, kt - 4 * qc)
                    S.op("act", lambda: nc.scalar.activation(out=pt[:, j0 * 128:512], in_=st[:, j0 * 128:512], func=AF.Exp,
                                                             scale=scale), r=[st.b], w=[pt.b])
                    if kt >= 4 * qc:
                        S.op("dve", lambda: nc.vector.tensor_tensor(out=pt[:, j0 * 128:(j0 + 1) * 128],
                                                                    in0=pt[:, j0 * 128:(j0 + 1) * 128],
                                                                    in1=K.tri_le[:, :], op=ALU.mult),
                             r=[pt.b, K.tri_le.b], w=[pt.b])
                    return (kt, pt, j0)

                def s2(c):
                    kt, pt, j0 = c
                    for j in range(j0, 4):
                        qt = 4 * qc + j
                        S.op("pe", lambda: nc.tensor.matmul(ob[j][:, 0:65], lhsT=pt[:, j * 128:(j + 1) * 128],
                                                            rhs=vp[:, kt, :], start=(kt == 0), stop=(kt == qt)),
                             r=[pt.b, vp.b], w=[ob[j].b])
                pend = []
                for kt in range(nk):
                    pend.append(s1(kt))
                    if len(pend) > 2:
                        s2(pend.pop(0))
                while pend:
                    s2(pend.pop(0))
                o = osb[oc % 2]
                oc += 1
                for j in range(4):
                    S.op("dve", lambda j=j: nc.vector.reciprocal(out=rden[:, j:j + 1], in_=ob[j][:, 64:65]),
                         r=[ob[j].b], w=[rden.b])
                    S.op("dve", lambda j=j: nc.vector.tensor_scalar(out=o[:, j, :], in0=ob[j][:, 0:64],
                                                                    scalar1=rden[:, j:j + 1], scalar2=None, op0=ALU.mult),
                         r=[ob[j].b, rden.b], w=[o.b])
                S.dma("sp", lambda: nc.sync.dma_start(
                    out=d["attn"][qc * 512:(qc + 1) * 512, h * 64:(h + 1) * 64].rearrange("(j p) e -> p j e", p=128),
                    in_=o[:, :, :]), r=[o.b], w=[K.db("attn", s, qc, h)])
        S.barrier()


def phase_oproj_ln(K, s, w_o_ap, res_fn, res_deps_fn, g_ap, b_ap, dst_fn, dst_buf_fn, tag):
    nc, S, d = K.nc, K.S, K.d
    ph = Phase(K)
    with ph.es:
        stage = ph.tile("wstage", [128, 8192])
        wo = load_w_bf16(K, ph, "wo", w_o_ap.rearrange("(c p) n -> p c n", p=128), [128, 8, 1024], stage)
        g_bc = load_bc(K, ph, "g_bc", g_ap, 1024)
        b_bc = load_bc(K, ph, "b_bc", b_ap, 1024)
        ats = [ph.tile("at%d" % i, [128, 1024]) for i in range(2)]
        xs = [ph.tile("xr%d" % i, [128, 1024]) for i in range(2)]
        aT = ph.tile("aT", [128, 8, 128], BF16)
        y = ph.tile("y", [128, 1024])
        outs = [ph.tile("ho%d" % i, [128, 1024]) for i in range(2)]
        for t in range(NT):
            p = t % 2
            t0 = t * 128
            at, x, o = ats[p], xs[p], outs[p]
            S.dma("sp", lambda: nc.sync.dma_start(out=at[:, :], in_=d["attn"][t0:t0 + 128, :]),
                  r=[K.db("attn", s, t // 4, h) for h in range(16)], w=[at.b])
            S.dma("sp", lambda: nc.sync.dma_start(out=x[:, :], in_=res_fn(t0)), r=res_deps_fn(t), w=[x.b])
            transpose_to(K, at, [at[:, c * 128:(c + 1) * 128] for c in range(8)], aT,
                         lambda g0, n: aT[:, g0:g0 + n, :].rearrange("p a b -> p (a b)"), 128)
            for n in range(2):
                bk = K.bank()
                for c in range(8):
                    S.op("pe", lambda c=c, n=n, bk=bk: nc.tensor.matmul(bk[:, 0:512], lhsT=aT[:, c, :],
                                                                         rhs=wo[:, c, n * 512:(n + 1) * 512],
                                                                         start=(c == 0), stop=(c == 7)),
                         r=[aT.b, wo.b], w=[bk.b])
                S.op("dve", lambda n=n, bk=bk: nc.vector.scalar_tensor_tensor(
                    out=y[:, n * 512:(n + 1) * 512], in0=x[:, n * 512:(n + 1) * 512], scalar=ALPHA, in1=bk[:, 0:512],
                    op0=ALU.mult, op1=ALU.add), r=[x.b, bk.b], w=[y.b])
            layer_norm(K, ph, y, g_bc, b_bc, o, "%s%d" % (tag, t))
            S.dma("sp", lambda: nc.sync.dma_start(out=dst_fn(t0), in_=o[:, :]), r=[o.b], w=[dst_buf_fn(t)])
        S.barrier()


def prologue_tables(K):
    nc, S, d = K.nc, K.S, K.d
    ph = Phase(K)
    with ph.es:
        inb = [ph.tile("tin%d" % i, [128, 4, 2048]) for i in range(3)]
        outb = [ph.tile("tout%d" % i, [128, 4, 2048], BF16) for i in range(3)]
        k = 0
        for layer in range(2):
            src = d["p_uv%d" % layer].rearrange("(p j) n -> p j n", p=128)
            dst = d["p_uvb%d" % layer].rearrange("(p j) n -> p j n", p=128)
            for c in range(32):
                a, b = inb[k % 3], outb[k % 3]
                S.dma("sp", lambda: nc.sync.dma_start(out=a[:, :, :], in_=src[:, 4 * c:4 * c + 4, :]), w=[a.b])
                if k % 2 == 0:
                    S.op("act", lambda: nc.scalar.copy(out=b[:, :, :], in_=a[:, :, :]), r=[a.b], w=[b.b])
                else:
                    S.op("dve", lambda: nc.vector.tensor_copy(out=b[:, :, :], in_=a[:, :, :]), r=[a.b], w=[b.b])
                S.dma("sp", lambda: nc.sync.dma_start(out=dst[:, 4 * c:4 * c + 4, :], in_=b[:, :, :]), r=[b.b], w=[K.db("uvb", layer, c)])
                k += 1
        S.barrier()


def phase_peer(K, s, layer, src_fn, src_buf_fn, dst_fn, dst_buf_fn, ntiles=NT):
    nc, S, d = K.nc, K.S, K.d
    ph = Phase(K)
    NG = 11
    K.bank_list = [0, 1, 2, 3, 4, 5]
    accb = [K.ps[6], K.ps[7]]
    with ph.es:
        wq = ph.tile("pwq", [128, 8, 2048])
        S.dma("sp", lambda: nc.sync.dma_start(out=wq[:, :, :], in_=d["p_w_q"][layer].rearrange("(c p) n -> p c n", p=128)), w=[wq.b])
        skT = ph.tile("skT", [128, 16, 128])
        S.dma("sp", lambda: nc.sync.dma_start(out=skT[:, :, :], in_=d["p_skT"][layer]), w=[skT.b])
        g_bc = load_bc(K, ph, "pg_bc", d["ln_g"][2 * layer + 1:2 * layer + 2, :], 1024)
        b_bc = load_bc(K, ph, "pb_bc", d["ln_b"][2 * layer + 1:2 * layer + 2, :], 1024)
        iota = ph.tile("iota", [128, 16])
        S.dma("sp", lambda: nc.sync.dma_start(out=iota[:, :], in_=d["iota16"]), w=[iota.b])
        identb = ph.tile("identb", [128, 128], BF16)
        S.op("dve", lambda: nc.vector.tensor_copy(out=identb[:, :], in_=K.ident[:, :]), r=[K.ident.b], w=[identb.b])
        G = [ph.tile("G%d" % i, [128, 2048], BF16) for i in range(NG)]
        hs = [ph.tile("ph%d" % i, [128, 1024]) for i in range(2)]
        eidxs = [ph.tile("peidx%d" % i, [128, 128], I32) for i in range(2)]
        gws = [ph.tile("pgw%d" % i, [128, 128]) for i in range(2)]
        hT = ph.tile("phT", [128, 8, 128])
        q_sb = ph.tile("pq", [128, 2048])
        qT = ph.tile("pqT", [128, 16, 128])
        sc = Tile(q_sb.t[:, :].rearrange("p (a b) -> p a b", b=128), "sc_alias")
        sc.b = q_sb.b
        sc2 = ph.tile("psc2", [128, 128])
        m1 = ph.tile("pm1", [128, 16, 16])
        i1 = ph.tile("pi1", [128, 16, 16], U32)
        i1f = ph.tile("pi1f", [128, 16, 16])
        cand2 = ph.tile("pcand2", [128, 256])
        best = ph.tile("pbest", [128, 8, 16])
        pos = ph.tile("ppos", [128, 8, 16], U32)
        hi = ph.tile("phi", [128, 8, 16], U32)
        lo = ph.tile("plo", [128, 8, 16], U32)
        hif = ph.tile("phif", [128, 8, 16])
        lof = ph.tile("plof", [128, 8, 16])
        eq = ph.tile("peq", [128, 8, 16, 16])
        cand = Tile(eq.t[:, :, :, :].rearrange("p h i j -> p h (i j)"), "cand_alias")
        cand.b = eq.b
        e0 = ph.tile("pe0", [128, 8, 16])
        e1 = ph.tile("pe1", [128, 8, 16])
        ef = ph.tile("pef", [128, 128])
        bm = ph.tile("pbm", [128, 8, 16])
        se = ph.tile("pse", [128, 8])
        a = ph.tile("pa", [128, 128])
        ga = ph.tile("pga", [128, 128])
        w = ph.tile("pw", [128, 128])
        diags = [ph.tile("pdiag%d" % i, [128, 4, 128], BF16) for i in range(3)]
        prods = [ph.tile("pprod%d" % i, [128, 1024], BF16) for i in range(6)]
        hbs = [ph.tile("phb%d" % i, [128, 1024], BF16) for i in range(2)]
        y = ph.tile("py", [128, 1024])
        o = ph.tile("po", [128, 1024])
        m1v = m1[:, :, :].rearrange("p (h two) k -> p h two k", two=2)
        i1fv = i1f[:, :, :].rearrange("p (h two) k -> p h two k", two=2)
        iota_b = iota[:, :].unsqueeze(1).unsqueeze(1).to_broadcast([128, 8, 16, 16])
        ident_b = identb[:, :].unsqueeze(1).to_broadcast([128, 4, 128])

        def front(t):
            t0 = t * 128
            h, eidx, gw = hs[t % 2], eidxs[t % 2], gws[t % 2]
            S.dma("sp", lambda: nc.sync.dma_start(out=h[:, :], in_=src_fn(t0)), r=src_buf_fn(t), w=[h.b])
            transpose_to(K, h, [h[:, c * 128:(c + 1) * 128] for c in range(8)], hT,
                         lambda g0, n: hT[:, g0:g0 + n, :].rearrange("p a b -> p (a b)"), 128)
            yield
            for n in range(4):
                bk = K.bank()
                for c in range(8):
                    S.op("pe", lambda: nc.tensor.matmul(bk[:, 0:512], lhsT=hT[:, c, :], rhs=wq[:, c, n * 512:(n + 1) * 512],
                                                        start=(c == 0), stop=(c == 7)), r=[hT.b, wq.b], w=[bk.b])
                evac(K, n, q_sb[:, n * 512:(n + 1) * 512], bk[:, 0:512], r=[bk.b], w=[q_sb.b])
                yield
            transpose_to(K, q_sb, [q_sb[:, c * 128:(c + 1) * 128] for c in range(16)], qT,
                         lambda g0, n: qT[:, g0:g0 + n, :].rearrange("p a b -> p (a b)"), 128)
            yield
            for n in range(4):
                bk = K.bank()
                for j in range(4):
                    hp = n * 4 + j
                    S.op("pe", lambda: nc.tensor.matmul(bk[:, j * 128:(j + 1) * 128], lhsT=qT[:, hp, :], rhs=skT[:, hp, :],
                                                        start=True, stop=True), r=[qT.b, skT.b], w=[bk.b])
                evac(K, n, sc[:, n * 4:(n + 1) * 4, :].rearrange("p a b -> p (a b)"), bk[:, 0:512], r=[bk.b], w=[sc.b])
            yield
            for hp in range(16):
                S.op("dve", lambda: nc.vector.max(out=m1[:, hp, 0:8], in_=sc[:, hp, :]), r=[sc.b], w=[m1.b])
                S.op("dve", lambda: nc.vector.max_index(out=i1[:, hp, 0:8], in_max=m1[:, hp, 0:8], in_values=sc[:, hp, :]),
                     r=[sc.b, m1.b], w=[i1.b])
                S.op("dve", lambda: nc.vector.match_replace(out=sc2[:, :], in_to_replace=m1[:, hp, 0:8],
                                                            in_values=sc[:, hp, :], imm_value=NEG), r=[sc.b, m1.b], w=[sc2.b])
                S.op("dve", lambda: nc.vector.max(out=m1[:, hp, 8:16], in_=sc2[:, :]), r=[sc2.b], w=[m1.b])
                S.op("dve", lambda: nc.vector.max_index(out=i1[:, hp, 8:16], in_max=m1[:, hp, 8:16], in_values=sc2[:, :]),
                     r=[sc2.b, m1.b], w=[i1.b])
                yield
            S.op("dve", lambda: nc.vector.tensor_tensor(
                out=cand[:, :, :].rearrange("p h (i j) -> p h i j", j=16),
                in0=m1v[:, :, 0, :].unsqueeze(3).to_broadcast([128, 8, 16, 16]),
                in1=m1v[:, :, 1, :].unsqueeze(2).to_broadcast([128, 8, 16, 16]), op=ALU.add), r=[m1.b], w=[cand.b])
            for hh in range(8):
                S.op("dve", lambda: nc.vector.max(out=best[:, hh, 0:8], in_=cand[:, hh, :]), r=[cand.b], w=[best.b])
                S.op("dve", lambda: nc.vector.max_index(out=pos[:, hh, 0:8], in_max=best[:, hh, 0:8], in_values=cand[:, hh, :]),
                     r=[cand.b, best.b], w=[pos.b])
                S.op("dve", lambda: nc.vector.match_replace(out=cand2[:, :], in_to_replace=best[:, hh, 0:8],
                                                            in_values=cand[:, hh, :], imm_value=NEG), r=[cand.b, best.b], w=[cand2.b])
                S.op("dve", lambda: nc.vector.max(out=best[:, hh, 8:16], in_=cand2[:, :]), r=[cand2.b], w=[best.b])
                S.op("dve", lambda: nc.vector.max_index(out=pos[:, hh, 8:16], in_max=best[:, hh, 8:16], in_values=cand2[:, :]),
                     r=[cand2.b, best.b], w=[pos.b])
                yield
            S.op("dve", lambda: nc.vector.tensor_tensor(out=bm[:, :, :], in0=best[:, :, :],
                                                        in1=best[:, :, 0:1].to_broadcast([128, 8, 16]), op=ALU.subtract),
                 r=[best.b], w=[bm.b])
            S.op("act", lambda: nc.scalar.activation(out=bm[:, :, :], in_=bm[:, :, :], func=AF.Exp), r=[bm.b], w=[bm.b])
            S.op("dve", lambda: nc.vector.tensor_reduce(out=se[:, :], in_=bm[:, :, :], axis=AX.X, op=ALU.add), r=[bm.b], w=[se.b])
            S.op("dve", lambda: nc.vector.reciprocal(out=se[:, :], in_=se[:, :]), r=[se.b], w=[se.b])
            S.op("dve", lambda: nc.vector.tensor_tensor(out=gw[:, :].rearrange("p (h k) -> p h k", k=16), in0=bm[:, :, :],
                                                        in1=se[:, :].unsqueeze(2).to_broadcast([128, 8, 16]), op=ALU.mult),
                 r=[bm.b, se.b], w=[gw.b])
            yield
            S.op("dve", lambda: nc.vector.tensor_single_scalar(out=hi[:, :, :], in_=pos[:, :, :], scalar=4,
                                                               op=ALU.logical_shift_right), r=[pos.b], w=[hi.b])
            S.op("dve", lambda: nc.vector.tensor_single_scalar(out=lo[:, :, :], in_=pos[:, :, :], scalar=15,
                                                               op=ALU.bitwise_and), r=[pos.b], w=[lo.b])
            S.op("dve", lambda: nc.vector.tensor_copy(out=hif[:, :, :], in_=hi[:, :, :]), r=[hi.b], w=[hif.b])
            S.op("dve", lambda: nc.vector.tensor_copy(out=lof[:, :, :], in_=lo[:, :, :]), r=[lo.b], w=[lof.b])
            S.op("dve", lambda: nc.vector.tensor_copy(out=i1f[:, :, :], in_=i1[:, :, :]), r=[i1.b], w=[i1f.b])
            yield
            for (xf, half, eo) in ((hif, 0, e0), (lof, 1, e1)):
                S.op("dve", lambda: nc.vector.tensor_tensor(out=eq[:, :, :, :],
                                                            in0=xf[:, :, :].unsqueeze(3).to_broadcast([128, 8, 16, 16]),
                                                            in1=iota_b, op=ALU.is_equal), r=[xf.b, iota.b], w=[eq.b])
                S.op("dve", lambda: nc.vector.tensor_tensor(out=eq[:, :, :, :], in0=eq[:, :, :, :],
                                                            in1=i1fv[:, :, half, :].unsqueeze(2).to_broadcast([128, 8, 16, 16]),
                                                            op=ALU.mult), r=[eq.b, i1f.b], w=[eq.b])
                S.op("dve", lambda: nc.vector.tensor_reduce(out=eo[:, :, :], in_=eq[:, :, :, :], axis=AX.X, op=ALU.add),
                     r=[eq.b], w=[eo.b])
                yield
            S.op("dve", lambda: nc.vector.scalar_tensor_tensor(out=ef[:, :], in0=e0[:, :, :].rearrange("p h k -> p (h k)"),
                                                               scalar=128.0, in1=e1[:, :, :].rearrange("p h k -> p (h k)"),
                                                               op0=ALU.mult, op1=ALU.add), r=[e0.b, e1.b], w=[ef.b])
            S.op("dve", lambda: nc.vector.tensor_copy(out=eidx[:, :], in_=ef[:, :]), r=[ef.b], w=[eidx.b])
            yield

        gi = 0
        pi = 0
        di = 0
        a_b = [Buf("a%d" % i) for i in range(128)]
        ga_b = [Buf("ga%d" % i) for i in range(32)]
        w_b = [Buf("w%d" % i) for i in range(32)]

        def main(t, gen):
            nonlocal gi, pi, di
            t0 = t * 128
            h, eidx, gw = hs[t % 2], eidxs[t % 2], gws[t % 2]
            hb = hbs[t % 2]
            S.op("act", lambda: nc.scalar.copy(out=hb[:, :], in_=h[:, :]), r=[h.b], w=[hb.b])
            for g0 in range(0, 128, 4):
                gs = []
                for sl in range(g0, g0 + 4):
                    Gt = G[gi % NG]
                    gi += 1
                    gs.append(Gt)
                    S.dma("pool", lambda: nc.gpsimd.indirect_dma_start(
                        out=Gt[:, :], out_offset=None, in_=d["p_uvb%d" % layer],
                        in_offset=bass.IndirectOffsetOnAxis(ap=eidx[:, sl:sl + 1], axis=0)), r=[eidx.b], w=[Gt.b])
                    pj = prods[pi % 6]
                    pi += 1
                    S.op("dve", lambda: nc.vector.tensor_tensor(out=pj[:, :], in0=Gt[:, 0:1024], in1=hb[:, :], op=ALU.mult),
                         r=[Gt.b, hb.b], w=[pj.b])
                    if 'noacc' not in XP:
                        S.op("act", lambda: nc.scalar.activation(out=pj[:, :], in_=pj[:, :], func=AF.Identity,
                                                                 accum_out=a[:, sl:sl + 1]), r=[pj.b], w=[pj.b, a_b[sl]])
                gq = g0 // 4
                S.op("act", lambda: nc.scalar.activation(out=ga[:, g0:g0 + 4], in_=a[:, g0:g0 + 4], func=AF.Gelu),
                     r=a_b[g0:g0 + 4], w=[ga_b[gq]])
                S.op("dve", lambda: nc.vector.tensor_tensor(out=w[:, g0:g0 + 4], in0=ga[:, g0:g0 + 4], in1=gw[:, g0:g0 + 4],
                                                            op=ALU.mult), r=[ga_b[gq], gw.b], w=[w_b[gq]])
                dg = diags[di % 3]
                di += 1
                S.op("dve", lambda: nc.vector.tensor_tensor(out=dg[:, :, :], in0=ident_b,
                                                            in1=w[:, g0:g0 + 4].unsqueeze(2).to_broadcast([128, 4, 128]),
                                                            op=ALU.mult), r=[identb.b, w_b[gq]], w=[dg.b])
                for j, sl in enumerate(range(g0, g0 + 4)):
                    Gt = gs[j]
                    for n in range(0 if ('nope' in XP and sl not in (0, 127)) else 2):
                        S.op("pe", lambda: nc.tensor.matmul(accb[n][:, 0:512], lhsT=dg[:, j, :],
                                                            rhs=Gt[:, 1024 + n * 512:1024 + (n + 1) * 512],
                                                            start=(sl == 0), stop=(sl == 127)), r=[dg.b, Gt.b], w=[accb[n].b])
                if gen is not None and 'nofront' not in XP:
                    next(gen, None)
                    next(gen, None)
            for n in range(2):
                S.op("dve", lambda: nc.vector.scalar_tensor_tensor(out=y[:, n * 512:(n + 1) * 512], in0=h[:, n * 512:(n + 1) * 512],
                                                                   scalar=ALPHA, in1=accb[n][:, 0:512], op0=ALU.mult, op1=ALU.add),
                     r=[h.b, accb[n].b], w=[y.b])
            layer_norm(K, ph, y, g_bc, b_bc, o, "p%d_%d" % (layer, t))
            S.dma("sp", lambda: nc.sync.dma_start(out=dst_fn(t0), in_=o[:, :]), r=[o.b], w=[dst_buf_fn(t)])

        for _ in front(0):
            pass
        for t in range(ntiles):
            gen = front(t + 1) if t + 1 < ntiles else None
            main(t, gen)
            if gen is not None:
                for _ in gen:
                    pass
        S.barrier()
    K.bank_list = list(range(8))


def phase_b0(K, s):
    nc, S, d = K.nc, K.S, K.d
    ph = Phase(K)
    with ph.es:
        stage = ph.tile("wstage", [128, 12288])
        wkv = load_w_bf16(K, ph, "swkv", d["s_w_kv"].rearrange("(c p) n -> p c n", p=128), [128, 8, 1536], stage)
        wqb = load_w_bf16(K, ph, "bwin", d["b_w_in"].rearrange("(c p) n -> p c n", p=128), [128, 8, 1072], stage)
        h = ph.tile("bh", [128, 1024])
        cs = ph.tile("bcs", [128, 64])
        hT = ph.tile("bhT", [128, 8, 128], BF16)
        kvs = ph.tile("kvs", [128, 3, 4, 128])
        q_sb = ph.tile("bq", [128, 16, 64])
        qr = ph.tile("bqr", [128, 16, 64])
        kr = ph.tile("bkr", [128, 2, 4, 64])
        gts = ph.tile("bgts", [128, 48])
        vps = [ph.tile("bvp%d" % i, [128, 4, 65], BF16) for i in range(2)]
        qT = ph.tile("bqT", [64, 16, 128], BF16)
        qrT = ph.tile("bqrT", [64, 16, 128], BF16)
        kT = ph.tile("bkT", [64, 8, 128])
        kTr = ph.tile("bkTr", [64, 8, 128], BF16)
        for v in vps:
            S.op("dve", lambda: nc.vector.memset(v[:, :, 64:65], 1.0), w=[v.b])
        for t in range(K.ntiles):
            t0 = t * 128
            S.dma("sp", lambda: nc.sync.dma_start(out=h[:, :], in_=d["h2"][t0:t0 + 128, :]), r=[K.db("h2", s, t)], w=[h.b])
            S.dma("sp", lambda: nc.sync.dma_start(out=cs[:, :], in_=d["rope64"][t0:t0 + 128, :]), w=[cs.b])
            transpose_to(K, h, [h[:, c * 128:(c + 1) * 128] for c in range(8)], hT,
                         lambda g0, n: hT[:, g0:g0 + n, :].rearrange("p a b -> p (a b)"), 128)
            kv_flat = kvs[:, :, :, :].rearrange("p a b c -> p (a b c)")
            for n in range(3):
                bk = K.bank()
                for c in range(8):
                    S.op("pe", lambda: nc.tensor.matmul(bk[:, 0:512], lhsT=hT[:, c, :], rhs=wkv[:, c, n * 512:(n + 1) * 512],
                                                        start=(c == 0), stop=(c == 7)), r=[hT.b, wkv.b], w=[bk.b])
                evac(K, n, kv_flat[:, n * 512:(n + 1) * 512], bk[:, 0:512], r=[bk.b], w=[kvs.b])
            q_flat = q_sb[:, :, :].rearrange("p a b -> p (a b)")
            for n in range(2):
                bk = K.bank()
                for c in range(8):
                    S.op("pe", lambda: nc.tensor.matmul(bk[:, 0:512], lhsT=hT[:, c, :], rhs=wqb[:, c, n * 512:(n + 1) * 512],
                                                        start=(c == 0), stop=(c == 7)), r=[hT.b, wqb.b], w=[bk.b])
                evac(K, n + 1, q_flat[:, n * 512:(n + 1) * 512], bk[:, 0:512], r=[bk.b], w=[q_sb.b])
            bk = K.bank()
            for c in range(8):
                S.op("pe", lambda: nc.tensor.matmul(bk[:, 0:48], lhsT=hT[:, c, :], rhs=wqb[:, c, 1024:1072],
                                                    start=(c == 0), stop=(c == 7)), r=[hT.b, wqb.b], w=[bk.b])
            S.op("act", lambda: nc.scalar.activation(out=gts[:, :], in_=bk[:, 0:48], func=AF.Sigmoid), r=[bk.b], w=[gts.b])
            S.dma("sp", lambda: nc.sync.dma_start(out=d["gates"][t0:t0 + 128, :], in_=gts[:, :]), r=[gts.b], w=[K.db("b0", s, t)])
            cos_b = cs[:, 0:32].unsqueeze(1).to_broadcast([128, 16, 32])
            sin_b = cs[:, 32:64].unsqueeze(1).to_broadcast([128, 16, 32])
            rope_tok(K, ph, q_sb[:, :, 0:32], q_sb[:, :, 32:64], qr[:, :, 0:32], qr[:, :, 32:64], cos_b, sin_b,
                     [128, 16, 32], [q_sb.b, cs.b], [qr.b], "brq%d" % t)
            cos_k = cs[:, 0:32].unsqueeze(1).to_broadcast([128, 4, 32])
            sin_k = cs[:, 32:64].unsqueeze(1).to_broadcast([128, 4, 32])
            for br in range(2):
                rope_tok(K, ph, kvs[:, 1 + br, :, 0:32], kvs[:, 1 + br, :, 32:64], kr[:, br, :, 0:32], kr[:, br, :, 32:64],
                         cos_k, sin_k, [128, 4, 32], [kvs.b, cs.b], [kr.b], "brk%d_%d" % (t, br))
            transpose_to(K, q_sb, [q_sb[:, hh, :] for hh in range(16)], qT,
                         lambda g0, n: qT[:, g0:g0 + n, :].rearrange("p a b -> p (a b)"), 64)
            transpose_to(K, qr, [qr[:, hh, :] for hh in range(16)], qrT,
                         lambda g0, n: qrT[:, g0:g0 + n, :].rearrange("p a b -> p (a b)"), 64)
            srcs = [kvs[:, 0, g, 0:64] for g in range(4)] + [kvs[:, 0, g, 64:128] for g in range(4)]
            transpose_to(K, kvs, srcs, kT, lambda g0, n: kT[:, g0:g0 + n, :].rearrange("p a b -> p (a b)"), 64)
            srcs = [kr[:, 0, g, :] for g in range(4)] + [kr[:, 1, g, :] for g in range(4)]
            transpose_to(K, kr, srcs, kTr, lambda g0, n: kTr[:, g0:g0 + n, :].rearrange("p a b -> p (a b)"), 64)
            wb = [K.db("b0", s, t)]
            S.dma("sp", lambda: nc.sync.dma_start(out=d["bqT"][:, :, t0:t0 + 128].rearrange("h e t -> e h t"), in_=qT[:, :, :]),
                  r=[qT.b], w=wb)
            S.dma("sp", lambda: nc.sync.dma_start(out=d["bqrT"][:, :, t0:t0 + 128].rearrange("h e t -> e h t"), in_=qrT[:, :, :]),
                  r=[qrT.b], w=wb)
            S.dma("sp", lambda: nc.sync.dma_start(out=d["bkcT"][:, :, t0:t0 + 128].rearrange("h e t -> e h t"), in_=kT[:, :, :]),
                  r=[kT.b], w=wb)
            S.dma("sp", lambda: nc.sync.dma_start(out=d["bkrT"][:, :, t0:t0 + 128].rearrange("h e t -> e h t"), in_=kTr[:, :, :]),
                  r=[kTr.b], w=wb)
            for br in range(2):
                vp = vps[br]
                S.op("act", lambda: nc.scalar.copy(out=vp[:, :, 0:64], in_=kvs[:, 1 + br, :, 64:128]), r=[kvs.b], w=[vp.b])
                S.dma("sp", lambda: nc.sync.dma_start(out=d["bvp"][br, :, t0:t0 + 128, :].rearrange("g t e -> t g e"),
                                                      in_=vp[:, :, :]), r=[vp.b], w=wb)
        S.barrier()


def phase_b1(K, s):
    nc, S, d = K.nc, K.S, K.d
    ph = Phase(K)
    allb0 = [K.db("b0", s, t) for t in range(NT)]
    with ph.es:
        ovl = ph.tile("ovl", [128, 32])
        S.dma("sp", lambda: nc.sync.dma_start(out=ovl[:, :], in_=d["overlap"]), w=[ovl.b])
        for kv in range(2):
            nm = "k" if kv == 0 else "v"
            w1 = ph.tile("cw1" + nm, [64, 32, 256])
            S.dma("sp", lambda: nc.sync.dma_start(out=w1[:, :, :], in_=d["s_cmp_%s_w1" % nm].rearrange("(l e) n -> e l n", e=64)), w=[w1.b])
            w2 = ph.tile("cw2" + nm, [128, 2, 64])
            S.dma("sp", lambda: nc.sync.dma_start(out=w2[:, :, :], in_=d["s_cmp_%s_w2" % nm].rearrange("(c p) n -> p c n", p=128)), w=[w2.b])
            b1 = ph.tile("cb1" + nm, [128, 2])
            S.dma("sp", lambda: nc.sync.dma_start(out=b1[:, :], in_=d["s_cmp_%s_b1" % nm]), w=[b1.b])
            posT = ph.tile("cpos" + nm, [64, 32])
            S.dma("sp", lambda: nc.sync.dma_start(out=posT[:, :], in_=d["s_cmp_pos_%sT" % nm]), w=[posT.b])
            xT = ph.tile("cxT" + nm, [64, SEQ])
            Xp = ph.tile("cXp" + nm, [64, 32, 127])
            hid = ph.tile("chid" + nm, [128, 2, 127])
            res = ph.tile("cres" + nm, [128, 128], BF16)
            for g in range(4):
                S.dma("sp", lambda: nc.sync.dma_start(out=xT[:, :], in_=d["bkcT"][kv * 4 + g, :, :]), r=allb0, w=[xT.b])
                for half in range(2):
                    S.op("dve", lambda: nc.vector.tensor_tensor(
                        out=Xp[:, half * 16:(half + 1) * 16, :],
                        in0=xT[:, half * 16:half * 16 + 2032].rearrange("p (j l) -> p l j", l=16),
                        in1=posT[:, half * 16:(half + 1) * 16].unsqueeze(2).to_broadcast([64, 16, 127]), op=ALU.add),
                        r=[xT.b, posT.b], w=[Xp.b])
                for hc in range(2):
                    bk = K.bank()
                    for l in range(32):
                        S.op("pe", lambda: nc.tensor.matmul(bk[:, 0:127], lhsT=w1[:, l, hc * 128:(hc + 1) * 128], rhs=Xp[:, l, :],
                                                            start=(l == 0), stop=(l == 31)), r=[w1.b, Xp.b], w=[bk.b])
                    S.op("act", lambda: nc.scalar.activation(out=hid[:, hc, :], in_=bk[:, 0:127], func=AF.Gelu, bias=b1[:, hc:hc + 1],
                                                             scale=1.0), r=[bk.b, b1.b], w=[hid.b])
                bk = K.bank()
                if kv == 0:
                    for hc in range(2):
                        S.op("pe", lambda: nc.tensor.matmul(bk[0:64, 0:127], lhsT=w2[:, hc, :], rhs=hid[:, hc, :],
                                                            start=(hc == 0), stop=(hc == 1)), r=[w2.b, hid.b], w=[bk.b])
                    S.op("dve", lambda: nc.vector.memset(res[0:64, :], 0.0), w=[res.b])
                    S.op("dve", lambda: nc.vector.tensor_copy(out=res[0:64, 0:127], in_=bk[0:64, 0:127]), r=[bk.b], w=[res.b])
                    S.dma("sp", lambda: nc.sync.dma_start(out=d["kcmpT"][g, :, :], in_=res[0:64, :]), r=[res.b], w=[K.db("b1", s)])
                else:
                    for hc in range(2):
                        S.op("pe", lambda: nc.tensor.matmul(bk[0:127, 0:64], lhsT=hid[:, hc, :], rhs=w2[:, hc, :],
                                                            start=(hc == 0), stop=(hc == 1)), r=[w2.b, hid.b], w=[bk.b])
                    S.op("dve", lambda: nc.vector.memset(res[:, :], 0.0), w=[res.b])
                    S.op("dve", lambda: nc.vector.tensor_copy(out=res[0:127, 0:64], in_=bk[0:127, 0:64]), r=[bk.b], w=[res.b])
                    S.op("dve", lambda: nc.vector.memset(res[0:127, 64:65], 1.0), r=[], w=[res.b])
                    S.op("dve", lambda: nc.vector.tensor_copy(out=res[0:127, 65:97], in_=ovl[0:127, :]), r=[ovl.b], w=[res.b])
                    S.dma("sp", lambda: nc.sync.dma_start(out=d["vcmp"][g, :, :], in_=res[:, 0:97]), r=[res.b], w=[K.db("b1", s)])
        S.barrier()


def phase_b2(K, s):
    nc, S, d = K.nc, K.S, K.d
    scale = 64.0 ** -0.5
    ph = Phase(K)
    allb0 = [K.db("b0", s, t) for t in range(NT)]
    b1b = [K.db("b1", s)]
    with ph.es:
        cmask = ph.tile("cmask", [128, SEQ])
        S.dma("sp", lambda: nc.sync.dma_start(out=cmask[:, :], in_=d["cmask"]), w=[cmask.b])
        Eall32 = ph.tile("Eall32", [32, 16, 128])
        S.dma("sp", lambda: nc.sync.dma_start(out=Eall32[:, :, :], in_=d["Eall"]), w=[Eall32.b])
        Eall = ph.tile("Eall", [32, 16, 128], BF16)
        S.op("dve", lambda: nc.vector.tensor_copy(out=Eall[:, :, :], in_=Eall32[:, :, :]), r=[Eall32.b], w=[Eall.b])
        At = ph.tile("At", [128, 16, 32])
        Bt = ph.tile("Bt", [128, 16, 32])
        S.dma("sp", lambda: nc.sync.dma_start(out=At[:, :, :], in_=d["selA"].rearrange("(q p) n -> p q n", p=128)), w=[At.b])
        S.dma("sp", lambda: nc.sync.dma_start(out=Bt[:, :, :], in_=d["selB"].rearrange("(q p) n -> p q n", p=128)), w=[Bt.b])
        gts = ph.tile("gts", [128, 16, 48])
        S.dma("sp", lambda: nc.sync.dma_start(out=gts[:, :, :], in_=d["gates"].rearrange("(q p) n -> p q n", p=128)), r=allb0, w=[gts.b])
        ksT = ph.tile("ksT", [64, SEQ], BF16)
        kwT = ph.tile("kwT", [64, SEQ], BF16)
        vs = ph.tile("vs", [128, 16, 65], BF16)
        vw = ph.tile("vw", [128, 16, 65], BF16)
        kcT = ph.tile("kcT", [64, 128], BF16)
        vc = ph.tile("vc", [128, 97], BF16)
        qTs = [ph.tile("nqT%d" % i, [64, SEQ], BF16) for i in range(4)]
        qrTs = [ph.tile("nqrT%d" % i, [64, SEQ], BF16) for i in range(4)]
        comb = ph.tile("comb", [128, 4, 16, 64])
        imp = ph.tile("imp", [128, 16, 32])
        sel = ph.tile("sel", [128, 16, 32])
        selT = ph.tile("selT", [32, SEQ], BF16)
        tmp32 = ph.tile("tmp32", [128, 32])
        m8 = ph.tile("m8", [128, 16])
        pts = [ph.tile("npt%d" % i, [128, 512], BF16) for i in range(4)]
        mts = [ph.tile("nmt%d" % i, [128, 512], BF16) for i in range(2)]
        rd = ph.tile("nrd", [128, 2])
        otmp = ph.tile("notmp", [128, 64])
        ctr = 0
        for g in range(4):
            S.dma("sp", lambda: nc.sync.dma_start(out=ksT[:, :], in_=d["bkrT"][g, :, :]), r=allb0, w=[ksT.b])
            S.dma("sp", lambda: nc.sync.dma_start(out=kwT[:, :], in_=d["bkrT"][4 + g, :, :]), r=allb0, w=[kwT.b])
            S.dma("sp", lambda: nc.sync.dma_start(out=vs[:, :, :], in_=d["bvp"][0, g, :, :].rearrange("(k p) e -> p k e", p=128)), r=allb0, w=[vs.b])
            S.dma("sp", lambda: nc.sync.dma_start(out=vw[:, :, :], in_=d["bvp"][1, g, :, :].rearrange("(k p) e -> p k e", p=128)), r=allb0, w=[vw.b])
            S.dma("sp", lambda: nc.sync.dma_start(out=kcT[:, :], in_=d["kcmpT"][g, :, :]), r=b1b, w=[kcT.b])
            S.dma("sp", lambda: nc.sync.dma_start(out=vc[:, :], in_=d["vcmp"][g, :, :]), r=b1b, w=[vc.b])
            for j in range(4):
                hh = g * 4 + j
                S.dma("sp", lambda: nc.sync.dma_start(out=qTs[j][:, :], in_=d["bqT"][hh, :, :]), r=allb0, w=[qTs[j].b])
                S.dma("sp", lambda: nc.sync.dma_start(out=qrTs[j][:, :], in_=d["bqrT"][hh, :, :]), r=allb0, w=[qrTs[j].b])
            for j in range(4):
                hh = g * 4 + j
                qT = qTs[j]
                for qc in range(4):
                    st = K.bank()
                    pt = pts[ctr % 3]
                    ctr += 1
                    S.op("pe", lambda: nc.tensor.matmul(st[0:127, 0:512], lhsT=kcT[:, 0:127], rhs=qT[:, qc * 512:(qc + 1) * 512],
                                                        start=True, stop=True), r=[kcT.b, qT.b], w=[st.b])
                    S.op("act", lambda: nc.scalar.activation(out=pt[0:127, :], in_=st[0:127, 0:512], func=AF.Exp, scale=scale),
                         r=[st.b], w=[pt.b])
                    S.op("dve", lambda: nc.vector.tensor_tensor(out=pt[0:127, :], in0=pt[0:127, :],
                                                                in1=cmask[0:127, qc * 512:(qc + 1) * 512], op=ALU.mult),
                         r=[pt.b, cmask.b], w=[pt.b])
                    for jj in range(4):
                        qt = qc * 4 + jj
                        ob = K.bank()
                        S.op("pe", lambda: nc.tensor.matmul(ob[:, 0:97], lhsT=pt[0:127, jj * 128:(jj + 1) * 128], rhs=vc[0:127, :],
                                                            start=True, stop=True), r=[pt.b, vc.b], w=[ob.b])
                        S.op("dve", lambda: nc.vector.tensor_scalar_max(out=rd[:, 0:1], in0=ob[:, 64:65], scalar1=1e-30),
                             r=[ob.b], w=[rd.b])
                        S.op("dve", lambda: nc.vector.reciprocal(out=rd[:, 1:2], in_=rd[:, 0:1]), r=[rd.b], w=[rd.b])
                        S.op("dve", lambda: nc.vector.tensor_scalar(out=comb[:, j, qt, :], in0=ob[:, 0:64], scalar1=rd[:, 1:2],
                                                                    scalar2=gts[:, qt, hh:hh + 1], op0=ALU.mult, op1=ALU.mult),
                             r=[ob.b, rd.b, gts.b], w=[comb.b])
                        if j == 0:
                            S.op("dve", lambda: nc.vector.tensor_scalar(out=imp[:, qt, :], in0=ob[:, 65:97], scalar1=rd[:, 1:2],
                                                                        scalar2=None, op0=ALU.mult), r=[ob.b, rd.b], w=[imp.b])
                        else:
                            S.op("dve", lambda: nc.vector.scalar_tensor_tensor(out=imp[:, qt, :], in0=ob[:, 65:97], scalar=rd[:, 1:2],
                                                                               in1=imp[:, qt, :], op0=ALU.mult, op1=ALU.add),
                                 r=[ob.b, rd.b, imp.b], w=[imp.b])
            S.op("dve", lambda: nc.vector.tensor_tensor(out=imp[:, :, :], in0=imp[:, :, :], in1=At[:, :, :], op=ALU.mult),
                 r=[imp.b, At.b], w=[imp.b])
            S.op("dve", lambda: nc.vector.tensor_tensor(out=imp[:, :, :], in0=imp[:, :, :], in1=Bt[:, :, :], op=ALU.add),
                 r=[imp.b, Bt.b], w=[imp.b])
            for qt in range(16):
                S.op("dve", lambda: nc.vector.max(out=m8[:, 0:8], in_=imp[:, qt, :]), r=[imp.b], w=[m8.b])
                S.op("dve", lambda: nc.vector.match_replace(out=tmp32[:, :], in_to_replace=m8[:, 0:8], in_values=imp[:, qt, :],
                                                            imm_value=-3e38), r=[imp.b, m8.b], w=[tmp32.b])
                S.op("dve", lambda: nc.vector.max(out=m8[:, 8:16], in_=tmp32[:, :]), r=[tmp32.b], w=[m8.b])
                S.op("dve", lambda: nc.vector.tensor_scalar(out=sel[:, qt, :], in0=imp[:, qt, :], scalar1=m8[:, 15:16], scalar2=None,
                                                            op0=ALU.is_ge), r=[imp.b, m8.b], w=[sel.b])
            transpose_to(K, sel, [sel[:, qt, :] for qt in range(16)], selT,
                         lambda g0, n: selT[:, g0 * 128:(g0 + n) * 128], 32)
            for j in range(4):
                hh = g * 4 + j
                qrT = qrTs[j]
                for qc in range(4):
                    ob = [K.ps[4 + jj] for jj in range(4)]
                    nk = 4 * qc + 4
                    def s1(kt):
                        nonlocal ctr
                        st = K.ps[ctr % 2]
                        mb = K.ps[2 + ctr % 2]
                        pt = pts[ctr % 4]
                        ctr += 1
                        S.op("pe", lambda: nc.tensor.matmul(st[:, 0:512], lhsT=ksT[:, kt * 128:(kt + 1) * 128],
                                                            rhs=qrT[:, qc * 512:(qc + 1) * 512], start=True, stop=True),
                             r=[ksT.b, qrT.b], w=[st.b])
                        S.op("pe", lambda: nc.tensor.matmul(mb[:, 0:512], lhsT=Eall[:, kt, :], rhs=selT[:, qc * 512:(qc + 1) * 512],
                                                            start=True, stop=True), r=[Eall.b, selT.b], w=[mb.b])
                        j0 = max(0, kt - 4 * qc)
                        S.op("act", lambda: nc.scalar.activation(out=pt[:, j0 * 128:512], in_=st[:, j0 * 128:512], func=AF.Exp,
                                                                 scale=scale), r=[st.b], w=[pt.b])
                        S.op("dve", lambda: nc.vector.tensor_tensor(out=pt[:, j0 * 128:512], in0=pt[:, j0 * 128:512],
                                                                    in1=mb[:, j0 * 128:512], op=ALU.mult), r=[pt.b, mb.b], w=[pt.b])
                        if kt >= 4 * qc:
                            S.op("dve", lambda: nc.vector.tensor_tensor(out=pt[:, j0 * 128:(j0 + 1) * 128],
                                                                        in0=pt[:, j0 * 128:(j0 + 1) * 128],
                                                                        in1=K.tri_le[:, :], op=ALU.mult),
                                 r=[pt.b, K.tri_le.b], w=[pt.b])
                        return (kt, pt, j0)

                    def s2(c):
                        kt, pt, j0 = c
                        for jj in range(j0, 4):
                            qt = 4 * qc + jj
                            S.op("pe", lambda: nc.tensor.matmul(ob[jj][:, 0:65], lhsT=pt[:, jj * 128:(jj + 1) * 128],
                                                                rhs=vs[:, kt, :], start=(kt == 0), stop=(kt == qt)),
                                 r=[pt.b, vs.b], w=[ob[jj].b])
                    pend = []
                    for kt in range(nk):
                        pend.append(s1(kt))
                        if len(pend) > 2:
                            s2(pend.pop(0))
                    while pend:
                        s2(pend.pop(0))
                    for jj in range(4):
                        qt = 4 * qc + jj
                        nsa_combine(K, ob[jj], rd, otmp, comb, j, qt, gts, 16 + hh)
                def w1(qt):
                    nonlocal ctr
                    kts = list(range(max(0, qt - 4), qt + 1))
                    stA = K.ps[ctr % 2]
                    stB = K.ps[2 + ctr % 2]
                    ptA = pts[ctr % 4]
                    ptB = mts[ctr % 2]
                    ctr += 1
                    for i, kt in enumerate(kts):
                        st = stA if i < 4 else stB
                        S.op("pe", lambda: nc.tensor.matmul(st[:, (i % 4) * 128:(i % 4 + 1) * 128], lhsT=kwT[:, kt * 128:(kt + 1) * 128],
                                                            rhs=qrT[:, qt * 128:(qt + 1) * 128], start=True, stop=True),
                             r=[kwT.b, qrT.b], w=[st.b])
                    na = min(4, len(kts))
                    S.op("act", lambda: nc.scalar.activation(out=ptA[:, 0:na * 128], in_=stA[:, 0:na * 128], func=AF.Exp, scale=scale),
                         r=[stA.b], w=[ptA.b])
                    if len(kts) == 5:
                        S.op("act", lambda: nc.scalar.activation(out=ptB[:, 0:128], in_=stB[:, 0:128], func=AF.Exp, scale=scale),
                             r=[stB.b], w=[ptB.b])
                    for i, kt in enumerate(kts):
                        pt = ptA if i < 4 else ptB
                        sl = slice((i % 4) * 128, (i % 4 + 1) * 128)
                        if kt == qt:
                            S.op("dve", lambda: nc.vector.tensor_tensor(out=pt[:, sl], in0=pt[:, sl], in1=K.tri_le[:, :], op=ALU.mult),
                                 r=[pt.b, K.tri_le.b], w=[pt.b])
                        elif kt == qt - 4:
                            S.op("dve", lambda: nc.vector.tensor_tensor(out=pt[:, sl], in0=pt[:, sl], in1=K.tri_gt[:, :], op=ALU.mult),
                                 r=[pt.b, K.tri_gt.b], w=[pt.b])
                    return (qt, kts, ptA, ptB)

                def w2(c):
                    qt, kts, ptA, ptB = c
                    ob = K.ps[4 + qt % 4]
                    for i, kt in enumerate(kts):
                        pt = ptA if i < 4 else ptB
                        sl = slice((i % 4) * 128, (i % 4 + 1) * 128)
                        S.op("pe", lambda: nc.tensor.matmul(ob[:, 0:65], lhsT=pt[:, sl], rhs=vw[:, kt, :], start=(i == 0),
                                                            stop=(i == len(kts) - 1)), r=[pt.b, vw.b], w=[ob.b])
                    nsa_combine(K, ob, rd, otmp, comb, j, qt, gts, 32 + hh)
                pend = []
                for qt in range(16):
                    pend.append(w1(qt))
                    if len(pend) > 1:
                        w2(pend.pop(0))
                while pend:
                    w2(pend.pop(0))
                S.dma("sp", lambda: nc.sync.dma_start(
                    out=d["attn"][:, hh * 64:(hh + 1) * 64].rearrange("(q p) e -> p q e", p=128), in_=comb[:, j, :, :]),
                    r=[comb.b], w=[K.db("attn", s, qq, hh) for qq in range(4)])
        S.barrier()


def nsa_combine(K, ob, rd, otmp, comb, j, qt, gts, gcol):
    nc, S = K.nc, K.S
    S.op("dve", lambda: nc.vector.reciprocal(out=rd[:, 1:2], in_=ob[:, 64:65]), r=[ob.b], w=[rd.b])
    S.op("dve", lambda: nc.vector.tensor_scalar(out=otmp[:, :], in0=ob[:, 0:64], scalar1=rd[:, 1:2], scalar2=gts[:, qt, gcol:gcol + 1],
                                                op0=ALU.mult, op1=ALU.mult), r=[ob.b, rd.b, gts.b], w=[otmp.b])
    S.op("dve", lambda: nc.vector.tensor_tensor(out=comb[:, j, qt, :], in0=comb[:, j, qt, :], in1=otmp[:, :], op=ALU.add),
         r=[comb.b, otmp.b], w=[comb.b])

def build(nseq=4, stages=("a1", "a2", "a3"), dbg=(), ntiles=NT):
    nc = bass.Bass("TRN2", target_bir_lowering=False)
    K = Ctx()
    K.nc = nc
    K.uid = 0
    K.evi = 0
    K.ntiles = ntiles
    es = ExitStack()
    K.es = es
    d = {}
    K.d = d

    def din(name, shape, dtype=F32):
        d[name] = nc.dram_tensor(name, list(shape), dtype, kind="ExternalInput").ap()

    def dscr(name, shape, dtype=F32):
        kind = "ExternalOutput" if name in dbg else "Internal"
        if name + "_in" in dbg:
            kind = "ExternalInput"
        d[name] = nc.dram_tensor(name, list(shape), dtype, kind=kind).ap()

    din("x", [nseq, SEQ, D])
    din("a_w_in", [1024, 800])
    din("a_q_norm", [1, 512])
    din("a_kv_norm", [1, 256])
    din("a_w_q_up", [512, 1536])
    din("a_w_kv_up", [256, 2048])
    din("a_w_o", [1024, 1024])
    din("s_w_kv", [1024, 1536])
    din("b_w_in", [1024, 1072])
    din("b_w_o", [1024, 1024])
    for nm in ("k", "v"):
        din("s_cmp_%s_w1" % nm, [2048, 256])
        din("s_cmp_%s_b1" % nm, [128, 2])
        din("s_cmp_%s_w2" % nm, [256, 64])
        din("s_cmp_pos_%sT" % nm, [64, 32])
    din("p_w_q", [2, 1024, 2048])
    din("p_skT", [2, 128, 16, 128])
    din("p_uv0", [16384, 2048])
    din("p_uv1", [16384, 2048])
    din("ln_g", [4, 1024])
    din("ln_b", [4, 1024])
    hc = host_consts()
    for k, v in hc.items():
        din(k, v.shape)
    d["out"] = nc.dram_tensor("out", [nseq, SEQ, D], F32, kind="ExternalOutput").ap()
    dscr("qnT", [16, 64, SEQ], BF16)
    dscr("qpT", [16, 32, SEQ], BF16)
    dscr("knT", [16, 64, SEQ], BF16)
    dscr("kpT", [32, SEQ], BF16)
    dscr("vp", [16, SEQ, 65], BF16)
    dscr("attn", [SEQ, 1024])
    dscr("h1", [SEQ, 1024])
    dscr("h2", [SEQ, 1024])
    dscr("p_uvb0", [16384, 2048], BF16)
    dscr("p_uvb1", [16384, 2048], BF16)
    dscr("h3", [SEQ, 1024])
    dscr("gates", [SEQ, 48])
    dscr("bqT", [16, 64, SEQ], BF16)
    dscr("bqrT", [16, 64, SEQ], BF16)
    dscr("bkcT", [8, 64, SEQ])
    dscr("bkrT", [8, 64, SEQ], BF16)
    dscr("bvp", [2, 4, SEQ, 65], BF16)
    dscr("kcmpT", [4, 64, 128], BF16)
    dscr("vcmp", [4, 128, 97], BF16)

    with es:
        S = Sched(nc, es)
        K.S = S
        K.ps = [Tile(es.enter_context(nc.psum_tensor("ps%d" % i, [128, 512], F32)), "ps%d" % i) for i in range(8)]
        K.bank_i = 0

        K.bank_list = list(range(8))

        def bank():
            b = K.ps[K.bank_list[K.bank_i % len(K.bank_list)]]
            K.bank_i += 1
            return b
        K.bank = bank
        K.dbufs = {}

        def db(*key):
            if key not in K.dbufs:
                K.dbufs[key] = Buf(str(key))
            return K.dbufs[key]
        K.db = db
        gl = Phase(K)
        gl.es = es
        K.ident = gl.tile("ident", [128, 128])
        K.tri_le = gl.tile("tri_le", [128, 128])
        K.tri_gt = gl.tile("tri_gt", [128, 128])
        K.eps_ln = gl.tile("eps_ln", [128, 1])
        K.eps_rms = gl.tile("eps_rms", [128, 1])
        for nm in ("ident", "tri_le", "tri_gt"):
            tl = getattr(K, nm)
            S.dma("sp", lambda tl=tl, nm=nm: nc.sync.dma_start(out=tl[:, :], in_=d[nm]), w=[tl.b])
        S.op("dve", lambda: nc.vector.memset(K.eps_ln[:, :], LN_EPS), w=[K.eps_ln.b])
        S.op("dve", lambda: nc.vector.memset(K.eps_rms[:, :], RMS_EPS), w=[K.eps_rms.b])

        if "p0" in stages or "p1" in stages:
            prologue_tables(K)
        for s in range(nseq):
            if "a1" in stages:
                phase_a1(K, s)
            if "a2" in stages:
                phase_a2(K, s)
            if "a3" in stages:
                dstname = "h1" if "h1" in dbg or len(stages) > 3 else "h1"
                phase_oproj_ln(K, s, d["a_w_o"], lambda t0: d["x"][s, t0:t0 + 128, :], lambda t: [],
                               d["ln_g"][0:1, :], d["ln_b"][0:1, :],
                               lambda t0: d["h1"][t0:t0 + 128, :], lambda t: K.db("h1", s, t), "a")
            if "p0" in stages:
                phase_peer(K, s, 0, lambda t0: d["h1"][t0:t0 + 128, :], lambda t: [K.db("h1", s, t)],
                           lambda t0: d["h2"][t0:t0 + 128, :], lambda t: K.db("h2", s, t), ntiles=K.ntiles)
            if "b0" in stages:
                phase_b0(K, s)
            if "b1" in stages:
                phase_b1(K, s)
            if "b2" in stages:
                phase_b2(K, s)
            if "b3" in stages:
                phase_oproj_ln(K, s, d["b_w_o"], lambda t0: d["h2"][t0:t0 + 128, :], lambda t: [K.db("h2", s, t)],
                               d["ln_g"][2:3, :], d["ln_b"][2:3, :],
                               lambda t0: d["h3"][t0:t0 + 128, :], lambda t: K.db("h3", s, t), "b")
            if "p1" in stages:
                phase_peer(K, s, 1, lambda t0: d["h3"][t0:t0 + 128, :], lambda t: [K.db("h3", s, t)],
                           lambda t0: d["out"][s, t0:t0 + 128, :], lambda t: K.db("out", s, t), ntiles=K.ntiles)
        S.barrier()
    K.hc = hc
    return nc, K


ALL_STAGES = ("a1", "a2", "a3", "p0", "b0", "b1", "b2", "b3", "p1")
_CACHE = {}


def _f32(a):
    return np.ascontiguousarray(np.asarray(a), dtype=np.float32)


def kernel(x, a_w_in, a_q_norm, a_kv_norm, a_w_q_up, a_w_kv_up, a_w_o, b_w_in, b_w_o,
           s_w_kv, s_cmp_pos_k, s_cmp_pos_v, s_cmp_k_w1, s_cmp_k_b1, s_cmp_k_w2,
           s_cmp_v_w1, s_cmp_v_b1, s_cmp_v_w2, p_w_q, p_subkeys, p_u, p_v, ln_g, ln_b):
    x = np.asarray(x)
    B = x.shape[0]
    if "nc" not in _CACHE:
        _CACHE["nc"] = build(nseq=B // NCORES, stages=ALL_STAGES)
    nc, K = _CACHE["nc"]
    p_subkeys = np.asarray(p_subkeys)
    p_u = np.asarray(p_u)
    p_v = np.asarray(p_v)
    w = {
        "a_w_in": _f32(np.asarray(a_w_in)[0]), "a_q_norm": _f32(np.asarray(a_q_norm).reshape(1, 512)),
        "a_kv_norm": _f32(np.asarray(a_kv_norm).reshape(1, 256)), "a_w_q_up": _f32(np.asarray(a_w_q_up)[0]),
        "a_w_kv_up": _f32(np.asarray(a_w_kv_up)[0]), "a_w_o": _f32(np.asarray(a_w_o)[0]),
        "b_w_in": _f32(np.asarray(b_w_in)[0]), "b_w_o": _f32(np.asarray(b_w_o)[0]), "s_w_kv": _f32(s_w_kv),
        "s_cmp_k_w1": _f32(s_cmp_k_w1), "s_cmp_v_w1": _f32(s_cmp_v_w1),
        "s_cmp_k_w2": _f32(s_cmp_k_w2), "s_cmp_v_w2": _f32(s_cmp_v_w2),
        "s_cmp_k_b1": _f32(np.asarray(s_cmp_k_b1).reshape(2, 128).T), "s_cmp_v_b1": _f32(np.asarray(s_cmp_v_b1).reshape(2, 128).T),
        "s_cmp_pos_kT": _f32(np.asarray(s_cmp_pos_k).T), "s_cmp_pos_vT": _f32(np.asarray(s_cmp_pos_v).T),
        "p_w_q": _f32(p_w_q),
        "p_skT": _f32(np.stack([p_subkeys[l].reshape(16, 128, 128).transpose(2, 0, 1) for l in range(2)])),
        "p_uv0": _f32(np.concatenate([p_u[0], p_v[0]], axis=1)), "p_uv1": _f32(np.concatenate([p_u[1], p_v[1]], axis=1)),
        "ln_g": _f32(np.asarray(ln_g).reshape(4, 1024)), "ln_b": _f32(np.asarray(ln_b).reshape(4, 1024)),
    }
    w.update({k: _f32(v) for k, v in K.hc.items()})
    nper = B // NCORES
    in_maps = []
    for c in range(NCORES):
        m = dict(w)
        m["x"] = _f32(x[c * nper:(c + 1) * nper])
        in_maps.append(m)
    res = run_bass_kernel_spmd(nc, in_maps, core_ids=list(range(NCORES)))
    out = np.empty((B, SEQ, D), dtype=np.float32)
    for c in range(NCORES):
        out[c * nper:(c + 1) * nper] = np.asarray(res.results[c]["out"]).reshape(nper, SEQ, D)
    return out
```

```python
import numpy as np
import os
XP = os.environ.get('XP', '')
from contextlib import ExitStack, contextmanager
import concourse.bass as bass
import concourse.mybir as mybir
from concourse.bass_utils import run_bass_kernel_spmd

F32 = mybir.dt.float32
I32 = mybir.dt.int32
U32 = mybir.dt.uint32
BF16 = mybir.dt.bfloat16
AF = mybir.ActivationFunctionType
ALU = mybir.AluOpType
AX = mybir.AxisListType

SEQ = 2048
D = 1024
NT = 16
NCORES = 8
ALPHA = 4.0 ** 0.25
LN_EPS = 1e-5
RMS_EPS = 1e-6
NEG = -1e30


class Buf:
    __slots__ = ("name", "w", "r")

    def __init__(self, name=""):
        self.name = name
        self.w = None
        self.r = {}


class Tile:
    def __init__(self, t, name):
        self.t = t
        self.b = Buf(name)

    def __getitem__(self, k):
        return self.t[k]


class Sched:
    EPOCH = 20000

    def __init__(self, nc, es):
        self.nc = nc
        self.es = es
        self.eng = {"pe": nc.tensor, "act": nc.scalar, "dve": nc.vector, "pool": nc.gpsimd, "sp": nc.sync}
        self.cur = {}
        self.waited = {}
        self.nsem = 0
        self.ninst = 0
        for e in ("pe", "act", "dve", "pool"):
            self._new_epoch(e)
        self.rings = {}
        for q, n in (("sp", 24), ("pool", 12), ("act", 8)):
            self.rings[q] = [[self._sem(), 0] for _ in range(n)]
        self.ring_i = {q: 0 for q in self.rings}

    def _sem(self):
        self.nsem += 1
        return self.es.enter_context(self.nc.semaphore("s%d" % self.nsem))

    def _new_epoch(self, e):
        self.cur[e] = [self._sem(), 0]

    def _wait(self, e, tok, strict=False):
        sem, val, src = tok
        if src == e and e == "pe" and not strict:
            return
        key = (e, id(sem))
        if self.waited.get(key, 0) >= val:
            return
        self.eng[e].wait_ge(sem, val)
        self.ninst += 1
        self.waited[key] = val

    @staticmethod
    def _deps(r, w):
        toks = []
        for b in r:
            if b.w is not None:
                toks.append(b.w)
        for b in w:
            if b.w is not None:
                toks.append(b.w)
            toks.extend(b.r.values())
        return toks

    @staticmethod
    def _commit(tok, r, w, key):
        for b in w:
            b.w = tok
            b.r = {}
        for b in r:
            if b not in w:
                b.r[key] = tok

    def op(self, e, fn, r=(), w=()):
        for t in self._deps(r, w):
            self._wait(e, t)
        st = self.cur[e]
        if st[1] >= self.EPOCH:
            self._new_epoch(e)
            st = self.cur[e]
        ins = fn()
        st[1] += 1
        ins.then_inc(st[0], 1)
        self.ninst += 1
        tok = (st[0], st[1], e)
        self._commit(tok, r, w, e)
        return tok

    def dma(self, q, fn, r=(), w=()):
        ring = self.rings[q]
        i = self.ring_i[q]
        self.ring_i[q] = (i + 1) % len(ring)
        slot = ring[i]
        if slot[1] > 0:
            self._wait(q, (slot[0], slot[1], None), strict=True)
        for t in self._deps(r, w):
            self._wait(q, t, strict=True)
        ins = fn()
        slot[1] += 16
        ins.then_inc(slot[0], 16)
        self.ninst += 1
        tok = (slot[0], slot[1], None)
        self._commit(tok, r, w, ("dma", q, i))
        return tok

    def barrier(self):
        toks = []
        for e in ("pe", "act", "dve", "pool"):
            st = self.cur[e]
            if st[1] > 0:
                toks.append((st[0], st[1], e))
        for q, ring in self.rings.items():
            for slot in ring:
                if slot[1] > 0:
                    toks.append((slot[0], slot[1], None))
        for e in ("pe", "act", "dve", "pool", "sp"):
            for t in toks:
                self._wait(e, t, strict=(t[2] is None))


class Phase:
    def __init__(self, K):
        self.K = K
        self.es = ExitStack()

    def tile(self, name, shape, dtype=F32):
        self.K.uid += 1
        nm = "%s_%d" % (name, self.K.uid)
        return Tile(self.es.enter_context(self.K.nc.sbuf_tensor(nm, list(shape), dtype)), nm)


class Ctx:
    pass


def host_consts():
    c = {}
    c["ident"] = np.eye(128, dtype=np.float32)
    kk = np.arange(128)[:, None]
    qq = np.arange(128)[None, :]
    c["tri_le"] = (kk <= qq).astype(np.float32)
    c["tri_gt"] = (kk > qq).astype(np.float32)
    pos = np.arange(SEQ, dtype=np.float32)[:, None]
    for d in (32, 64):
        inv = (10000.0 ** (-np.arange(0, d, 2, dtype=np.float32) / d)).astype(np.float32)
        ang = (pos * inv[None, :]).astype(np.float32)
        c["rope%d" % d] = np.concatenate([np.cos(ang), np.sin(ang)], axis=1).astype(np.float32)
    c["iota16"] = np.tile(np.arange(16, dtype=np.float32)[None, :], (128, 1))
    cc = np.arange(128)
    tt = np.arange(SEQ)
    c["cmask"] = ((16 * cc[:, None] + 31 <= tt[None, :]) & (cc[:, None] < 127)).astype(np.float32)
    nn = np.arange(32)
    ovl = ((16 * cc[:, None] < 64 * nn[None, :] + 64) & (16 * cc[:, None] + 31 >= 64 * nn[None, :]) & (cc[:, None] < 127))
    c["overlap"] = ovl.astype(np.float32)
    cur = tt // 64
    forced = (nn[None, :] == 0) | ((nn[None, :] <= cur[:, None]) & (nn[None, :] > cur[:, None] - 2))
    valid = nn[None, :] <= cur[:, None]
    c["selA"] = ((~forced) & valid).astype(np.float32)
    c["selB"] = np.where(valid, np.where(forced, 1e9, 0.0), -1e30).astype(np.float32)
    E = np.zeros((32, 16, 128), np.float32)
    for kt in range(16):
        for k in range(128):
            E[(kt * 128 + k) // 64, kt, k] = 1.0
    c["Eall"] = E
    return c


def rope_tok(K, ph, x1, x2, o1, o2, cos_b, sin_b, shape, rbufs, wbufs, tmpname):
    nc, S = K.nc, K.S
    if not hasattr(ph, "rtmp"):
        ph.rtmp = {}
    key = tuple(shape)
    if key not in ph.rtmp:
        ph.rtmp[key] = (ph.tile("ropea", shape), ph.tile("ropeb", shape))
    ta, tb = ph.rtmp[key]
    sl = tuple(slice(None) for _ in shape)
    S.op("dve", lambda: nc.vector.tensor_tensor(out=ta[sl], in0=x1, in1=cos_b, op=ALU.mult), r=rbufs, w=[ta.b])
    S.op("dve", lambda: nc.vector.tensor_tensor(out=tb[sl], in0=x2, in1=sin_b, op=ALU.mult), r=rbufs, w=[tb.b])
    S.op("dve", lambda: nc.vector.tensor_tensor(out=o1, in0=ta[sl], in1=tb[sl], op=ALU.subtract), r=[ta.b, tb.b], w=wbufs)
    S.op("dve", lambda: nc.vector.tensor_tensor(out=ta[sl], in0=x2, in1=cos_b, op=ALU.mult), r=rbufs, w=[ta.b])
    S.op("dve", lambda: nc.vector.tensor_tensor(out=tb[sl], in0=x1, in1=sin_b, op=ALU.mult), r=rbufs, w=[tb.b])
    S.op("dve", lambda: nc.vector.tensor_tensor(out=o2, in0=ta[sl], in1=tb[sl], op=ALU.add), r=[ta.b, tb.b], w=wbufs)


def evac(K, i, out, in_, r, w):
    nc, S = K.nc, K.S
    if i % 2 == 0:
        S.op("act", lambda: nc.scalar.copy(out=out, in_=in_), r=r, w=w)
    else:
        S.op("dve", lambda: nc.vector.tensor_copy(out=out, in_=in_), r=r, w=w)


def transpose_to(K, src_tile, src_aps, dst_tile, dst_ap_fn, rows, group=4):
    nc, S = K.nc, K.S
    n = len(src_aps)
    for g0 in range(0, n, group):
        cnt = min(group, n - g0)
        bank = K.bank()
        for j in range(cnt):
            ap = src_aps[g0 + j]
            S.op("pe", lambda ap=ap, j=j: nc.tensor.transpose(out=bank[0:rows, j * 128:(j + 1) * 128], in_=ap,
                                                               identity=K.ident[:, :]),
                 r=[src_tile.b, K.ident.b], w=[bank.b])
        evac(K, K.evi, dst_ap_fn(g0, cnt), bank[0:rows, 0:cnt * 128], r=[bank.b], w=[dst_tile.b])
        K.evi += 1


def layer_norm(K, ph, y, g_bc, b_bc, out, tag):
    nc, S = K.nc, K.S
    if not hasattr(ph, "lntmp"):
        ph.lntmp = (ph.tile("lnst", [128, 2, 6]), ph.tile("lnmv", [128, 2]), ph.tile("lnsd", [128, 1]), ph.tile("lnrs", [128, 1]))
    st, mv, sd, rs = ph.lntmp
    for j in range(2):
        S.op("dve", lambda j=j: nc.vector.bn_stats(out=st[:, j, :], in_=y[:, j * 512:(j + 1) * 512]), r=[y.b], w=[st.b])
    S.op("dve", lambda: nc.vector.bn_aggr(out=mv[:, :], in_=st[:, :, :].rearrange("p a b -> p (a b)")), r=[st.b], w=[mv.b])
    S.op("act", lambda: nc.scalar.activation(out=sd[:, :], in_=mv[:, 1:2], func=AF.Sqrt, bias=K.eps_ln[:, :], scale=1.0),
         r=[mv.b, K.eps_ln.b], w=[sd.b])
    S.op("dve", lambda: nc.vector.reciprocal(out=rs[:, :], in_=sd[:, :]), r=[sd.b], w=[rs.b])
    S.op("dve", lambda: nc.vector.tensor_scalar(out=out[:, :], in0=y[:, :], scalar1=mv[:, 0:1], scalar2=rs[:, 0:1],
                                                op0=ALU.subtract, op1=ALU.mult), r=[y.b, mv.b, rs.b], w=[out.b])
    S.op("dve", lambda: nc.vector.tensor_tensor(out=out[:, :], in0=out[:, :], in1=g_bc[:, :], op=ALU.mult),
         r=[out.b, g_bc.b], w=[out.b])
    S.op("dve", lambda: nc.vector.tensor_tensor(out=out[:, :], in0=out[:, :], in1=b_bc[:, :], op=ALU.add),
         r=[out.b, b_bc.b], w=[out.b])


def load_bc(K, ph, name, dram_row_ap, n):
    nc, S = K.nc, K.S
    t = ph.tile(name, [128, n])
    S.dma("sp", lambda: nc.sync.dma_start(out=t[:, :], in_=dram_row_ap.to_broadcast([128, n])), w=[t.b])
    return t


def load_w_bf16(K, ph, name, dram_ap, shape, stage):
    nc, S = K.nc, K.S
    n = 1
    for v in shape[1:]:
        n *= v
    wt = ph.tile(name, shape, BF16)
    letters = "abcd"[:len(shape) - 1]
    pat = "p (%s) -> p %s" % (" ".join(letters), " ".join(letters))
    kw = {letters[i]: shape[1 + i] for i in range(1, len(letters))}
    sv = stage[:, 0:n].rearrange(pat, **kw) if len(shape) > 2 else stage[:, 0:n]
    S.dma("sp", lambda: nc.sync.dma_start(out=sv, in_=dram_ap), w=[stage.b])
    flat_w = wt[tuple(slice(None) for _ in shape)]
    S.op("dve", lambda: nc.vector.tensor_copy(out=flat_w, in_=sv), r=[stage.b], w=[wt.b])
    return wt

def phase_a1(K, s):
    nc, S, d = K.nc, K.S, K.d
    ph = Phase(K)
    with ph.es:
        stage = ph.tile("wstage", [128, 6400])
        w_in = load_w_bf16(K, ph, "w_in", d["a_w_in"].rearrange("(c p) n -> p c n", p=128), [128, 8, 800], stage)
        wq = load_w_bf16(K, ph, "wq", d["a_w_q_up"].rearrange("(c p) n -> p c n", p=128), [128, 4, 1536], stage)
        wkv = load_w_bf16(K, ph, "wkv", d["a_w_kv_up"].rearrange("(c p) n -> p c n", p=128), [128, 2, 2048], stage)
        qn_bc = load_bc(K, ph, "qn_bc", d["a_q_norm"], 512)
        kvn_bc = load_bc(K, ph, "kvn_bc", d["a_kv_norm"], 256)
        xs = [ph.tile("x%d" % i, [128, 1024]) for i in range(2)]
        css = [ph.tile("cs%d" % i, [128, 32]) for i in range(2)]
        xT = ph.tile("xT", [128, 8, 128], BF16)
        junk = ph.tile("junk", [128, 512])
        ss = ph.tile("ss", [128, 2])
        rs = ph.tile("rs", [128, 2])
        rr = ph.tile("rr", [128, 2])
        cq = ph.tile("cq", [128, 512])
        ckv = ph.tile("ckv", [128, 256])
        kraw = ph.tile("kraw", [128, 32])
        kpe = ph.tile("kpe", [128, 32])
        cqT = ph.tile("cqT", [128, 4, 128], BF16)
        ckvT = ph.tile("ckvT", [128, 2, 128], BF16)
        q_sb = ph.tile("q_sb", [128, 16, 96])
        qpe = ph.tile("qpe", [128, 16, 32])
        kv_sb = ph.tile("kv_sb", [128, 16, 128])
        vps = [ph.tile("vp%d" % i, [128, 16, 65], BF16) for i in range(2)]
        qnTs = [ph.tile("qnT%d" % i, [64, 16, 128], BF16) for i in range(2)]
        qpTs = [ph.tile("qpT%d" % i, [32, 16, 128], BF16) for i in range(2)]
        knTs = [ph.tile("knT%d" % i, [64, 16, 128], BF16) for i in range(2)]
        kpTs = [ph.tile("kpT%d" % i, [32, 128], BF16) for i in range(2)]
        for v in vps:
            S.op("dve", lambda v=v: nc.vector.memset(v[:, :, 64:65], 1.0), w=[v.b])
        for t in range(NT):
            p = t % 2
            t0 = t * 128
            x, cs, vp, qnT, qpT, knT, kpT = xs[p], css[p], vps[p], qnTs[p], qpTs[p], knTs[p], kpTs[p]
            S.dma("sp", lambda: nc.sync.dma_start(out=x[:, :], in_=d["x"][s, t0:t0 + 128, :]), w=[x.b])
            S.dma("sp", lambda: nc.sync.dma_start(out=cs[:, :], in_=d["rope32"][t0:t0 + 128, :]), w=[cs.b])
            transpose_to(K, x, [x[:, c * 128:(c + 1) * 128] for c in range(8)], xT,
                         lambda g0, n: xT[:, g0:g0 + n, :].rearrange("p a b -> p (a b)"), 128)
            bA, bB = K.bank(), K.bank()
            for c in range(8):
                S.op("pe", lambda c=c: nc.tensor.matmul(bA[:, 0:512], lhsT=xT[:, c, :], rhs=w_in[:, c, 0:512],
                                                        start=(c == 0), stop=(c == 7)), r=[xT.b, w_in.b], w=[bA.b])
            for c in range(8):
                S.op("pe", lambda c=c: nc.tensor.matmul(bB[:, 0:288], lhsT=xT[:, c, :], rhs=w_in[:, c, 512:800],
                                                        start=(c == 0), stop=(c == 7)), r=[xT.b, w_in.b], w=[bB.b])
            S.op("act", lambda: nc.scalar.activation(out=junk[:, 0:512], in_=bA[:, 0:512], func=AF.Square,
                                                     accum_out=ss[:, 0:1]), r=[bA.b], w=[junk.b, ss.b])
            S.op("act", lambda: nc.scalar.activation(out=junk[:, 0:256], in_=bB[:, 0:256], func=AF.Square,
                                                     accum_out=ss[:, 1:2]), r=[bB.b], w=[junk.b, ss.b])
            S.op("act", lambda: nc.scalar.activation(out=rs[:, 0:1], in_=ss[:, 0:1], func=AF.Sqrt, bias=K.eps_rms[:, :],
                                                     scale=1.0 / 512), r=[ss.b, K.eps_rms.b], w=[rs.b])
            S.op("act", lambda: nc.scalar.activation(out=rs[:, 1:2], in_=ss[:, 1:2], func=AF.Sqrt, bias=K.eps_rms[:, :],
                                                     scale=1.0 / 256), r=[ss.b, K.eps_rms.b], w=[rs.b])
            S.op("dve", lambda: nc.vector.reciprocal(out=rr[:, :], in_=rs[:, :]), r=[rs.b], w=[rr.b])
            S.op("dve", lambda: nc.vector.scalar_tensor_tensor(out=cq[:, :], in0=bA[:, 0:512], scalar=rr[:, 0:1],
                                                               in1=qn_bc[:, :], op0=ALU.mult, op1=ALU.mult),
                 r=[bA.b, rr.b, qn_bc.b], w=[cq.b])
            S.op("dve", lambda: nc.vector.scalar_tensor_tensor(out=ckv[:, :], in0=bB[:, 0:256], scalar=rr[:, 1:2],
                                                               in1=kvn_bc[:, :], op0=ALU.mult, op1=ALU.mult),
                 r=[bB.b, rr.b, kvn_bc.b], w=[ckv.b])
            S.op("act", lambda: nc.scalar.copy(out=kraw[:, :], in_=bB[:, 256:288]), r=[bB.b], w=[kraw.b])
            rope_tok(K, ph, kraw[:, 0:16], kraw[:, 16:32], kpe[:, 0:16], kpe[:, 16:32], cs[:, 0:16], cs[:, 16:32],
                     [128, 16], [kraw.b, cs.b], [kpe.b], "rk%d" % t)
            transpose_to(K, cq, [cq[:, c * 128:(c + 1) * 128] for c in range(4)], cqT,
                         lambda g0, n: cqT[:, g0:g0 + n, :].rearrange("p a b -> p (a b)"), 128)
            transpose_to(K, ckv, [ckv[:, c * 128:(c + 1) * 128] for c in range(2)], ckvT,
                         lambda g0, n: ckvT[:, g0:g0 + n, :].rearrange("p a b -> p (a b)"), 128)
            q_flat = q_sb[:, :, :].rearrange("p a b -> p (a b)")
            for n in range(3):
                bk = K.bank()
                for c in range(4):
                    S.op("pe", lambda c=c, n=n, bk=bk: nc.tensor.matmul(bk[:, 0:512], lhsT=cqT[:, c, :],
                                                                         rhs=wq[:, c, n * 512:(n + 1) * 512],
                                                                         start=(c == 0), stop=(c == 3)),
                         r=[cqT.b, wq.b], w=[bk.b])
                evac(K, n, q_flat[:, n * 512:(n + 1) * 512], bk[:, 0:512], r=[bk.b], w=[q_sb.b])
            kv_flat = kv_sb[:, :, :].rearrange("p a b -> p (a b)")
            for n in range(4):
                bk = K.bank()
                for c in range(2):
                    S.op("pe", lambda c=c, n=n, bk=bk: nc.tensor.matmul(bk[:, 0:512], lhsT=ckvT[:, c, :],
                                                                         rhs=wkv[:, c, n * 512:(n + 1) * 512],
                                                                         start=(c == 0), stop=(c == 1)),
                         r=[ckvT.b, wkv.b], w=[bk.b])
                evac(K, n + 1, kv_flat[:, n * 512:(n + 1) * 512], bk[:, 0:512], r=[bk.b], w=[kv_sb.b])
            cos_b = cs[:, 0:16].unsqueeze(1).to_broadcast([128, 16, 16])
            sin_b = cs[:, 16:32].unsqueeze(1).to_broadcast([128, 16, 16])
            rope_tok(K, ph, q_sb[:, :, 64:80], q_sb[:, :, 80:96], qpe[:, :, 0:16], qpe[:, :, 16:32], cos_b, sin_b,
                     [128, 16, 16], [q_sb.b, cs.b], [qpe.b], "rq%d" % t)
            S.op("act", lambda: nc.scalar.copy(out=vp[:, :, 0:64], in_=kv_sb[:, :, 64:128]), r=[kv_sb.b], w=[vp.b])
            transpose_to(K, q_sb, [q_sb[:, h, 0:64] for h in range(16)], qnT,
                         lambda g0, n: qnT[:, g0:g0 + n, :].rearrange("p a b -> p (a b)"), 64)
            transpose_to(K, qpe, [qpe[:, h, :] for h in range(16)], qpT,
                         lambda g0, n: qpT[:, g0:g0 + n, :].rearrange("p a b -> p (a b)"), 32)
            transpose_to(K, kv_sb, [kv_sb[:, h, 0:64] for h in range(16)], knT,
                         lambda g0, n: knT[:, g0:g0 + n, :].rearrange("p a b -> p (a b)"), 64)
            transpose_to(K, kpe, [kpe[:, :]], kpT, lambda g0, n: kpT[:, :], 32)
            wb = [K.db("qk", s, t)]
            S.dma("sp", lambda: nc.sync.dma_start(out=d["qnT"][:, :, t0:t0 + 128].rearrange("h e t -> e h t"),
                                                  in_=qnT[:, :, :]), r=[qnT.b], w=wb)
            S.dma("sp", lambda: nc.sync.dma_start(out=d["qpT"][:, :, t0:t0 + 128].rearrange("h e t -> e h t"),
                                                  in_=qpT[:, :, :]), r=[qpT.b], w=wb)
            S.dma("sp", lambda: nc.sync.dma_start(out=d["knT"][:, :, t0:t0 + 128].rearrange("h e t -> e h t"),
                                                  in_=knT[:, :, :]), r=[knT.b], w=wb)
            S.dma("sp", lambda: nc.sync.dma_start(out=d["kpT"][:, t0:t0 + 128], in_=kpT[:, :]), r=[kpT.b], w=wb)
            S.dma("sp", lambda: nc.sync.dma_start(out=d["vp"][:, t0:t0 + 128, :].rearrange("h t e -> t h e"),
                                                  in_=vp[:, :, :]), r=[vp.b], w=wb)
        S.barrier()


def attn_core(K, ph, pairs_q, pairs_k, vp, nkeys_tile, scale, o_sb_fn, store_fn, tag):
    pass


def phase_a2(K, s):
    nc, S, d = K.nc, K.S, K.d
    scale = 96.0 ** -0.5
    ph = Phase(K)
    with ph.es:
        kp = ph.tile("kp", [32, SEQ], BF16)
        allqk = [K.db("qk", s, t) for t in range(NT)]
        S.dma("sp", lambda: nc.sync.dma_start(out=kp[:, :], in_=d["kpT"][:, :]), r=allqk, w=[kp.b])
        qns = [ph.tile("qn%d" % i, [64, SEQ], BF16) for i in range(2)]
        qps = [ph.tile("qp%d" % i, [32, SEQ], BF16) for i in range(2)]
        kns = [ph.tile("kn%d" % i, [64, SEQ], BF16) for i in range(2)]
        vpt = [ph.tile("vpa%d" % i, [128, 16, 65], BF16) for i in range(2)]
        pts = [ph.tile("pt%d" % i, [128, 512], BF16) for i in range(4)]
        osb = [ph.tile("osb%d" % i, [128, 4, 64]) for i in range(2)]
        rden = ph.tile("rden", [128, 4])
        ctr = 0
        oc = 0
        for h in range(16):
            p = h % 2
            qn, qp, kn, vp = qns[p], qps[p], kns[p], vpt[p]
            S.dma("sp", lambda: nc.sync.dma_start(out=qn[:, :], in_=d["qnT"][h, :, :]), r=allqk, w=[qn.b])
            S.dma("sp", lambda: nc.sync.dma_start(out=qp[:, :], in_=d["qpT"][h, :, :]), r=allqk, w=[qp.b])
            S.dma("sp", lambda: nc.sync.dma_start(out=kn[:, :], in_=d["knT"][h, :, :]), r=allqk, w=[kn.b])
            S.dma("sp", lambda: nc.sync.dma_start(out=vp[:, :, :], in_=d["vp"][h, :, :].rearrange("(k p) e -> p k e", p=128)),
                  r=allqk, w=[vp.b])
            for qc in range(4):
                ob = [K.ps[4 + j] for j in range(4)]
                nk = 4 * qc + 4
                def s1(kt):
                    nonlocal ctr
                    st = K.ps[ctr % 4]
                    pt = pts[ctr % 4]
                    ctr += 1
                    S.op("pe", lambda: nc.tensor.matmul(st[:, 0:512], lhsT=kn[:, kt * 128:(kt + 1) * 128],
                                                        rhs=qn[:, qc * 512:(qc + 1) * 512], start=True, stop=False),
                         r=[kn.b, qn.b], w=[st.b])
                    S.op("pe", lambda: nc.tensor.matmul(st[:, 0:512], lhsT=kp[:, kt * 128:(kt + 1) * 128],
                                                        rhs=qp[:, qc * 512:(qc + 1) * 512], start=False, stop=True),
                         r=[kp.b, qp.b], w=[st.b])
                    j0 = max(0, kt - 4 * qc)
                    S.op("act", lambda: nc.scalar.activation(out=pt[:, j0 * 128:512], in_=st[:, j0 * 128:512], func=AF.Exp,
                                                             scale=scale), r=[st.b], w=[pt.b])
                    if kt >= 4 * qc:
                        S.op("dve", lambda: nc.vector.tensor_tensor(out=pt[:, j0 * 128:(j0 + 1) * 128],
                                                                    in0=pt[:, j0 * 128:(j0 + 1) * 128],
                                                                    in1=K.tri_le[:, :], op=ALU.mult),
                             r=[pt.b, K.tri_le.b], w=[pt.b])
                    return (kt, pt, j0)

                def s2(c):
                    kt, pt, j0 = c
                    for j in range(j0, 4):
                        qt = 4 * qc + j
                        S.op("pe", lambda: nc.tensor.matmul(ob[j][:, 0:65], lhsT=pt[:, j * 128:(j + 1) * 128],
                                                            rhs=vp[:, kt, :], start=(kt == 0), stop=(kt == qt)),
                             r=[pt.b, vp.b], w=[ob[j].b])
                pend = []
                for kt in range(nk):
                    pend.append(s1(kt))
                    if len(pend) > 2:
                        s2(pend.pop(0))
                while pend:
                    s2(pend.pop(0))
                o = osb[oc % 2]
                oc += 1
                for j in range(4):
                    S.op("dve", lambda j=j: nc.vector.reciprocal(out=rden[:, j:j + 1], in_=ob[j][:, 64:65]),
                         r=[ob[j].b], w=[rden.b])
                    S.op("dve", lambda j=j: nc.vector.tensor_scalar(out=o[:, j, :], in0=ob[j][:, 0:64],
                                                                    scalar1=rden[:, j:j + 1], scalar2=None, op0=ALU.mult),
                         r=[ob[j].b, rden.b], w=[o.b])
                S.dma("sp", lambda: nc.sync.dma_start(
                    out=d["attn"][qc * 512:(qc + 1) * 512, h * 64:(h + 1) * 64].rearrange("(j p) e -> p j e", p=128),
                    in_=o[:, :, :]), r=[o.b], w=[K.db("attn", s, qc, h)])
        S.barrier()


def phase_oproj_ln(K, s, w_o_ap, res_fn, res_deps_fn, g_ap, b_ap, dst_fn, dst_buf_fn, tag):
    nc, S, d = K.nc, K.S, K.d
    ph = Phase(K)
    with ph.es:
        stage = ph.tile("wstage", [128, 8192])
        wo = load_w_bf16(K, ph, "wo", w_o_ap.rearrange("(c p) n -> p c n", p=128), [128, 8, 1024], stage)
        g_bc = load_bc(K, ph, "g_bc", g_ap, 1024)
        b_bc = load_bc(K, ph, "b_bc", b_ap, 1024)
        ats = [ph.tile("at%d" % i, [128, 1024]) for i in range(2)]
        xs = [ph.tile("xr%d" % i, [128, 1024]) for i in range(2)]
        aT = ph.tile("aT", [128, 8, 128], BF16)
        y = ph.tile("y", [128, 1024])
        outs = [ph.tile("ho%d" % i, [128, 1024]) for i in range(2)]
        for t in range(NT):
            p = t % 2
            t0 = t * 128
            at, x, o = ats[p], xs[p], outs[p]
            S.dma("sp", lambda: nc.sync.dma_start(out=at[:, :], in_=d["attn"][t0:t0 + 128, :]),
                  r=[K.db("attn", s, t // 4, h) for h in range(16)], w=[at.b])
            S.dma("sp", lambda: nc.sync.dma_start(out=x[:, :], in_=res_fn(t0)), r=res_deps_fn(t), w=[x.b])
            transpose_to(K, at, [at[:, c * 128:(c + 1) * 128] for c in range(8)], aT,
                         lambda g0, n: aT[:, g0:g0 + n, :].rearrange("p a b -> p (a b)"), 128)
            for n in range(2):
                bk = K.bank()
                for c in range(8):
                    S.op("pe", lambda c=c, n=n, bk=bk: nc.tensor.matmul(bk[:, 0:512], lhsT=aT[:, c, :],
                                                                         rhs=wo[:, c, n * 512:(n + 1) * 512],
                                                                         start=(c == 0), stop=(c == 7)),
                         r=[aT.b, wo.b], w=[bk.b])
                S.op("dve", lambda n=n, bk=bk: nc.vector.scalar_tensor_tensor(
                    out=y[:, n * 512:(n + 1) * 512], in0=x[:, n * 512:(n + 1) * 512], scalar=ALPHA, in1=bk[:, 0:512],
                    op0=ALU.mult, op1=ALU.add), r=[x.b, bk.b], w=[y.b])
            layer_norm(K, ph, y, g_bc, b_bc, o, "%s%d" % (tag, t))
            S.dma("sp", lambda: nc.sync.dma_start(out=dst_fn(t0), in_=o[:, :]), r=[o.b], w=[dst_buf_fn(t)])
        S.barrier()


def prologue_tables(K):
    nc, S, d = K.nc, K.S, K.d
    ph = Phase(K)
    with ph.es:
        inb = [ph.tile("tin%d" % i, [128, 4, 2048]) for i in range(3)]
        outb = [ph.tile("tout%d" % i, [128, 4, 2048], BF16) for i in range(3)]
        k = 0
        for layer in range(2):
            src = d["p_uv%d" % layer].rearrange("(p j) n -> p j n", p=128)
            dst = d["p_uvb%d" % layer].rearrange("(p j) n -> p j n", p=128)
            for c in range(32):
                a, b = inb[k % 3], outb[k % 3]
                S.dma("sp", lambda: nc.sync.dma_start(out=a[:, :, :], in_=src[:, 4 * c:4 * c + 4, :]), w=[a.b])
                if k % 2 == 0:
                    S.op("act", lambda: nc.scalar.copy(out=b[:, :, :], in_=a[:, :, :]), r=[a.b], w=[b.b])
                else:
                    S.op("dve", lambda: nc.vector.tensor_copy(out=b[:, :, :], in_=a[:, :, :]), r=[a.b], w=[b.b])
                S.dma("sp", lambda: nc.sync.dma_start(out=dst[:, 4 * c:4 * c + 4, :], in_=b[:, :, :]), r=[b.b], w=[K.db("uvb", layer, c)])
                k += 1
        S.barrier()


def phase_peer(K, s, layer, src_fn, src_buf_fn, dst_fn, dst_buf_fn, ntiles=NT):
    nc, S, d = K.nc, K.S, K.d
    ph = Phase(K)
    NG = 11
    K.bank_list = [0, 1, 2, 3, 4, 5]
    accb = [K.ps[6], K.ps[7]]
    with ph.es:
        wq = ph.tile("pwq", [128, 8, 2048])
        S.dma("sp", lambda: nc.sync.dma_start(out=wq[:, :, :], in_=d["p_w_q"][layer].rearrange("(c p) n -> p c n", p=128)), w=[wq.b])
        skT = ph.tile("skT", [128, 16, 128])
        S.dma("sp", lambda: nc.sync.dma_start(out=skT[:, :, :], in_=d["p_skT"][layer]), w=[skT.b])
        g_bc = load_bc(K, ph, "pg_bc", d["ln_g"][2 * layer + 1:2 * layer + 2, :], 1024)
        b_bc = load_bc(K, ph, "pb_bc", d["ln_b"][2 * layer + 1:2 * layer + 2, :], 1024)
        iota = ph.tile("iota", [128, 16])
        S.dma("sp", lambda: nc.sync.dma_start(out=iota[:, :], in_=d["iota16"]), w=[iota.b])
        identb = ph.tile("identb", [128, 128], BF16)
        S.op("dve", lambda: nc.vector.tensor_copy(out=identb[:, :], in_=K.ident[:, :]), r=[K.ident.b], w=[identb.b])
        G = [ph.tile("G%d" % i, [128, 2048], BF16) for i in range(NG)]
        hs = [ph.tile("ph%d" % i, [128, 1024]) for i in range(2)]
        eidxs = [ph.tile("peidx%d" % i, [128, 128], I32) for i in range(2)]
        gws = [ph.tile("pgw%d" % i, [128, 128]) for i in range(2)]
        hT = ph.tile("phT", [128, 8, 128])
        q_sb = ph.tile("pq", [128, 2048])
        qT = ph.tile("pqT", [128, 16, 128])
        sc = Tile(q_sb.t[:, :].rearrange("p (a b) -> p a b", b=128), "sc_alias")
        sc.b = q_sb.b
        sc2 = ph.tile("psc2", [128, 128])
        m1 = ph.tile("pm1", [128, 16, 16])
        i1 = ph.tile("pi1", [128, 16, 16], U32)
        i1f = ph.tile("pi1f", [128, 16, 16])
        cand2 = ph.tile("pcand2", [128, 256])
        best = ph.tile("pbest", [128, 8, 16])
        pos = ph.tile("ppos", [128, 8, 16], U32)
        hi = ph.tile("phi", [128, 8, 16], U32)
        lo = ph.tile("plo", [128, 8, 16], U32)
        hif = ph.tile("phif", [128, 8, 16])
        lof = ph.tile("plof", [128, 8, 16])
        eq = ph.tile("peq", [128, 8, 16, 16])
        cand = Tile(eq.t[:, :, :, :].rearrange("p h i j -> p h (i j)"), "cand_alias")
        cand.b = eq.b
        e0 = ph.tile("pe0", [128, 8, 16])
        e1 = ph.tile("pe1", [128, 8, 16])
        ef = ph.tile("pef", [128, 128])
        bm = ph.tile("pbm", [128, 8, 16])
        se = ph.tile("pse", [128, 8])
        a = ph.tile("pa", [128, 128])
        ga = ph.tile("pga", [128, 128])
        w = ph.tile("pw", [128, 128])
        diags = [ph.tile("pdiag%d" % i, [128, 4, 128], BF16) for i in range(3)]
        prods = [ph.tile("pprod%d" % i, [128, 1024], BF16) for i in range(6)]
        hbs = [ph.tile("phb%d" % i, [128, 1024], BF16) for i in range(2)]
        y = ph.tile("py", [128, 1024])
        o = ph.tile("po", [128, 1024])
        m1v = m1[:, :, :].rearrange("p (h two) k -> p h two k", two=2)
        i1fv = i1f[:, :, :].rearrange("p (h two) k -> p h two k", two=2)
        iota_b = iota[:, :].unsqueeze(1).unsqueeze(1).to_broadcast([128, 8, 16, 16])
        ident_b = identb[:, :].unsqueeze(1).to_broadcast([128, 4, 128])

        def front(t):
            t0 = t * 128
            h, eidx, gw = hs[t % 2], eidxs[t % 2], gws[t % 2]
            S.dma("sp", lambda: nc.sync.dma_start(out=h[:, :], in_=src_fn(t0)), r=src_buf_fn(t), w=[h.b])
            transpose_to(K, h, [h[:, c * 128:(c + 1) * 128] for c in range(8)], hT,
                         lambda g0, n: hT[:, g0:g0 + n, :].rearrange("p a b -> p (a b)"), 128)
            yield
            for n in range(4):
                bk = K.bank()
                for c in range(8):
                    S.op("pe", lambda: nc.tensor.matmul(bk[:, 0:512], lhsT=hT[:, c, :], rhs=wq[:, c, n * 512:(n + 1) * 512],
                                                        start=(c == 0), stop=(c == 7)), r=[hT.b, wq.b], w=[bk.b])
                evac(K, n, q_sb[:, n * 512:(n + 1) * 512], bk[:, 0:512], r=[bk.b], w=[q_sb.b])
                yield
            transpose_to(K, q_sb, [q_sb[:, c * 128:(c + 1) * 128] for c in range(16)], qT,
                         lambda g0, n: qT[:, g0:g0 + n, :].rearrange("p a b -> p (a b)"), 128)
            yield
            for n in range(4):
                bk = K.bank()
                for j in range(4):
                    hp = n * 4 + j
                    S.op("pe", lambda: nc.tensor.matmul(bk[:, j * 128:(j + 1) * 128], lhsT=qT[:, hp, :], rhs=skT[:, hp, :],
                                                        start=True, stop=True), r=[qT.b, skT.b], w=[bk.b])
                evac(K, n, sc[:, n * 4:(n + 1) * 4, :].rearrange("p a b -> p (a b)"), bk[:, 0:512], r=[bk.b], w=[sc.b])
            yield
            for hp in range(16):
                S.op("dve", lambda: nc.vector.max(out=m1[:, hp, 0:8], in_=sc[:, hp, :]), r=[sc.b], w=[m1.b])
                S.op("dve", lambda: nc.vector.max_index(out=i1[:, hp, 0:8], in_max=m1[:, hp, 0:8], in_values=sc[:, hp, :]),
                     r=[sc.b, m1.b], w=[i1.b])
                S.op("dve", lambda: nc.vector.match_replace(out=sc2[:, :], in_to_replace=m1[:, hp, 0:8],
                                                            in_values=sc[:, hp, :], imm_value=NEG), r=[sc.b, m1.b], w=[sc2.b])
                S.op("dve", lambda: nc.vector.max(out=m1[:, hp, 8:16], in_=sc2[:, :]), r=[sc2.b], w=[m1.b])
                S.op("dve", lambda: nc.vector.max_index(out=i1[:, hp, 8:16], in_max=m1[:, hp, 8:16], in_values=sc2[:, :]),
                     r=[sc2.b, m1.b], w=[i1.b])
                yield
            S.op("dve", lambda: nc.vector.tensor_tensor(
                out=cand[:, :, :].rearrange("p h (i j) -> p h i j", j=16),
                in0=m1v[:, :, 0, :].unsqueeze(3).to_broadcast([128, 8, 16, 16]),
                in1=m1v[:, :, 1, :].unsqueeze(2).to_broadcast([128, 8, 16, 16]), op=ALU.add), r=[m1.b], w=[cand.b])
            for hh in range(8):
                S.op("dve", lambda: nc.vector.max(out=best[:, hh, 0:8], in_=cand[:, hh, :]), r=[cand.b], w=[best.b])
                S.op("dve", lambda: nc.vector.max_index(out=pos[:, hh, 0:8], in_max=best[:, hh, 0:8], in_values=cand[:, hh, :]),
                     r=[cand.b, best.b], w=[pos.b])
                S.op("dve", lambda: nc.vector.match_replace(out=cand2[:, :], in_to_replace=best[:, hh, 0:8],
                                                            in_values=cand[:, hh, :], imm_value=NEG), r=[cand.b, best.b], w=[cand2.b])
                S.op("dve", lambda: nc.vector.max(out=best[:, hh, 8:16], in_=cand2[:, :]), r=[cand2.b], w=[best.b])
                S.op("dve", lambda: nc.vector.max_index(out=pos[:, hh, 8:16], in_max=best[:, hh, 8:16], in_values=cand2[:, :]),
                     r=[cand2.b, best.b], w=[pos.b])
                yield
            S.op("dve", lambda: nc.vector.tensor_tensor(out=bm[:, :, :], in0=best[:, :, :],
                                                        in1=best[:, :, 0:1].to_broadcast([128, 8, 16]), op=ALU.subtract),
                 r=[best.b], w=[bm.b])
            S.op("act", lambda: nc.scalar.activation(out=bm[:, :, :], in_=bm[:, :, :], func=AF.Exp), r=[bm.b], w=[bm.b])
            S.op("dve", lambda: nc.vector.tensor_reduce(out=se[:, :], in_=bm[:, :, :], axis=AX.X, op=ALU.add), r=[bm.b], w=[se.b])
            S.op("dve", lambda: nc.vector.reciprocal(out=se[:, :], in_=se[:, :]), r=[se.b], w=[se.b])
            S.op("dve", lambda: nc.vector.tensor_tensor(out=gw[:, :].rearrange("p (h k) -> p h k", k=16), in0=bm[:, :, :],
                                                        in1=se[:, :].unsqueeze(2).to_broadcast([128, 8, 16]), op=ALU.mult),
                 r=[bm.b, se.b], w=[gw.b])
            yield
            S.op("dve", lambda: nc.vector.tensor_single_scalar(out=hi[:, :, :], in_=pos[:, :, :], scalar=4,
                                                               op=ALU.logical_shift_right), r=[pos.b], w=[hi.b])
            S.op("dve", lambda: nc.vector.tensor_single_scalar(out=lo[:, :, :], in_=pos[:, :, :], scalar=15,
                                                               op=ALU.bitwise_and), r=[pos.b], w=[lo.b])
            S.op("dve", lambda: nc.vector.tensor_copy(out=hif[:, :, :], in_=hi[:, :, :]), r=[hi.b], w=[hif.b])
            S.op("dve", lambda: nc.vector.tensor_copy(out=lof[:, :, :], in_=lo[:, :, :]), r=[lo.b], w=[lof.b])
            S.op("dve", lambda: nc.vector.tensor_copy(out=i1f[:, :, :], in_=i1[:, :, :]), r=[i1.b], w=[i1f.b])
            yield
            for (xf, half, eo) in ((hif, 0, e0), (lof, 1, e1)):
                S.op("dve", lambda: nc.vector.tensor_tensor(out=eq[:, :, :, :],
                                                            in0=xf[:, :, :].unsqueeze(3).to_broadcast([128, 8, 16, 16]),
                                                            in1=iota_b, op=ALU.is_equal), r=[xf.b, iota.b], w=[eq.b])
                S.op("dve", lambda: nc.vector.tensor_tensor(out=eq[:, :, :, :], in0=eq[:, :, :, :],
                                                            in1=i1fv[:, :, half, :].unsqueeze(2).to_broadcast([128, 8, 16, 16]),
                                                            op=ALU.mult), r=[eq.b, i1f.b], w=[eq.b])
                S.op("dve", lambda: nc.vector.tensor_reduce(out=eo[:, :, :], in_=eq[:, :, :, :], axis=AX.X, op=ALU.add),
                     r=[eq.b], w=[eo.b])
                yield
            S.op("dve", lambda: nc.vector.scalar_tensor_tensor(out=ef[:, :], in0=e0[:, :, :].rearrange("p h k -> p (h k)"),
                                                               scalar=128.0, in1=e1[:, :, :].rearrange("p h k -> p (h k)"),
                                                               op0=ALU.mult, op1=ALU.add), r=[e0.b, e1.b], w=[ef.b])
            S.op("dve", lambda: nc.vector.tensor_copy(out=eidx[:, :], in_=ef[:, :]), r=[ef.b], w=[eidx.b])
            yield

        gi = 0
        pi = 0
        di = 0
        a_b = [Buf("a%d" % i) for i in range(128)]
        ga_b = [Buf("ga%d" % i) for i in range(32)]
        w_b = [Buf("w%d" % i) for i in range(32)]

        def main(t, gen):
            nonlocal gi, pi, di
            t0 = t * 128
            h, eidx, gw = hs[t % 2], eidxs[t % 2], gws[t % 2]
            hb = hbs[t % 2]
            S.op("act", lambda: nc.scalar.copy(out=hb[:, :], in_=h[:, :]), r=[h.b], w=[hb.b])
            for g0 in range(0, 128, 4):
                gs = []
                for sl in range(g0, g0 + 4):
                    Gt = G[gi % NG]
                    gi += 1
                    gs.append(Gt)
                    S.dma("pool", lambda: nc.gpsimd.indirect_dma_start(
                        out=Gt[:, :], out_offset=None, in_=d["p_uvb%d" % layer],
                        in_offset=bass.IndirectOffsetOnAxis(ap=eidx[:, sl:sl + 1], axis=0)), r=[eidx.b], w=[Gt.b])
                    pj = prods[pi % 6]
                    pi += 1
                    S.op("dve", lambda: nc.vector.tensor_tensor(out=pj[:, :], in0=Gt[:, 0:1024], in1=hb[:, :], op=ALU.mult),
                         r=[Gt.b, hb.b], w=[pj.b])
                    if 'noacc' not in XP:
                        S.op("act", lambda: nc.scalar.activation(out=pj[:, :], in_=pj[:, :], func=AF.Identity,
                                                                 accum_out=a[:, sl:sl + 1]), r=[pj.b], w=[pj.b, a_b[sl]])
                gq = g0 // 4
                S.op("act", lambda: nc.scalar.activation(out=ga[:, g0:g0 + 4], in_=a[:, g0:g0 + 4], func=AF.Gelu),
                     r=a_b[g0:g0 + 4], w=[ga_b[gq]])
                S.op("dve", lambda: nc.vector.tensor_tensor(out=w[:, g0:g0 + 4], in0=ga[:, g0:g0 + 4], in1=gw[:, g0:g0 + 4],
                                                            op=ALU.mult), r=[ga_b[gq], gw.b], w=[w_b[gq]])
                dg = diags[di % 3]
                di += 1
                S.op("dve", lambda: nc.vector.tensor_tensor(out=dg[:, :, :], in0=ident_b,
                                                            in1=w[:, g0:g0 + 4].unsqueeze(2).to_broadcast([128, 4, 128]),
                                                            op=ALU.mult), r=[identb.b, w_b[gq]], w=[dg.b])
                for j, sl in enumerate(range(g0, g0 + 4)):
                    Gt = gs[j]
                    for n in range(0 if ('nope' in XP and sl not in (0, 127)) else 2):
                        S.op("pe", lambda: nc.tensor.matmul(accb[n][:, 0:512], lhsT=dg[:, j, :],
                                                            rhs=Gt[:, 1024 + n * 512:1024 + (n + 1) * 512],
                                                            start=(sl == 0), stop=(sl == 127)), r=[dg.b, Gt.b], w=[accb[n].b])
                if gen is not None and 'nofront' not in XP:
                    next(gen, None)
                    next(gen, None)
            for n in range(2):
                S.op("dve", lambda: nc.vector.scalar_tensor_tensor(out=y[:, n * 512:(n + 1) * 512], in0=h[:, n * 512:(n + 1) * 512],
                                                                   scalar=ALPHA, in1=accb[n][:, 0:512], op0=ALU.mult, op1=ALU.add),
                     r=[h.b, accb[n].b], w=[y.b])
            layer_norm(K, ph, y, g_bc, b_bc, o, "p%d_%d" % (layer, t))
            S.dma("sp", lambda: nc.sync.dma_start(out=dst_fn(t0), in_=o[:, :]), r=[o.b], w=[dst_buf_fn(t)])

        for _ in front(0):
            pass
        for t in range(ntiles):
            gen = front(t + 1) if t + 1 < ntiles else None
            main(t, gen)
            if gen is not None:
                for _ in gen:
                    pass
        S.barrier()
    K.bank_list = list(range(8))


def phase_b0(K, s):
    nc, S, d = K.nc, K.S, K.d
    ph = Phase(K)
    with ph.es:
        stage = ph.tile("wstage", [128, 12288])
        wkv = load_w_bf16(K, ph, "swkv", d["s_w_kv"].rearrange("(c p) n -> p c n", p=128), [128, 8, 1536], stage)
        wqb = load_w_bf16(K, ph, "bwin", d["b_w_in"].rearrange("(c p) n -> p c n", p=128), [128, 8, 1072], stage)
        h = ph.tile("bh", [128, 1024])
        cs = ph.tile("bcs", [128, 64])
        hT = ph.tile("bhT", [128, 8, 128], BF16)
        kvs = ph.tile("kvs", [128, 3, 4, 128])
        q_sb = ph.tile("bq", [128, 16, 64])
        qr = ph.tile("bqr", [128, 16, 64])
        kr = ph.tile("bkr", [128, 2, 4, 64])
        gts = ph.tile("bgts", [128, 48])
        vps = [ph.tile("bvp%d" % i, [128, 4, 65], BF16) for i in range(2)]
        qT = ph.tile("bqT", [64, 16, 128], BF16)
        qrT = ph.tile("bqrT", [64, 16, 128], BF16)
        kT = ph.tile("bkT", [64, 8, 128])
        kTr = ph.tile("bkTr", [64, 8, 128], BF16)
        for v in vps:
            S.op("dve", lambda: nc.vector.memset(v[:, :, 64:65], 1.0), w=[v.b])
        for t in range(K.ntiles):
            t0 = t * 128
            S.dma("sp", lambda: nc.sync.dma_start(out=h[:, :], in_=d["h2"][t0:t0 + 128, :]), r=[K.db("h2", s, t)], w=[h.b])
            S.dma("sp", lambda: nc.sync.dma_start(out=cs[:, :], in_=d["rope64"][t0:t0 + 128, :]), w=[cs.b])
            transpose_to(K, h, [h[:, c * 128:(c + 1) * 128] for c in range(8)], hT,
                         lambda g0, n: hT[:, g0:g0 + n, :].rearrange("p a b -> p (a b)"), 128)
            kv_flat = kvs[:, :, :, :].rearrange("p a b c -> p (a b c)")
            for n in range(3):
                bk = K.bank()
                for c in range(8):
                    S.op("pe", lambda: nc.tensor.matmul(bk[:, 0:512], lhsT=hT[:, c, :], rhs=wkv[:, c, n * 512:(n + 1) * 512],
                                                        start=(c == 0), stop=(c == 7)), r=[hT.b, wkv.b], w=[bk.b])
                evac(K, n, kv_flat[:, n * 512:(n + 1) * 512], bk[:, 0:512], r=[bk.b], w=[kvs.b])
            q_flat = q_sb[:, :, :].rearrange("p a b -> p (a b)")
            for n in range(2):
                bk = K.bank()
                for c in range(8):
                    S.op("pe", lambda: nc.tensor.matmul(bk[:, 0:512], lhsT=hT[:, c, :], rhs=wqb[:, c, n * 512:(n + 1) * 512],
                                                        start=(c == 0), stop=(c == 7)), r=[hT.b, wqb.b], w=[bk.b])
                evac(K, n + 1, q_flat[:, n * 512:(n + 1) * 512], bk[:, 0:512], r=[bk.b], w=[q_sb.b])
            bk = K.bank()
            for c in range(8):
                S.op("pe", lambda: nc.tensor.matmul(bk[:, 0:48], lhsT=hT[:, c, :], rhs=wqb[:, c, 1024:1072],
                                                    start=(c == 0), stop=(c == 7)), r=[hT.b, wqb.b], w=[bk.b])
            S.op("act", lambda: nc.scalar.activation(out=gts[:, :], in_=bk[:, 0:48], func=AF.Sigmoid), r=[bk.b], w=[gts.b])
            S.dma("sp", lambda: nc.sync.dma_start(out=d["gates"][t0:t0 + 128, :], in_=gts[:, :]), r=[gts.b], w=[K.db("b0", s, t)])
            cos_b = cs[:, 0:32].unsqueeze(1).to_broadcast([128, 16, 32])
            sin_b = cs[:, 32:64].unsqueeze(1).to_broadcast([128, 16, 32])
            rope_tok(K, ph, q_sb[:, :, 0:32], q_sb[:, :, 32:64], qr[:, :, 0:32], qr[:, :, 32:64], cos_b, sin_b,
                     [128, 16, 32], [q_sb.b, cs.b], [qr.b], "brq%d" % t)
            cos_k = cs[:, 0:32].unsqueeze(1).to_broadcast([128, 4, 32])
            sin_k = cs[:, 32:64].unsqueeze(1).to_broadcast([128, 4, 32])
            for br in range(2):
                rope_tok(K, ph, kvs[:, 1 + br, :, 0:32], kvs[:, 1 + br, :, 32:64], kr[:, br, :, 0:32], kr[:, br, :, 32:64],
                         cos_k, sin_k, [128, 4, 32], [kvs.b, cs.b], [kr.b], "brk%d_%d" % (t, br))
            transpose_to(K, q_sb, [q_sb[:, hh, :] for hh in range(16)], qT,
                         lambda g0, n: qT[:, g0:g0 + n, :].rearrange("p a b -> p (a b)"), 64)
            transpose_to(K, qr, [qr[:, hh, :] for hh in range(16)], qrT,
                         lambda g0, n: qrT[:, g0:g0 + n, :].rearrange("p a b -> p (a b)"), 64)
            srcs = [kvs[:, 0, g, 0:64] for g in range(4)] + [kvs[:, 0, g, 64:128] for g in range(4)]
            transpose_to(K, kvs, srcs, kT, lambda g0, n: kT[:, g0:g0 + n, :].rearrange("p a b -> p (a b)"), 64)
            srcs = [kr[:, 0, g, :] for g in range(4)] + [kr[:, 1, g, :] for g in range(4)]
            transpose_to(K, kr, srcs, kTr, lambda g0, n: kTr[:, g0:g0 + n, :].rearrange("p a b -> p (a b)"), 64)
            wb = [K.db("b0", s, t)]
            S.dma("sp", lambda: nc.sync.dma_start(out=d["bqT"][:, :, t0:t0 + 128].rearrange("h e t -> e h t"), in_=qT[:, :, :]),
                  r=[qT.b], w=wb)
            S.dma("sp", lambda: nc.sync.dma_start(out=d["bqrT"][:, :, t0:t0 + 128].rearrange("h e t -> e h t"), in_=qrT[:, :, :]),
                  r=[qrT.b], w=wb)
            S.dma("sp", lambda: nc.sync.dma_start(out=d["bkcT"][:, :, t0:t0 + 128].rearrange("h e t -> e h t"), in_=kT[:, :, :]),
                  r=[kT.b], w=wb)
            S.dma("sp", lambda: nc.sync.dma_start(out=d["bkrT"][:, :, t0:t0 + 128].rearrange("h e t -> e h t"), in_=kTr[:, :, :]),
                  r=[kTr.b], w=wb)
            for br in range(2):
                vp = vps[br]
                S.op("act", lambda: nc.scalar.copy(out=vp[:, :, 0:64], in_=kvs[:, 1 + br, :, 64:128]), r=[kvs.b], w=[vp.b])
                S.dma("sp", lambda: nc.sync.dma_start(out=d["bvp"][br, :, t0:t0 + 128, :].rearrange("g t e -> t g e"),
                                                      in_=vp[:, :, :]), r=[vp.b], w=wb)
        S.barrier()


def phase_b1(K, s):
    nc, S, d = K.nc, K.S, K.d
    ph = Phase(K)
    allb0 = [K.db("b0", s, t) for t in range(NT)]
    with ph.es:
        ovl = ph.tile("ovl", [128, 32])
        S.dma("sp", lambda: nc.sync.dma_start(out=ovl[:, :], in_=d["overlap"]), w=[ovl.b])
        for kv in range(2):
            nm = "k" if kv == 0 else "v"
            w1 = ph.tile("cw1" + nm, [64, 32, 256])
            S.dma("sp", lambda: nc.sync.dma_start(out=w1[:, :, :], in_=d["s_cmp_%s_w1" % nm].rearrange("(l e) n -> e l n", e=64)), w=[w1.b])
            w2 = ph.tile("cw2" + nm, [128, 2, 64])
            S.dma("sp", lambda: nc.sync.dma_start(out=w2[:, :, :], in_=d["s_cmp_%s_w2" % nm].rearrange("(c p) n -> p c n", p=128)), w=[w2.b])
            b1 = ph.tile("cb1" + nm, [128, 2])
            S.dma("sp", lambda: nc.sync.dma_start(out=b1[:, :], in_=d["s_cmp_%s_b1" % nm]), w=[b1.b])
            posT = ph.tile("cpos" + nm, [64, 32])
            S.dma("sp", lambda: nc.sync.dma_start(out=posT[:, :], in_=d["s_cmp_pos_%sT" % nm]), w=[posT.b])
            xT = ph.tile("cxT" + nm, [64, SEQ])
            Xp = ph.tile("cXp" + nm, [64, 32, 127])
            hid = ph.tile("chid" + nm, [128, 2, 127])
            res = ph.tile("cres" + nm, [128, 128], BF16)
            for g in range(4):
                S.dma("sp", lambda: nc.sync.dma_start(out=xT[:, :], in_=d["bkcT"][kv * 4 + g, :, :]), r=allb0, w=[xT.b])
                for half in range(2):
                    S.op("dve", lambda: nc.vector.tensor_tensor(
                        out=Xp[:, half * 16:(half + 1) * 16, :],
                        in0=xT[:, half * 16:half * 16 + 2032].rearrange("p (j l) -> p l j", l=16),
                        in1=posT[:, half * 16:(half + 1) * 16].unsqueeze(2).to_broadcast([64, 16, 127]), op=ALU.add),
                        r=[xT.b, posT.b], w=[Xp.b])
                for hc in range(2):
                    bk = K.bank()
                    for l in range(32):
                        S.op("pe", lambda: nc.tensor.matmul(bk[:, 0:127], lhsT=w1[:, l, hc * 128:(hc + 1) * 128], rhs=Xp[:, l, :],
                                                            start=(l == 0), stop=(l == 31)), r=[w1.b, Xp.b], w=[bk.b])
                    S.op("act", lambda: nc.scalar.activation(out=hid[:, hc, :], in_=bk[:, 0:127], func=AF.Gelu, bias=b1[:, hc:hc + 1],
                                                             scale=1.0), r=[bk.b, b1.b], w=[hid.b])
                bk = K.bank()
                if kv == 0:
                    for hc in range(2):
                        S.op("pe", lambda: nc.tensor.matmul(bk[0:64, 0:127], lhsT=w2[:, hc, :], rhs=hid[:, hc, :],
                                                            start=(hc == 0), stop=(hc == 1)), r=[w2.b, hid.b], w=[bk.b])
                    S.op("dve", lambda: nc.vector.memset(res[0:64, :], 0.0), w=[res.b])
                    S.op("dve", lambda: nc.vector.tensor_copy(out=res[0:64, 0:127], in_=bk[0:64, 0:127]), r=[bk.b], w=[res.b])
                    S.dma("sp", lambda: nc.sync.dma_start(out=d["kcmpT"][g, :, :], in_=res[0:64, :]), r=[res.b], w=[K.db("b1", s)])
                else:
                    for hc in range(2):
                        S.op("pe", lambda: nc.tensor.matmul(bk[0:127, 0:64], lhsT=hid[:, hc, :], rhs=w2[:, hc, :],
                                                            start=(hc == 0), stop=(hc == 1)), r=[w2.b, hid.b], w=[bk.b])
                    S.op("dve", lambda: nc.vector.memset(res[:, :], 0.0), w=[res.b])
                    S.op("dve", lambda: nc.vector.tensor_copy(out=res[0:127, 0:64], in_=bk[0:127, 0:64]), r=[bk.b], w=[res.b])
                    S.op("dve", lambda: nc.vector.memset(res[0:127, 64:65], 1.0), r=[], w=[res.b])
                    S.op("dve", lambda: nc.vector.tensor_copy(out=res[0:127, 65:97], in_=ovl[0:127, :]), r=[ovl.b], w=[res.b])
                    S.dma("sp", lambda: nc.sync.dma_start(out=d["vcmp"][g, :, :], in_=res[:, 0:97]), r=[res.b], w=[K.db("b1", s)])
        S.barrier()


def phase_b2(K, s):
    nc, S, d = K.nc, K.S, K.d
    scale = 64.0 ** -0.5
    ph = Phase(K)
    allb0 = [K.db("b0", s, t) for t in range(NT)]
    b1b = [K.db("b1", s)]
    with ph.es:
        cmask = ph.tile("cmask", [128, SEQ])
        S.dma("sp", lambda: nc.sync.dma_start(out=cmask[:, :], in_=d["cmask"]), w=[cmask.b])
        Eall32 = ph.tile("Eall32", [32, 16, 128])
        S.dma("sp", lambda: nc.sync.dma_start(out=Eall32[:, :, :], in_=d["Eall"]), w=[Eall32.b])
        Eall = ph.tile("Eall", [32, 16, 128], BF16)
        S.op("dve", lambda: nc.vector.tensor_copy(out=Eall[:, :, :], in_=Eall32[:, :, :]), r=[Eall32.b], w=[Eall.b])
        At = ph.tile("At", [128, 16, 32])
        Bt = ph.tile("Bt", [128, 16, 32])
        S.dma("sp", lambda: nc.sync.dma_start(out=At[:, :, :], in_=d["selA"].rearrange("(q p) n -> p q n", p=128)), w=[At.b])
        S.dma("sp", lambda: nc.sync.dma_start(out=Bt[:, :, :], in_=d["selB"].rearrange("(q p) n -> p q n", p=128)), w=[Bt.b])
        gts = ph.tile("gts", [128, 16, 48])
        S.dma("sp", lambda: nc.sync.dma_start(out=gts[:, :, :], in_=d["gates"].rearrange("(q p) n -> p q n", p=128)), r=allb0, w=[gts.b])
        ksT = ph.tile("ksT", [64, SEQ], BF16)
        kwT = ph.tile("kwT", [64, SEQ], BF16)
        vs = ph.tile("vs", [128, 16, 65], BF16)
        vw = ph.tile("vw", [128, 16, 65], BF16)
        kcT = ph.tile("kcT", [64, 128], BF16)
        vc = ph.tile("vc", [128, 97], BF16)
        qTs = [ph.tile("nqT%d" % i, [64, SEQ], BF16) for i in range(4)]
        qrTs = [ph.tile("nqrT%d" % i, [64, SEQ], BF16) for i in range(4)]
        comb = ph.tile("comb", [128, 4, 16, 64])
        imp = ph.tile("imp", [128, 16, 32])
        sel = ph.tile("sel", [128, 16, 32])
        selT = ph.tile("selT", [32, SEQ], BF16)
        tmp32 = ph.tile("tmp32", [128, 32])
        m8 = ph.tile("m8", [128, 16])
        pts = [ph.tile("npt%d" % i, [128, 512], BF16) for i in range(4)]
        mts = [ph.tile("nmt%d" % i, [128, 512], BF16) for i in range(2)]
        rds = [ph.tile("nrd%d" % i, [128, 2]) for i in range(6)]
        otmps = [ph.tile("notmp%d" % i, [128, 64]) for i in range(6)]
        K.rdi = 0
        ctr = 0
        for g in range(4):
            S.dma("sp", lambda: nc.sync.dma_start(out=ksT[:, :], in_=d["bkrT"][g, :, :]), r=allb0, w=[ksT.b])
            S.dma("sp", lambda: nc.sync.dma_start(out=kwT[:, :], in_=d["bkrT"][4 + g, :, :]), r=allb0, w=[kwT.b])
            S.dma("sp", lambda: nc.sync.dma_start(out=vs[:, :, :], in_=d["bvp"][0, g, :, :].rearrange("(k p) e -> p k e", p=128)), r=allb0, w=[vs.b])
            S.dma("sp", lambda: nc.sync.dma_start(out=vw[:, :, :], in_=d["bvp"][1, g, :, :].rearrange("(k p) e -> p k e", p=128)), r=allb0, w=[vw.b])
            S.dma("sp", lambda: nc.sync.dma_start(out=kcT[:, :], in_=d["kcmpT"][g, :, :]), r=b1b, w=[kcT.b])
            S.dma("sp", lambda: nc.sync.dma_start(out=vc[:, :], in_=d["vcmp"][g, :, :]), r=b1b, w=[vc.b])
            for j in range(4):
                hh = g * 4 + j
                S.dma("sp", lambda: nc.sync.dma_start(out=qTs[j][:, :], in_=d["bqT"][hh, :, :]), r=allb0, w=[qTs[j].b])
                S.dma("sp", lambda: nc.sync.dma_start(out=qrTs[j][:, :], in_=d["bqrT"][hh, :, :]), r=allb0, w=[qrTs[j].b])
            for j in range(4):
                hh = g * 4 + j
                qT = qTs[j]
                for qc in range(4):
                    st = K.bank()
                    pt = pts[ctr % 3]
                    ctr += 1
                    S.op("pe", lambda: nc.tensor.matmul(st[0:127, 0:512], lhsT=kcT[:, 0:127], rhs=qT[:, qc * 512:(qc + 1) * 512],
                                                        start=True, stop=True), r=[kcT.b, qT.b], w=[st.b])
                    S.op("act", lambda: nc.scalar.activation(out=pt[0:127, :], in_=st[0:127, 0:512], func=AF.Exp, scale=scale),
                         r=[st.b], w=[pt.b])
                    S.op("dve", lambda: nc.vector.tensor_tensor(out=pt[0:127, :], in0=pt[0:127, :],
                                                                in1=cmask[0:127, qc * 512:(qc + 1) * 512], op=ALU.mult),
                         r=[pt.b, cmask.b], w=[pt.b])
                    for jj in range(4):
                        qt = qc * 4 + jj
                        ob = K.bank()
                        rd = rds[K.rdi % 6]
                        K.rdi += 1
                        S.op("pe", lambda: nc.tensor.matmul(ob[:, 0:97], lhsT=pt[0:127, jj * 128:(jj + 1) * 128], rhs=vc[0:127, :],
                                                            start=True, stop=True), r=[pt.b, vc.b], w=[ob.b])
                        S.op("dve", lambda: nc.vector.tensor_scalar_max(out=rd[:, 0:1], in0=ob[:, 64:65], scalar1=1e-30),
                             r=[ob.b], w=[rd.b])
                        S.op("dve", lambda: nc.vector.reciprocal(out=rd[:, 1:2], in_=rd[:, 0:1]), r=[rd.b], w=[rd.b])
                        S.op("dve", lambda: nc.vector.tensor_scalar(out=comb[:, j, qt, :], in0=ob[:, 0:64], scalar1=rd[:, 1:2],
                                                                    scalar2=gts[:, qt, hh:hh + 1], op0=ALU.mult, op1=ALU.mult),
                             r=[ob.b, rd.b, gts.b], w=[comb.b])
                        if j == 0:
                            S.op("dve", lambda: nc.vector.tensor_scalar(out=imp[:, qt, :], in0=ob[:, 65:97], scalar1=rd[:, 1:2],
                                                                        scalar2=None, op0=ALU.mult), r=[ob.b, rd.b], w=[imp.b])
                        else:
                            S.op("dve", lambda: nc.vector.scalar_tensor_tensor(out=imp[:, qt, :], in0=ob[:, 65:97], scalar=rd[:, 1:2],
                                                                               in1=imp[:, qt, :], op0=ALU.mult, op1=ALU.add),
                                 r=[ob.b, rd.b, imp.b], w=[imp.b])
            S.op("dve", lambda: nc.vector.tensor_tensor(out=imp[:, :, :], in0=imp[:, :, :], in1=At[:, :, :], op=ALU.mult),
                 r=[imp.b, At.b], w=[imp.b])
            S.op("dve", lambda: nc.vector.tensor_tensor(out=imp[:, :, :], in0=imp[:, :, :], in1=Bt[:, :, :], op=ALU.add),
                 r=[imp.b, Bt.b], w=[imp.b])
            for qt in range(16):
                S.op("dve", lambda: nc.vector.max(out=m8[:, 0:8], in_=imp[:, qt, :]), r=[imp.b], w=[m8.b])
                S.op("dve", lambda: nc.vector.match_replace(out=tmp32[:, :], in_to_replace=m8[:, 0:8], in_values=imp[:, qt, :],
                                                            imm_value=-3e38), r=[imp.b, m8.b], w=[tmp32.b])
                S.op("dve", lambda: nc.vector.max(out=m8[:, 8:16], in_=tmp32[:, :]), r=[tmp32.b], w=[m8.b])
                S.op("dve", lambda: nc.vector.tensor_scalar(out=sel[:, qt, :], in0=imp[:, qt, :], scalar1=m8[:, 15:16], scalar2=None,
                                                            op0=ALU.is_ge), r=[imp.b, m8.b], w=[sel.b])
            transpose_to(K, sel, [sel[:, qt, :] for qt in range(16)], selT,
                         lambda g0, n: selT[:, g0 * 128:(g0 + n) * 128], 32)
            for j in range(4):
                hh = g * 4 + j
                qrT = qrTs[j]
                for qc in range(4):
                    ob = [K.ps[4 + jj] for jj in range(4)]
                    nk = 4 * qc + 4
                    def s1(kt):
                        nonlocal ctr
                        st = K.ps[ctr % 2]
                        mb = K.ps[2 + ctr % 2]
                        pt = pts[ctr % 4]
                        ctr += 1
                        S.op("pe", lambda: nc.tensor.matmul(st[:, 0:512], lhsT=ksT[:, kt * 128:(kt + 1) * 128],
                                                            rhs=qrT[:, qc * 512:(qc + 1) * 512], start=True, stop=True),
                             r=[ksT.b, qrT.b], w=[st.b])
                        S.op("pe", lambda: nc.tensor.matmul(mb[:, 0:512], lhsT=Eall[:, kt, :], rhs=selT[:, qc * 512:(qc + 1) * 512],
                                                            start=True, stop=True), r=[Eall.b, selT.b], w=[mb.b])
                        j0 = max(0, kt - 4 * qc)
                        S.op("act", lambda: nc.scalar.activation(out=pt[:, j0 * 128:512], in_=st[:, j0 * 128:512], func=AF.Exp,
                                                                 scale=scale), r=[st.b], w=[pt.b])
                        S.op("dve", lambda: nc.vector.tensor_tensor(out=pt[:, j0 * 128:512], in0=pt[:, j0 * 128:512],
                                                                    in1=mb[:, j0 * 128:512], op=ALU.mult), r=[pt.b, mb.b], w=[pt.b])
                        if kt >= 4 * qc:
                            S.op("dve", lambda: nc.vector.tensor_tensor(out=pt[:, j0 * 128:(j0 + 1) * 128],
                                                                        in0=pt[:, j0 * 128:(j0 + 1) * 128],
                                                                        in1=K.tri_le[:, :], op=ALU.mult),
                                 r=[pt.b, K.tri_le.b], w=[pt.b])
                        return (kt, pt, j0)

                    def s2(c):
                        kt, pt, j0 = c
                        for jj in range(j0, 4):
                            qt = 4 * qc + jj
                            S.op("pe", lambda: nc.tensor.matmul(ob[jj][:, 0:65], lhsT=pt[:, jj * 128:(jj + 1) * 128],
                                                                rhs=vs[:, kt, :], start=(kt == 0), stop=(kt == qt)),
                                 r=[pt.b, vs.b], w=[ob[jj].b])
                    pend = []
                    for kt in range(nk):
                        pend.append(s1(kt))
                        if len(pend) > 2:
                            s2(pend.pop(0))
                    while pend:
                        s2(pend.pop(0))
                    for jj in range(4):
                        qt = 4 * qc + jj
                        nsa_combine(K, ob[jj], rds, otmps, comb, j, qt, gts, 16 + hh)
                def w1(qt):
                    nonlocal ctr
                    kts = list(range(max(0, qt - 4), qt + 1))
                    stA = K.ps[ctr % 2]
                    stB = K.ps[2 + ctr % 2]
                    ptA = pts[ctr % 4]
                    ptB = mts[ctr % 2]
                    ctr += 1
                    for i, kt in enumerate(kts):
                        st = stA if i < 4 else stB
                        S.op("pe", lambda: nc.tensor.matmul(st[:, (i % 4) * 128:(i % 4 + 1) * 128], lhsT=kwT[:, kt * 128:(kt + 1) * 128],
                                                            rhs=qrT[:, qt * 128:(qt + 1) * 128], start=True, stop=True),
                             r=[kwT.b, qrT.b], w=[st.b])
                    na = min(4, len(kts))
                    S.op("act", lambda: nc.scalar.activation(out=ptA[:, 0:na * 128], in_=stA[:, 0:na * 128], func=AF.Exp, scale=scale),
                         r=[stA.b], w=[ptA.b])
                    if len(kts) == 5:
                        S.op("act", lambda: nc.scalar.activation(out=ptB[:, 0:128], in_=stB[:, 0:128], func=AF.Exp, scale=scale),
                             r=[stB.b], w=[ptB.b])
                    for i, kt in enumerate(kts):
                        pt = ptA if i < 4 else ptB
                        sl = slice((i % 4) * 128, (i % 4 + 1) * 128)
                        if kt == qt:
                            S.op("dve", lambda: nc.vector.tensor_tensor(out=pt[:, sl], in0=pt[:, sl], in1=K.tri_le[:, :], op=ALU.mult),
                                 r=[pt.b, K.tri_le.b], w=[pt.b])
                        elif kt == qt - 4:
                            S.op("dve", lambda: nc.vector.tensor_tensor(out=pt[:, sl], in0=pt[:, sl], in1=K.tri_gt[:, :], op=ALU.mult),
                                 r=[pt.b, K.tri_gt.b], w=[pt.b])
                    return (qt, kts, ptA, ptB)

                def w2(c):
                    qt, kts, ptA, ptB = c
                    ob = K.ps[4 + qt % 4]
                    for i, kt in enumerate(kts):
                        pt = ptA if i < 4 else ptB
                        sl = slice((i % 4) * 128, (i % 4 + 1) * 128)
                        S.op("pe", lambda: nc.tensor.matmul(ob[:, 0:65], lhsT=pt[:, sl], rhs=vw[:, kt, :], start=(i == 0),
                                                            stop=(i == len(kts) - 1)), r=[pt.b, vw.b], w=[ob.b])
                    nsa_combine(K, ob, rds, otmps, comb, j, qt, gts, 32 + hh)
                pend = []
                for qt in range(16):
                    pend.append(w1(qt))
                    if len(pend) > 1:
                        w2(pend.pop(0))
                while pend:
                    w2(pend.pop(0))
                S.dma("sp", lambda: nc.sync.dma_start(
                    out=d["attn"][:, hh * 64:(hh + 1) * 64].rearrange("(q p) e -> p q e", p=128), in_=comb[:, j, :, :]),
                    r=[comb.b], w=[K.db("attn", s, qq, hh) for qq in range(4)])
        S.barrier()


def nsa_combine(K, ob, rds, otmps, comb, j, qt, gts, gcol):
    nc, S = K.nc, K.S
    rd = rds[K.rdi % 6]
    otmp = otmps[K.rdi % 6]
    K.rdi += 1
    S.op("dve", lambda: nc.vector.reciprocal(out=rd[:, 1:2], in_=ob[:, 64:65]), r=[ob.b], w=[rd.b])
    S.op("dve", lambda: nc.vector.tensor_scalar(out=otmp[:, :], in0=ob[:, 0:64], scalar1=rd[:, 1:2], scalar2=gts[:, qt, gcol:gcol + 1],
                                                op0=ALU.mult, op1=ALU.mult), r=[ob.b, rd.b, gts.b], w=[otmp.b])
    S.op("dve", lambda: nc.vector.tensor_tensor(out=comb[:, j, qt, :], in0=comb[:, j, qt, :], in1=otmp[:, :], op=ALU.add),
         r=[comb.b, otmp.b], w=[comb.b])

def build(nseq=4, stages=("a1", "a2", "a3"), dbg=(), ntiles=NT):
    nc = bass.Bass("TRN2", target_bir_lowering=False)
    K = Ctx()
    K.nc = nc
    K.uid = 0
    K.evi = 0
    K.ntiles = ntiles
    es = ExitStack()
    K.es = es
    d = {}
    K.d = d

    def din(name, shape, dtype=F32):
        d[name] = nc.dram_tensor(name, list(shape), dtype, kind="ExternalInput").ap()

    def dscr(name, shape, dtype=F32):
        kind = "ExternalOutput" if name in dbg else "Internal"
        if name + "_in" in dbg:
            kind = "ExternalInput"
        d[name] = nc.dram_tensor(name, list(shape), dtype, kind=kind).ap()

    din("x", [nseq, SEQ, D])
    din("a_w_in", [1024, 800])
    din("a_q_norm", [1, 512])
    din("a_kv_norm", [1, 256])
    din("a_w_q_up", [512, 1536])
    din("a_w_kv_up", [256, 2048])
    din("a_w_o", [1024, 1024])
    din("s_w_kv", [1024, 1536])
    din("b_w_in", [1024, 1072])
    din("b_w_o", [1024, 1024])
    for nm in ("k", "v"):
        din("s_cmp_%s_w1" % nm, [2048, 256])
        din("s_cmp_%s_b1" % nm, [128, 2])
        din("s_cmp_%s_w2" % nm, [256, 64])
        din("s_cmp_pos_%sT" % nm, [64, 32])
    din("p_w_q", [2, 1024, 2048])
    din("p_skT", [2, 128, 16, 128])
    din("p_uv0", [16384, 2048])
    din("p_uv1", [16384, 2048])
    din("ln_g", [4, 1024])
    din("ln_b", [4, 1024])
    hc = host_consts()
    for k, v in hc.items():
        din(k, v.shape)
    d["out"] = nc.dram_tensor("out", [nseq, SEQ, D], F32, kind="ExternalOutput").ap()
    dscr("qnT", [16, 64, SEQ], BF16)
    dscr("qpT", [16, 32, SEQ], BF16)
    dscr("knT", [16, 64, SEQ], BF16)
    dscr("kpT", [32, SEQ], BF16)
    dscr("vp", [16, SEQ, 65], BF16)
    dscr("attn", [SEQ, 1024])
    dscr("h1", [SEQ, 1024])
    dscr("h2", [SEQ, 1024])
    dscr("p_uvb0", [16384, 2048], BF16)
    dscr("p_uvb1", [16384, 2048], BF16)
    dscr("h3", [SEQ, 1024])
    dscr("gates", [SEQ, 48])
    dscr("bqT", [16, 64, SEQ], BF16)
    dscr("bqrT", [16, 64, SEQ], BF16)
    dscr("bkcT", [8, 64, SEQ])
    dscr("bkrT", [8, 64, SEQ], BF16)
    dscr("bvp", [2, 4, SEQ, 65], BF16)
    dscr("kcmpT", [4, 64, 128], BF16)
    dscr("vcmp", [4, 128, 97], BF16)

    with es:
        S = Sched(nc, es)
        K.S = S
        K.ps = [Tile(es.enter_context(nc.psum_tensor("ps%d" % i, [128, 512], F32)), "ps%d" % i) for i in range(8)]
        K.bank_i = 0

        K.bank_list = list(range(8))

        def bank():
            b = K.ps[K.bank_list[K.bank_i % len(K.bank_list)]]
            K.bank_i += 1
            return b
        K.bank = bank
        K.dbufs = {}

        def db(*key):
            if key not in K.dbufs:
                K.dbufs[key] = Buf(str(key))
            return K.dbufs[key]
        K.db = db
        gl = Phase(K)
        gl.es = es
        K.ident = gl.tile("ident", [128, 128])
        K.tri_le = gl.tile("tri_le", [128, 128])
        K.tri_gt = gl.tile("tri_gt", [128, 128])
        K.eps_ln = gl.tile("eps_ln", [128, 1])
        K.eps_rms = gl.tile("eps_rms", [128, 1])
        for nm in ("ident", "tri_le", "tri_gt"):
            tl = getattr(K, nm)
            S.dma("sp", lambda tl=tl, nm=nm: nc.sync.dma_start(out=tl[:, :], in_=d[nm]), w=[tl.b])
        S.op("dve", lambda: nc.vector.memset(K.eps_ln[:, :], LN_EPS), w=[K.eps_ln.b])
        S.op("dve", lambda: nc.vector.memset(K.eps_rms[:, :], RMS_EPS), w=[K.eps_rms.b])

        if "p0" in stages or "p1" in stages:
            prologue_tables(K)
        for s in range(nseq):
            if "a1" in stages:
                phase_a1(K, s)
            if "a2" in stages:
                phase_a2(K, s)
            if "a3" in stages:
                dstname = "h1" if "h1" in dbg or len(stages) > 3 else "h1"
                phase_oproj_ln(K, s, d["a_w_o"], lambda t0: d["x"][s, t0:t0 + 128, :], lambda t: [],
                               d["ln_g"][0:1, :], d["ln_b"][0:1, :],
                               lambda t0: d["h1"][t0:t0 + 128, :], lambda t: K.db("h1", s, t), "a")
            if "p0" in stages:
                phase_peer(K, s, 0, lambda t0: d["h1"][t0:t0 + 128, :], lambda t: [K.db("h1", s, t)],
                           lambda t0: d["h2"][t0:t0 + 128, :], lambda t: K.db("h2", s, t), ntiles=K.ntiles)
            if "b0" in stages:
                phase_b0(K, s)
            if "b1" in stages:
                phase_b1(K, s)
            if "b2" in stages:
                phase_b2(K, s)
            if "b3" in stages:
                phase_oproj_ln(K, s, d["b_w_o"], lambda t0: d["h2"][t0:t0 + 128, :], lambda t: [K.db("h2", s, t)],
                               d["ln_g"][2:3, :], d["ln_b"][2:3, :],
                               lambda t0: d["h3"][t0:t0 + 128, :], lambda t: K.db("h3", s, t), "b")
            if "p1" in stages:
                phase_peer(K, s, 1, lambda t0: d["h3"][t0:t0 + 128, :], lambda t: [K.db("h3", s, t)],
                           lambda t0: d["out"][s, t0:t0 + 128, :], lambda t: K.db("out", s, t), ntiles=K.ntiles)
        S.barrier()
    K.hc = hc
    return nc, K


ALL_STAGES = ("a1", "a2", "a3", "p0", "b0", "b1", "b2", "b3", "p1")
_CACHE = {}


def _f32(a):
    return np.ascontiguousarray(np.asarray(a), dtype=np.float32)


def kernel(x, a_w_in, a_q_norm, a_kv_norm, a_w_q_up, a_w_kv_up, a_w_o, b_w_in, b_w_o,
           s_w_kv, s_cmp_pos_k, s_cmp_pos_v, s_cmp_k_w1, s_cmp_k_b1, s_cmp_k_w2,
           s_cmp_v_w1, s_cmp_v_b1, s_cmp_v_w2, p_w_q, p_subkeys, p_u, p_v, ln_g, ln_b):
    x = np.asarray(x)
    B = x.shape[0]
    if "nc" not in _CACHE:
        _CACHE["nc"] = build(nseq=B // NCORES, stages=ALL_STAGES)
    nc, K = _CACHE["nc"]
    p_subkeys = np.asarray(p_subkeys)
    p_u = np.asarray(p_u)
    p_v = np.asarray(p_v)
    w = {
        "a_w_in": _f32(np.asarray(a_w_in)[0]), "a_q_norm": _f32(np.asarray(a_q_norm).reshape(1, 512)),
        "a_kv_norm": _f32(np.asarray(a_kv_norm).reshape(1, 256)), "a_w_q_up": _f32(np.asarray(a_w_q_up)[0]),
        "a_w_kv_up": _f32(np.asarray(a_w_kv_up)[0]), "a_w_o": _f32(np.asarray(a_w_o)[0]),
        "b_w_in": _f32(np.asarray(b_w_in)[0]), "b_w_o": _f32(np.asarray(b_w_o)[0]), "s_w_kv": _f32(s_w_kv),
        "s_cmp_k_w1": _f32(s_cmp_k_w1), "s_cmp_v_w1": _f32(s_cmp_v_w1),
        "s_cmp_k_w2": _f32(s_cmp_k_w2), "s_cmp_v_w2": _f32(s_cmp_v_w2),
        "s_cmp_k_b1": _f32(np.asarray(s_cmp_k_b1).reshape(2, 128).T), "s_cmp_v_b1": _f32(np.asarray(s_cmp_v_b1).reshape(2, 128).T),
        "s_cmp_pos_kT": _f32(np.asarray(s_cmp_pos_k).T), "s_cmp_pos_vT": _f32(np.asarray(s_cmp_pos_v).T),
        "p_w_q": _f32(p_w_q),
        "p_skT": _f32(np.stack([p_subkeys[l].reshape(16, 128, 128).transpose(2, 0, 1) for l in range(2)])),
        "p_uv0": _f32(np.concatenate([p_u[0], p_v[0]], axis=1)), "p_uv1": _f32(np.concatenate([p_u[1], p_v[1]], axis=1)),
        "ln_g": _f32(np.asarray(ln_g).reshape(4, 1024)), "ln_b": _f32(np.asarray(ln_b).reshape(4, 1024)),
    }
    w.update({k: _f32(v) for k, v in K.hc.items()})
    nper = B // NCORES
    in_maps = []
    for c in range(NCORES):
        m = dict(w)
        m["x"] = _f32(x[c * nper:(c + 1) * nper])
        in_maps.append(m)
    res = run_bass_kernel_spmd(nc, in_maps, core_ids=list(range(NCORES)))
    out = np.empty((B, SEQ, D), dtype=np.float32)
    for c in range(NCORES):
        out[c * nper:(c + 1) * nper] = np.asarray(res.results[c]["out"]).reshape(nper, SEQ, D)
    return out
```
